# Optimizing a Trainium2 kernel written in Bass

```python
import math
import jax
import jax.numpy as jnp
from jax import lax
import numpy as np

D_MODEL = 2048
BATCH = 4
SEQ = 2048
DEPTH = 2

CTX_LEN = 256
GRID_W = 64
N_SUB = 3
D_FF = 5632
FFN_RES = 0.5
HGRN_HEADS = 8
HGRN_DK = 128
HGRN_DV = 128
DN_HEADS = 8
DN_DK = 128
DN_DV = 128
DN_CONV = 5
CHUNK = 64
EPS = 1e-6
KA = HGRN_HEADS * HGRN_DK
WA = HGRN_HEADS * HGRN_DV
KB = DN_HEADS * DN_DK
WB = DN_HEADS * DN_DV
QKV_B = 2 * KB + WB
IN_SPLITS = (KA, KA, KA, WA, WA, QKV_B, WB, 2 * DN_HEADS, 2 * DN_HEADS, D_MODEL, D_MODEL)
P_IN = 3 * KA + 2 * WA + QKV_B + WB + 4 * DN_HEADS + 2 * D_MODEL

kernel_name = 'hybrid_hgrn2_gdn_macaron_dit'


def _rms_norm(u, g):
    uf = u.astype(jnp.float32)
    y = uf * lax.rsqrt(jnp.mean(uf * uf, axis=-1, keepdims=True) + EPS)
    return (y * g.astype(jnp.float32)).astype(u.dtype)


def _l2norm(u):
    uf = u.astype(jnp.float32)
    return uf * lax.rsqrt(jnp.sum(uf * uf, axis=-1, keepdims=True) + EPS)


def _heads(u, n_heads):
    bsz, length, _ = u.shape
    return u.reshape(bsz, length, n_heads, -1).transpose(0, 2, 1, 3)


def _unheads(o):
    bsz, nh, length, d = o.shape
    return o.transpose(0, 2, 1, 3).reshape(bsz, length, nh * d)


def _adaln(cvec, w, b):
    m = jax.nn.silu(cvec) @ w + b
    m = m.reshape(cvec.shape[:-1] + (N_SUB, 3, 1, D_MODEL))
    return jnp.moveaxis(jnp.moveaxis(m, -4, 0), -3, 1)


def _ffn_sublayer(s, g, shift, scale, gate, wg, wu, wd):
    h = _rms_norm(s, g) * (1 + scale) + shift
    return s + FFN_RES * gate * ((jax.nn.silu(h @ wg) * (h @ wu)) @ wd)


def _split_cols(p):
    out = []
    start = 0
    for size in IN_SPLITS:
        out.append(p[..., start:start + size])
        start += size
    return out


def _short_conv(u, w, grid):
    bsz, length, ch = u.shape
    if grid:
        rows = length // GRID_W
        u = u.reshape(bsz * rows, GRID_W, ch)
    y = lax.conv_general_dilated(u, w.astype(u.dtype)[:, None, :], window_strides=(1,),
                                 padding=[(DN_CONV // 2, DN_CONV // 2)],
                                 dimension_numbers=('NWC', 'WIO', 'NWC'), feature_group_count=ch)
    return y.reshape(bsz, length, ch)


def _hgrn_gate(f_logit, lb):
    ff = f_logit.astype(jnp.float32)
    logf = jnp.log(lb + (1.0 - lb) * jax.nn.sigmoid(ff))
    k = (1.0 - lb) * jax.nn.sigmoid(-ff)
    return logf, k


def _masked_exp(mask, z):
    return jnp.where(mask, jnp.exp(jnp.where(mask, z, 0.0)), 0.0)


def _hgrn2_scan(q, k, v, logf, s0):
    bsz, nh, length, _ = q.shape
    n = length // CHUNK
    tril = jnp.tril(jnp.ones((CHUNK, CHUNK), dtype=bool))

    def to_chunks(u):
        return jnp.moveaxis(u.reshape(bsz, nh, n, CHUNK, u.shape[-1]), 2, 0)

    def step(s, inp):
        qc, kc, vc, lc = inp
        b = jnp.cumsum(lc, axis=2)
        decay = _masked_exp(tril[:, :, None], b[:, :, :, None, :] - b[:, :, None, :, :])
        attn = jnp.einsum('bhtk,bhsk,bhtsk->bhts', qc, kc, decay)
        o = (jnp.einsum('bhtk,bhkv->bhtv', qc * jnp.exp(b), s)
             + jnp.einsum('bhts,bhsv->bhtv', attn, vc))
        b_last = b[:, :, -1:, :]
        s = (jnp.exp(b_last[:, :, 0, :])[..., None] * s
             + jnp.einsum('bhsk,bhsv->bhkv', kc * jnp.exp(b_last - b), vc))
        return s, o

    s_fin, o = lax.scan(step, s0, (to_chunks(q), to_chunks(k), to_chunks(v), to_chunks(logf)))
    return jnp.moveaxis(o, 0, 2).reshape(bsz, nh, length, -1), s_fin


def _gated_delta_scan(q, k, v, g, beta, s0):
    bsz, nh, length, _ = q.shape
    dv = v.shape[-1]
    n = length // CHUNK
    q, k, v = (u.reshape(bsz, nh, n, CHUNK, u.shape[-1]) for u in (q, k, v))
    g, beta = (u.reshape(bsz, nh, n, CHUNK) for u in (g, beta))
    tril = jnp.tril(jnp.ones((CHUNK, CHUNK), dtype=bool))
    strict = jnp.tril(jnp.ones((CHUNK, CHUNK), dtype=bool), -1)
    gc = jnp.cumsum(g, axis=-1)
    gamma = _masked_exp(tril, gc[..., :, None] - gc[..., None, :])
    kb = k * beta[..., None]
    a = jnp.where(strict, jnp.einsum('bhntk,bhnsk->bhnts', kb, k) * gamma, 0.0)
    lhs = a + jnp.eye(CHUNK, dtype=a.dtype)
    rhs = jnp.concatenate([v * beta[..., None], kb * jnp.exp(gc)[..., None]], axis=-1)
    sol = lax.linalg.triangular_solve(lhs, rhs, left_side=True, lower=True, unit_diagonal=True)
    u_c, w_c = sol[..., :dv], sol[..., dv:]
    qk = jnp.einsum('bhntk,bhnsk->bhnts', q, k) * gamma
    qg = q * jnp.exp(gc)[..., None]
    kd = k * jnp.exp(gc[..., -1:] - gc)[..., None]
    dl = jnp.exp(gc[..., -1])

    def step(s, inp):
        u_n, w_n, qk_n, qg_n, kd_n, dl_n = inp
        v_new = u_n - jnp.einsum('bhtk,bhkv->bhtv', w_n, s)
        o = jnp.einsum('bhtk,bhkv->bhtv', qg_n, s) + jnp.einsum('bhts,bhsv->bhtv', qk_n, v_new)
        s = dl_n[..., None, None] * s + jnp.einsum('bhsk,bhsv->bhkv', kd_n, v_new)
        return s, o

    xs = tuple(jnp.moveaxis(t, 2, 0) for t in (u_c, w_c, qk, qg, kd, dl))
    s_fin, o = lax.scan(step, s0, xs)
    return jnp.moveaxis(o, 0, 2).reshape(bsz, nh, length, dv), s_fin


def _flip(arrs):
    return tuple(jnp.flip(t, axis=2) for t in arrs)


def _bidir(core, ctx_dirs, lat_dirs, s0):
    (cf, cb), (xf, xb) = ctx_dirs, lat_dirs
    oc_f, sc_f = core(*cf, s0)
    oc_b, sc_b = core(*_flip(cb), s0)
    ox_f, _ = core(*xf, sc_f)
    ox_b, _ = core(*_flip(xb), sc_b)
    return oc_f + jnp.flip(oc_b, axis=2), ox_f + jnp.flip(ox_b, axis=2)


def _mixer_project(h, w_in, conv_w, lb, a_log, dt_bias, grid):
    f32 = jnp.float32
    aq, af_f, af_b, av, ag, bqkv, bz, ba, bb, ga, gb = _split_cols(h @ w_in)
    qa = _heads(jax.nn.silu(aq), HGRN_HEADS).astype(f32)
    va = _heads(av, HGRN_HEADS).astype(f32)
    lf_f, ka_f = _hgrn_gate(af_f, lb[0])
    lf_b, ka_b = _hgrn_gate(af_b, lb[1])
    hg = ((qa, _heads(ka_f, HGRN_HEADS), va, _heads(lf_f, HGRN_HEADS)),
          (qa, _heads(ka_b, HGRN_HEADS), va, _heads(lf_b, HGRN_HEADS)))
    qkv = jax.nn.silu(_short_conv(bqkv, conv_w, grid))
    qb = _l2norm(_heads(qkv[..., :KB], DN_HEADS)) * (DN_DK ** -0.5)
    kb = _l2norm(_heads(qkv[..., KB:2 * KB], DN_HEADS))
    vb = _heads(qkv[..., 2 * KB:], DN_HEADS).astype(f32)
    bsz, length, _ = h.shape
    a_in = ba.astype(f32).reshape(bsz, length, 2, DN_HEADS)
    g = -jnp.exp(a_log.astype(f32)) * jax.nn.softplus(a_in + dt_bias.astype(f32))
    g = g.transpose(2, 0, 3, 1)
    beta = jax.nn.sigmoid(bb.astype(f32).reshape(bsz, length, 2, DN_HEADS)).transpose(2, 0, 3, 1)
    dn = ((qb, kb, vb, g[0], beta[0]), (qb, kb, vb, g[1], beta[1]))
    return hg, dn, (ag, bz, ga, gb)


def _merge(oa, ob, gates, hgrn_g, dn_g, w_br_a, w_br_b, w_out, dtype):
    ag, bz, ga, gb = gates
    ya = _unheads(_rms_norm(oa, hgrn_g)).astype(dtype) * jax.nn.silu(ag)
    yb = _unheads(_rms_norm(ob, dn_g)).astype(dtype) * jax.nn.silu(bz)
    y = jax.nn.sigmoid(ga) * (ya @ w_br_a) + jax.nn.sigmoid(gb) * (yb @ w_br_b)
    return y @ w_out


def _mixer(hc, hx, w_in, conv_w, lb, a_log, dt_bias, hgrn_g, dn_g, w_br_a, w_br_b, w_out, need_ctx):
    hg_c, dn_c, gt_c = _mixer_project(hc, w_in, conv_w, lb, a_log, dt_bias, False)
    hg_x, dn_x, gt_x = _mixer_project(hx, w_in, conv_w, lb, a_log, dt_bias, True)
    bsz = hx.shape[0]
    zero_a = jnp.zeros((bsz, HGRN_HEADS, HGRN_DK, HGRN_DV), jnp.float32)
    zero_b = jnp.zeros((bsz, DN_HEADS, DN_DK, DN_DV), jnp.float32)
    oa_c, oa_x = _bidir(_hgrn2_scan, hg_c, hg_x, zero_a)
    ob_c, ob_x = _bidir(_gated_delta_scan, dn_c, dn_x, zero_b)
    y_x = _merge(oa_x, ob_x, gt_x, hgrn_g, dn_g, w_br_a, w_br_b, w_out, hx.dtype)
    y_c = _merge(oa_c, ob_c, gt_c, hgrn_g, dn_g, w_br_a, w_br_b, w_out, hc.dtype) if need_ctx else None
    return y_c, y_x


def setup_inputs(seed: int = 0) -> dict:
    key = jax.random.key(seed)
    ks = jax.random.split(key, 22)
    f32 = jnp.float32
    D = D_MODEL

    def nrm(k, shape, fan_in):
        return jax.random.normal(k, shape, f32) * (fan_in ** -0.5)

    dt = jnp.exp(jax.random.uniform(ks[15], (DEPTH, 2, DN_HEADS), f32)
                 * (math.log(0.1) - math.log(0.001)) + math.log(0.001))
    return {
        'x': jax.random.normal(ks[0], (BATCH, SEQ, D), f32),
        'c': jax.random.normal(ks[1], (BATCH, D), f32),
        'ctx': jax.random.normal(ks[2], (BATCH, CTX_LEN, D), f32),
        'c_ctx': jax.random.normal(ks[3], (D,), f32),
        'w_ada': nrm(ks[4], (DEPTH, D, N_SUB * 3 * D), D) * 0.5,
        'b_ada': 0.02 * jax.random.normal(ks[5], (DEPTH, N_SUB * 3 * D), f32),
        'norm_g': 1.0 + 0.05 * jax.random.normal(ks[6], (DEPTH, N_SUB, D), f32),
        'final_norm_g': 1.0 + 0.05 * jax.random.normal(ks[7], (D,), f32),
        'ffn_w_gate': nrm(ks[8], (DEPTH, 2, D, D_FF), D),
        'ffn_w_up': nrm(ks[9], (DEPTH, 2, D, D_FF), D),
        'ffn_w_down': nrm(ks[10], (DEPTH, 2, D_FF, D), D_FF),
        'w_in': nrm(ks[11], (DEPTH, D, P_IN), D),
        'hgrn_lower_bounds': 0.5 * jax.random.normal(ks[12], (DEPTH, 2, KA), f32),
        'hgrn_norm_g': 1.0 + 0.05 * jax.random.normal(ks[13], (DEPTH, HGRN_DV), f32),
        'dn_conv_w': jax.random.normal(ks[14], (DEPTH, DN_CONV, QKV_B), f32) * (DN_CONV ** -0.5),
        'dn_a_log': jnp.log(jax.random.uniform(ks[16], (DEPTH, 2, DN_HEADS), f32, minval=1.0, maxval=16.0)),
        'dn_dt_bias': dt + jnp.log(-jnp.expm1(-dt)),
        'dn_norm_g': 1.0 + 0.05 * jax.random.normal(ks[17], (DEPTH, DN_DV), f32),
        'w_branch_a': nrm(ks[18], (DEPTH, WA, D), WA),
        'w_branch_b': nrm(ks[19], (DEPTH, WB, D), WB),
        'w_out': nrm(ks[20], (DEPTH, D, D), D),
    }


def reference(x, c, ctx, c_ctx, w_ada, b_ada, norm_g, final_norm_g, ffn_w_gate, ffn_w_up, ffn_w_down,
              w_in, hgrn_lower_bounds, hgrn_norm_g, dn_conv_w, dn_a_log, dn_dt_bias, dn_norm_g,
              w_branch_a, w_branch_b, w_out):
    lbs = jax.nn.softmax(hgrn_lower_bounds.astype(jnp.float32), axis=0)
    lb_all = jnp.cumsum(lbs, axis=0) - lbs[0]
    cx = ctx
    for l in range(DEPTH):
        last = l == DEPTH - 1
        mx = _adaln(c, w_ada[l], b_ada[l])
        mc = _adaln(c_ctx, w_ada[l], b_ada[l])
        x = _ffn_sublayer(x, norm_g[l, 0], mx[0, 0], mx[0, 1], mx[0, 2],
                          ffn_w_gate[l, 0], ffn_w_up[l, 0], ffn_w_down[l, 0])
        cx = _ffn_sublayer(cx, norm_g[l, 0], mc[0, 0], mc[0, 1], mc[0, 2],
                           ffn_w_gate[l, 0], ffn_w_up[l, 0], ffn_w_down[l, 0])
        hx = _rms_norm(x, norm_g[l, 1]) * (1 + mx[1, 1]) + mx[1, 0]
        hc = _rms_norm(cx, norm_g[l, 1]) * (1 + mc[1, 1]) + mc[1, 0]
        y_c, y_x = _mixer(hc, hx, w_in[l], dn_conv_w[l], lb_all[l], dn_a_log[l], dn_dt_bias[l],
                          hgrn_norm_g[l], dn_norm_g[l], w_branch_a[l], w_branch_b[l], w_out[l],
                          not last)
        x = x + mx[1, 2] * y_x
        x = _ffn_sublayer(x, norm_g[l, 2], mx[2, 0], mx[2, 1], mx[2, 2],
                          ffn_w_gate[l, 1], ffn_w_up[l, 1], ffn_w_down[l, 1])
        if not last:
            cx = cx + mc[1, 2] * y_c
            cx = _ffn_sublayer(cx, norm_g[l, 2], mc[2, 0], mc[2, 1], mc[2, 2],
                               ffn_w_gate[l, 1], ffn_w_up[l, 1], ffn_w_down[l, 1])
    return _rms_norm(x, final_norm_g)
```

```python
import numpy as np
from contextlib import ExitStack
import concourse.bass as bass
import concourse.mybir as mybir
from concourse.bass_utils import run_bass_kernel_spmd

F32 = mybir.dt.float32
F32R = mybir.dt.float32r
BF16 = mybir.dt.bfloat16
AF = mybir.ActivationFunctionType
ALU = mybir.AluOpType

D = 2048
NT = 2304
NCTX = 256
DFF = 5632
NKC = 16
NFC = 44
PIN = 13344
EPS = 1e-6
SB0 = 16512
SB1 = 229376


class Prog:
    ENG = ("pe", "act", "dve", "pool", "sp")

    def __init__(self, nc):
        self.nc = nc
        self.ops = []
        self.last_w = {}
        self.readers = {}

    def op(self, eng, fn, reads=(), writes=(), lane=None):
        i = len(self.ops)
        deps = set()
        for r in reads:
            w = self.last_w.get(r)
            if w is not None:
                deps.add(w)
        for w_ in writes:
            w = self.last_w.get(w_)
            if w is not None:
                deps.add(w)
            for rd in self.readers.get(w_, ()):
                deps.add(rd)
        deps.discard(i)
        for r in reads:
            self.readers.setdefault(r, []).append(i)
        for w_ in writes:
            self.last_w[w_] = i
            self.readers[w_] = []
        self.ops.append(dict(eng=eng, fn=fn, deps=deps, lane=lane, sig=lane is not None))
        return i

    def barrier(self):
        toks = [("__bar", e, len(self.ops)) for e in self.ENG]
        allres = list(self.last_w.keys())
        for e, t in zip(self.ENG, toks):
            self.op(e, None, reads=allres, writes=[t])
        for e in self.ENG:
            self.op(e, None, reads=toks, writes=[("__bar2", e)])
        self.last_w = {k: v for k, v in self.last_w.items() if k[0] == "__bar2"} if False else self.last_w

    def lanes(self):
        return sorted({o["lane"] for o in self.ops if o["lane"] is not None})

    def emit(self, sems):
        ops = self.ops
        for o in ops:
            for d in o["deps"]:
                dop = ops[d]
                if dop["lane"] is None:
                    if dop["eng"] == "pe" and o["eng"] == "pe" and o["lane"] is None:
                        continue
                    dop["sig"] = True
        cnt = {}
        for o in ops:
            if o["sig"]:
                key = o["lane"] if o["lane"] is not None else o["eng"]
                step = 16 if o["lane"] is not None else 1
                cnt[key] = cnt.get(key, 0) + step
                o["key"] = key
                o["val"] = cnt[key]
                o["step"] = step
        final = dict(cnt)
        per_eng = {e: [] for e in self.ENG}
        for o in ops:
            per_eng[o["eng"]].append(o)

        def run(ename, e):
            seen = {}
            for o in per_eng[ename]:
                waits = {}
                for d in o["deps"]:
                    dop = ops[d]
                    if not dop["sig"]:
                        continue
                    if dop["lane"] is None and dop["eng"] == "pe" and ename == "pe" and o["lane"] is None:
                        continue
                    k, v = dop["key"], dop["val"]
                    if v > waits.get(k, 0):
                        waits[k] = v
                for k, v in waits.items():
                    if v > seen.get(k, 0):
                        e.wait_ge(sems[k], v)
                        seen[k] = v
                if o["fn"] is None:
                    ins = e.nop() if o["sig"] else None
                else:
                    ins = o["fn"](e)
                if o["sig"]:
                    ins.then_inc(sems[o["key"]], o["step"])
            if ename == "sp":
                for k, v in final.items():
                    if v > seen.get(k, 0):
                        e.wait_ge(sems[k], v)

        with self.nc.Block() as block:
            @block.tensor
            def _(e):
                run("pe", e)

            @block.scalar
            def _(e):
                run("act", e)

            @block.vector
            def _(e):
                run("dve", e)

            @block.gpsimd
            def _(e):
                run("pool", e)

            @block.sync
            def _(e):
                run("sp", e)


class Arena:
    def __init__(self, nc):
        self.nc = nc
        self.off = SB0
        self.stack = []
        self.n = 0

    def push(self):
        self.stack.append(self.off)

    def pop(self):
        self.off = self.stack.pop()

    def tile(self, shape, dtype, name="t"):
        esz = 2 if dtype == BF16 else 4
        per = esz
        for s in shape[1:]:
            per *= s
        per = (per + 63) // 64 * 64
        assert self.off + per <= SB1, f"SBUF overflow allocating {name} {shape}: {self.off + per - SB1} bytes over"
        self.n += 1
        h = self.nc.alloc_sbuf_tensor_at(f"{name}_{self.n}", list(shape), dtype, offset=self.off)
        self.off += per
        return h


class Builder:
    def __init__(self, stages, dbg=()):
        self.stages = stages
        self.dbg = dbg
        self.heads = list(range(8))
        nc = self.nc = bass.Bass("TRN2", target_bir_lowering=False)
        self.P = Prog(nc)
        self.A = Arena(nc)
        self.uid = 0
        dt = nc.dram_tensor
        self.inp = {}
        specs = dict(
            x=[2048, D], c=[1, D], ctx=[NCTX, D], c_ctx=[1, D],
            w_ada=[2, D, 18432], b_ada=[2, 18432], norm_g=[2, 3, D], final_norm_g=[1, D],
            ffn_w_gate=[2, 2, D, DFF], ffn_w_up=[2, 2, D, DFF], ffn_w_down=[2, 2, DFF, D],
            w_in=[2, D, PIN], hgrn_lower_bounds=[2, 2, 1024], hgrn_norm_g=[2, 128],
            dn_conv_w=[2, 5, 3072], dn_a_log=[2, 2, 8], dn_dt_bias=[2, 2, 8], dn_norm_g=[2, 128],
            w_branch_a=[2, 1024, D], w_branch_b=[2, 1024, D], w_out=[2, D, D])
        for k, shp in specs.items():
            self.inp[k] = dt(k, shp, F32, kind="ExternalInput").ap()
        self.out = dt("out", [2048, D], F32, kind="ExternalOutput").ap()
        self.xT_d = dt("xT_d", [NKC, 128, NT], F32, kind="Internal").ap()
        self.hT_d = dt("hT_d", [NKC, 128, NT], BF16, kind="Internal").ap()
        self.yaT_d = dt("yaT_d", [8, 128, NT], BF16, kind="Internal").ap()
        self.ybT_d = dt("ybT_d", [8, 128, NT], BF16, kind="Internal").ap()
        self.dbg_out = {}
        self.ps = [nc.alloc_psum_tensor(f"ps{i}", [128, 512], F32) for i in range(8)]

    def tok(self, *a):
        return a

    def new(self, prefix):
        self.uid += 1
        return f"{prefix}{self.uid}"

    def consts(self):
        P, A, nc = self.P, self.A, self.nc
        self.ident = A.tile([128, 128], F32, "ident")
        self.ones_f = A.tile([128, 128], F32, "ones_f")
        self.ones_r = A.tile([128, 128], F32R, "ones_r")
        self.ident_b = A.tile([128, 128], BF16, "ident_b")
        self.epsc = A.tile([128, 1], F32, "epsc")
        P.op("pool", lambda e: e.memset(self.epsc[:], EPS), writes=["epsc"])
        P.op("pool", lambda e: e.memset(self.ones_f[:], 1.0), writes=["ones_f"])
        P.op("act", lambda e: e.activation(out=self.ones_r[:], in_=self.ones_f[:], func=AF.Copy), reads=["ones_f"], writes=["ones_r"])
        P.op("pool", lambda e: e.affine_select(out=self.ident[:], in_=self.ones_f[:], pattern=[[-1, 128]],
                                               compare_op=ALU.is_equal, fill=0.0, base=0, channel_multiplier=1),
             reads=["ones_f"], writes=["ident"])
        P.op("dve", lambda e: e.tensor_copy(out=self.ident_b[:], in_=self.ident[:]), reads=["ident"], writes=["ident_b"])

    def load_T(self, src2d, rows, dst_ap, dst_tok, scratch, ps_idx=7):
        P = self.P
        lane = "ldT"
        P.op("sp", lambda e: e.dma_start(out=scratch[0:rows, :], in_=src2d), writes=["ldT_s"], lane=lane)
        ps = self.ps[ps_idx]
        P.op("pe", lambda e: e.transpose(ps[:, 0:rows], scratch[0:rows, :], self.ident[0:rows, 0:rows]),
             reads=["ldT_s", "ident"], writes=[("ps", ps_idx)])
        P.op("dve", lambda e: e.tensor_copy(out=dst_ap, in_=ps[:, 0:rows]), reads=[("ps", ps_idx)], writes=[dst_tok])

    def phase_in(self):
        P, A = self.P, self.A
        A.push()
        tin = [A.tile([128, D], F32, "tin") for _ in range(2)]
        tout = [A.tile([128, NKC, 128], F32, "tout") for _ in range(2)]
        for t in range(NT // 128):
            s = t % 2
            src = self.inp["ctx"][t * 128:(t + 1) * 128, :] if t < 2 else self.inp["x"][(t - 2) * 128:(t - 1) * 128, :]
            P.op("sp", lambda e, s=s, src=src: e.dma_start(out=tin[s][:], in_=src), writes=[("tin", s)], lane=f"tin{s}")
            for q in range(4):
                bank = (t % 2) * 4 + q
                for j in range(4):
                    c = q * 4 + j
                    P.op("pe", lambda e, s=s, c=c, bank=bank, j=j: e.transpose(
                        self.ps[bank][:, j * 128:(j + 1) * 128], tin[s][:, c * 128:(c + 1) * 128], self.ident[:]),
                        reads=[("tin", s), "ident"], writes=[("ps", bank)])
                eng = "act" if q % 2 == 0 else "dve"
                if eng == "act":
                    P.op("act", lambda e, s=s, q=q, bank=bank: e.activation(
                        out=tout[s][:, q * 4:(q + 1) * 4, :], in_=self.ps[bank][:].rearrange("p (a b) -> p a b", a=4), func=AF.Copy),
                        reads=[("ps", bank)], writes=[("tout", s, q)])
                else:
                    P.op("dve", lambda e, s=s, q=q, bank=bank: e.tensor_copy(
                        out=tout[s][:, q * 4:(q + 1) * 4, :], in_=self.ps[bank][:].rearrange("p (a b) -> p a b", a=4)),
                        reads=[("ps", bank)], writes=[("tout", s, q)])
            P.op("sp", lambda e, s=s, t=t: e.dma_start(
                out=self.xT_d[:, :, t * 128:(t + 1) * 128].rearrange("c p n -> p c n"), in_=tout[s][:]),
                reads=[("tout", s, q) for q in range(4)], writes=[("xT", c, t) for c in range(NKC)], lane=f"tout{s}")
        A.pop()
        P.barrier()

    def phase_adaln(self):
        P, A = self.P, self.A
        self.modT = [A.tile([128, 144, 2], F32, f"modT{l}") for l in range(2)]
        self.gT = A.tile([128, 2 * 3 * NKC + NKC], F32, "gT")
        A.push()
        scr = A.tile([128, 128], F32, "scr")
        sc = A.tile([128, NKC, 2], F32, "sc")
        craw = A.tile([128, 2 * NKC], F32, "craw")
        one2 = A.tile([1, 2], F32, "one2")
        P.op("pool", lambda e: e.memset(one2[:], 1.0), writes=["one2"])
        self.load_T(self.inp["c"].rearrange("o (c p) -> (o c) p", p=128), NKC, craw[:, 0:NKC], "craw0", scr)
        self.load_T(self.inp["c_ctx"].rearrange("o (c p) -> (o c) p", p=128), NKC, craw[:, NKC:2 * NKC], "craw1", scr)
        for v in range(2):
            P.op("act", lambda e, v=v: e.activation(out=sc[:, :, v], in_=craw[:, v * NKC:(v + 1) * NKC], func=AF.Silu),
                 reads=[f"craw{v}"], writes=[("sc", v)])
        self.load_T(self.inp["norm_g"].rearrange("l s (c p) -> (l s c) p", p=128), 96, self.gT[:, 0:96], "gT0", scr)
        self.load_T(self.inp["final_norm_g"].rearrange("o (c p) -> (o c) p", p=128), NKC, self.gT[:, 96:112], "gT1", scr)
        wsl = [A.tile([128, NKC, 512], F32, "wada") for _ in range(2)]
        brow = [A.tile([1, 512], F32, "brow") for _ in range(2)]
        for l in range(2):
            wv = self.inp["w_ada"][l].rearrange("(kc p) n -> p kc n", p=128)
            bank = 6
            for s in range(36):
                sl = s % 2
                P.op("sp", lambda e, sl=sl, s=s, wv=wv: e.dma_start(out=wsl[sl][:], in_=wv[:, :, s * 512:(s + 1) * 512]),
                     writes=[("wada", sl)], lane=f"wada{sl}")
                P.op("sp", lambda e, sl=sl, s=s, l=l: e.dma_start(out=brow[sl][:], in_=self.inp["b_ada"][l:l + 1, s * 512:(s + 1) * 512]),
                     writes=[("brow", sl)], lane=f"brow{sl}")
                for j in range(4):
                    ch = s * 4 + j
                    for k in range(NKC):
                        P.op("pe", lambda e, sl=sl, j=j, k=k, ch=ch: e.matmul(
                            self.ps[bank][:, ch * 2:ch * 2 + 2], lhsT=wsl[sl][:, k, j * 128:(j + 1) * 128], rhs=sc[:, k, :],
                            start=(k == 0), stop=False),
                            reads=[("wada", sl), ("sc", 0), ("sc", 1)], writes=[("ps", bank)])
                    P.op("pe", lambda e, sl=sl, j=j, ch=ch: e.matmul(
                        self.ps[bank][:, ch * 2:ch * 2 + 2], lhsT=brow[sl][0:1, j * 128:(j + 1) * 128], rhs=one2[0:1, :],
                        start=False, stop=True),
                        reads=[("brow", sl), "one2"], writes=[("ps", bank)])
            P.op("dve", lambda e, l=l: e.tensor_copy(out=self.modT[l][:].rearrange("p a b -> p (a b)"), in_=self.ps[bank][:, 0:288]),
                 reads=[("ps", bank)], writes=[("modT", l)])
        A.pop()
        self.gmod = A.tile([128, 2, 3, 2, NKC], F32, "gmod")
        self.shift = A.tile([128, 2, 3, 2, NKC], F32, "shift")
        self.gate = A.tile([128, 2, 3, 2, NKC], F32, "gate")
        for l in range(2):
            for sub in range(3):
                for v in range(2):
                    base = sub * 3 * NKC
                    m = self.modT[l]
                    g = self.gT[:, (l * 3 + sub) * NKC:(l * 3 + sub + 1) * NKC]
                    P.op("dve", lambda e, l=l, sub=sub, v=v, m=m, g=g, base=base: e.scalar_tensor_tensor(
                        out=self.gmod[:, l, sub, v, :], in0=m[:, base + NKC:base + 2 * NKC, v], scalar=1.0, in1=g,
                        op0=ALU.add, op1=ALU.mult), reads=[("modT", l), "gT0"], writes=[("gmod", l, sub, v)])
                    P.op("dve", lambda e, l=l, sub=sub, v=v, m=m, base=base: e.tensor_copy(
                        out=self.shift[:, l, sub, v, :], in_=m[:, base:base + NKC, v]),
                        reads=[("modT", l)], writes=[("shift", l, sub, v)])
                    fac = 1.0 if sub == 1 else 0.5
                    P.op("dve", lambda e, l=l, sub=sub, v=v, m=m, base=base, fac=fac: e.tensor_scalar(
                        out=self.gate[:, l, sub, v, :], in0=m[:, base + 2 * NKC:base + 3 * NKC, v], scalar1=fac, scalar2=None,
                        op0=ALU.mult), reads=[("modT", l)], writes=[("gate", l, sub, v)])
        P.barrier()

    def groups(self):
        return [(0, 256, 1), (256, 1024, 0), (1280, 1024, 0)]

    def prologue(self, t0, G, gm_ap, sh_ap, hT, hT_tok, to_bf16=True):
        P, A = self.P, self.A
        A.push()
        xc = [A.tile([128, G], F32, "xc") for _ in range(4)]
        sq = [A.tile([128, G], F32R, "sq") for _ in range(2)]
        tmp = [A.tile([128, G], F32, "tmp") for _ in range(2)]
        rstd = A.tile([128, G], F32, "rstd")
        nh = (G + 511) // 512
        hw = G // nh
        u = self.new("pg")
        for c in range(NKC):
            s = c % 4
            P.op("sp", lambda e, s=s, c=c: e.dma_start(out=xc[s][:], in_=self.xT_d[c, :, t0:t0 + G]),
                 reads=[("xT", c, t) for t in range(t0 // 128, (t0 + G) // 128)], writes=[(u, "xc", s)], lane=f"xc{s}")
            q = c % 2
            P.op("act", lambda e, s=s, q=q: e.activation(out=sq[q][:], in_=xc[s][:], func=AF.Square),
                 reads=[(u, "xc", s)], writes=[(u, "sq", q)])
            for h in range(nh):
                P.op("pe", lambda e, q=q, h=h, c=c: e.matmul(self.ps[h][:, 0:hw], lhsT=self.ones_r[:], rhs=sq[q][:, h * hw:(h + 1) * hw],
                                                            start=(c == 0), stop=(c == NKC - 1)),
                     reads=[(u, "sq", q), "ones_r"], writes=[("ps", h)])
        for h in range(nh):
            P.op("act", lambda e, h=h: e.activation(out=rstd[:, h * hw:(h + 1) * hw], in_=self.ps[h][:, 0:hw], func=AF.Ln,
                                                    bias=self.epsc[:, 0:1], scale=1.0 / D),
                 reads=[("ps", h), "epsc"], writes=[(u, "rstd", h)])
            P.op("act", lambda e, h=h: e.activation(out=rstd[:, h * hw:(h + 1) * hw], in_=rstd[:, h * hw:(h + 1) * hw], func=AF.Exp, scale=-0.5),
                 reads=[(u, "rstd", h)], writes=[(u, "rstd", h)])
        for c in range(NKC):
            s = c % 4
            q = c % 2
            P.op("sp", lambda e, s=s, c=c: e.dma_start(out=xc[s][:], in_=self.xT_d[c, :, t0:t0 + G]),
                 reads=[("xT", c, t) for t in range(t0 // 128, (t0 + G) // 128)], writes=[(u, "xc", s)], lane=f"xc{s}")
            P.op("dve", lambda e, s=s, q=q: e.tensor_tensor(out=tmp[q][:], in0=xc[s][:], in1=rstd[:], op=ALU.mult),
                 reads=[(u, "xc", s)] + [(u, "rstd", h) for h in range(nh)], writes=[(u, "tmp", q)])
            if sh_ap is not None:
                P.op("act", lambda e, q=q, c=c: e.activation(out=hT[:, c, 0:G], in_=tmp[q][:], func=AF.Identity,
                                                             bias=sh_ap[:, c:c + 1], scale=gm_ap[:, c:c + 1]),
                     reads=[(u, "tmp", q)], writes=[(hT_tok, c)])
            else:
                P.op("act", lambda e, q=q, c=c: e.activation(out=hT[:, c, 0:G], in_=tmp[q][:], func=AF.Identity,
                                                             scale=gm_ap[:, c:c + 1]),
                     reads=[(u, "tmp", q)], writes=[(hT_tok, c)])
        A.pop()
        P.barrier()

    def ffn(self, l, f):
        P, A = self.P, self.A
        sub = 0 if f == 0 else 2
        wgv = self.inp["ffn_w_gate"][l, f].rearrange("(kc p) n -> p kc n", p=128)
        wuv = self.inp["ffn_w_up"][l, f].rearrange("(kc p) n -> p kc n", p=128)
        wdv = self.inp["ffn_w_down"][l, f].rearrange("(kc p) n -> p kc n", p=128)
        A.push()
        Gmax = 1024
        hT = A.tile([128, NKC, Gmax], BF16, "hT")
        wg = [A.tile([128, NKC, 128], BF16, "wg") for _ in range(3)]
        wu = [A.tile([128, NKC, 128], BF16, "wu") for _ in range(3)]
        wd = [A.tile([128, NFC, 128], BF16, "wd") for _ in range(2)]
        for (t0, G, v) in self.groups():
            self.ffn_group(l, f, sub, t0, G, v, hT, wg, wu, wd, wgv, wuv, wdv)
        A.pop()

    def ffn_group(self, l, f, sub, t0, G, v, hT, wg, wu, wd, wgv, wuv, wdv):
        P, A = self.P, self.A
        u = self.new("ffn")
        self.prologue(t0, G, self.gmod[:, l, sub, v, :], self.shift[:, l, sub, v, :], hT, (u, "hT"))
        if "dump_ffn" in self.stages and t0 == 0 and l == 0 and f == 0:
            self.dump_tile("dbg_hT", hT[:, :, 0:G], [128, NKC, G], [((u, "hT"), c) for c in range(NKC)])
        A.push()
        actT = A.tile([128, NFC, G], BF16, "actT")
        nh = (G + 511) // 512
        hw = G // nh
        sg = [A.tile([128, hw], F32, "sg") for _ in range(2 * nh)]
        xc = [A.tile([128, G], F32, "xe") for _ in range(3)]
        hreads = [((u, "hT"), c) for c in range(NKC)]

        def load_gu(j):
            s = j % 3
            P.op("pool", lambda e, s=s, j=j: e.dma_start(out=wg[s][:], in_=wgv[:, :, j * 128:(j + 1) * 128]),
                 writes=[("wg", s)], lane=f"wg{s}")
            P.op("pool", lambda e, s=s, j=j: e.dma_start(out=wu[s][:], in_=wuv[:, :, j * 128:(j + 1) * 128]),
                 writes=[("wu", s)], lane=f"wu{s}")

        def load_d(n):
            s = n % 2
            P.op("pool", lambda e, s=s, n=n: e.dma_start(out=wd[s][:], in_=wdv[:, :, n * 128:(n + 1) * 128]),
                 writes=[("wd", s)], lane=f"wd{s}")

        load_gu(0)
        load_gu(1)
        for j in range(NFC):
            if j + 2 < NFC:
                load_gu(j + 2)
            elif j + 2 == NFC:
                load_d(0)
            else:
                load_d(1)
            s = j % 3
            par = j % 2
            for h in range(nh):
                bg = par * 4 + h
                bu = par * 4 + 2 + h
                for k in range(NKC):
                    P.op("pe", lambda e, s=s, k=k, h=h, bg=bg: e.matmul(
                        self.ps[bg][:, 0:hw], lhsT=wg[s][:, k, :], rhs=hT[:, k, h * hw:(h + 1) * hw],
                        start=(k == 0), stop=(k == NKC - 1)),
                        reads=[("wg", s)] + (hreads if k == 0 else []), writes=[("ps", bg)])
                for k in range(NKC):
                    P.op("pe", lambda e, s=s, k=k, h=h, bu=bu: e.matmul(
                        self.ps[bu][:, 0:hw], lhsT=wu[s][:, k, :], rhs=hT[:, k, h * hw:(h + 1) * hw],
                        start=(k == 0), stop=(k == NKC - 1)),
                        reads=[("wu", s)] + (hreads if k == 0 else []), writes=[("ps", bu)])
                si = par * nh + h
                P.op("act", lambda e, si=si, bg=bg: e.activation(out=sg[si][:], in_=self.ps[bg][:, 0:hw], func=AF.Silu),
                     reads=[("ps", bg)], writes=[(u, "sg", si)])
                P.op("dve", lambda e, si=si, bu=bu, j=j, h=h: e.tensor_tensor(
                    out=actT[:, j, h * hw:(h + 1) * hw], in0=sg[si][:], in1=self.ps[bu][:, 0:hw], op=ALU.mult),
                    reads=[(u, "sg", si), ("ps", bu)], writes=[(u, "actT", j, h)])
        if "dump_ffn" in self.stages and t0 == 0 and l == 0 and f == 0:
            self.dump_tile("dbg_actT", actT[:], [128, NFC, G], [(u, "actT", j, h) for j in range(NFC) for h in range(nh)])
            self.dump_tile("dbg_wg", wg[(NFC - 1) % 3][:], [128, NKC, 128], [("wg", (NFC - 1) % 3)])
        for n in range(NKC):
            s = n % 2
            xs = n % 3
            P.op("sp", lambda e, xs=xs, n=n: e.dma_start(out=xc[xs][:], in_=self.xT_d[n, :, t0:t0 + G]),
                 reads=[("xT", n, t) for t in range(t0 // 128, (t0 + G) // 128)], writes=[(u, "xe", xs)], lane=f"xe{xs}")
            for h in range(nh):
                by = (n % 2) * 2 + h
                for j in range(NFC):
                    P.op("pe", lambda e, s=s, j=j, h=h, by=by: e.matmul(
                        self.ps[by][:, 0:hw], lhsT=wd[s][:, j, :], rhs=actT[:, j, h * hw:(h + 1) * hw],
                        start=(j == 0), stop=(j == NFC - 1)),
                        reads=[("wd", s), (u, "actT", j, h)], writes=[("ps", by)])
                P.op("dve", lambda e, xs=xs, h=h, by=by, n=n: e.scalar_tensor_tensor(
                    out=xc[xs][:, h * hw:(h + 1) * hw], in0=self.ps[by][:, 0:hw], scalar=self.gate[:, l, sub, v, n:n + 1],
                    in1=xc[xs][:, h * hw:(h + 1) * hw], op0=ALU.mult, op1=ALU.add),
                    reads=[("ps", by), (u, "xe", xs)], writes=[(u, "xe", xs)])
            P.op("sp", lambda e, xs=xs, n=n: e.dma_start(out=self.xT_d[n, :, t0:t0 + G], in_=xc[xs][:]),
                 reads=[(u, "xe", xs)], writes=[("xT", n, t) for t in range(t0 // 128, (t0 + G) // 128)], lane=f"xs{xs}")
            if n + 2 < NKC:
                load_d(n + 2)
        A.pop()
        P.barrier()

    def nb(self):
        self.bank_rr = (getattr(self, "bank_rr", -1) + 1) % 8
        return self.bank_rr

    def phase_small(self):
        P, A = self.P, self.A
        self.lb = A.tile([128, 2, 16], F32, "lb")
        self.oml = A.tile([128, 2, 16], F32, "oml")
        self.hng = A.tile([128, 2], F32, "hng")
        self.dng = A.tile([128, 2], F32, "dng")
        self.convT = A.tile([128, 2, 120], F32, "convT")
        self.alog = A.tile([128, 2, 16], F32, "alog")
        self.dtb = A.tile([128, 2, 16], F32, "dtb")
        self.maskF = A.tile([128, 128], F32, "maskF")
        self.maskB = A.tile([128, 128], F32, "maskB")
        self.mLT = A.tile([128, 128], F32, "mLT")
        self.mGT = A.tile([128, 128], F32, "mGT")
        self.bigP = [A.tile([128, 128], F32, "bigP") for _ in range(4)]
        self.bigN = [A.tile([128, 128], F32, "bigN") for _ in range(4)]
        self.maskF4 = A.tile([128, 4, 128], F32, "maskF4")
        self.maskB4 = A.tile([128, 4, 128], F32, "maskB4")
        A.push()
        scr = A.tile([128, 128], F32, "scr")
        raw = A.tile([128, 32], F32, "lbraw")
        self.load_T(self.inp["hgrn_lower_bounds"].rearrange("l d (c p) -> (l d c) p", p=128), 32, raw[:, :], "lbraw", scr)
        P.op("pool", lambda e: e.memset(self.lb[:, 0, :], 0.0), writes=["lb0"])
        P.op("dve", lambda e: e.tensor_tensor(out=self.lb[:, 1, :], in0=raw[:, 16:32], in1=raw[:, 0:16], op=ALU.subtract),
             reads=["lbraw"], writes=["lb1"])
        P.op("act", lambda e: e.activation(out=self.lb[:, 1, :], in_=self.lb[:, 1, :], func=AF.Sigmoid), reads=["lb1"], writes=["lb1"])
        P.op("dve", lambda e: e.tensor_scalar(out=self.oml[:].rearrange("p a b -> p (a b)"), in0=self.lb[:].rearrange("p a b -> p (a b)"),
                                              scalar1=-1.0, scalar2=1.0, op0=ALU.mult, op1=ALU.add),
             reads=["lb0", "lb1"], writes=["oml"])
        self.load_T(self.inp["hgrn_norm_g"], 2, self.hng[:, :], "hng", scr)
        self.load_T(self.inp["dn_norm_g"], 2, self.dng[:, :], "dng", scr)
        for l in range(2):
            self.load_T(self.inp["dn_conv_w"][l].rearrange("j (c p) -> (j c) p", p=128), 120, self.convT[:, l, :], ("convT", l), scr)
        P.op("sp", lambda e: e.dma_start(out=self.alog[:].rearrange("p a b -> p (a b)"),
                                         in_=self.inp["dn_a_log"].rearrange("l d h -> (l d h)").partition_broadcast(128)),
             writes=["alog"], lane="smallc")
        P.op("sp", lambda e: e.dma_start(out=self.dtb[:].rearrange("p a b -> p (a b)"),
                                         in_=self.inp["dn_dt_bias"].rearrange("l d h -> (l d h)").partition_broadcast(128)),
             writes=["dtb"], lane="smallc")
        P.op("act", lambda e: e.activation(out=self.alog[:].rearrange("p a b -> p (a b)"), in_=self.alog[:].rearrange("p a b -> p (a b)"), func=AF.Exp),
             reads=["alog"], writes=["alog"])
        P.op("dve", lambda e: e.tensor_scalar(out=self.alog[:].rearrange("p a b -> p (a b)"), in0=self.alog[:].rearrange("p a b -> p (a b)"),
                                              scalar1=-1.0, scalar2=None, op0=ALU.mult), reads=["alog"], writes=["alog"])
        P.op("pool", lambda e: e.affine_select(out=self.maskF[:], in_=self.ones_f[:], pattern=[[1, 128]],
                                               compare_op=ALU.is_ge, fill=0.0, base=0, channel_multiplier=-1),
             reads=["ones_f"], writes=["maskF"])
        P.op("pool", lambda e: e.memset(self.maskF[0:64, 64:128], 0.0), reads=["maskF"], writes=["maskF"])
        P.op("pool", lambda e: e.affine_select(out=self.maskB[:], in_=self.ones_f[:], pattern=[[-1, 128]],
                                               compare_op=ALU.is_ge, fill=0.0, base=0, channel_multiplier=1),
             reads=["ones_f"], writes=["maskB"])
        P.op("pool", lambda e: e.memset(self.maskB[64:128, 0:64], 0.0), reads=["maskB"], writes=["maskB"])
        P.op("pool", lambda e: e.tensor_tensor(out=self.mLT[:], in0=self.maskB[:], in1=self.ident[:], op=ALU.subtract), reads=["maskB", "ident"], writes=["mLT"])
        P.op("pool", lambda e: e.tensor_tensor(out=self.mGT[:], in0=self.maskF[:], in1=self.ident[:], op=ALU.subtract), reads=["maskF", "ident"], writes=["mGT"])
        for i, (m, mt) in enumerate(((self.mLT, "mLT"), (self.mGT, "mGT"), (self.maskF, "maskF"), (self.maskB, "maskB"))):
            P.op("dve", lambda e, i=i, m=m: e.tensor_scalar(out=self.bigP[i][:], in0=m[:], scalar1=-30000.0, scalar2=30000.0, op0=ALU.mult, op1=ALU.add),
                 reads=[mt], writes=["bigm"])
            P.op("dve", lambda e, i=i, m=m: e.tensor_scalar(out=self.bigN[i][:], in0=m[:], scalar1=30000.0, scalar2=-30000.0, op0=ALU.mult, op1=ALU.add),
                 reads=[mt], writes=["bigm"])
        for j in range(4):
            P.op("pool", lambda e, j=j: e.tensor_copy(out=self.maskF4[:, j, :], in_=self.maskF[:]), reads=["maskF"], writes=["mask4"])
            P.op("pool", lambda e, j=j: e.tensor_copy(out=self.maskB4[:, j, :], in_=self.maskB[:]), reads=["maskB"], writes=["mask4"])
        A.pop()
        P.barrier()

    def mixer_prep(self, l):
        A = self.A
        A.push()
        hT = A.tile([128, NKC, 1024], BF16, "hTm")
        for (t0, G, v) in self.groups():
            self.mixer_prep_group(l, t0, G, v, hT)
        A.pop()

    def mixer_prep_group(self, l, t0, G, v, hT):
        P = self.P
        u = self.new("mp")
        self.prologue(t0, G, self.gmod[:, l, 1, v, :], self.shift[:, l, 1, v, :], hT, (u, "hT"))
        P.op("sp", lambda e: e.dma_start(out=self.hT_d[:, :, t0:t0 + G].rearrange("c p n -> p c n"), in_=hT[:, :, 0:G]),
             reads=[((u, "hT"), c) for c in range(NKC)], writes=[("hTd", t) for t in range(t0 // 128, (t0 + G) // 128)], lane="hTd")
        P.barrier()

    TG = [(0, 512), (512, 512), (1024, 512), (1536, 512), (2048, 256)]

    def proj_fm(self, w, hg, G, evac):
        P = self.P
        bank = self.nb()
        wt, wtok = w
        hgt, hgtok = hg
        for k in range(NKC):
            P.op("pe", lambda e, k=k: e.matmul(self.ps[bank][:, 0:G], lhsT=wt[:, k, :], rhs=hgt[:, k, 0:G],
                                               start=(k == 0), stop=(k == NKC - 1)),
                 reads=[wtok, hgtok], writes=[("ps", bank)])
        evac(bank)

    def proj_tm(self, w, hg, G, ncols, evac):
        P = self.P
        wt, wtok = w
        hgt, hgtok = hg
        for ti in range(G // 128):
            bank = self.nb()
            for k in range(NKC):
                P.op("pe", lambda e, k=k, ti=ti, bank=bank: e.matmul(self.ps[bank][:, 0:ncols], lhsT=hgt[:, k, ti * 128:(ti + 1) * 128],
                                                                   rhs=wt[:, k, 0:ncols], start=(k == 0), stop=(k == NKC - 1)),
                     reads=[wtok, hgtok], writes=[("ps", bank)])
            evac(bank, ti)

    def load_w_in(self, l, col, ncols, tile_, tok, lane):
        wv = self.inp["w_in"][l].rearrange("(kc p) n -> p kc n", p=128)
        self.P.op("pool", lambda e: e.dma_start(out=tile_[:, :, 0:ncols], in_=wv[:, :, col:col + ncols]), writes=[tok], lane=lane)

    def load_hg(self, hg, slot, t0, G, u):
        self.P.op("sp", lambda e: e.dma_start(out=hg[slot][:, :, 0:G], in_=self.hT_d[:, :, t0:t0 + G].rearrange("c p n -> p c n")),
                  reads=[("hTd", t) for t in range(t0 // 128, (t0 + G) // 128)], writes=[(u, "hg", slot)], lane=f"hg{slot}")

    def hgrn_head(self, l, h):
        P, A = self.P, self.A
        u = self.new("hg")
        A.push()
        qa = A.tile([128, NT], F32, "qa")
        lf = [A.tile([128, NT], F32, "lf") for _ in range(2)]
        kk = [A.tile([128, NT], F32, "kk") for _ in range(2)]
        ga = A.tile([128, NT], F32, "ga")
        Vt = A.tile([128, NT // 128, 128], BF16, "Vt")
        oT = A.tile([128, NT], F32, "oT")
        self.hgrn_proj(l, h, u, qa, lf, kk, ga, Vt)
        if "hg_stop1" in self.stages:
            self.dump_tile("dbg_qa", qa[:], [128, NT], [(u, "qa", gi) for gi in range(5)])
            self.dump_tile("dbg_lf0", lf[0][:], [128, NT], [(u, "lf", 0, gi) for gi in range(5)])
            self.dump_tile("dbg_kk1", kk[1][:], [128, NT], [(u, "kk", 1, gi) for gi in range(5)])
            self.dump_tile("dbg_Vt", Vt[:], [128, NT // 128, 128], [(u, "Vt", t) for t in range(18)])
            A.pop()
            P.barrier()
            return
        for d in range(2):
            self.hgrn_dir(l, h, u, d, qa, lf[d], kk[d], Vt, oT)
            if "hg_stop2" in self.stages:
                self.dump_tile("dbg_oT0", oT[:], [128, NT], [(u, "oT", q) for q in range(5)])
                A.pop()
                P.barrier()
                return
        self.head_out(u, oT, ga, "ga", self.hng[:, l:l + 1], self.yaT_d[h], ("ya", h))
        if "dump_oa" in self.stages and l == 0:
            self.dump_tile(f"dbg_oa{h}", oT[:], [128, NT], [(u, "oT", q) for q in range(5)])
        A.pop()
        P.barrier()

    def hgrn_proj(self, l, h, u, qa, lf, kk, ga, Vt):
        P, A = self.P, self.A
        A.push()
        cols = [0 + 128 * h, 1024 + 128 * h, 2048 + 128 * h, 3072 + 128 * h, 4096 + 128 * h]
        w = [A.tile([128, NKC, 128], BF16, "wm") for _ in range(5)]
        hg = [A.tile([128, NKC, 512], BF16, "hgm") for _ in range(2)]
        sgm = [A.tile([128, 512], F32, "sgm") for _ in range(2)]
        for i in range(5):
            self.load_w_in(l, cols[i], 128, w[i], (u, "w", i), f"wm{i}")
        for gi, (t0, G) in enumerate(self.TG):
            self.hgrn_proj_group(l, h, u, gi, t0, G, w, hg, sgm, qa, lf, kk, ga, Vt)
        A.pop()
        P.barrier()

    def hgrn_proj_group(self, l, h, u, gi, t0, G, w, hg, sgm, qa, lf, kk, ga, Vt):
        P = self.P
        slot = gi % 2
        self.load_hg(hg, slot, t0, G, u)
        hgs = (hg[slot], (u, "hg", slot))
        tl = (u, "g", gi)
        self.proj_fm((w[0], (u, "w", 0)), hgs, G, lambda bank: P.op(
            "act", lambda e: e.activation(out=qa[:, t0:t0 + G], in_=self.ps[bank][:, 0:G], func=AF.Silu),
            reads=[("ps", bank)], writes=[(u, "qa", gi)]))
        for d in range(2):
            idx = d * 8 + h

            def ev(bank, d=d, idx=idx):
                P.op("act", lambda e: e.activation(out=sgm[d][:, 0:G], in_=self.ps[bank][:, 0:G], func=AF.Sigmoid),
                     reads=[("ps", bank)], writes=[(u, "sgm", d)])
                P.op("dve", lambda e: e.tensor_scalar(out=sgm[d][:, 0:G], in0=sgm[d][:, 0:G], scalar1=self.oml[:, l, idx:idx + 1],
                                                      scalar2=self.lb[:, l, idx:idx + 1], op0=ALU.mult, op1=ALU.add),
                     reads=[(u, "sgm", d), "oml", "lb0", "lb1"], writes=[(u, "sgm", d)])
                P.op("dve", lambda e: e.tensor_scalar(out=kk[d][:, t0:t0 + G], in0=sgm[d][:, 0:G], scalar1=-1.0, scalar2=1.0,
                                                      op0=ALU.mult, op1=ALU.add),
                     reads=[(u, "sgm", d)], writes=[(u, "kk", d, gi)])
                P.op("act", lambda e: e.activation(out=lf[d][:, t0:t0 + G], in_=sgm[d][:, 0:G], func=AF.Ln),
                     reads=[(u, "sgm", d)], writes=[(u, "lf", d, gi)])
            self.proj_fm((w[1 + d], (u, "w", 1 + d)), hgs, G, ev)
        self.proj_fm((w[4], (u, "w", 4)), hgs, G, lambda bank: P.op(
            "act", lambda e: e.activation(out=ga[:, t0:t0 + G], in_=self.ps[bank][:, 0:G], func=AF.Silu),
            reads=[("ps", bank)], writes=[(u, "ga", gi)]))
        self.proj_tm((w[3], (u, "w", 3)), hgs, G, 128, lambda bank, ti: P.op(
            "dve", lambda e: e.tensor_copy(out=Vt[:, t0 // 128 + ti, :], in_=self.ps[bank][:, 0:128]),
            reads=[("ps", bank)], writes=[(u, "Vt", t0 // 128 + ti)]))

    def chain_pos(self, d, c):
        if d == 0:
            return c
        return 3 - c if c < 4 else 4 + (35 - c)

    def hgrn_dir(self, l, h, u0, d, qa, lf, kk, Vt, oT):
        P, A = self.P, self.A
        u = self.new("hd")
        NC = NT // 64
        A.push()
        X = A.tile([128, NT], F32, "X")
        dd = A.tile([128, NT], F32, "dd")
        ex = A.tile([128, NT], F32, "ex")
        Qt = A.tile([128, NT], BF16, "Qt")
        Kt = A.tile([128, NT], BF16, "Kt")
        Qi = A.tile([128, NT], BF16, "Qi")
        KuT = A.tile([128, NT], BF16, "KuT")
        Kutok = A.tile([128, NT // 128, 128], BF16, "Kutok")
        ATm = A.tile([128, NT // 128, 128], BF16, "ATm")
        U = A.tile([128, 128, NC], F32, "U")
        decb = A.tile([128, 128, NC], F32, "decb")
        R = A.tile([128, 128, NC], F32, "R")
        S = A.tile([128, NC, 128], BF16, "S")
        refs = A.tile([128, 4, NC], F32, "refs")
        allg = lambda name, dd_=None: [(u0, name, d, gi) for gi in range(5)] if dd_ is None else None
        lf_r = [(u0, "lf", d, gi) for gi in range(5)]
        kk_r = [(u0, "kk", d, gi) for gi in range(5)]
        qa_r = [(u0, "qa", gi) for gi in range(5)]
        decf = decb[:].rearrange("p a b -> p (a b)")
        P.op("pool", lambda e: e.memset(decf[:, 0:NT], 1.0), writes=[(u, "decb")])
        P.op("dve", lambda e: e.tensor_tensor_scan(out=X[:], data0=decf[:, 0:NT], data1=lf[:], initial=0.0,
                                                   op0=ALU.mult, op1=ALU.add), reads=lf_r + [(u, "decb")], writes=[(u, "X")])
        X3 = X[:].rearrange("p (c t) -> p c t", t=64)
        lf3 = lf[:].rearrange("p (c t) -> p c t", t=64)
        if d == 0:
            P.op("dve", lambda e: e.tensor_copy(out=refs[:, 0, :], in_=X3[:, :, 31]), reads=[(u, "X")], writes=[(u, "rm")])
            P.op("dve", lambda e: e.tensor_tensor(out=refs[:, 1, :], in0=X3[:, :, 0], in1=lf3[:, :, 0], op=ALU.subtract),
                 reads=[(u, "X")] + lf_r, writes=[(u, "r0")])
            P.op("dve", lambda e: e.tensor_copy(out=refs[:, 2, :], in_=X3[:, :, 63]), reads=[(u, "X")], writes=[(u, "r1")])
        else:
            P.op("dve", lambda e: e.tensor_copy(out=refs[:, 2, :], in_=X3[:, :, 63]), reads=[(u, "X")], writes=[(u, "r1")])
            P.op("dve", lambda e: e.tensor_tensor(out=X[:], in0=X[:], in1=lf[:], op=ALU.subtract),
                 reads=[(u, "X"), (u, "r1")] + lf_r, writes=[(u, "X")])
            P.op("dve", lambda e: e.tensor_copy(out=refs[:, 0, :], in_=X3[:, :, 32]), reads=[(u, "X")], writes=[(u, "rm")])
            P.op("dve", lambda e: e.tensor_copy(out=refs[:, 1, :], in_=X3[:, :, 0]), reads=[(u, "X")], writes=[(u, "r0")])
        sig = 1.0 if d == 0 else -1.0
        rA = 1 if d == 0 else 2
        rB = 2 if d == 0 else 1
        dd3 = dd[:].rearrange("p (c t) -> p c t", t=64)

        def derive(ri, rtok, outs):
            P.op("dve", lambda e: e.tensor_tensor(out=dd3, in0=X3, in1=refs[:, ri, :].unsqueeze(2).to_broadcast([128, NC, 64]),
                                                  op=ALU.subtract), reads=[(u, "X"), rtok], writes=[(u, "dd")])
            for (sg_, src, sreads, dst, dtok) in outs:
                P.op("act", lambda e, sg_=sg_: e.activation(out=ex[:], in_=dd[:], func=AF.Exp, scale=sg_),
                     reads=[(u, "dd")], writes=[(u, "ex")])
                P.op("dve", lambda e, src=src, dst=dst: e.tensor_tensor(out=dst[:], in0=src[:], in1=ex[:], op=ALU.mult),
                     reads=[(u, "ex")] + sreads, writes=[dtok])
        derive(0, (u, "rm"), [(sig, qa, qa_r, Qt, (u, "Qt")), (-sig, kk, kk_r, Kt, (u, "Kt"))])
        derive(rA, (u, "r0") if rA == 1 else (u, "r1"), [(sig, qa, qa_r, Qi, (u, "Qi"))])
        derive(rB, (u, "r0") if rB == 1 else (u, "r1"), [(-sig, kk, kk_r, KuT, (u, "KuT"))])
        P.op("dve", lambda e: e.tensor_tensor(out=refs[:, 3, :], in0=refs[:, 2, :], in1=refs[:, 1, :], op=ALU.subtract),
             reads=[(u, "r0"), (u, "r1")], writes=[(u, "dec")])
        P.op("act", lambda e: e.activation(out=refs[:, 3, :], in_=refs[:, 3, :], func=AF.Exp), reads=[(u, "dec")], writes=[(u, "dec")])
        if d == 0:
            P.op("dve", lambda e: e.tensor_copy(out=decb[:], in_=refs[:, 3, :].unsqueeze(1).to_broadcast([128, 128, NC])),
                 reads=[(u, "dec")], writes=[(u, "decb")])
        else:
            for c in range(NC):
                pos = self.chain_pos(1, c)
                P.op("act", lambda e, c=c, pos=pos: e.activation(out=refs[:, 0, pos:pos + 1], in_=refs[:, 3, c:c + 1], func=AF.Copy),
                     reads=[(u, "dec"), (u, "Qt"), (u, "Kt")], writes=[(u, "decc")])
            P.op("dve", lambda e: e.tensor_copy(out=decb[:], in_=refs[:, 0, :].unsqueeze(1).to_broadcast([128, 128, NC])),
                 reads=[(u, "decc")], writes=[(u, "decb")])
        P.op("dve", lambda e: e.memset(decb[:, :, 0], 0.0), reads=[(u, "decb")], writes=[(u, "decb")])
        for q in range(5):
            bank = self.nb()
            nt_ = 4 if q < 4 else 2
            psb = self.ps[bank][:].bitcast(BF16)
            for j in range(nt_):
                t = q * 4 + j
                P.op("pe", lambda e, j=j, t=t, psb=psb: e.transpose(psb[:, j * 128:(j + 1) * 128], KuT[:, t * 128:(t + 1) * 128], self.ident_b[:]),
                     reads=[(u, "KuT"), "ident_b"], writes=[("ps", bank)])
            P.op("act", lambda e, q=q, nt_=nt_, psb=psb: e.activation(
                out=Kutok[:, q * 4:q * 4 + nt_, :], in_=psb[:, 0:nt_ * 128].rearrange("p (a b) -> p a b", b=128), func=AF.Copy),
                reads=[("ps", bank)], writes=[(u, "Kutok", q)])
        mask4 = self.maskF4 if d == 0 else self.maskB4
        P.op("pool", lambda e: e.memset(ATm[:], 0.0), writes=[(u, "ATz")])
        for q in range(5):
            bank = self.nb()
            nt_ = 4 if q < 4 else 2
            for j in range(nt_):
                t = q * 4 + j
                P.op("pe", lambda e, j=j, t=t, bank=bank: e.matmul(self.ps[bank][:, j * 128:(j + 1) * 128], lhsT=Kt[:, t * 128:(t + 1) * 128],
                                                                  rhs=Qt[:, t * 128:(t + 1) * 128], start=True, stop=True),
                     reads=[(u, "Kt"), (u, "Qt")], writes=[("ps", bank)])
            P.op("dve", lambda e, q=q, nt_=nt_, bank=bank: e.copy_predicated(
                out=ATm[:, q * 4:q * 4 + nt_, :], mask=mask4[:, 0:nt_, :].bitcast(mybir.dt.uint32),
                data=self.ps[bank][:, 0:nt_ * 128].rearrange("p (a b) -> p a b", b=128)),
                reads=[("ps", bank), "mask4", (u, "ATz")], writes=[(u, "ATm", q)])
        for c in range(NC):
            t, half = c // 2, c % 2
            if c % 8 == 0:
                bankpair = (self.nb(), self.nb())
            bank = bankpair[half]
            j = (c // 2) % 4
            r0_, r1_ = half * 64, half * 64 + 64
            P.op("pe", lambda e, t=t, j=j, r0_=r0_, r1_=r1_, bank=bank: e.matmul(
                self.ps[bank][:, j * 128:(j + 1) * 128], lhsT=Kutok[r0_:r1_, t, :], rhs=Vt[r0_:r1_, t, :], start=True, stop=True),
                reads=[(u, "Kutok", t // 4), (u0, "Vt", t)], writes=[("ps", bank)])
            pos = self.chain_pos(d, c)
            eng = "act" if c % 2 == 0 else "dve"
            if eng == "act":
                P.op("act", lambda e, j=j, pos=pos, bank=bank: e.activation(out=U[:, :, pos], in_=self.ps[bank][:, j * 128:(j + 1) * 128], func=AF.Copy),
                     reads=[("ps", bank)], writes=[(u, "U", c)])
            else:
                P.op("dve", lambda e, j=j, pos=pos, bank=bank: e.tensor_copy(out=U[:, :, pos], in_=self.ps[bank][:, j * 128:(j + 1) * 128]),
                     reads=[("ps", bank)], writes=[(u, "U", c)])
        P.op("dve", lambda e: e.tensor_tensor_scan(out=R[:].rearrange("p a b -> p (a b)"), data0=decb[:].rearrange("p a b -> p (a b)"),
                                                   data1=U[:].rearrange("p a b -> p (a b)"), initial=0.0, op0=ALU.mult, op1=ALU.add),
             reads=[(u, "decb")] + [(u, "U", c) for c in range(NC)], writes=[(u, "R")])
        P.op("pool", lambda e: e.memset(S[:, 0, :], 0.0), writes=[(u, "S0")])
        P.op("act", lambda e: e.activation(out=S[:, 1:NC, :], in_=R[:, :, 0:NC - 1].rearrange("p d c -> p c d"), func=AF.Copy),
             reads=[(u, "R")], writes=[(u, "S")])
        for q in range(5):
            bank = self.nb()
            nt_ = 4 if q < 4 else 2
            for j in range(nt_):
                t = q * 4 + j
                P.op("pe", lambda e, j=j, t=t, bank=bank: e.matmul(self.ps[bank][:, j * 128:(j + 1) * 128], lhsT=Vt[:, t, :], rhs=ATm[:, t, :],
                                                                  start=True, stop=False),
                     reads=[(u0, "Vt", t), (u, "ATm", q)], writes=[("ps", bank)])
                for half in range(2):
                    c = 2 * t + half
                    pos = self.chain_pos(d, c)
                    P.op("pe", lambda e, j=j, c=c, pos=pos, half=half, bank=bank: e.matmul(
                        self.ps[bank][:, j * 128 + half * 64:j * 128 + half * 64 + 64], lhsT=S[:, pos, :], rhs=Qi[:, c * 64:(c + 1) * 64],
                        start=False, stop=(half == 1)),
                        reads=[(u, "S"), (u, "S0"), (u, "Qi")], writes=[("ps", bank)])
            c0_, c1_ = q * 512, q * 512 + nt_ * 128
            if d == 0:
                P.op("act", lambda e, c0_=c0_, c1_=c1_, nt_=nt_, bank=bank: e.activation(out=oT[:, c0_:c1_], in_=self.ps[bank][:, 0:nt_ * 128], func=AF.Copy),
                     reads=[("ps", bank)], writes=[(u0, "oT", q)])
            else:
                P.op("dve", lambda e, c0_=c0_, c1_=c1_, nt_=nt_, bank=bank: e.tensor_tensor(out=oT[:, c0_:c1_], in0=self.ps[bank][:, 0:nt_ * 128],
                                                                                      in1=oT[:, c0_:c1_], op=ALU.add),
                     reads=[("ps", bank), (u0, "oT", q)], writes=[(u0, "oT", q)])
        A.pop()
        P.barrier()

    def head_out(self, u, oT, gate_t, gate_tok, g_ap, dst_d, dtok):
        P, A = self.P, self.A
        A.push()
        sq = [A.tile([128, 512], F32R, "hsq") for _ in range(2)]
        rs = [A.tile([128, 512], F32, "hrs") for _ in range(2)]
        yo = [A.tile([128, 512], BF16, "hyo") for _ in range(2)]
        for gi, (t0, G) in enumerate(self.TG):
            s = gi % 2
            bank = self.nb()
            P.op("act", lambda e, s=s, t0=t0, G=G: e.activation(out=sq[s][:, 0:G], in_=oT[:, t0:t0 + G], func=AF.Square),
                 reads=[(u, "oT", gi)], writes=[(u, "hsq", s)])
            P.op("pe", lambda e, s=s, G=G, bank=bank: e.matmul(self.ps[bank][:, 0:G], lhsT=self.ones_r[:], rhs=sq[s][:, 0:G], start=True, stop=True),
                 reads=[(u, "hsq", s), "ones_r"], writes=[("ps", bank)])
            P.op("act", lambda e, s=s, G=G, bank=bank: e.activation(out=rs[s][:, 0:G], in_=self.ps[bank][:, 0:G], func=AF.Ln, bias=self.epsc[:, 0:1], scale=1.0 / 128),
                 reads=[("ps", bank), "epsc"], writes=[(u, "hrs", s)])
            P.op("act", lambda e, s=s, G=G: e.activation(out=rs[s][:, 0:G], in_=rs[s][:, 0:G], func=AF.Exp, scale=-0.5), reads=[(u, "hrs", s)], writes=[(u, "hrs", s)])
            P.op("dve", lambda e, s=s, t0=t0, G=G: e.scalar_tensor_tensor(out=rs[s][:, 0:G], in0=rs[s][:, 0:G], scalar=g_ap, in1=gate_t[:, t0:t0 + G],
                                                                        op0=ALU.mult, op1=ALU.mult),
                 reads=[(u, "hrs", s), (u, gate_tok, gi), "hng", "dng"], writes=[(u, "hrs", s)])
            P.op("dve", lambda e, s=s, t0=t0, G=G: e.tensor_tensor(out=yo[s][:, 0:G], in0=oT[:, t0:t0 + G], in1=rs[s][:, 0:G], op=ALU.mult),
                 reads=[(u, "hrs", s), (u, "oT", gi)], writes=[(u, "hyo", s)])
            P.op("sp", lambda e, s=s, t0=t0, G=G: e.dma_start(out=dst_d[:, t0:t0 + G], in_=yo[s][:, 0:G]),
                 reads=[(u, "hyo", s)], writes=[(dtok, gi)], lane=f"hyo{s}")
        A.pop()

    def dn_scalars(self, l):
        P, A = self.P, self.A
        NTL = NT // 128
        self.dn_beta = A.tile([128, NTL, 16], F32, "dn_beta")
        self.dn_gc = A.tile([128, NTL, 16], F32, "dn_gc")
        self.dn_egc = A.tile([128, NTL, 16], F32, "dn_egc")
        self.dn_egl = A.tile([128, NTL, 16], F32, "dn_egl")
        self.dn_bg = A.tile([128, NTL, 16], F32, "dn_bg")
        self.dn_dl = A.tile([128, NTL, 2, 16], F32, "dn_dl")
        u = self.new("dns")
        A.push()
        wab = A.tile([128, NKC, 32], BF16, "wab")
        hg = [A.tile([128, NKC, 512], BF16, "hgs") for _ in range(2)]
        ab = A.tile([128, NTL, 32], F32, "ab")
        g = A.tile([128, NTL, 16], F32, "gdn")
        tmp = A.tile([128, NTL, 16], F32, "tdn")
        maskC = A.tile([128, 128], F32, "maskC")
        maskLo = A.tile([128, 128], F32, "maskLo")
        maskHi = A.tile([128, 128], F32, "maskHi")
        P.op("pool", lambda e: e.memset(maskC[:], 0.0), writes=[(u, "mC")])
        P.op("pool", lambda e: e.memset(maskC[0:64, 0:64], 1.0), reads=[(u, "mC")], writes=[(u, "mC")])
        P.op("pool", lambda e: e.memset(maskC[64:128, 64:128], 1.0), reads=[(u, "mC")], writes=[(u, "mC")])
        P.op("pool", lambda e: e.memset(maskLo[:], 0.0), writes=[(u, "mLo")])
        P.op("pool", lambda e: e.memset(maskLo[0:64, :], 1.0), reads=[(u, "mLo")], writes=[(u, "mLo")])
        P.op("pool", lambda e: e.memset(maskHi[:], 0.0), writes=[(u, "mHi")])
        P.op("pool", lambda e: e.memset(maskHi[64:128, :], 1.0), reads=[(u, "mHi")], writes=[(u, "mHi")])
        self.load_w_in(l, 9216, 32, wab, (u, "wab"), "wab")
        for gi, (t0, G) in enumerate(self.TG):
            self.dn_scalars_group(u, gi, t0, G, wab, hg, ab)
        abr = [(u, "ab", t) for t in range(NTL)]
        P.op("dve", lambda e: e.tensor_tensor(out=tmp[:], in0=ab[:, :, 0:16], in1=self.dtb[:, l, :].unsqueeze(1).to_broadcast([128, NTL, 16]), op=ALU.add),
             reads=abr + ["dtb"], writes=[(u, "tmp")])
        P.op("act", lambda e: e.activation(out=tmp[:], in_=tmp[:], func=AF.Exp), reads=[(u, "tmp")], writes=[(u, "tmp")])
        P.op("act", lambda e: e.activation(out=tmp[:], in_=tmp[:], func=AF.Ln, bias=1.0), reads=[(u, "tmp")], writes=[(u, "tmp")])
        P.op("dve", lambda e: e.tensor_tensor(out=g[:], in0=tmp[:], in1=self.alog[:, l, :].unsqueeze(1).to_broadcast([128, NTL, 16]), op=ALU.mult),
             reads=[(u, "tmp"), "alog"], writes=[(u, "g")])
        P.op("act", lambda e: e.activation(out=self.dn_beta[:], in_=ab[:, :, 16:32], func=AF.Sigmoid), reads=abr, writes=["dn_beta"])
        for t in range(NTL):
            bank = self.nb()
            ps = self.ps[bank]
            for (m, mt, c0, c1, o0) in ((self.maskF, "maskF", 0, 8, 0), (self.maskB, "maskB", 8, 16, 8), (maskC, (u, "mC"), 0, 16, 16),
                                        (maskLo, (u, "mLo"), 0, 16, 32), (maskHi, (u, "mHi"), 0, 16, 48)):
                P.op("pe", lambda e, m=m, c0=c0, c1=c1, o0=o0, t=t, ps=ps: e.matmul(ps[:, o0:o0 + (c1 - c0)], lhsT=m[:], rhs=g[:, t, c0:c1], start=True, stop=True),
                     reads=[mt, (u, "g")], writes=[("ps", bank)])
            P.op("dve", lambda e, t=t, ps=ps: e.tensor_copy(out=self.dn_gc[:, t, :], in_=ps[:, 0:16]), reads=[("ps", bank)], writes=[("dn_gc", t)])
            P.op("act", lambda e, t=t, ps=ps: e.activation(out=self.dn_egc[:, t, :], in_=ps[:, 0:16], func=AF.Exp), reads=[("ps", bank)], writes=[("dn_egc", t)])
            P.op("dve", lambda e, t=t, ps=ps: e.tensor_tensor(out=self.dn_egl[:, t, :], in0=ps[:, 16:32], in1=self.dn_gc[:, t, :], op=ALU.subtract),
                 reads=[("ps", bank), ("dn_gc", t)], writes=[("dn_egl", t)])
            P.op("act", lambda e, t=t: e.activation(out=self.dn_egl[:, t, :], in_=self.dn_egl[:, t, :], func=AF.Exp), reads=[("dn_egl", t)], writes=[("dn_egl", t)])
            P.op("act", lambda e, t=t, ps=ps: e.activation(out=self.dn_dl[:, t, :, :], in_=ps[:, 32:64].rearrange("p (a b) -> p a b", a=2), func=AF.Exp),
                 reads=[("ps", bank)], writes=[("dn_dl", t)])
            P.op("dve", lambda e, t=t: e.tensor_tensor(out=self.dn_bg[:, t, :], in0=self.dn_beta[:, t, :], in1=self.dn_egc[:, t, :], op=ALU.mult),
                 reads=["dn_beta", ("dn_egc", t)], writes=[("dn_bg", t)])
        A.pop()
        P.barrier()

    def dn_scalars_group(self, u, gi, t0, G, wab, hg, ab):
        P = self.P
        slot = gi % 2
        self.load_hg(hg, slot, t0, G, u)
        self.proj_tm((wab, (u, "wab")), (hg[slot], (u, "hg", slot)), G, 32, lambda bank, ti: P.op(
            "dve", lambda e: e.tensor_copy(out=ab[:, t0 // 128 + ti, :], in_=self.ps[bank][:, 0:32]),
            reads=[("ps", bank)], writes=[(u, "ab", t0 // 128 + ti)]))

    def dn_head(self, l, h):
        P, A = self.P, self.A
        u = self.new("dn")
        A.push()
        QnT = A.tile([128, NT], BF16, "QnT")
        KnT = A.tile([128, NT], BF16, "KnT")
        Ktok = A.tile([128, NT // 128, 128], BF16, "Ktok")
        Vtok = A.tile([128, NT // 128, 128], BF16, "Vtok")
        gb = A.tile([128, NT], F32, "gb")
        oT = A.tile([128, NT], F32, "oTd")
        self.dn_proj(l, h, u, QnT, KnT, Ktok, Vtok, gb)
        if "dn_stop1" in self.stages:
            self.dump_tile("dbg_QnT", QnT[:], [128, NT], [(u, "QnT", gi) for gi in range(5)])
            self.dump_tile("dbg_KnT", KnT[:], [128, NT], [(u, "KnT", gi) for gi in range(5)])
            self.dump_tile("dbg_Vtok", Vtok[:], [128, NT // 128, 128], [(u, "Vtok", q) for q in range(5)])
            A.pop()
            P.barrier()
            return
        self.dn_dirs(l, h, u, QnT, KnT, Ktok, Vtok, oT)
        if "dump_ob" in self.stages and l == 0:
            self.dump_tile(f"dbg_ob{h}", oT[:], [128, NT], [(u, "oT", q) for q in range(5)])
        self.head_out(u, oT, gb, "gb", self.dng[:, l:l + 1], self.ybT_d[h], ("yb", h))
        A.pop()
        P.barrier()

    def dn_proj(self, l, h, u, QnT, KnT, Ktok, Vtok, gb):
        P, A = self.P, self.A
        A.push()
        cols = [5120 + 128 * h, 6144 + 128 * h, 7168 + 128 * h, 8192 + 128 * h]
        raw = [A.tile([128, NT], F32, "raw") for _ in range(3)]
        cv = [A.tile([128, NT], F32, "cv") for _ in range(3)]
        VnT = A.tile([128, NT], BF16, "VnT")
        ctmp = A.tile([128, NT], F32, "ctmp")
        A.push()
        w = [A.tile([128, NKC, 128], BF16, "wd_") for _ in range(4)]
        hg = [A.tile([128, NKC, 512], BF16, "hgd") for _ in range(2)]
        for i in range(4):
            self.load_w_in(l, cols[i], 128, w[i], (u, "w", i), f"wm{i}")
        for gi, (t0, G) in enumerate(self.TG):
            self.dn_proj_group(u, gi, t0, G, w, hg, raw, gb)
        A.pop()
        P.barrier()
        sq = [A.tile([128, 512], F32R, "dsq") for _ in range(2)]
        rn = [A.tile([128, 512], F32, "drn") for _ in range(2)]
        for i in range(3):
            self.dn_conv(l, u, i, i * 8 + h, raw[i], cv[i], ctmp)
        for i in range(2):
            dst = QnT if i == 0 else KnT
            nm = "QnT" if i == 0 else "KnT"
            scl = 128.0 ** -0.5 if i == 0 else 1.0
            for gi, (t0, G) in enumerate(self.TG):
                self.dn_l2(u, i, gi, t0, G, cv[i], sq, rn, dst, nm, scl)
        P.op("act", lambda e: e.activation(out=VnT[:], in_=cv[2][:], func=AF.Copy), reads=[(u, "cv", 2)], writes=[(u, "VnT")])
        for (src, srd, dst, nm) in ((KnT, [(u, "KnT", gi) for gi in range(5)], Ktok, "Ktok"), (VnT, [(u, "VnT")], Vtok, "Vtok")):
            for q in range(5):
                self.tr_group(u, src, srd, dst, nm, q)
        A.pop()
        P.barrier()

    def tr_group(self, u, src, srd, dst, nm, q):
        P = self.P
        bank = self.nb()
        nt_ = 4 if q < 4 else 2
        psb = self.ps[bank][:].bitcast(BF16)
        for j in range(nt_):
            t = q * 4 + j
            P.op("pe", lambda e, j=j, t=t: e.transpose(psb[:, j * 128:(j + 1) * 128], src[:, t * 128:(t + 1) * 128], self.ident_b[:]),
                 reads=srd + ["ident_b"], writes=[("ps", bank)])
        P.op("act", lambda e: e.activation(out=dst[:, q * 4:q * 4 + nt_, :], in_=psb[:, 0:nt_ * 128].rearrange("p (a b) -> p a b", b=128), func=AF.Copy),
             reads=[("ps", bank)], writes=[(u, nm, q)])

    def dn_proj_group(self, u, gi, t0, G, w, hg, raw, gb):
        P = self.P
        slot = gi % 2
        self.load_hg(hg, slot, t0, G, u)
        hgs = (hg[slot], (u, "hg", slot))
        for i in range(3):
            if i % 2 == 0:
                self.proj_fm((w[i], (u, "w", i)), hgs, G, lambda bank, i=i: P.op(
                    "act", lambda e: e.activation(out=raw[i][:, t0:t0 + G], in_=self.ps[bank][:, 0:G], func=AF.Copy),
                    reads=[("ps", bank)], writes=[(u, "raw", i, gi)]))
            else:
                self.proj_fm((w[i], (u, "w", i)), hgs, G, lambda bank, i=i: P.op(
                    "dve", lambda e: e.tensor_copy(out=raw[i][:, t0:t0 + G], in_=self.ps[bank][:, 0:G]),
                    reads=[("ps", bank)], writes=[(u, "raw", i, gi)]))
        self.proj_fm((w[3], (u, "w", 3)), hgs, G, lambda bank: P.op(
            "act", lambda e: e.activation(out=gb[:, t0:t0 + G], in_=self.ps[bank][:, 0:G], func=AF.Silu),
            reads=[("ps", bank)], writes=[(u, "gb", gi)]))

    def dn_conv(self, l, u, i, ci, raw, cv, ctmp):
        P = self.P
        rr = [(u, "raw", i, gi) for gi in range(5)]
        wcol = lambda j: self.convT[:, l, j * 24 + ci:j * 24 + ci + 1]
        P.op("act", lambda e: e.activation(out=cv[:], in_=raw[:], func=AF.Identity, scale=wcol(2)),
             reads=rr + [("convT", l)], writes=[(u, "cv", i)])
        segs = [(raw[:, 0:NCTX].rearrange("p (r w) -> p r w", w=NCTX), cv[:, 0:NCTX].rearrange("p (r w) -> p r w", w=NCTX), NCTX),
                (raw[:, NCTX:NT].rearrange("p (r w) -> p r w", w=64), cv[:, NCTX:NT].rearrange("p (r w) -> p r w", w=64), 64)]
        for j in (0, 1, 3, 4):
            o = j - 2
            for (r3, c3, W) in segs:
                d0, d1 = max(0, -o), W - max(0, o)
                P.op("dve", lambda e, r3=r3, c3=c3, d0=d0, d1=d1, o=o, j=j: e.scalar_tensor_tensor(
                    out=c3[:, :, d0:d1], in0=r3[:, :, d0 + o:d1 + o], scalar=wcol(j), in1=c3[:, :, d0:d1], op0=ALU.mult, op1=ALU.add),
                    reads=[(u, "cv", i)], writes=[(u, "cv", i)])
        P.op("act", lambda e: e.activation(out=cv[:], in_=cv[:], func=AF.Silu), reads=[(u, "cv", i)], writes=[(u, "cv", i)])

    def dn_l2(self, u, i, gi, t0, G, cv, sq, rn, dst, nm, scl):
        P = self.P
        s = gi % 2
        bank = self.nb()
        P.op("act", lambda e: e.activation(out=sq[s][:, 0:G], in_=cv[:, t0:t0 + G], func=AF.Square), reads=[(u, "cv", i)], writes=[(u, "dsq", s)])
        P.op("pe", lambda e: e.matmul(self.ps[bank][:, 0:G], lhsT=self.ones_r[:], rhs=sq[s][:, 0:G], start=True, stop=True),
             reads=[(u, "dsq", s), "ones_r"], writes=[("ps", bank)])
        P.op("act", lambda e: e.activation(out=rn[s][:, 0:G], in_=self.ps[bank][:, 0:G], func=AF.Ln, bias=self.epsc[:, 0:1], scale=1.0),
             reads=[("ps", bank), "epsc"], writes=[(u, "drn", s)])
        P.op("act", lambda e: e.activation(out=rn[s][:, 0:G], in_=rn[s][:, 0:G], func=AF.Exp, scale=-0.5), reads=[(u, "drn", s)], writes=[(u, "drn", s)])
        P.op("dve", lambda e: e.scalar_tensor_tensor(out=dst[:, t0:t0 + G], in0=cv[:, t0:t0 + G], scalar=scl, in1=rn[s][:, 0:G], op0=ALU.mult, op1=ALU.mult),
             reads=[(u, "drn", s), (u, "cv", i)], writes=[(u, nm, gi)])

    def dn_live(self, d):
        A, P = self.A, self.P
        NTL = NT // 128
        NC = NT // 64
        u = self.new("dd")
        lv = dict(u=u, d=d)
        lv["kdz"] = A.tile([128, NTL, 2, 128], BF16, "kdz")
        lv["qkT"] = A.tile([128, NTL, 128], BF16, "qkT")
        lv["qgT"] = A.tile([128, NT], BF16, "qgT")
        lv["nwT"] = A.tile([128, NT], BF16, "nwT")
        lv["usb"] = A.tile([128, NTL, 128], F32, "usb")
        lv["vn"] = A.tile([128, NC, 128], BF16, "vn")
        lv["Sall"] = A.tile([128, NC, 128], BF16, "Sall")
        lv["S32"] = A.tile([128, 128], F32, "S32")
        kdz, Sall, S32 = lv["kdz"], lv["Sall"], lv["S32"]
        P.op("pool", lambda e: e.memset(kdz[:].rearrange("p a b c -> p (a b c)"), 0.0), writes=[(u, "kdz0")])
        P.op("pool", lambda e: e.memset(Sall[:, 0, :], 0.0), writes=[(u, "Sall", 0)])
        P.op("pool", lambda e: e.memset(S32[:], 0.0), writes=[(u, "S32")])
        return lv

    def dn_dir_prep(self, l, h, u0, lv, QnT, KnT, Ktok, Vtok):
        P, A = self.P, self.A
        u, d = lv["u"], lv["d"]
        NTL = NT // 128
        col = d * 8 + h
        A.push()
        kbg = A.tile([128, NTL, 128], BF16, "kbg")
        vb = A.tile([128, NTL, 128], BF16, "vb")
        Lc = [A.tile([128, NTL, 128], BF16, "Lc") for _ in range(2)]
        Nc = [A.tile([128, NTL, 128], BF16, "Nc") for _ in range(2)]
        Rc = [A.tile([128, NTL, 128], BF16, "Rc") for _ in range(2)]
        dg3 = [A.tile([128, 384], F32, "dg3") for _ in range(2)]
        WA = [A.tile([128, 128], F32, "WA") for _ in range(2)]
        WBs = [A.tile([128, 128], F32, "WBs") for _ in range(2)]
        WBi = [A.tile([128, 128], F32, "WBi") for _ in range(2)]
        t2 = [A.tile([128, 128], F32, "t2") for _ in range(2)]
        mAs = self.bigP[0] if d == 0 else self.bigP[1]
        mBs = self.bigN[1] if d == 0 else self.bigN[0]
        mBi = self.bigN[2] if d == 0 else self.bigN[3]
        Kr = [(u0, "Ktok", q) for q in range(5)]
        Vr = [(u0, "Vtok", q) for q in range(5)]
        KnR = [(u0, "KnT", gi) for gi in range(5)]
        QnR = [(u0, "QnT", gi) for gi in range(5)]
        for t in range(NTL):
            self.dn_prep_tile(u, u0, d, t, col, kbg, vb, lv["kdz"], Lc[0], Nc[0], Rc[0], lv["qkT"], lv["qgT"], dg3[t % 2], None, None, WA[t % 2], WBs[t % 2],
                              WBi[t % 2], t2[t % 2], mAs, mBs, mBi, Ktok, Vtok, KnT, QnT, Kr, Vr, KnR, QnR)
        cur = 0
        for lev in range(5):
            nxt = 1 - cur
            for q in range(5):
                self.dn_neumann(u, lev, q, Lc[cur], Nc[cur], Rc[cur], Lc[nxt], Nc[nxt], Rc[nxt])
            cur = nxt
        TT = Rc[cur]
        for q in range(5):
            self.dn_uw(u, q, TT, vb, kbg, lv["usb"], lv["nwT"])
        A.pop()
        P.barrier()

    def dn_dirs(self, l, h, u0, QnT, KnT, Ktok, Vtok, oT):
        P, A = self.P, self.A
        NC = NT // 64
        A.push()
        lvs = [self.dn_live(d) for d in range(2)]
        for lv in lvs:
            self.dn_dir_prep(l, h, u0, lv, QnT, KnT, Ktok, Vtok)
        orders = [sorted(range(NC), key=lambda c, d=d: self.chain_pos(d, c)) for d in range(2)]
        for pos in range(NC):
            for lv in lvs:
                d = lv["d"]
                self.dn_chain_step(lv["u"], d, pos, orders[d][pos], d * 8 + h, lv["nwT"], lv["usb"], lv["vn"], lv["kdz"], lv["Sall"], lv["S32"])
        for q in range(5):
            for lv in lvs:
                self.dn_out(lv["u"], u0, lv["d"], q, lv["Sall"], lv["qgT"], lv["vn"], lv["qkT"], oT)
        A.pop()
        P.barrier()

    def dn_prep_tile(self, u, u0, d, t, col, kbg, vb, kdz, L0, N0, R0, qkT, qgT, dg3, tA, tB, WA, WBs, WBi, t2, mAs, mBs, mBi,
                     Ktok, Vtok, KnT, QnT, Kr, Vr, KnR, QnR):
        P = self.P
        s = t % 2
        beta = self.dn_beta[:, t, col:col + 1]
        gc = self.dn_gc[:, t, col:col + 1]
        egc = self.dn_egc[:, t, col:col + 1]
        bg = self.dn_bg[:, t, col:col + 1]
        sc_r = ["dn_beta", ("dn_gc", t), ("dn_egc", t), ("dn_egl", t), ("dn_bg", t)]
        P.op("act", lambda e: e.activation(out=kbg[:, t, :], in_=Ktok[:, t, :], func=AF.Identity, scale=bg),
             reads=Kr + sc_r, writes=[(u, "kbg", t)])
        P.op("act", lambda e: e.activation(out=vb[:, t, :], in_=Vtok[:, t, :], func=AF.Identity, scale=beta),
             reads=Vr + sc_r, writes=[(u, "vb", t)])
        for hf in range(2):
            r0, r1 = hf * 64, hf * 64 + 64
            P.op("act", lambda e, hf=hf, r0=r0, r1=r1: e.activation(out=kdz[r0:r1, t, hf, :], in_=Ktok[r0:r1, t, :], func=AF.Identity,
                                                                 scale=self.dn_egl[r0:r1, t, col:col + 1]),
                 reads=Kr + sc_r + [(u, "kdz0")], writes=[(u, "kdz", t, hf)])
        for i, sc in enumerate((gc, beta, egc)):
            if i == 1:
                P.op("act", lambda e, i=i, sc=sc: e.activation(out=dg3[:, i * 128:(i + 1) * 128], in_=self.ident[:], func=AF.Identity, scale=sc),
                     reads=["ident"] + sc_r, writes=[(u, "dg3", s, i)])
            else:
                P.op("dve", lambda e, i=i, sc=sc: e.tensor_scalar(out=dg3[:, i * 128:(i + 1) * 128], in0=self.ident[:], scalar1=sc, scalar2=None, op0=ALU.mult),
                     reads=["ident"] + sc_r, writes=[(u, "dg3", s, i)])
        bR = self.nb()
        P.op("pe", lambda e: e.matmul(self.ps[bR][:, 0:384], lhsT=self.ones_f[:], rhs=dg3[:, :], start=True, stop=True),
             reads=[(u, "dg3", s, i) for i in range(3)] + ["ones_f"], writes=[("ps", bR)])
        RB = self.ps[bR][:, 0:128]
        RBb = self.ps[bR][:, 128:256]
        RBe = self.ps[bR][:, 256:384]
        bG = self.nb()
        ts = slice(t * 128, (t + 1) * 128)
        P.op("pe", lambda e: e.matmul(self.ps[bG][:, 0:128], lhsT=KnT[:, ts], rhs=KnT[:, ts], start=True, stop=True),
             reads=KnR, writes=[("ps", bG)])
        P.op("pe", lambda e: e.matmul(self.ps[bG][:, 128:256], lhsT=KnT[:, ts], rhs=QnT[:, ts], start=True, stop=True),
             reads=KnR + QnR, writes=[("ps", bG)])
        Gm = self.ps[bG][:, 0:128]
        QK = self.ps[bG][:, 128:256]
        P.op("dve", lambda e: e.scalar_tensor_tensor(out=WA[:], in0=RB, scalar=gc, in1=mAs[:], op0=ALU.subtract, op1=ALU.max),
             reads=[("ps", bR), "bigm"] + sc_r, writes=[(u, "WA", s)])
        P.op("act", lambda e: e.activation(out=WA[:], in_=WA[:], func=AF.Exp, scale=-1.0), reads=[(u, "WA", s)], writes=[(u, "WA", s)])
        P.op("dve", lambda e: e.scalar_tensor_tensor(out=WBs[:], in0=RB, scalar=gc, in1=mBs[:], op0=ALU.subtract, op1=ALU.min),
             reads=[("ps", bR), "bigm"] + sc_r, writes=[(u, "WBs", s)])
        P.op("act", lambda e: e.activation(out=WBs[:], in_=WBs[:], func=AF.Exp), reads=[(u, "WBs", s)], writes=[(u, "WBs", s)])
        P.op("dve", lambda e: e.scalar_tensor_tensor(out=WBi[:], in0=RB, scalar=gc, in1=mBi[:], op0=ALU.subtract, op1=ALU.min),
             reads=[("ps", bR), "bigm"] + sc_r, writes=[(u, "WBi", s)])
        P.op("act", lambda e: e.activation(out=WBi[:], in_=WBi[:], func=AF.Exp), reads=[(u, "WBi", s)], writes=[(u, "WBi", s)])
        P.op("dve", lambda e: e.scalar_tensor_tensor(out=L0[:, t, :], in0=Gm, scalar=beta, in1=WA[:], op0=ALU.mult, op1=ALU.mult),
             reads=[("ps", bG), (u, "WA", s)] + sc_r, writes=[(u, "L", 0, t)])
        P.op("dve", lambda e: e.tensor_tensor(out=t2[:], in0=RBb, in1=WBs[:], op=ALU.mult), reads=[("ps", bR), (u, "WBs", s)], writes=[(u, "t2", s)])
        P.op("dve", lambda e: e.tensor_tensor(out=N0[:, t, :], in0=Gm, in1=t2[:], op=ALU.mult), reads=[("ps", bG), (u, "t2", s)], writes=[(u, "N", 0, t)])
        P.op("dve", lambda e: e.scalar_tensor_tensor(out=R0[:, t, :], in0=N0[:, t, :], scalar=-1.0, in1=self.ident[:], op0=ALU.mult, op1=ALU.add),
             reads=[(u, "N", 0, t), "ident"], writes=[(u, "R", 0, t)])
        P.op("dve", lambda e: e.tensor_tensor(out=qkT[:, t, :], in0=QK, in1=WBi[:], op=ALU.mult), reads=[("ps", bG), (u, "WBi", s)], writes=[(u, "qkT", t)])
        P.op("dve", lambda e: e.tensor_tensor(out=qgT[:, ts], in0=RBe, in1=QnT[:, ts], op=ALU.mult), reads=[("ps", bR)] + QnR, writes=[(u, "qgT", t)])

    def dn_neumann(self, u, lev, q, Lc, Nc, Rc, Ln, Nn, Rn):
        P = self.P
        nt_ = 4 if q < 4 else 2
        tiles = [q * 4 + j for j in range(nt_)]
        last = lev == 4

        def rd(nm, t):
            return [(u, nm, lev, t)] if lev == 0 else [(u, nm, lev, t // 4)]
        bL = self.nb()
        for j, t in enumerate(tiles):
            P.op("pe", lambda e, j=j, t=t: e.matmul(self.ps[bL][:, j * 128:(j + 1) * 128], lhsT=Nc[:, t, :], rhs=Lc[:, t, :], start=True, stop=True),
                 reads=rd("N", t) + rd("L", t), writes=[("ps", bL)])
        P.op("act", lambda e: e.activation(out=Ln[:, q * 4:q * 4 + nt_, :], in_=self.ps[bL][:, 0:nt_ * 128].rearrange("p (a b) -> p a b", b=128), func=AF.Copy),
             reads=[("ps", bL)], writes=[(u, "L", lev + 1, q)])
        if not last:
            bN = self.nb()
            for j, t in enumerate(tiles):
                P.op("pe", lambda e, j=j, t=t: e.matmul(self.ps[bN][:, j * 128:(j + 1) * 128], lhsT=Lc[:, t, :], rhs=Nc[:, t, :], start=True, stop=True),
                     reads=rd("N", t) + rd("L", t), writes=[("ps", bN)])
            P.op("dve", lambda e: e.tensor_copy(out=Nn[:, q * 4:q * 4 + nt_, :], in_=self.ps[bN][:, 0:nt_ * 128].rearrange("p (a b) -> p a b", b=128)),
                 reads=[("ps", bN)], writes=[(u, "N", lev + 1, q)])
        bR = self.nb()
        for j, t in enumerate(tiles):
            P.op("pe", lambda e, j=j, t=t: e.matmul(self.ps[bR][:, j * 128:(j + 1) * 128], lhsT=self.ident_b[:], rhs=Rc[:, t, :], start=True, stop=False),
                 reads=rd("R", t) + ["ident_b"], writes=[("ps", bR)])
            P.op("pe", lambda e, j=j, t=t: e.matmul(self.ps[bR][:, j * 128:(j + 1) * 128], lhsT=Ln[:, t, :], rhs=Rc[:, t, :], start=False, stop=True),
                 reads=rd("R", t) + [(u, "L", lev + 1, q)], writes=[("ps", bR)])
        eng = "dve" if q % 2 == 0 else "act"
        if eng == "dve":
            P.op("dve", lambda e: e.tensor_copy(out=Rn[:, q * 4:q * 4 + nt_, :], in_=self.ps[bR][:, 0:nt_ * 128].rearrange("p (a b) -> p a b", b=128)),
                 reads=[("ps", bR)], writes=[(u, "R", lev + 1, q)])
        else:
            P.op("act", lambda e: e.activation(out=Rn[:, q * 4:q * 4 + nt_, :], in_=self.ps[bR][:, 0:nt_ * 128].rearrange("p (a b) -> p a b", b=128), func=AF.Copy),
                 reads=[("ps", bR)], writes=[(u, "R", lev + 1, q)])

    def dn_uw(self, u, q, TT, vb, kbg, usb, nwT):
        P = self.P
        nt_ = 4 if q < 4 else 2
        tiles = [q * 4 + j for j in range(nt_)]
        bU = self.nb()
        for j, t in enumerate(tiles):
            P.op("pe", lambda e, j=j, t=t: e.matmul(self.ps[bU][:, j * 128:(j + 1) * 128], lhsT=TT[:, t, :], rhs=vb[:, t, :], start=True, stop=True),
                 reads=[(u, "R", 5, q), (u, "vb", t)], writes=[("ps", bU)])
        P.op("dve", lambda e: e.tensor_copy(out=usb[:, q * 4:q * 4 + nt_, :], in_=self.ps[bU][:, 0:nt_ * 128].rearrange("p (a b) -> p a b", b=128)),
             reads=[("ps", bU)], writes=[(u, "usb", q)])
        bW = self.nb()
        for j, t in enumerate(tiles):
            P.op("pe", lambda e, j=j, t=t: e.matmul(self.ps[bW][:, j * 128:(j + 1) * 128], lhsT=kbg[:, t, :], rhs=TT[:, t, :], start=True, stop=True),
                 reads=[(u, "R", 5, q), (u, "kbg", t)], writes=[("ps", bW)])
        P.op("act", lambda e: e.activation(out=nwT[:, q * 512:q * 512 + nt_ * 128], in_=self.ps[bW][:, 0:nt_ * 128], func=AF.Copy, scale=-1.0),
             reads=[("ps", bW)], writes=[(u, "nwT", q)])

    def dn_chain_step(self, u, d, pos, c, col, nwT, usb, vn, kdz, Sall, S32):
        P = self.P
        t, hf = c // 2, c % 2
        b1 = self.nb()
        P.op("pe", lambda e: e.matmul(self.ps[b1][:, 0:128], lhsT=nwT[:, t * 128:(t + 1) * 128], rhs=Sall[:, pos, :], start=True, stop=True),
             reads=[(u, "nwT", t // 4), (u, "Sall", pos)], writes=[("ps", b1)])
        P.op("dve", lambda e: e.tensor_tensor(out=vn[:, c, :], in0=self.ps[b1][:, 0:128], in1=usb[:, t, :], op=ALU.add),
             reads=[("ps", b1), (u, "usb", t // 4)], writes=[(u, "vn", c)])
        b2 = self.nb()
        P.op("pe", lambda e: e.matmul(self.ps[b2][:, 0:128], lhsT=kdz[:, t, hf, :], rhs=vn[:, c, :], start=True, stop=True),
             reads=[(u, "kdz", t, hf), (u, "kdz0"), (u, "vn", c)], writes=[("ps", b2)])
        if pos + 1 < NT // 64:
            P.op("dve", lambda e: e.scalar_tensor_tensor(out=Sall[:, pos + 1, :], in0=S32[:], scalar=self.dn_dl[:, t, hf, col:col + 1], in1=self.ps[b2][:, 0:128],
                                                         op0=ALU.mult, op1=ALU.add),
                 reads=[("ps", b2), (u, "S32"), ("dn_dl", t)], writes=[(u, "Sall", pos + 1)])
            P.op("dve", lambda e: e.scalar_tensor_tensor(out=S32[:], in0=S32[:], scalar=self.dn_dl[:, t, hf, col:col + 1], in1=self.ps[b2][:, 0:128],
                                                         op0=ALU.mult, op1=ALU.add),
                 reads=[("ps", b2), (u, "S32"), ("dn_dl", t), (u, "Sall", pos + 1)], writes=[(u, "S32")])

    def dn_out(self, u, u0, d, q, Sall, qgT, vn, qkT, oT):
        P = self.P
        nt_ = 4 if q < 4 else 2
        bank = self.nb()
        for j in range(nt_):
            t = q * 4 + j
            for hf in range(2):
                c = 2 * t + hf
                pos = self.chain_pos(d, c)
                cs = slice(j * 128 + hf * 64, j * 128 + hf * 64 + 64)
                P.op("pe", lambda e, c=c, pos=pos, cs=cs: e.matmul(self.ps[bank][:, cs], lhsT=Sall[:, pos, :], rhs=qgT[:, c * 64:(c + 1) * 64], start=True, stop=False),
                     reads=[(u, "Sall", pos), (u, "qgT", t)], writes=[("ps", bank)])
                P.op("pe", lambda e, c=c, t=t, hf=hf, cs=cs: e.matmul(self.ps[bank][:, cs], lhsT=vn[:, c, :], rhs=qkT[:, t, hf * 64:hf * 64 + 64], start=False, stop=True),
                     reads=[(u, "vn", c), (u, "qkT", t)], writes=[("ps", bank)])
        c0_, c1_ = q * 512, q * 512 + nt_ * 128
        if d == 0:
            P.op("act", lambda e: e.activation(out=oT[:, c0_:c1_], in_=self.ps[bank][:, 0:nt_ * 128], func=AF.Copy),
                 reads=[("ps", bank)], writes=[(u0, "oT", q)])
        else:
            P.op("dve", lambda e: e.tensor_tensor(out=oT[:, c0_:c1_], in0=self.ps[bank][:, 0:nt_ * 128], in1=oT[:, c0_:c1_], op=ALU.add),
                 reads=[("ps", bank), (u0, "oT", q)], writes=[(u0, "oT", q)])

    def merge(self, l):
        A = self.A
        A.push()
        wa = [A.tile([128, 8, 128], BF16, "wa") for _ in range(2)]
        wb = [A.tile([128, 8, 128], BF16, "wb") for _ in range(2)]
        wga = [A.tile([128, NKC, 128], BF16, "wga") for _ in range(2)]
        wgb = [A.tile([128, NKC, 128], BF16, "wgb") for _ in range(2)]
        wo = [A.tile([128, NKC, 128], BF16, "wo") for _ in range(2)]
        for (t0, G, v) in self.groups():
            self.merge_group(l, t0, G, v, wa, wb, wga, wgb, wo)
        A.pop()

    def merge_group(self, l, t0, G, v, wa, wb, wga, wgb, wo):
        P, A = self.P, self.A
        u = self.new("mg")
        nh = (G + 511) // 512
        hw = G // nh
        A.push()
        yaT = A.tile([128, 8, G], BF16, "yaT")
        ybT = A.tile([128, 8, G], BF16, "ybT")
        hT = A.tile([128, NKC, G], BF16, "hTg")
        yT = A.tile([128, NKC, G], BF16, "yT")
        sga = [A.tile([128, hw], F32, "sga") for _ in range(2)]
        sgb = [A.tile([128, hw], F32, "sgb") for _ in range(2)]
        xc = [A.tile([128, G], F32, "xm") for _ in range(3)]
        yr = [(("ya", h), gi) for h in range(8) for gi in range(5)] + [(("yb", h), gi) for h in range(8) for gi in range(5)]
        P.op("sp", lambda e: e.dma_start(out=yaT[:], in_=self.yaT_d[:, :, t0:t0 + G].rearrange("h p n -> p h n")), reads=yr, writes=[(u, "yaT")], lane="mga")
        P.op("sp", lambda e: e.dma_start(out=ybT[:], in_=self.ybT_d[:, :, t0:t0 + G].rearrange("h p n -> p h n")), reads=yr, writes=[(u, "ybT")], lane="mgb")
        P.op("sp", lambda e: e.dma_start(out=hT[:], in_=self.hT_d[:, :, t0:t0 + G].rearrange("c p n -> p c n")),
             reads=[("hTd", t) for t in range(t0 // 128, (t0 + G) // 128)], writes=[(u, "hT")], lane="mgh")
        wav = self.inp["w_branch_a"][l].rearrange("(hd p) n -> p hd n", p=128)
        wbv = self.inp["w_branch_b"][l].rearrange("(hd p) n -> p hd n", p=128)
        wiv = self.inp["w_in"][l].rearrange("(kc p) n -> p kc n", p=128)
        wov = self.inp["w_out"][l].rearrange("(kc p) n -> p kc n", p=128)

        def load1(n):
            s = n % 2
            cs = slice(n * 128, (n + 1) * 128)
            P.op("pool", lambda e: e.dma_start(out=wa[s][:], in_=wav[:, :, cs]), writes=[("wa", s)], lane=f"wa{s}")
            P.op("pool", lambda e: e.dma_start(out=wb[s][:], in_=wbv[:, :, cs]), writes=[("wb", s)], lane=f"wb{s}")
            P.op("pool", lambda e: e.dma_start(out=wga[s][:], in_=wiv[:, :, 9248 + n * 128:9248 + (n + 1) * 128]), writes=[("wga", s)], lane=f"wga{s}")
            P.op("pool", lambda e: e.dma_start(out=wgb[s][:], in_=wiv[:, :, 11296 + n * 128:11296 + (n + 1) * 128]), writes=[("wgb", s)], lane=f"wgb{s}")

        def load2(n):
            s = n % 2
            P.op("pool", lambda e: e.dma_start(out=wo[s][:], in_=wov[:, :, n * 128:(n + 1) * 128]), writes=[("wo", s)], lane=f"wo{s}")

        def acc(wt, wtok, nk, rhs_t, rtok, h):
            bank = self.nb()
            for k in range(nk):
                P.op("pe", lambda e, k=k: e.matmul(self.ps[bank][:, 0:hw], lhsT=wt[:, k, :], rhs=rhs_t[:, k, h * hw:(h + 1) * hw],
                                                   start=(k == 0), stop=(k == nk - 1)),
                     reads=[wtok] + rtok, writes=[("ps", bank)])
            return bank

        load1(0)
        for n in range(NKC):
            if n + 1 < NKC:
                load1(n + 1)
            else:
                load2(0)
            s = n % 2
            for h in range(nh):
                i = h % 2
                bga = acc(wga[s], ("wga", s), NKC, hT, [(u, "hT")], h)
                P.op("act", lambda e, i=i, bga=bga: e.activation(out=sga[i][:], in_=self.ps[bga][:, 0:hw], func=AF.Sigmoid),
                     reads=[("ps", bga)], writes=[(u, "sga", i)])
                bgb = acc(wgb[s], ("wgb", s), NKC, hT, [(u, "hT")], h)
                P.op("act", lambda e, i=i, bgb=bgb: e.activation(out=sgb[i][:], in_=self.ps[bgb][:, 0:hw], func=AF.Sigmoid),
                     reads=[("ps", bgb)], writes=[(u, "sgb", i)])
                ba = acc(wa[s], ("wa", s), 8, yaT, [(u, "yaT")], h)
                P.op("dve", lambda e, i=i, ba=ba: e.tensor_tensor(out=sga[i][:], in0=sga[i][:], in1=self.ps[ba][:, 0:hw], op=ALU.mult),
                     reads=[("ps", ba), (u, "sga", i)], writes=[(u, "sga", i)])
                bb = acc(wb[s], ("wb", s), 8, ybT, [(u, "ybT")], h)
                P.op("dve", lambda e, i=i, bb=bb: e.tensor_tensor(out=sgb[i][:], in0=sgb[i][:], in1=self.ps[bb][:, 0:hw], op=ALU.mult),
                     reads=[("ps", bb), (u, "sgb", i)], writes=[(u, "sgb", i)])
                P.op("pool", lambda e, i=i, n=n, h=h: e.tensor_tensor(out=yT[:, n, h * hw:(h + 1) * hw], in0=sga[i][:], in1=sgb[i][:], op=ALU.add),
                     reads=[(u, "sga", i), (u, "sgb", i)], writes=[(u, "yT", n, h)])
        for n in range(NKC):
            if n + 1 < NKC:
                load2(n + 1)
            s = n % 2
            xs = n % 3
            P.op("sp", lambda e, xs=xs, n=n: e.dma_start(out=xc[xs][:], in_=self.xT_d[n, :, t0:t0 + G]),
                 reads=[("xT", n, t) for t in range(t0 // 128, (t0 + G) // 128)], writes=[(u, "xm", xs)], lane=f"xe{xs}")
            for h in range(nh):
                bo = acc(wo[s], ("wo", s), NKC, yT, [(u, "yT", k, h) for k in range(NKC)], h)
                P.op("dve", lambda e, xs=xs, h=h, bo=bo, n=n: e.scalar_tensor_tensor(
                    out=xc[xs][:, h * hw:(h + 1) * hw], in0=self.ps[bo][:, 0:hw], scalar=self.gate[:, l, 1, v, n:n + 1],
                    in1=xc[xs][:, h * hw:(h + 1) * hw], op0=ALU.mult, op1=ALU.add),
                    reads=[("ps", bo), (u, "xm", xs)], writes=[(u, "xm", xs)])
            P.op("sp", lambda e, xs=xs, n=n: e.dma_start(out=self.xT_d[n, :, t0:t0 + G], in_=xc[xs][:]),
                 reads=[(u, "xm", xs)], writes=[("xT", n, t) for t in range(t0 // 128, (t0 + G) // 128)], lane=f"xs{xs}")
        A.pop()
        P.barrier()

    def final(self):
        P, A = self.P, self.A
        A.push()
        hT = A.tile([128, NKC, 1024], F32, "hTf")
        to = [A.tile([128, D], F32, "to") for _ in range(2)]
        for gi, (t0, G, v) in enumerate(self.groups()[1:]):
            u = self.new("fin")
            self.prologue(t0, G, self.gT[:, 96:112], None, hT, (u, "hT"))
            for t in range(G // 128):
                s = t % 2
                for q in range(4):
                    bank = (t % 2) * 4 + q
                    for j in range(4):
                        c = q * 4 + j
                        P.op("pe", lambda e, c=c, bank=bank, j=j, t=t: e.transpose(
                            self.ps[bank][:, j * 128:(j + 1) * 128], hT[:, c, t * 128:(t + 1) * 128], self.ident[:]),
                            reads=[((u, "hT"), c), "ident"], writes=[("ps", bank)])
                    if q % 2 == 0:
                        P.op("act", lambda e, s=s, q=q, bank=bank: e.activation(out=to[s][:, q * 512:(q + 1) * 512], in_=self.ps[bank][:], func=AF.Copy),
                             reads=[("ps", bank)], writes=[("to", s, q)])
                    else:
                        P.op("dve", lambda e, s=s, q=q, bank=bank: e.tensor_copy(out=to[s][:, q * 512:(q + 1) * 512], in_=self.ps[bank][:]),
                             reads=[("ps", bank)], writes=[("to", s, q)])
                row = (t0 - NCTX) + t * 128
                P.op("sp", lambda e, s=s, row=row: e.dma_start(out=self.out[row:row + 128, :], in_=to[s][:]),
                     reads=[("to", s, q) for q in range(4)], lane=f"to{s}")
        A.pop()
        P.barrier()

    def dump_xT(self, name):
        d = self.nc.dram_tensor(name, [NKC, 128, NT], F32, kind="ExternalOutput").ap()
        self.dbg_out[name] = d
        for c in range(NKC):
            self.P.op("sp", lambda e, c=c: e.dma_start(out=d[c], in_=self.xT_d[c]),
                      reads=[("xT", c, t) for t in range(NT // 128)], lane="dump")

    def dump_tile(self, name, ap, shape, reads):
        d = self.nc.dram_tensor(name, list(shape), ap.dtype, kind="ExternalOutput").ap()
        self.dbg_out[name] = d
        self.P.op("sp", lambda e: e.dma_start(out=d, in_=ap), reads=reads, lane="dump")

    def build(self):
        st = self.stages
        self.consts()
        self.P.barrier()
        self.phase_in()
        if "dump_in" in st:
            self.dump_xT("dbg_xin")
        self.phase_adaln()
        if "dump_mod" in st:
            for l in range(2):
                self.dump_tile(f"dbg_modT{l}", self.modT[l][:], [128, 144, 2], [("modT", l)])
            self.dump_tile("dbg_gT", self.gT[:], [128, 112], ["gT0", "gT1"])
            self.dump_tile("dbg_gmod", self.gmod[:], [128, 2, 3, 2, NKC], [])
            self.dump_tile("dbg_gate", self.gate[:], [128, 2, 3, 2, NKC], [])
        self.phase_small()
        for l in range(2):
            if ("ffn", l, 0) in st:
                self.ffn(l, 0)
            if ("dump", l, 0) in st:
                self.dump_xT(f"dbg_x_ffn1_{l}")
            if ("mix", l) in st:
                self.mixer_prep(l)
                for h in self.heads:
                    if ("hgrn", l) in st:
                        self.hgrn_head(l, h)
                if ("dn", l) in st:
                    self.A.push()
                    self.dn_scalars(l)
                    for h in self.heads:
                        self.dn_head(l, h)
                    self.A.pop()
                    self.P.barrier()
                if ("merge", l) in st:
                    self.merge(l)
                if ("dumpmix", l) in st:
                    self.dump_xT(f"dbg_x_mix_{l}")
            if ("ffn", l, 1) in st:
                self.ffn(l, 1)
        self.final()
        with ExitStack() as es:
            sems = {k: es.enter_context(self.nc.semaphore("s_" + k)) for k in list(Prog.ENG) + self.P.lanes()}
            self.P.emit(sems)
        return self.nc


FULL = {("ffn", 0, 0), ("ffn", 0, 1), ("ffn", 1, 0), ("ffn", 1, 1), ("mix", 0), ("mix", 1), ("hgrn", 0), ("hgrn", 1),
        ("dn", 0), ("dn", 1), ("merge", 0), ("merge", 1)}


def make_in_maps(inputs, ncores=4):
    maps = []
    for b in range(ncores):
        m = {}
        for k, v in inputs.items():
            v = np.asarray(v)
            if k == "x":
                m[k] = np.ascontiguousarray(v[b])
            elif k == "c":
                m[k] = np.ascontiguousarray(v[b:b + 1])
            elif k == "ctx":
                m[k] = np.ascontiguousarray(v[b])
            elif k in ("c_ctx", "final_norm_g"):
                m[k] = np.ascontiguousarray(v.reshape(1, -1))
            else:
                m[k] = np.ascontiguousarray(v)
        maps.append(m)
    return maps


def kernel(**inputs):
    bld = Builder(FULL)
    nc = bld.build()
    maps = make_in_maps(inputs, 4)
    res = run_bass_kernel_spmd(nc, maps, core_ids=list(range(4)))
    return np.stack([np.asarray(r["out"]) for r in res.results], axis=0).astype(np.float32)
```

```python
import numpy as np
from contextlib import ExitStack
import concourse.bass as bass
import concourse.mybir as mybir
from concourse.bass_utils import run_bass_kernel_spmd

F32 = mybir.dt.float32
F32R = mybir.dt.float32r
BF16 = mybir.dt.bfloat16
AF = mybir.ActivationFunctionType
ALU = mybir.AluOpType

D = 2048
NT = 2304
NCTX = 256
NTO = 1152
NG = 6
DFF = 5632
NKC = 16
NFC = 44
PIN = 13344
EPS = 1e-6
SB0 = 16512
SB1 = 229376


class Prog:
    ENG = ("pe", "act", "dve", "pool", "sp")

    def __init__(self, nc):
        self.nc = nc
        self.ops = []
        self.last_w = {}
        self.readers = {}

    def op(self, eng, fn, reads=(), writes=(), lane=None, step=16):
        i = len(self.ops)
        deps = set()
        for r in reads:
            w = self.last_w.get(r)
            if w is not None:
                deps.add(w)
        for w_ in writes:
            w = self.last_w.get(w_)
            if w is not None:
                deps.add(w)
            for rd in self.readers.get(w_, ()):
                deps.add(rd)
        deps.discard(i)
        for r in reads:
            self.readers.setdefault(r, []).append(i)
        for w_ in writes:
            self.last_w[w_] = i
            self.readers[w_] = []
        self.ops.append(dict(eng=eng, fn=fn, deps=deps, lane=lane, sig=lane is not None, lstep=step))
        return i

    def barrier(self):
        toks = [("__bar", e, len(self.ops)) for e in self.ENG]
        allres = list(self.last_w.keys())
        for e, t in zip(self.ENG, toks):
            self.op(e, None, reads=allres, writes=[t])
        for e in self.ENG:
            self.op(e, None, reads=toks, writes=[("__bar2", e)])
        self.last_w = {k: v for k, v in self.last_w.items() if k[0] == "__bar2"} if False else self.last_w

    def lanes(self):
        return sorted({o["lane"] for o in self.ops if o["lane"] is not None})

    def emit(self, sems):
        ops = self.ops
        for o in ops:
            for d in o["deps"]:
                dop = ops[d]
                if dop["lane"] is None:
                    if dop["eng"] == "pe" and o["eng"] == "pe" and o["lane"] is None:
                        continue
                    dop["sig"] = True
        cnt = {}
        for o in ops:
            if o["sig"]:
                key = o["lane"] if o["lane"] is not None else o["eng"]
                step = o["lstep"] if o["lane"] is not None else 1
                cnt[key] = cnt.get(key, 0) + step
                o["key"] = key
                o["val"] = cnt[key]
                o["step"] = step
        final = dict(cnt)
        per_eng = {e: [] for e in self.ENG}
        for o in ops:
            per_eng[o["eng"]].append(o)

        def run(ename, e):
            seen = {}
            for o in per_eng[ename]:
                waits = {}
                for d in o["deps"]:
                    dop = ops[d]
                    if not dop["sig"]:
                        continue
                    if dop["lane"] is None and dop["eng"] == "pe" and ename == "pe" and o["lane"] is None:
                        continue
                    k, v = dop["key"], dop["val"]
                    if v > waits.get(k, 0):
                        waits[k] = v
                for k, v in waits.items():
                    if v > seen.get(k, 0):
                        e.wait_ge(sems[k], v)
                        seen[k] = v
                if o["fn"] is None:
                    ins = e.nop() if o["sig"] else None
                else:
                    ins = o["fn"](e)
                if o["sig"]:
                    ins.then_inc(sems[o["key"]], o["step"])
            if ename == "sp":
                for k, v in final.items():
                    if v > seen.get(k, 0):
                        e.wait_ge(sems[k], v)

        with self.nc.Block() as block:
            @block.tensor
            def _(e):
                run("pe", e)

            @block.scalar
            def _(e):
                run("act", e)

            @block.vector
            def _(e):
                run("dve", e)

            @block.gpsimd
            def _(e):
                run("pool", e)

            @block.sync
            def _(e):
                run("sp", e)


class Arena:
    def __init__(self, nc):
        self.nc = nc
        self.off = SB0
        self.stack = []
        self.n = 0

    def push(self):
        self.stack.append(self.off)

    def pop(self):
        self.off = self.stack.pop()

    def tile(self, shape, dtype, name="t"):
        esz = 2 if dtype == BF16 else 4
        per = esz
        for s in shape[1:]:
            per *= s
        per = (per + 63) // 64 * 64
        assert self.off + per <= SB1, f"SBUF overflow allocating {name} {shape}: {self.off + per - SB1} bytes over"
        self.n += 1
        h = self.nc.alloc_sbuf_tensor_at(f"{name}_{self.n}", list(shape), dtype, offset=self.off)
        self.off += per
        return h


class Builder:
    def __init__(self, stages, dbg=()):
        self.stages = stages
        self.dbg = dbg
        self.heads = list(range(4))
        nc = self.nc = bass.Bass("TRN2", target_bir_lowering=False)
        self.P = Prog(nc)
        self.A = Arena(nc)
        self.uid = 0
        dt = nc.dram_tensor
        self.inp = {}
        specs = dict(
            x=[NTO, D], c=[1, D], c_ctx=[1, D], sel=[128, 2],
            w_ada=[2, D, 18432], b_ada=[2, 18432], norm_g=[2, 3, D], final_norm_g=[1, D],
            ffn_w_gate=[2, 2, D, DFF], ffn_w_up=[2, 2, D, DFF], ffn_w_down=[2, 2, DFF, D],
            w_in=[2, D, PIN], hgrn_lower_bounds=[2, 2, 1024], hgrn_norm_g=[2, 128],
            dn_conv_w=[2, 5, 3072], dn_a_log=[2, 2, 8], dn_dt_bias=[2, 2, 8], dn_norm_g=[2, 128],
            w_branch_a=[2, 1024, D], w_branch_b=[2, 1024, D], w_out=[2, D, D])
        for k, shp in specs.items():
            self.inp[k] = dt(k, shp, F32, kind="ExternalInput").ap()
        self.out = dt("out", [NTO, D], F32, kind="ExternalOutput").ap()
        self.xT_d = dt("xT_d", [NKC, 128, NTO], F32, kind="Internal").ap()
        self.hT_d = dt("hT_own_d", [NKC, 128, NTO], BF16, kind="Internal").ap()
        self.hT_all_d = dt("hT_all_d", [4, 2, 4, 128, NTO], BF16, kind="Internal").ap()
        self.y_own_d = dt("y_own_d", [2, 4, 128, NT], BF16, kind="Internal").ap()
        self.y_all_d = dt("y_all_d", [4, 2, 2, 128, NT], BF16, kind="Internal").ap()
        self.yaT_d = self.y_own_d[0]
        self.ybT_d = self.y_own_d[1]
        self.dbg_out = {}
        self.ps = [nc.alloc_psum_tensor(f"ps{i}", [128, 512], F32) for i in range(8)]

    def tok(self, *a):
        return a

    def new(self, prefix):
        self.uid += 1
        return f"{prefix}{self.uid}"

    def consts(self):
        P, A, nc = self.P, self.A, self.nc
        self.ident = A.tile([128, 128], F32, "ident")
        self.ones_f = A.tile([128, 128], F32, "ones_f")
        self.ones_r = A.tile([128, 128], F32R, "ones_r")
        self.ident_b = A.tile([128, 128], BF16, "ident_b")
        self.selT = A.tile([128, 2], F32, "selT")
        P.op("sp", lambda e: e.dma_start(out=self.selT[:], in_=self.inp["sel"]), writes=["selT"], lane="selT")
        self.epsc = A.tile([128, 1], F32, "epsc")
        P.op("pool", lambda e: e.memset(self.epsc[:], EPS), writes=["epsc"])
        P.op("pool", lambda e: e.memset(self.ones_f[:], 1.0), writes=["ones_f"])
        P.op("act", lambda e: e.activation(out=self.ones_r[:], in_=self.ones_f[:], func=AF.Copy), reads=["ones_f"], writes=["ones_r"])
        P.op("pool", lambda e: e.affine_select(out=self.ident[:], in_=self.ones_f[:], pattern=[[-1, 128]],
                                               compare_op=ALU.is_equal, fill=0.0, base=0, channel_multiplier=1),
             reads=["ones_f"], writes=["ident"])
        P.op("dve", lambda e: e.tensor_copy(out=self.ident_b[:], in_=self.ident[:]), reads=["ident"], writes=["ident_b"])

    def load_T(self, src2d, rows, dst_ap, dst_tok, scratch, ps_idx=7):
        P = self.P
        lane = "ldT"
        P.op("sp", lambda e: e.dma_start(out=scratch[0:rows, :], in_=src2d), writes=["ldT_s"], lane=lane)
        ps = self.ps[ps_idx]
        P.op("pe", lambda e: e.transpose(ps[:, 0:rows], scratch[0:rows, :], self.ident[0:rows, 0:rows]),
             reads=["ldT_s", "ident"], writes=[("ps", ps_idx)])
        P.op("dve", lambda e: e.tensor_copy(out=dst_ap, in_=ps[:, 0:rows]), reads=[("ps", ps_idx)], writes=[dst_tok])

    def phase_in(self):
        P, A = self.P, self.A
        A.push()
        tin = [A.tile([128, D], F32, "tin") for _ in range(2)]
        tout = [A.tile([128, NKC, 128], F32, "tout") for _ in range(2)]
        for t in range(NTO // 128):
            s = t % 2
            src = self.inp["x"][t * 128:(t + 1) * 128, :]
            P.op("sp", lambda e, s=s, src=src: e.dma_start(out=tin[s][:], in_=src), writes=[("tin", s)], lane=f"tin{s}")
            for q in range(4):
                bank = (t % 2) * 4 + q
                for j in range(4):
                    c = q * 4 + j
                    P.op("pe", lambda e, s=s, c=c, bank=bank, j=j: e.transpose(
                        self.ps[bank][:, j * 128:(j + 1) * 128], tin[s][:, c * 128:(c + 1) * 128], self.ident[:]),
                        reads=[("tin", s), "ident"], writes=[("ps", bank)])
                eng = "act" if q % 2 == 0 else "dve"
                if eng == "act":
                    P.op("act", lambda e, s=s, q=q, bank=bank: e.activation(
                        out=tout[s][:, q * 4:(q + 1) * 4, :], in_=self.ps[bank][:].rearrange("p (a b) -> p a b", a=4), func=AF.Copy),
                        reads=[("ps", bank)], writes=[("tout", s, q)])
                else:
                    P.op("dve", lambda e, s=s, q=q, bank=bank: e.tensor_copy(
                        out=tout[s][:, q * 4:(q + 1) * 4, :], in_=self.ps[bank][:].rearrange("p (a b) -> p a b", a=4)),
                        reads=[("ps", bank)], writes=[("tout", s, q)])
            P.op("sp", lambda e, s=s, t=t: e.dma_start(
                out=self.xT_d[:, :, t * 128:(t + 1) * 128].rearrange("c p n -> p c n"), in_=tout[s][:]),
                reads=[("tout", s, q) for q in range(4)], writes=[("xT", c, t) for c in range(NKC)], lane=f"tout{s}")
        A.pop()
        P.barrier()

    def phase_adaln(self):
        P, A = self.P, self.A
        self.modT = [A.tile([128, 144, 2], F32, f"modT{l}") for l in range(2)]
        self.gT = A.tile([128, 2 * 3 * NKC + NKC], F32, "gT")
        A.push()
        scr = A.tile([128, 128], F32, "scr")
        sc = A.tile([128, NKC, 2], F32, "sc")
        craw = A.tile([128, 2 * NKC], F32, "craw")
        one2 = A.tile([1, 2], F32, "one2")
        P.op("pool", lambda e: e.memset(one2[:], 1.0), writes=["one2"])
        self.load_T(self.inp["c"].rearrange("o (c p) -> (o c) p", p=128), NKC, craw[:, 0:NKC], "craw0", scr)
        self.load_T(self.inp["c_ctx"].rearrange("o (c p) -> (o c) p", p=128), NKC, craw[:, NKC:2 * NKC], "craw1", scr)
        for v in range(2):
            P.op("act", lambda e, v=v: e.activation(out=sc[:, :, v], in_=craw[:, v * NKC:(v + 1) * NKC], func=AF.Silu),
                 reads=[f"craw{v}"], writes=[("sc", v)])
        self.load_T(self.inp["norm_g"].rearrange("l s (c p) -> (l s c) p", p=128), 96, self.gT[:, 0:96], "gT0", scr)
        self.load_T(self.inp["final_norm_g"].rearrange("o (c p) -> (o c) p", p=128), NKC, self.gT[:, 96:112], "gT1", scr)
        wsl = [A.tile([128, NKC, 512], F32, "wada") for _ in range(2)]
        brow = [A.tile([1, 512], F32, "brow") for _ in range(2)]
        for l in range(2):
            wv = self.inp["w_ada"][l].rearrange("(kc p) n -> p kc n", p=128)
            bank = 6
            for s in range(36):
                sl = s % 2
                P.op("sp", lambda e, sl=sl, s=s, wv=wv: e.dma_start(out=wsl[sl][:], in_=wv[:, :, s * 512:(s + 1) * 512]),
                     writes=[("wada", sl)], lane=f"wada{sl}")
                P.op("sp", lambda e, sl=sl, s=s, l=l: e.dma_start(out=brow[sl][:], in_=self.inp["b_ada"][l:l + 1, s * 512:(s + 1) * 512]),
                     writes=[("brow", sl)], lane=f"brow{sl}")
                for j in range(4):
                    ch = s * 4 + j
                    for k in range(NKC):
                        P.op("pe", lambda e, sl=sl, j=j, k=k, ch=ch: e.matmul(
                            self.ps[bank][:, ch * 2:ch * 2 + 2], lhsT=wsl[sl][:, k, j * 128:(j + 1) * 128], rhs=sc[:, k, :],
                            start=(k == 0), stop=False),
                            reads=[("wada", sl), ("sc", 0), ("sc", 1)], writes=[("ps", bank)])
                    P.op("pe", lambda e, sl=sl, j=j, ch=ch: e.matmul(
                        self.ps[bank][:, ch * 2:ch * 2 + 2], lhsT=brow[sl][0:1, j * 128:(j + 1) * 128], rhs=one2[0:1, :],
                        start=False, stop=True),
                        reads=[("brow", sl), "one2"], writes=[("ps", bank)])
            P.op("dve", lambda e, l=l: e.tensor_copy(out=self.modT[l][:].rearrange("p a b -> p (a b)"), in_=self.ps[bank][:, 0:288]),
                 reads=[("ps", bank)], writes=[("modT", l)])
        A.pop()
        self.gmod = A.tile([128, 2, 3, 2, NKC], F32, "gmod")
        self.shift = A.tile([128, 2, 3, 2, NKC], F32, "shift")
        self.gate = A.tile([128, 2, 3, 2, NKC], F32, "gate")
        for l in range(2):
            for sub in range(3):
                for v in range(2):
                    base = sub * 3 * NKC
                    m = self.modT[l]
                    g = self.gT[:, (l * 3 + sub) * NKC:(l * 3 + sub + 1) * NKC]
                    P.op("dve", lambda e, l=l, sub=sub, v=v, m=m, g=g, base=base: e.scalar_tensor_tensor(
                        out=self.gmod[:, l, sub, v, :], in0=m[:, base + NKC:base + 2 * NKC, v], scalar=1.0, in1=g,
                        op0=ALU.add, op1=ALU.mult), reads=[("modT", l), "gT0"], writes=[("gmod", l, sub, v)])
                    P.op("dve", lambda e, l=l, sub=sub, v=v, m=m, base=base: e.tensor_copy(
                        out=self.shift[:, l, sub, v, :], in_=m[:, base:base + NKC, v]),
                        reads=[("modT", l)], writes=[("shift", l, sub, v)])
                    fac = 1.0 if sub == 1 else 0.5
                    P.op("dve", lambda e, l=l, sub=sub, v=v, m=m, base=base, fac=fac: e.tensor_scalar(
                        out=self.gate[:, l, sub, v, :], in0=m[:, base + 2 * NKC:base + 3 * NKC, v], scalar1=fac, scalar2=None,
                        op0=ALU.mult), reads=[("modT", l)], writes=[("gate", l, sub, v)])
        P.barrier()

    def groups(self):
        return [(0, 256, 1), (256, 896, 0)]

    def prologue(self, t0, G, gm_ap, sh_ap, hT, hT_tok, to_bf16=True):
        P, A = self.P, self.A
        A.push()
        xc = [A.tile([128, G], F32, "xc") for _ in range(4)]
        sq = [A.tile([128, G], F32R, "sq") for _ in range(2)]
        tmp = [A.tile([128, G], F32, "tmp") for _ in range(2)]
        rstd = A.tile([128, G], F32, "rstd")
        nh = (G + 511) // 512
        hw = G // nh
        u = self.new("pg")
        for c in range(NKC):
            s = c % 4
            P.op("sp", lambda e, s=s, c=c: e.dma_start(out=xc[s][:], in_=self.xT_d[c, :, t0:t0 + G]),
                 reads=[("xT", c, t) for t in range(t0 // 128, (t0 + G) // 128)], writes=[(u, "xc", s)], lane=f"xc{s}")
            q = c % 2
            P.op("act", lambda e, s=s, q=q: e.activation(out=sq[q][:], in_=xc[s][:], func=AF.Square),
                 reads=[(u, "xc", s)], writes=[(u, "sq", q)])
            for h in range(nh):
                P.op("pe", lambda e, q=q, h=h, c=c: e.matmul(self.ps[h][:, 0:hw], lhsT=self.ones_r[:], rhs=sq[q][:, h * hw:(h + 1) * hw],
                                                            start=(c == 0), stop=(c == NKC - 1)),
                     reads=[(u, "sq", q), "ones_r"], writes=[("ps", h)])
        for h in range(nh):
            P.op("act", lambda e, h=h: e.activation(out=rstd[:, h * hw:(h + 1) * hw], in_=self.ps[h][:, 0:hw], func=AF.Ln,
                                                    bias=self.epsc[:, 0:1], scale=1.0 / D),
                 reads=[("ps", h), "epsc"], writes=[(u, "rstd", h)])
            P.op("act", lambda e, h=h: e.activation(out=rstd[:, h * hw:(h + 1) * hw], in_=rstd[:, h * hw:(h + 1) * hw], func=AF.Exp, scale=-0.5),
                 reads=[(u, "rstd", h)], writes=[(u, "rstd", h)])
        for c in range(NKC):
            s = c % 4
            q = c % 2
            P.op("sp", lambda e, s=s, c=c: e.dma_start(out=xc[s][:], in_=self.xT_d[c, :, t0:t0 + G]),
                 reads=[("xT", c, t) for t in range(t0 // 128, (t0 + G) // 128)], writes=[(u, "xc", s)], lane=f"xc{s}")
            P.op("dve", lambda e, s=s, q=q: e.tensor_tensor(out=tmp[q][:], in0=xc[s][:], in1=rstd[:], op=ALU.mult),
                 reads=[(u, "xc", s)] + [(u, "rstd", h) for h in range(nh)], writes=[(u, "tmp", q)])
            if sh_ap is not None:
                P.op("act", lambda e, q=q, c=c: e.activation(out=hT[:, c, 0:G], in_=tmp[q][:], func=AF.Identity,
                                                             bias=sh_ap[:, c:c + 1], scale=gm_ap[:, c:c + 1]),
                     reads=[(u, "tmp", q)], writes=[(hT_tok, c)])
            else:
                P.op("act", lambda e, q=q, c=c: e.activation(out=hT[:, c, 0:G], in_=tmp[q][:], func=AF.Identity,
                                                             scale=gm_ap[:, c:c + 1]),
                     reads=[(u, "tmp", q)], writes=[(hT_tok, c)])
        A.pop()
        P.barrier()

    def ffn(self, l, f):
        P, A = self.P, self.A
        sub = 0 if f == 0 else 2
        wgv = self.inp["ffn_w_gate"][l, f].rearrange("(kc p) n -> p kc n", p=128)
        wuv = self.inp["ffn_w_up"][l, f].rearrange("(kc p) n -> p kc n", p=128)
        wdv = self.inp["ffn_w_down"][l, f].rearrange("(kc p) n -> p kc n", p=128)
        A.push()
        Gmax = 1024
        hT = A.tile([128, NKC, Gmax], BF16, "hT")
        wg = [A.tile([128, NKC, 128], BF16, "wg") for _ in range(3)]
        wu = [A.tile([128, NKC, 128], BF16, "wu") for _ in range(3)]
        wd = [A.tile([128, NFC, 128], BF16, "wd") for _ in range(2)]
        for (t0, G, v) in self.groups():
            self.ffn_group(l, f, sub, t0, G, v, hT, wg, wu, wd, wgv, wuv, wdv)
        A.pop()

    def ffn_group(self, l, f, sub, t0, G, v, hT, wg, wu, wd, wgv, wuv, wdv):
        P, A = self.P, self.A
        u = self.new("ffn")
        self.prologue(t0, G, self.gmod[:, l, sub, v, :], self.shift[:, l, sub, v, :], hT, (u, "hT"))
        if "dump_ffn" in self.stages and t0 == 0 and l == 0 and f == 0:
            self.dump_tile("dbg_hT", hT[:, :, 0:G], [128, NKC, G], [((u, "hT"), c) for c in range(NKC)])
        A.push()
        actT = A.tile([128, NFC, G], BF16, "actT")
        nh = (G + 511) // 512
        hw = G // nh
        sg = [A.tile([128, hw], F32, "sg") for _ in range(2 * nh)]
        xc = [A.tile([128, G], F32, "xe") for _ in range(3)]
        hreads = [((u, "hT"), c) for c in range(NKC)]

        def load_gu(j):
            s = j % 3
            P.op("pool", lambda e, s=s, j=j: e.dma_start(out=wg[s][:], in_=wgv[:, :, j * 128:(j + 1) * 128]),
                 writes=[("wg", s)], lane=f"wg{s}")
            P.op("pool", lambda e, s=s, j=j: e.dma_start(out=wu[s][:], in_=wuv[:, :, j * 128:(j + 1) * 128]),
                 writes=[("wu", s)], lane=f"wu{s}")

        def load_d(n):
            s = n % 2
            P.op("pool", lambda e, s=s, n=n: e.dma_start(out=wd[s][:], in_=wdv[:, :, n * 128:(n + 1) * 128]),
                 writes=[("wd", s)], lane=f"wd{s}")

        load_gu(0)
        load_gu(1)
        for j in range(NFC):
            if j + 2 < NFC:
                load_gu(j + 2)
            elif j + 2 == NFC:
                load_d(0)
            else:
                load_d(1)
            s = j % 3
            par = j % 2
            for h in range(nh):
                bg = par * 4 + h
                bu = par * 4 + 2 + h
                for k in range(NKC):
                    P.op("pe", lambda e, s=s, k=k, h=h, bg=bg: e.matmul(
                        self.ps[bg][:, 0:hw], lhsT=wg[s][:, k, :], rhs=hT[:, k, h * hw:(h + 1) * hw],
                        start=(k == 0), stop=(k == NKC - 1)),
                        reads=[("wg", s)] + (hreads if k == 0 else []), writes=[("ps", bg)])
                for k in range(NKC):
                    P.op("pe", lambda e, s=s, k=k, h=h, bu=bu: e.matmul(
                        self.ps[bu][:, 0:hw], lhsT=wu[s][:, k, :], rhs=hT[:, k, h * hw:(h + 1) * hw],
                        start=(k == 0), stop=(k == NKC - 1)),
                        reads=[("wu", s)] + (hreads if k == 0 else []), writes=[("ps", bu)])
                si = par * nh + h
                P.op("act", lambda e, si=si, bg=bg: e.activation(out=sg[si][:], in_=self.ps[bg][:, 0:hw], func=AF.Silu),
                     reads=[("ps", bg)], writes=[(u, "sg", si)])
                P.op("dve", lambda e, si=si, bu=bu, j=j, h=h: e.tensor_tensor(
                    out=actT[:, j, h * hw:(h + 1) * hw], in0=sg[si][:], in1=self.ps[bu][:, 0:hw], op=ALU.mult),
                    reads=[(u, "sg", si), ("ps", bu)], writes=[(u, "actT", j, h)])
        if "dump_ffn" in self.stages and t0 == 0 and l == 0 and f == 0:
            self.dump_tile("dbg_actT", actT[:], [128, NFC, G], [(u, "actT", j, h) for j in range(NFC) for h in range(nh)])
            self.dump_tile("dbg_wg", wg[(NFC - 1) % 3][:], [128, NKC, 128], [("wg", (NFC - 1) % 3)])
        for n in range(NKC):
            s = n % 2
            xs = n % 3
            P.op("sp", lambda e, xs=xs, n=n: e.dma_start(out=xc[xs][:], in_=self.xT_d[n, :, t0:t0 + G]),
                 reads=[("xT", n, t) for t in range(t0 // 128, (t0 + G) // 128)], writes=[(u, "xe", xs)], lane=f"xe{xs}")
            for h in range(nh):
                by = (n % 2) * 2 + h
                for j in range(NFC):
                    P.op("pe", lambda e, s=s, j=j, h=h, by=by: e.matmul(
                        self.ps[by][:, 0:hw], lhsT=wd[s][:, j, :], rhs=actT[:, j, h * hw:(h + 1) * hw],
                        start=(j == 0), stop=(j == NFC - 1)),
                        reads=[("wd", s), (u, "actT", j, h)], writes=[("ps", by)])
                P.op("dve", lambda e, xs=xs, h=h, by=by, n=n: e.scalar_tensor_tensor(
                    out=xc[xs][:, h * hw:(h + 1) * hw], in0=self.ps[by][:, 0:hw], scalar=self.gate[:, l, sub, v, n:n + 1],
                    in1=xc[xs][:, h * hw:(h + 1) * hw], op0=ALU.mult, op1=ALU.add),
                    reads=[("ps", by), (u, "xe", xs)], writes=[(u, "xe", xs)])
            P.op("sp", lambda e, xs=xs, n=n: e.dma_start(out=self.xT_d[n, :, t0:t0 + G], in_=xc[xs][:]),
                 reads=[(u, "xe", xs)], writes=[("xT", n, t) for t in range(t0 // 128, (t0 + G) // 128)], lane=f"xs{xs}")
            if n + 2 < NKC:
                load_d(n + 2)
        A.pop()
        P.barrier()

    def nb(self):
        self.bank_rr = (getattr(self, "bank_rr", -1) + 1) % 8
        return self.bank_rr

    def phase_small(self):
        P, A = self.P, self.A
        self.lb = A.tile([128, 2, 16], F32, "lb")
        self.oml = A.tile([128, 2, 16], F32, "oml")
        self.hng = A.tile([128, 2], F32, "hng")
        self.dng = A.tile([128, 2], F32, "dng")
        self.convT = A.tile([128, 2, 120], F32, "convT")
        self.alog = A.tile([128, 2, 16], F32, "alog")
        self.dtb = A.tile([128, 2, 16], F32, "dtb")
        self.maskF = A.tile([128, 128], F32, "maskF")
        self.maskB = A.tile([128, 128], F32, "maskB")
        self.mLT = A.tile([128, 128], F32, "mLT")
        self.mGT = A.tile([128, 128], F32, "mGT")
        self.bigP = [A.tile([128, 128], F32, "bigP") for _ in range(4)]
        self.bigN = [A.tile([128, 128], F32, "bigN") for _ in range(4)]
        self.maskF4 = A.tile([128, 4, 128], F32, "maskF4")
        self.maskB4 = A.tile([128, 4, 128], F32, "maskB4")
        A.push()
        scr = A.tile([128, 128], F32, "scr")
        raw = A.tile([128, 32], F32, "lbraw")
        self.load_T(self.inp["hgrn_lower_bounds"].rearrange("l d (c p) -> (l d c) p", p=128), 32, raw[:, :], "lbraw", scr)
        P.op("pool", lambda e: e.memset(self.lb[:, 0, :], 0.0), writes=["lb0"])
        P.op("dve", lambda e: e.tensor_tensor(out=self.lb[:, 1, :], in0=raw[:, 16:32], in1=raw[:, 0:16], op=ALU.subtract),
             reads=["lbraw"], writes=["lb1"])
        P.op("act", lambda e: e.activation(out=self.lb[:, 1, :], in_=self.lb[:, 1, :], func=AF.Sigmoid), reads=["lb1"], writes=["lb1"])
        P.op("dve", lambda e: e.tensor_scalar(out=self.oml[:].rearrange("p a b -> p (a b)"), in0=self.lb[:].rearrange("p a b -> p (a b)"),
                                              scalar1=-1.0, scalar2=1.0, op0=ALU.mult, op1=ALU.add),
             reads=["lb0", "lb1"], writes=["oml"])
        self.load_T(self.inp["hgrn_norm_g"], 2, self.hng[:, :], "hng", scr)
        self.load_T(self.inp["dn_norm_g"], 2, self.dng[:, :], "dng", scr)
        for l in range(2):
            self.load_T(self.inp["dn_conv_w"][l].rearrange("j (c p) -> (j c) p", p=128), 120, self.convT[:, l, :], ("convT", l), scr)
        P.op("sp", lambda e: e.dma_start(out=self.alog[:].rearrange("p a b -> p (a b)"),
                                         in_=self.inp["dn_a_log"].rearrange("l d h -> (l d h)").partition_broadcast(128)),
             writes=["alog"], lane="smallc")
        P.op("sp", lambda e: e.dma_start(out=self.dtb[:].rearrange("p a b -> p (a b)"),
                                         in_=self.inp["dn_dt_bias"].rearrange("l d h -> (l d h)").partition_broadcast(128)),
             writes=["dtb"], lane="smallc")
        P.op("act", lambda e: e.activation(out=self.alog[:].rearrange("p a b -> p (a b)"), in_=self.alog[:].rearrange("p a b -> p (a b)"), func=AF.Exp),
             reads=["alog"], writes=["alog"])
        P.op("dve", lambda e: e.tensor_scalar(out=self.alog[:].rearrange("p a b -> p (a b)"), in0=self.alog[:].rearrange("p a b -> p (a b)"),
                                              scalar1=-1.0, scalar2=None, op0=ALU.mult), reads=["alog"], writes=["alog"])
        P.op("pool", lambda e: e.affine_select(out=self.maskF[:], in_=self.ones_f[:], pattern=[[1, 128]],
                                               compare_op=ALU.is_ge, fill=0.0, base=0, channel_multiplier=-1),
             reads=["ones_f"], writes=["maskF"])
        P.op("pool", lambda e: e.memset(self.maskF[0:64, 64:128], 0.0), reads=["maskF"], writes=["maskF"])
        P.op("pool", lambda e: e.affine_select(out=self.maskB[:], in_=self.ones_f[:], pattern=[[-1, 128]],
                                               compare_op=ALU.is_ge, fill=0.0, base=0, channel_multiplier=1),
             reads=["ones_f"], writes=["maskB"])
        P.op("pool", lambda e: e.memset(self.maskB[64:128, 0:64], 0.0), reads=["maskB"], writes=["maskB"])
        P.op("pool", lambda e: e.tensor_tensor(out=self.mLT[:], in0=self.maskB[:], in1=self.ident[:], op=ALU.subtract), reads=["maskB", "ident"], writes=["mLT"])
        P.op("pool", lambda e: e.tensor_tensor(out=self.mGT[:], in0=self.maskF[:], in1=self.ident[:], op=ALU.subtract), reads=["maskF", "ident"], writes=["mGT"])
        for i, (m, mt) in enumerate(((self.mLT, "mLT"), (self.mGT, "mGT"), (self.maskF, "maskF"), (self.maskB, "maskB"))):
            P.op("dve", lambda e, i=i, m=m: e.tensor_scalar(out=self.bigP[i][:], in0=m[:], scalar1=-30000.0, scalar2=30000.0, op0=ALU.mult, op1=ALU.add),
                 reads=[mt], writes=["bigm"])
            P.op("dve", lambda e, i=i, m=m: e.tensor_scalar(out=self.bigN[i][:], in0=m[:], scalar1=30000.0, scalar2=-30000.0, op0=ALU.mult, op1=ALU.add),
                 reads=[mt], writes=["bigm"])
        for j in range(4):
            P.op("pool", lambda e, j=j: e.tensor_copy(out=self.maskF4[:, j, :], in_=self.maskF[:]), reads=["maskF"], writes=["mask4"])
            P.op("pool", lambda e, j=j: e.tensor_copy(out=self.maskB4[:, j, :], in_=self.maskB[:]), reads=["maskB"], writes=["mask4"])
        A.pop()
        P.barrier()

    def mixer_prep(self, l):
        A = self.A
        A.push()
        hT = A.tile([128, NKC, 1024], BF16, "hTm")
        for (t0, G, v) in self.groups():
            self.mixer_prep_group(l, t0, G, v, hT)
        A.pop()
        for j in range(4):
            self.P.op("pool", lambda e, j=j: e.collective_compute(
                "AllGather", ALU.bypass, replica_groups=[[0, 1], [2, 3], [4, 5], [6, 7]],
                ins=[self.hT_d[4 * j:4 * j + 4].rearrange("c p n -> (c p) n")], outs=[self.hT_all_d[j].rearrange("r c p n -> (r c p) n")]),
                reads=[("hTd", t) for t in range(NTO // 128)], writes=[("hTall", j)], lane="cc_h", step=1)
        self.P.barrier()

    def mixer_prep_group(self, l, t0, G, v, hT):
        P = self.P
        u = self.new("mp")
        self.prologue(t0, G, self.gmod[:, l, 1, v, :], self.shift[:, l, 1, v, :], hT, (u, "hT"))
        P.op("sp", lambda e: e.dma_start(out=self.hT_d[:, :, t0:t0 + G].rearrange("c p n -> p c n"), in_=hT[:, :, 0:G]),
             reads=[((u, "hT"), c) for c in range(NKC)], writes=[("hTd", t) for t in range(t0 // 128, (t0 + G) // 128)], lane="hTd")
        P.barrier()

    TG = [(0, 384), (384, 384), (768, 384), (1152, 384), (1536, 384), (1920, 384)]

    def proj_fm(self, w, hg, G, evac):
        P = self.P
        bank = self.nb()
        wt, wtok = w
        hgt, hgtok = hg
        for k in range(NKC):
            P.op("pe", lambda e, k=k: e.matmul(self.ps[bank][:, 0:G], lhsT=wt[:, k, :], rhs=hgt[:, k, 0:G],
                                               start=(k == 0), stop=(k == NKC - 1)),
                 reads=[wtok] + [hgtok + (j,) for j in range(4)], writes=[("ps", bank)])
        evac(bank)

    def proj_tm(self, w, hg, G, ncols, evac):
        P = self.P
        wt, wtok = w
        hgt, hgtok = hg
        for ti in range(G // 128):
            bank = self.nb()
            for k in range(NKC):
                P.op("pe", lambda e, k=k, ti=ti, bank=bank: e.matmul(self.ps[bank][:, 0:ncols], lhsT=hgt[:, k, ti * 128:(ti + 1) * 128],
                                                                   rhs=wt[:, k, 0:ncols], start=(k == 0), stop=(k == NKC - 1)),
                     reads=[wtok] + [hgtok + (j,) for j in range(4)], writes=[("ps", bank)])
            evac(bank, ti)

    def load_w_in(self, l, col, ncols, tile_, tok, lane):
        wv = self.inp["w_in"][l].rearrange("(kc p) n -> p kc n", p=128)
        self.P.op("pool", lambda e: e.dma_start(out=tile_[:, :, 0:ncols], in_=wv[:, :, col:col + ncols]), writes=[tok], lane=lane)

    def load_hg(self, hg, slot, t0, G, u):
        r, off = t0 // NTO, t0 % NTO
        for j in range(4):
            self.P.op("sp", lambda e, j=j: e.dma_start(out=hg[slot][:, 4 * j:4 * j + 4, 0:G],
                                                       in_=self.hT_all_d[j, r][:, :, off:off + G].rearrange("c p n -> p c n")),
                      reads=[("hTall", j)], writes=[(u, "hg", slot, j)], lane=f"hg{slot}")

    def hgrn_head(self, l, h):
        P, A = self.P, self.A
        u = self.new("hg")
        A.push()
        qa = A.tile([128, NT], F32, "qa")
        lf = [A.tile([128, NT], F32, "lf") for _ in range(2)]
        kk = [A.tile([128, NT], F32, "kk") for _ in range(2)]
        ga = A.tile([128, NT], F32, "ga")
        Vt = A.tile([128, NT // 128, 128], BF16, "Vt")
        oT = A.tile([128, NT], F32, "oT")
        self.hgrn_proj(l, h, u, qa, lf, kk, ga, Vt)
        if "hg_stop1" in self.stages:
            self.dump_tile("dbg_qa", qa[:], [128, NT], [(u, "qa", gi) for gi in range(NG)])
            self.dump_tile("dbg_lf0", lf[0][:], [128, NT], [(u, "lf", 0, gi) for gi in range(NG)])
            self.dump_tile("dbg_kk1", kk[1][:], [128, NT], [(u, "kk", 1, gi) for gi in range(NG)])
            self.dump_tile("dbg_Vt", Vt[:], [128, NT // 128, 128], [(u, "Vt", t) for t in range(18)])
            A.pop()
            P.barrier()
            return
        for d in range(2):
            self.hgrn_dir(l, h, u, d, qa, lf[d], kk[d], Vt, oT)
            if "hg_stop2" in self.stages:
                self.dump_tile("dbg_oT0", oT[:], [128, NT], [(u, "oT", q) for q in range(5)])
                A.pop()
                P.barrier()
                return
        self.head_out(u, oT, ga, "ga", self.hng[:, l:l + 1], self.yaT_d[h], ("ya", h))
        if "dump_oa" in self.stages and l == 0:
            self.dump_tile(f"dbg_oa{h}", oT[:], [128, NT], [(u, "oT", q) for q in range(5)])
        A.pop()
        P.barrier()

    def hgrn_proj(self, l, h, u, qa, lf, kk, ga, Vt):
        P, A = self.P, self.A
        A.push()
        cols = [0 + 128 * h, 1024 + 128 * h, 2048 + 128 * h, 3072 + 128 * h, 4096 + 128 * h]
        w = [A.tile([128, NKC, 128], BF16, "wm") for _ in range(5)]
        hg = [A.tile([128, NKC, 512], BF16, "hgm") for _ in range(2)]
        sgm = [A.tile([128, 512], F32, "sgm") for _ in range(2)]
        for i in range(5):
            self.load_w_in(l, cols[i], 128, w[i], (u, "w", i), f"wm{i}")
        for gi, (t0, G) in enumerate(self.TG):
            self.hgrn_proj_group(l, h, u, gi, t0, G, w, hg, sgm, qa, lf, kk, ga, Vt)
        A.pop()
        P.barrier()

    def hgrn_proj_group(self, l, h, u, gi, t0, G, w, hg, sgm, qa, lf, kk, ga, Vt):
        P = self.P
        slot = gi % 2
        self.load_hg(hg, slot, t0, G, u)
        hgs = (hg[slot], (u, "hg", slot))
        tl = (u, "g", gi)
        self.proj_fm((w[0], (u, "w", 0)), hgs, G, lambda bank: P.op(
            "act", lambda e: e.activation(out=qa[:, t0:t0 + G], in_=self.ps[bank][:, 0:G], func=AF.Silu),
            reads=[("ps", bank)], writes=[(u, "qa", gi)]))
        for d in range(2):
            idx = d * 8 + h

            def ev(bank, d=d, idx=idx):
                P.op("act", lambda e: e.activation(out=sgm[d][:, 0:G], in_=self.ps[bank][:, 0:G], func=AF.Sigmoid),
                     reads=[("ps", bank)], writes=[(u, "sgm", d)])
                P.op("dve", lambda e: e.tensor_scalar(out=sgm[d][:, 0:G], in0=sgm[d][:, 0:G], scalar1=self.oml[:, l, idx:idx + 1],
                                                      scalar2=self.lb[:, l, idx:idx + 1], op0=ALU.mult, op1=ALU.add),
                     reads=[(u, "sgm", d), "oml", "lb0", "lb1"], writes=[(u, "sgm", d)])
                P.op("dve", lambda e: e.tensor_scalar(out=kk[d][:, t0:t0 + G], in0=sgm[d][:, 0:G], scalar1=-1.0, scalar2=1.0,
                                                      op0=ALU.mult, op1=ALU.add),
                     reads=[(u, "sgm", d)], writes=[(u, "kk", d, gi)])
                P.op("act", lambda e: e.activation(out=lf[d][:, t0:t0 + G], in_=sgm[d][:, 0:G], func=AF.Ln),
                     reads=[(u, "sgm", d)], writes=[(u, "lf", d, gi)])
            self.proj_fm((w[1 + d], (u, "w", 1 + d)), hgs, G, ev)
        self.proj_fm((w[4], (u, "w", 4)), hgs, G, lambda bank: P.op(
            "act", lambda e: e.activation(out=ga[:, t0:t0 + G], in_=self.ps[bank][:, 0:G], func=AF.Silu),
            reads=[("ps", bank)], writes=[(u, "ga", gi)]))
        self.proj_tm((w[3], (u, "w", 3)), hgs, G, 128, lambda bank, ti: P.op(
            "dve", lambda e: e.tensor_copy(out=Vt[:, t0 // 128 + ti, :], in_=self.ps[bank][:, 0:128]),
            reads=[("ps", bank)], writes=[(u, "Vt", t0 // 128 + ti)]))

    def chain_pos(self, d, c):
        if d == 0:
            return c
        return 3 - c if c < 4 else 4 + (35 - c)

    def hgrn_dir(self, l, h, u0, d, qa, lf, kk, Vt, oT):
        P, A = self.P, self.A
        u = self.new("hd")
        NC = NT // 64
        A.push()
        X = A.tile([128, NT], F32, "X")
        dd = A.tile([128, NT], F32, "dd")
        ex = A.tile([128, NT], F32, "ex")
        Qt = A.tile([128, NT], BF16, "Qt")
        Kt = A.tile([128, NT], BF16, "Kt")
        Qi = A.tile([128, NT], BF16, "Qi")
        KuT = A.tile([128, NT], BF16, "KuT")
        Kutok = A.tile([128, NT // 128, 128], BF16, "Kutok")
        ATm = A.tile([128, NT // 128, 128], BF16, "ATm")
        U = A.tile([128, 128, NC], F32, "U")
        decb = A.tile([128, 128, NC], F32, "decb")
        R = A.tile([128, 128, NC], F32, "R")
        S = A.tile([128, NC, 128], BF16, "S")
        refs = A.tile([128, 4, NC], F32, "refs")
        allg = lambda name, dd_=None: [(u0, name, d, gi) for gi in range(5)] if dd_ is None else None
        lf_r = [(u0, "lf", d, gi) for gi in range(NG)]
        kk_r = [(u0, "kk", d, gi) for gi in range(NG)]
        qa_r = [(u0, "qa", gi) for gi in range(NG)]
        decf = decb[:].rearrange("p a b -> p (a b)")
        P.op("pool", lambda e: e.memset(decf[:, 0:NT], 1.0), writes=[(u, "decb")])
        P.op("dve", lambda e: e.tensor_tensor_scan(out=X[:], data0=decf[:, 0:NT], data1=lf[:], initial=0.0,
                                                   op0=ALU.mult, op1=ALU.add), reads=lf_r + [(u, "decb")], writes=[(u, "X")])
        X3 = X[:].rearrange("p (c t) -> p c t", t=64)
        lf3 = lf[:].rearrange("p (c t) -> p c t", t=64)
        if d == 0:
            P.op("dve", lambda e: e.tensor_copy(out=refs[:, 0, :], in_=X3[:, :, 31]), reads=[(u, "X")], writes=[(u, "rm")])
            P.op("dve", lambda e: e.tensor_tensor(out=refs[:, 1, :], in0=X3[:, :, 0], in1=lf3[:, :, 0], op=ALU.subtract),
                 reads=[(u, "X")] + lf_r, writes=[(u, "r0")])
            P.op("dve", lambda e: e.tensor_copy(out=refs[:, 2, :], in_=X3[:, :, 63]), reads=[(u, "X")], writes=[(u, "r1")])
        else:
            P.op("dve", lambda e: e.tensor_copy(out=refs[:, 2, :], in_=X3[:, :, 63]), reads=[(u, "X")], writes=[(u, "r1")])
            P.op("dve", lambda e: e.tensor_tensor(out=X[:], in0=X[:], in1=lf[:], op=ALU.subtract),
                 reads=[(u, "X"), (u, "r1")] + lf_r, writes=[(u, "X")])
            P.op("dve", lambda e: e.tensor_copy(out=refs[:, 0, :], in_=X3[:, :, 32]), reads=[(u, "X")], writes=[(u, "rm")])
            P.op("dve", lambda e: e.tensor_copy(out=refs[:, 1, :], in_=X3[:, :, 0]), reads=[(u, "X")], writes=[(u, "r0")])
        sig = 1.0 if d == 0 else -1.0
        rA = 1 if d == 0 else 2
        rB = 2 if d == 0 else 1
        dd3 = dd[:].rearrange("p (c t) -> p c t", t=64)

        def derive(ri, rtok, outs):
            P.op("dve", lambda e: e.tensor_tensor(out=dd3, in0=X3, in1=refs[:, ri, :].unsqueeze(2).to_broadcast([128, NC, 64]),
                                                  op=ALU.subtract), reads=[(u, "X"), rtok], writes=[(u, "dd")])
            for (sg_, src, sreads, dst, dtok) in outs:
                P.op("act", lambda e, sg_=sg_: e.activation(out=ex[:], in_=dd[:], func=AF.Exp, scale=sg_),
                     reads=[(u, "dd")], writes=[(u, "ex")])
                P.op("dve", lambda e, src=src, dst=dst: e.tensor_tensor(out=dst[:], in0=src[:], in1=ex[:], op=ALU.mult),
                     reads=[(u, "ex")] + sreads, writes=[dtok])
        derive(0, (u, "rm"), [(sig, qa, qa_r, Qt, (u, "Qt")), (-sig, kk, kk_r, Kt, (u, "Kt"))])
        derive(rA, (u, "r0") if rA == 1 else (u, "r1"), [(sig, qa, qa_r, Qi, (u, "Qi"))])
        derive(rB, (u, "r0") if rB == 1 else (u, "r1"), [(-sig, kk, kk_r, KuT, (u, "KuT"))])
        P.op("dve", lambda e: e.tensor_tensor(out=refs[:, 3, :], in0=refs[:, 2, :], in1=refs[:, 1, :], op=ALU.subtract),
             reads=[(u, "r0"), (u, "r1")], writes=[(u, "dec")])
        P.op("act", lambda e: e.activation(out=refs[:, 3, :], in_=refs[:, 3, :], func=AF.Exp), reads=[(u, "dec")], writes=[(u, "dec")])
        if d == 0:
            P.op("dve", lambda e: e.tensor_copy(out=decb[:], in_=refs[:, 3, :].unsqueeze(1).to_broadcast([128, 128, NC])),
                 reads=[(u, "dec")], writes=[(u, "decb")])
        else:
            for c in range(NC):
                pos = self.chain_pos(1, c)
                P.op("act", lambda e, c=c, pos=pos: e.activation(out=refs[:, 0, pos:pos + 1], in_=refs[:, 3, c:c + 1], func=AF.Copy),
                     reads=[(u, "dec"), (u, "Qt"), (u, "Kt")], writes=[(u, "decc")])
            P.op("dve", lambda e: e.tensor_copy(out=decb[:], in_=refs[:, 0, :].unsqueeze(1).to_broadcast([128, 128, NC])),
                 reads=[(u, "decc")], writes=[(u, "decb")])
        P.op("dve", lambda e: e.memset(decb[:, :, 0], 0.0), reads=[(u, "decb")], writes=[(u, "decb")])
        for q in range(5):
            bank = self.nb()
            nt_ = 4 if q < 4 else 2
            psb = self.ps[bank][:].bitcast(BF16)
            for j in range(nt_):
                t = q * 4 + j
                P.op("pe", lambda e, j=j, t=t, psb=psb: e.transpose(psb[:, j * 128:(j + 1) * 128], KuT[:, t * 128:(t + 1) * 128], self.ident_b[:]),
                     reads=[(u, "KuT"), "ident_b"], writes=[("ps", bank)])
            P.op("act", lambda e, q=q, nt_=nt_, psb=psb: e.activation(
                out=Kutok[:, q * 4:q * 4 + nt_, :], in_=psb[:, 0:nt_ * 128].rearrange("p (a b) -> p a b", b=128), func=AF.Copy),
                reads=[("ps", bank)], writes=[(u, "Kutok", q)])
        mask4 = self.maskF4 if d == 0 else self.maskB4
        P.op("pool", lambda e: e.memset(ATm[:], 0.0), writes=[(u, "ATz")])
        for q in range(5):
            bank = self.nb()
            nt_ = 4 if q < 4 else 2
            for j in range(nt_):
                t = q * 4 + j
                P.op("pe", lambda e, j=j, t=t, bank=bank: e.matmul(self.ps[bank][:, j * 128:(j + 1) * 128], lhsT=Kt[:, t * 128:(t + 1) * 128],
                                                                  rhs=Qt[:, t * 128:(t + 1) * 128], start=True, stop=True),
                     reads=[(u, "Kt"), (u, "Qt")], writes=[("ps", bank)])
            P.op("dve", lambda e, q=q, nt_=nt_, bank=bank: e.copy_predicated(
                out=ATm[:, q * 4:q * 4 + nt_, :], mask=mask4[:, 0:nt_, :].bitcast(mybir.dt.uint32),
                data=self.ps[bank][:, 0:nt_ * 128].rearrange("p (a b) -> p a b", b=128)),
                reads=[("ps", bank), "mask4", (u, "ATz")], writes=[(u, "ATm", q)])
        for c in range(NC):
            t, half = c // 2, c % 2
            if c % 8 == 0:
                bankpair = (self.nb(), self.nb())
            bank = bankpair[half]
            j = (c // 2) % 4
            r0_, r1_ = half * 64, half * 64 + 64
            P.op("pe", lambda e, t=t, j=j, r0_=r0_, r1_=r1_, bank=bank: e.matmul(
                self.ps[bank][:, j * 128:(j + 1) * 128], lhsT=Kutok[r0_:r1_, t, :], rhs=Vt[r0_:r1_, t, :], start=True, stop=True),
                reads=[(u, "Kutok", t // 4), (u0, "Vt", t)], writes=[("ps", bank)])
            pos = self.chain_pos(d, c)
            eng = "act" if c % 2 == 0 else "dve"
            if eng == "act":
                P.op("act", lambda e, j=j, pos=pos, bank=bank: e.activation(out=U[:, :, pos], in_=self.ps[bank][:, j * 128:(j + 1) * 128], func=AF.Copy),
                     reads=[("ps", bank)], writes=[(u, "U", c)])
            else:
                P.op("dve", lambda e, j=j, pos=pos, bank=bank: e.tensor_copy(out=U[:, :, pos], in_=self.ps[bank][:, j * 128:(j + 1) * 128]),
                     reads=[("ps", bank)], writes=[(u, "U", c)])
        P.op("dve", lambda e: e.tensor_tensor_scan(out=R[:].rearrange("p a b -> p (a b)"), data0=decb[:].rearrange("p a b -> p (a b)"),
                                                   data1=U[:].rearrange("p a b -> p (a b)"), initial=0.0, op0=ALU.mult, op1=ALU.add),
             reads=[(u, "decb")] + [(u, "U", c) for c in range(NC)], writes=[(u, "R")])
        P.op("pool", lambda e: e.memset(S[:, 0, :], 0.0), writes=[(u, "S0")])
        P.op("act", lambda e: e.activation(out=S[:, 1:NC, :], in_=R[:, :, 0:NC - 1].rearrange("p d c -> p c d"), func=AF.Copy),
             reads=[(u, "R")], writes=[(u, "S")])
        for q in range(5):
            bank = self.nb()
            nt_ = 4 if q < 4 else 2
            for j in range(nt_):
                t = q * 4 + j
                P.op("pe", lambda e, j=j, t=t, bank=bank: e.matmul(self.ps[bank][:, j * 128:(j + 1) * 128], lhsT=Vt[:, t, :], rhs=ATm[:, t, :],
                                                                  start=True, stop=False),
                     reads=[(u0, "Vt", t), (u, "ATm", q)], writes=[("ps", bank)])
                for half in range(2):
                    c = 2 * t + half
                    pos = self.chain_pos(d, c)
                    P.op("pe", lambda e, j=j, c=c, pos=pos, half=half, bank=bank: e.matmul(
                        self.ps[bank][:, j * 128 + half * 64:j * 128 + half * 64 + 64], lhsT=S[:, pos, :], rhs=Qi[:, c * 64:(c + 1) * 64],
                        start=False, stop=(half == 1)),
                        reads=[(u, "S"), (u, "S0"), (u, "Qi")], writes=[("ps", bank)])
            c0_, c1_ = q * 512, q * 512 + nt_ * 128
            if d == 0:
                P.op("act", lambda e, c0_=c0_, c1_=c1_, nt_=nt_, bank=bank: e.activation(out=oT[:, c0_:c1_], in_=self.ps[bank][:, 0:nt_ * 128], func=AF.Copy),
                     reads=[("ps", bank)], writes=[(u0, "oT", q)])
            else:
                P.op("dve", lambda e, c0_=c0_, c1_=c1_, nt_=nt_, bank=bank: e.tensor_tensor(out=oT[:, c0_:c1_], in0=self.ps[bank][:, 0:nt_ * 128],
                                                                                      in1=oT[:, c0_:c1_], op=ALU.add),
                     reads=[("ps", bank), (u0, "oT", q)], writes=[(u0, "oT", q)])
        A.pop()
        P.barrier()

    def head_out(self, u, oT, gate_t, gate_tok, g_ap, dst_d, dtok):
        P, A = self.P, self.A
        A.push()
        sq = [A.tile([128, 512], F32R, "hsq") for _ in range(2)]
        rs = [A.tile([128, 512], F32, "hrs") for _ in range(2)]
        yo = [A.tile([128, 512], BF16, "hyo") for _ in range(2)]
        for gi, (t0, G) in enumerate(self.TG):
            s = gi % 2
            bank = self.nb()
            P.op("act", lambda e, s=s, t0=t0, G=G: e.activation(out=sq[s][:, 0:G], in_=oT[:, t0:t0 + G], func=AF.Square),
                 reads=[(u, "oT", q) for q in range(5)], writes=[(u, "hsq", s)])
            P.op("pe", lambda e, s=s, G=G, bank=bank: e.matmul(self.ps[bank][:, 0:G], lhsT=self.ones_r[:], rhs=sq[s][:, 0:G], start=True, stop=True),
                 reads=[(u, "hsq", s), "ones_r"], writes=[("ps", bank)])
            P.op("act", lambda e, s=s, G=G, bank=bank: e.activation(out=rs[s][:, 0:G], in_=self.ps[bank][:, 0:G], func=AF.Ln, bias=self.epsc[:, 0:1], scale=1.0 / 128),
                 reads=[("ps", bank), "epsc"], writes=[(u, "hrs", s)])
            P.op("act", lambda e, s=s, G=G: e.activation(out=rs[s][:, 0:G], in_=rs[s][:, 0:G], func=AF.Exp, scale=-0.5), reads=[(u, "hrs", s)], writes=[(u, "hrs", s)])
            P.op("dve", lambda e, s=s, t0=t0, G=G: e.scalar_tensor_tensor(out=rs[s][:, 0:G], in0=rs[s][:, 0:G], scalar=g_ap, in1=gate_t[:, t0:t0 + G],
                                                                        op0=ALU.mult, op1=ALU.mult),
                 reads=[(u, "hrs", s), (u, gate_tok, gi), "hng", "dng"], writes=[(u, "hrs", s)])
            P.op("dve", lambda e, s=s, t0=t0, G=G: e.tensor_tensor(out=yo[s][:, 0:G], in0=oT[:, t0:t0 + G], in1=rs[s][:, 0:G], op=ALU.mult),
                 reads=[(u, "hrs", s)] + [(u, "oT", q) for q in range(5)], writes=[(u, "hyo", s)])
            P.op("sp", lambda e, s=s, t0=t0, G=G: e.dma_start(out=dst_d[:, t0:t0 + G], in_=yo[s][:, 0:G]),
                 reads=[(u, "hyo", s)], writes=[(dtok, gi)], lane=f"hyo{s}")
        A.pop()

    def dn_scalars(self, l):
        P, A = self.P, self.A
        NTL = NT // 128
        self.dn_beta = A.tile([128, NTL, 16], F32, "dn_beta")
        self.dn_gc = A.tile([128, NTL, 16], F32, "dn_gc")
        self.dn_egc = A.tile([128, NTL, 16], F32, "dn_egc")
        self.dn_egl = A.tile([128, NTL, 16], F32, "dn_egl")
        self.dn_bg = A.tile([128, NTL, 16], F32, "dn_bg")
        self.dn_dl = A.tile([128, NTL, 2, 16], F32, "dn_dl")
        u = self.new("dns")
        A.push()
        wab = A.tile([128, NKC, 32], BF16, "wab")
        hg = [A.tile([128, NKC, 512], BF16, "hgs") for _ in range(2)]
        ab = A.tile([128, NTL, 32], F32, "ab")
        g = A.tile([128, NTL, 16], F32, "gdn")
        tmp = A.tile([128, NTL, 16], F32, "tdn")
        maskC = A.tile([128, 128], F32, "maskC")
        maskLo = A.tile([128, 128], F32, "maskLo")
        maskHi = A.tile([128, 128], F32, "maskHi")
        P.op("pool", lambda e: e.memset(maskC[:], 0.0), writes=[(u, "mC")])
        P.op("pool", lambda e: e.memset(maskC[0:64, 0:64], 1.0), reads=[(u, "mC")], writes=[(u, "mC")])
        P.op("pool", lambda e: e.memset(maskC[64:128, 64:128], 1.0), reads=[(u, "mC")], writes=[(u, "mC")])
        P.op("pool", lambda e: e.memset(maskLo[:], 0.0), writes=[(u, "mLo")])
        P.op("pool", lambda e: e.memset(maskLo[0:64, :], 1.0), reads=[(u, "mLo")], writes=[(u, "mLo")])
        P.op("pool", lambda e: e.memset(maskHi[:], 0.0), writes=[(u, "mHi")])
        P.op("pool", lambda e: e.memset(maskHi[64:128, :], 1.0), reads=[(u, "mHi")], writes=[(u, "mHi")])
        self.load_w_in(l, 9216, 32, wab, (u, "wab"), "wab")
        for gi, (t0, G) in enumerate(self.TG):
            self.dn_scalars_group(u, gi, t0, G, wab, hg, ab)
        abr = [(u, "ab", t) for t in range(NTL)]
        P.op("dve", lambda e: e.tensor_tensor(out=tmp[:], in0=ab[:, :, 0:16], in1=self.dtb[:, l, :].unsqueeze(1).to_broadcast([128, NTL, 16]), op=ALU.add),
             reads=abr + ["dtb"], writes=[(u, "tmp")])
        P.op("act", lambda e: e.activation(out=tmp[:], in_=tmp[:], func=AF.Exp), reads=[(u, "tmp")], writes=[(u, "tmp")])
        P.op("act", lambda e: e.activation(out=tmp[:], in_=tmp[:], func=AF.Ln, bias=1.0), reads=[(u, "tmp")], writes=[(u, "tmp")])
        P.op("dve", lambda e: e.tensor_tensor(out=g[:], in0=tmp[:], in1=self.alog[:, l, :].unsqueeze(1).to_broadcast([128, NTL, 16]), op=ALU.mult),
             reads=[(u, "tmp"), "alog"], writes=[(u, "g")])
        P.op("act", lambda e: e.activation(out=self.dn_beta[:], in_=ab[:, :, 16:32], func=AF.Sigmoid), reads=abr, writes=["dn_beta"])
        for t in range(NTL):
            bank = self.nb()
            ps = self.ps[bank]
            for (m, mt, c0, c1, o0) in ((self.maskF, "maskF", 0, 8, 0), (self.maskB, "maskB", 8, 16, 8), (maskC, (u, "mC"), 0, 16, 16),
                                        (maskLo, (u, "mLo"), 0, 16, 32), (maskHi, (u, "mHi"), 0, 16, 48)):
                P.op("pe", lambda e, m=m, c0=c0, c1=c1, o0=o0, t=t, ps=ps: e.matmul(ps[:, o0:o0 + (c1 - c0)], lhsT=m[:], rhs=g[:, t, c0:c1], start=True, stop=True),
                     reads=[mt, (u, "g")], writes=[("ps", bank)])
            P.op("dve", lambda e, t=t, ps=ps: e.tensor_copy(out=self.dn_gc[:, t, :], in_=ps[:, 0:16]), reads=[("ps", bank)], writes=[("dn_gc", t)])
            P.op("act", lambda e, t=t, ps=ps: e.activation(out=self.dn_egc[:, t, :], in_=ps[:, 0:16], func=AF.Exp), reads=[("ps", bank)], writes=[("dn_egc", t)])
            P.op("dve", lambda e, t=t, ps=ps: e.tensor_tensor(out=self.dn_egl[:, t, :], in0=ps[:, 16:32], in1=self.dn_gc[:, t, :], op=ALU.subtract),
                 reads=[("ps", bank), ("dn_gc", t)], writes=[("dn_egl", t)])
            P.op("act", lambda e, t=t: e.activation(out=self.dn_egl[:, t, :], in_=self.dn_egl[:, t, :], func=AF.Exp), reads=[("dn_egl", t)], writes=[("dn_egl", t)])
            P.op("act", lambda e, t=t, ps=ps: e.activation(out=self.dn_dl[:, t, :, :], in_=ps[:, 32:64].rearrange("p (a b) -> p a b", a=2), func=AF.Exp),
                 reads=[("ps", bank)], writes=[("dn_dl", t)])
            P.op("dve", lambda e, t=t: e.tensor_tensor(out=self.dn_bg[:, t, :], in0=self.dn_beta[:, t, :], in1=self.dn_egc[:, t, :], op=ALU.mult),
                 reads=["dn_beta", ("dn_egc", t)], writes=[("dn_bg", t)])
        A.pop()
        P.barrier()

    def dn_scalars_group(self, u, gi, t0, G, wab, hg, ab):
        P = self.P
        slot = gi % 2
        self.load_hg(hg, slot, t0, G, u)
        self.proj_tm((wab, (u, "wab")), (hg[slot], (u, "hg", slot)), G, 32, lambda bank, ti: P.op(
            "dve", lambda e: e.tensor_copy(out=ab[:, t0 // 128 + ti, :], in_=self.ps[bank][:, 0:32]),
            reads=[("ps", bank)], writes=[(u, "ab", t0 // 128 + ti)]))

    def dn_head(self, l, h):
        P, A = self.P, self.A
        u = self.new("dn")
        A.push()
        QnT = A.tile([128, NT], BF16, "QnT")
        KnT = A.tile([128, NT], BF16, "KnT")
        Ktok = A.tile([128, NT // 128, 128], BF16, "Ktok")
        Vtok = A.tile([128, NT // 128, 128], BF16, "Vtok")
        gb = A.tile([128, NT], F32, "gb")
        oT = A.tile([128, NT], F32, "oTd")
        self.dn_proj(l, h, u, QnT, KnT, Ktok, Vtok, gb)
        if "dn_stop1" in self.stages:
            self.dump_tile("dbg_QnT", QnT[:], [128, NT], [(u, "QnT", gi) for gi in range(NG)])
            self.dump_tile("dbg_KnT", KnT[:], [128, NT], [(u, "KnT", gi) for gi in range(NG)])
            self.dump_tile("dbg_Vtok", Vtok[:], [128, NT // 128, 128], [(u, "Vtok", q) for q in range(5)])
            A.pop()
            P.barrier()
            return
        for d in range(2):
            self.dn_dir(l, h, u, d, QnT, KnT, Ktok, Vtok, oT)
        if "dump_ob" in self.stages and l == 0:
            self.dump_tile(f"dbg_ob{h}", oT[:], [128, NT], [(u, "oT", q) for q in range(5)])
        self.head_out(u, oT, gb, "gb", self.dng[:, l:l + 1], self.ybT_d[h], ("yb", h))
        A.pop()
        P.barrier()

    def dn_proj(self, l, h, u, QnT, KnT, Ktok, Vtok, gb):
        P, A = self.P, self.A
        A.push()
        cols = [5120 + 128 * h, 6144 + 128 * h, 7168 + 128 * h, 8192 + 128 * h]
        raw = [A.tile([128, NT], F32, "raw") for _ in range(3)]
        cv = [A.tile([128, NT], F32, "cv") for _ in range(3)]
        VnT = A.tile([128, NT], BF16, "VnT")
        ctmp = A.tile([128, NT], F32, "ctmp")
        A.push()
        w = [A.tile([128, NKC, 128], BF16, "wd_") for _ in range(4)]
        hg = [A.tile([128, NKC, 512], BF16, "hgd") for _ in range(2)]
        for i in range(4):
            self.load_w_in(l, cols[i], 128, w[i], (u, "w", i), f"wm{i}")
        for gi, (t0, G) in enumerate(self.TG):
            self.dn_proj_group(u, gi, t0, G, w, hg, raw, gb)
        A.pop()
        P.barrier()
        sq = [A.tile([128, 512], F32R, "dsq") for _ in range(2)]
        rn = [A.tile([128, 512], F32, "drn") for _ in range(2)]
        for i in range(3):
            self.dn_conv(l, u, i, i * 8 + h, raw[i], cv[i], ctmp)
        for i in range(2):
            dst = QnT if i == 0 else KnT
            nm = "QnT" if i == 0 else "KnT"
            scl = 128.0 ** -0.5 if i == 0 else 1.0
            for gi, (t0, G) in enumerate(self.TG):
                self.dn_l2(u, i, gi, t0, G, cv[i], sq, rn, dst, nm, scl)
        P.op("act", lambda e: e.activation(out=VnT[:], in_=cv[2][:], func=AF.Copy), reads=[(u, "cv", 2)], writes=[(u, "VnT")])
        for (src, srd, dst, nm) in ((KnT, [(u, "KnT", gi) for gi in range(NG)], Ktok, "Ktok"), (VnT, [(u, "VnT")], Vtok, "Vtok")):
            for q in range(5):
                self.tr_group(u, src, srd, dst, nm, q)
        A.pop()
        P.barrier()

    def tr_group(self, u, src, srd, dst, nm, q):
        P = self.P
        bank = self.nb()
        nt_ = 4 if q < 4 else 2
        psb = self.ps[bank][:].bitcast(BF16)
        for j in range(nt_):
            t = q * 4 + j
            P.op("pe", lambda e, j=j, t=t: e.transpose(psb[:, j * 128:(j + 1) * 128], src[:, t * 128:(t + 1) * 128], self.ident_b[:]),
                 reads=srd + ["ident_b"], writes=[("ps", bank)])
        P.op("act", lambda e: e.activation(out=dst[:, q * 4:q * 4 + nt_, :], in_=psb[:, 0:nt_ * 128].rearrange("p (a b) -> p a b", b=128), func=AF.Copy),
             reads=[("ps", bank)], writes=[(u, nm, q)])

    def dn_proj_group(self, u, gi, t0, G, w, hg, raw, gb):
        P = self.P
        slot = gi % 2
        self.load_hg(hg, slot, t0, G, u)
        hgs = (hg[slot], (u, "hg", slot))
        for i in range(3):
            if i % 2 == 0:
                self.proj_fm((w[i], (u, "w", i)), hgs, G, lambda bank, i=i: P.op(
                    "act", lambda e: e.activation(out=raw[i][:, t0:t0 + G], in_=self.ps[bank][:, 0:G], func=AF.Copy),
                    reads=[("ps", bank)], writes=[(u, "raw", i, gi)]))
            else:
                self.proj_fm((w[i], (u, "w", i)), hgs, G, lambda bank, i=i: P.op(
                    "dve", lambda e: e.tensor_copy(out=raw[i][:, t0:t0 + G], in_=self.ps[bank][:, 0:G]),
                    reads=[("ps", bank)], writes=[(u, "raw", i, gi)]))
        self.proj_fm((w[3], (u, "w", 3)), hgs, G, lambda bank: P.op(
            "act", lambda e: e.activation(out=gb[:, t0:t0 + G], in_=self.ps[bank][:, 0:G], func=AF.Silu),
            reads=[("ps", bank)], writes=[(u, "gb", gi)]))

    def dn_conv(self, l, u, i, ci, raw, cv, ctmp):
        P = self.P
        rr = [(u, "raw", i, gi) for gi in range(NG)]
        wcol = lambda j: self.convT[:, l, j * 24 + ci:j * 24 + ci + 1]
        P.op("act", lambda e: e.activation(out=cv[:], in_=raw[:], func=AF.Identity, scale=wcol(2)),
             reads=rr + [("convT", l)], writes=[(u, "cv", i)])
        segs = [(raw[:, 0:NCTX].rearrange("p (r w) -> p r w", w=NCTX), cv[:, 0:NCTX].rearrange("p (r w) -> p r w", w=NCTX), NCTX),
                (raw[:, NCTX:NT].rearrange("p (r w) -> p r w", w=64), cv[:, NCTX:NT].rearrange("p (r w) -> p r w", w=64), 64)]
        for j in (0, 1, 3, 4):
            o = j - 2
            for (r3, c3, W) in segs:
                d0, d1 = max(0, -o), W - max(0, o)
                P.op("dve", lambda e, r3=r3, c3=c3, d0=d0, d1=d1, o=o, j=j: e.scalar_tensor_tensor(
                    out=c3[:, :, d0:d1], in0=r3[:, :, d0 + o:d1 + o], scalar=wcol(j), in1=c3[:, :, d0:d1], op0=ALU.mult, op1=ALU.add),
                    reads=[(u, "cv", i)], writes=[(u, "cv", i)])
        P.op("act", lambda e: e.activation(out=cv[:], in_=cv[:], func=AF.Silu), reads=[(u, "cv", i)], writes=[(u, "cv", i)])

    def dn_l2(self, u, i, gi, t0, G, cv, sq, rn, dst, nm, scl):
        P = self.P
        s = gi % 2
        bank = self.nb()
        P.op("act", lambda e: e.activation(out=sq[s][:, 0:G], in_=cv[:, t0:t0 + G], func=AF.Square), reads=[(u, "cv", i)], writes=[(u, "dsq", s)])
        P.op("pe", lambda e: e.matmul(self.ps[bank][:, 0:G], lhsT=self.ones_r[:], rhs=sq[s][:, 0:G], start=True, stop=True),
             reads=[(u, "dsq", s), "ones_r"], writes=[("ps", bank)])
        P.op("act", lambda e: e.activation(out=rn[s][:, 0:G], in_=self.ps[bank][:, 0:G], func=AF.Ln, bias=self.epsc[:, 0:1], scale=1.0),
             reads=[("ps", bank), "epsc"], writes=[(u, "drn", s)])
        P.op("act", lambda e: e.activation(out=rn[s][:, 0:G], in_=rn[s][:, 0:G], func=AF.Exp, scale=-0.5), reads=[(u, "drn", s)], writes=[(u, "drn", s)])
        P.op("dve", lambda e: e.scalar_tensor_tensor(out=dst[:, t0:t0 + G], in0=cv[:, t0:t0 + G], scalar=scl, in1=rn[s][:, 0:G], op0=ALU.mult, op1=ALU.mult),
             reads=[(u, "drn", s), (u, "cv", i)], writes=[(u, nm, gi)])

    def dn_dir(self, l, h, u0, d, QnT, KnT, Ktok, Vtok, oT):
        P, A = self.P, self.A
        u = self.new("dd")
        NTL = NT // 128
        NC = NT // 64
        col = d * 8 + h
        A.push()
        kbg = A.tile([128, NTL, 128], BF16, "kbg")
        vb = A.tile([128, NTL, 128], BF16, "vb")
        kdz = A.tile([128, NTL, 2, 128], BF16, "kdz")
        Lc = [A.tile([128, NTL, 128], BF16, "Lc") for _ in range(2)]
        Nc = [A.tile([128, NTL, 128], BF16, "Nc") for _ in range(2)]
        Rc = [A.tile([128, NTL, 128], BF16, "Rc") for _ in range(2)]
        qkT = A.tile([128, NTL, 128], BF16, "qkT")
        qgT = A.tile([128, NT], BF16, "qgT")
        nwT = A.tile([128, NT], BF16, "nwT")
        usb = A.tile([128, NTL, 128], F32, "usb")
        vn = A.tile([128, NC, 128], BF16, "vn")
        Sall = A.tile([128, NC, 128], BF16, "Sall")
        S32 = A.tile([128, 128], F32, "S32")
        dg3 = [A.tile([128, 384], F32, "dg3") for _ in range(2)]
        tA = [A.tile([128, 128], F32, "tA") for _ in range(2)]
        tB = [A.tile([128, 128], F32, "tB") for _ in range(2)]
        WA = [A.tile([128, 128], F32, "WA") for _ in range(2)]
        WBs = [A.tile([128, 128], F32, "WBs") for _ in range(2)]
        WBi = [A.tile([128, 128], F32, "WBi") for _ in range(2)]
        t2 = [A.tile([128, 128], F32, "t2") for _ in range(2)]
        mAs = self.bigP[0] if d == 0 else self.bigP[1]
        mBs = self.bigN[1] if d == 0 else self.bigN[0]
        mBi = self.bigN[2] if d == 0 else self.bigN[3]
        Kr = [(u0, "Ktok", q) for q in range(5)]
        Vr = [(u0, "Vtok", q) for q in range(5)]
        KnR = [(u0, "KnT", gi) for gi in range(NG)]
        QnR = [(u0, "QnT", gi) for gi in range(NG)]
        P.op("pool", lambda e: e.memset(kdz[:].rearrange("p a b c -> p (a b c)"), 0.0), writes=[(u, "kdz0")])
        P.op("pool", lambda e: e.memset(Sall[:, 0, :], 0.0), writes=[(u, "Sall", 0)])
        P.op("pool", lambda e: e.memset(S32[:], 0.0), writes=[(u, "S32")])
        for t in range(NTL):
            self.dn_prep_tile(u, u0, d, t, col, kbg, vb, kdz, Lc[0], Nc[0], Rc[0], qkT, qgT, dg3[t % 2], tA[t % 2], tB[t % 2], WA[t % 2], WBs[t % 2],
                              WBi[t % 2], t2[t % 2], mAs, mBs, mBi, Ktok, Vtok, KnT, QnT, Kr, Vr, KnR, QnR)
        cur = 0
        for lev in range(5):
            nxt = 1 - cur
            for q in range(5):
                self.dn_neumann(u, lev, q, Lc[cur], Nc[cur], Rc[cur], Lc[nxt], Nc[nxt], Rc[nxt])
            cur = nxt
        TT = Rc[cur]
        for q in range(5):
            self.dn_uw(u, q, TT, vb, kbg, usb, nwT)
        if "dn_stop2" in self.stages:
            self.dump_tile("dbg_TT", TT[:], [128, NTL, 128], [(u, "R", 5, q) for q in range(5)])
            self.dump_tile("dbg_usb", usb[:], [128, NTL, 128], [(u, "usb", q) for q in range(5)])
            self.dump_tile("dbg_nwT", nwT[:], [128, NT], [(u, "nwT", q) for q in range(5)])
            self.dump_tile("dbg_qkT", qkT[:], [128, NTL, 128], [(u, "qkT", t) for t in range(NTL)])
        order = sorted(range(NC), key=lambda c: self.chain_pos(d, c))
        for pos, c in enumerate(order):
            self.dn_chain_step(u, d, pos, c, col, nwT, usb, vn, kdz, Sall, S32)
        for q in range(5):
            self.dn_out(u, u0, d, q, Sall, qgT, vn, qkT, oT)
        A.pop()
        P.barrier()

    def dn_prep_tile(self, u, u0, d, t, col, kbg, vb, kdz, L0, N0, R0, qkT, qgT, dg3, tA, tB, WA, WBs, WBi, t2, mAs, mBs, mBi,
                     Ktok, Vtok, KnT, QnT, Kr, Vr, KnR, QnR):
        P = self.P
        s = t % 2
        beta = self.dn_beta[:, t, col:col + 1]
        gc = self.dn_gc[:, t, col:col + 1]
        egc = self.dn_egc[:, t, col:col + 1]
        bg = self.dn_bg[:, t, col:col + 1]
        sc_r = ["dn_beta", ("dn_gc", t), ("dn_egc", t), ("dn_egl", t), ("dn_bg", t)]
        P.op("act", lambda e: e.activation(out=kbg[:, t, :], in_=Ktok[:, t, :], func=AF.Identity, scale=bg),
             reads=Kr + sc_r, writes=[(u, "kbg", t)])
        P.op("act", lambda e: e.activation(out=vb[:, t, :], in_=Vtok[:, t, :], func=AF.Identity, scale=beta),
             reads=Vr + sc_r, writes=[(u, "vb", t)])
        for hf in range(2):
            r0, r1 = hf * 64, hf * 64 + 64
            P.op("act", lambda e, hf=hf, r0=r0, r1=r1: e.activation(out=kdz[r0:r1, t, hf, :], in_=Ktok[r0:r1, t, :], func=AF.Identity,
                                                                 scale=self.dn_egl[r0:r1, t, col:col + 1]),
                 reads=Kr + sc_r + [(u, "kdz0")], writes=[(u, "kdz", t, hf)])
        for i, sc in enumerate((gc, beta, egc)):
            if i == 1:
                P.op("act", lambda e, i=i, sc=sc: e.activation(out=dg3[:, i * 128:(i + 1) * 128], in_=self.ident[:], func=AF.Identity, scale=sc),
                     reads=["ident"] + sc_r, writes=[(u, "dg3", s, i)])
            else:
                P.op("dve", lambda e, i=i, sc=sc: e.tensor_scalar(out=dg3[:, i * 128:(i + 1) * 128], in0=self.ident[:], scalar1=sc, scalar2=None, op0=ALU.mult),
                     reads=["ident"] + sc_r, writes=[(u, "dg3", s, i)])
        bR = self.nb()
        P.op("pe", lambda e: e.matmul(self.ps[bR][:, 0:384], lhsT=self.ones_f[:], rhs=dg3[:, :], start=True, stop=True),
             reads=[(u, "dg3", s, i) for i in range(3)] + ["ones_f"], writes=[("ps", bR)])
        RB = self.ps[bR][:, 0:128]
        RBb = self.ps[bR][:, 128:256]
        RBe = self.ps[bR][:, 256:384]
        bG = self.nb()
        ts = slice(t * 128, (t + 1) * 128)
        P.op("pe", lambda e: e.matmul(self.ps[bG][:, 0:128], lhsT=KnT[:, ts], rhs=KnT[:, ts], start=True, stop=True),
             reads=KnR, writes=[("ps", bG)])
        P.op("pe", lambda e: e.matmul(self.ps[bG][:, 128:256], lhsT=KnT[:, ts], rhs=QnT[:, ts], start=True, stop=True),
             reads=KnR + QnR, writes=[("ps", bG)])
        Gm = self.ps[bG][:, 0:128]
        QK = self.ps[bG][:, 128:256]
        P.op("dve", lambda e: e.scalar_tensor_tensor(out=WA[:], in0=RB, scalar=gc, in1=mAs[:], op0=ALU.subtract, op1=ALU.max),
             reads=[("ps", bR), "bigm"] + sc_r, writes=[(u, "WA", s)])
        P.op("act", lambda e: e.activation(out=WA[:], in_=WA[:], func=AF.Exp, scale=-1.0), reads=[(u, "WA", s)], writes=[(u, "WA", s)])
        P.op("dve", lambda e: e.scalar_tensor_tensor(out=WBs[:], in0=RB, scalar=gc, in1=mBs[:], op0=ALU.subtract, op1=ALU.min),
             reads=[("ps", bR), "bigm"] + sc_r, writes=[(u, "WBs", s)])
        P.op("act", lambda e: e.activation(out=WBs[:], in_=WBs[:], func=AF.Exp), reads=[(u, "WBs", s)], writes=[(u, "WBs", s)])
        P.op("dve", lambda e: e.scalar_tensor_tensor(out=WBi[:], in0=RB, scalar=gc, in1=mBi[:], op0=ALU.subtract, op1=ALU.min),
             reads=[("ps", bR), "bigm"] + sc_r, writes=[(u, "WBi", s)])
        P.op("act", lambda e: e.activation(out=WBi[:], in_=WBi[:], func=AF.Exp), reads=[(u, "WBi", s)], writes=[(u, "WBi", s)])
        P.op("dve", lambda e: e.scalar_tensor_tensor(out=L0[:, t, :], in0=Gm, scalar=beta, in1=WA[:], op0=ALU.mult, op1=ALU.mult),
             reads=[("ps", bG), (u, "WA", s)] + sc_r, writes=[(u, "L", 0, t)])
        P.op("dve", lambda e: e.tensor_tensor(out=t2[:], in0=RBb, in1=WBs[:], op=ALU.mult), reads=[("ps", bR), (u, "WBs", s)], writes=[(u, "t2", s)])
        P.op("dve", lambda e: e.tensor_tensor(out=N0[:, t, :], in0=Gm, in1=t2[:], op=ALU.mult), reads=[("ps", bG), (u, "t2", s)], writes=[(u, "N", 0, t)])
        P.op("dve", lambda e: e.scalar_tensor_tensor(out=R0[:, t, :], in0=N0[:, t, :], scalar=-1.0, in1=self.ident[:], op0=ALU.mult, op1=ALU.add),
             reads=[(u, "N", 0, t), "ident"], writes=[(u, "R", 0, t)])
        P.op("dve", lambda e: e.tensor_tensor(out=qkT[:, t, :], in0=QK, in1=WBi[:], op=ALU.mult), reads=[("ps", bG), (u, "WBi", s)], writes=[(u, "qkT", t)])
        P.op("dve", lambda e: e.tensor_tensor(out=qgT[:, ts], in0=RBe, in1=QnT[:, ts], op=ALU.mult), reads=[("ps", bR)] + QnR, writes=[(u, "qgT", t)])

    def dn_neumann(self, u, lev, q, Lc, Nc, Rc, Ln, Nn, Rn):
        P = self.P
        nt_ = 4 if q < 4 else 2
        tiles = [q * 4 + j for j in range(nt_)]
        last = lev == 4

        def rd(nm, t):
            return [(u, nm, lev, t)] if lev == 0 else [(u, nm, lev, t // 4)]
        bL = self.nb()
        for j, t in enumerate(tiles):
            P.op("pe", lambda e, j=j, t=t: e.matmul(self.ps[bL][:, j * 128:(j + 1) * 128], lhsT=Nc[:, t, :], rhs=Lc[:, t, :], start=True, stop=True),
                 reads=rd("N", t) + rd("L", t), writes=[("ps", bL)])
        P.op("act", lambda e: e.activation(out=Ln[:, q * 4:q * 4 + nt_, :], in_=self.ps[bL][:, 0:nt_ * 128].rearrange("p (a b) -> p a b", b=128), func=AF.Copy),
             reads=[("ps", bL)], writes=[(u, "L", lev + 1, q)])
        if not last:
            bN = self.nb()
            for j, t in enumerate(tiles):
                P.op("pe", lambda e, j=j, t=t: e.matmul(self.ps[bN][:, j * 128:(j + 1) * 128], lhsT=Lc[:, t, :], rhs=Nc[:, t, :], start=True, stop=True),
                     reads=rd("N", t) + rd("L", t), writes=[("ps", bN)])
            P.op("dve", lambda e: e.tensor_copy(out=Nn[:, q * 4:q * 4 + nt_, :], in_=self.ps[bN][:, 0:nt_ * 128].rearrange("p (a b) -> p a b", b=128)),
                 reads=[("ps", bN)], writes=[(u, "N", lev + 1, q)])
        bR = self.nb()
        for j, t in enumerate(tiles):
            P.op("pe", lambda e, j=j, t=t: e.matmul(self.ps[bR][:, j * 128:(j + 1) * 128], lhsT=self.ident_b[:], rhs=Rc[:, t, :], start=True, stop=False),
                 reads=rd("R", t) + ["ident_b"], writes=[("ps", bR)])
            P.op("pe", lambda e, j=j, t=t: e.matmul(self.ps[bR][:, j * 128:(j + 1) * 128], lhsT=Ln[:, t, :], rhs=Rc[:, t, :], start=False, stop=True),
                 reads=rd("R", t) + [(u, "L", lev + 1, q)], writes=[("ps", bR)])
        eng = "dve" if q % 2 == 0 else "act"
        if eng == "dve":
            P.op("dve", lambda e: e.tensor_copy(out=Rn[:, q * 4:q * 4 + nt_, :], in_=self.ps[bR][:, 0:nt_ * 128].rearrange("p (a b) -> p a b", b=128)),
                 reads=[("ps", bR)], writes=[(u, "R", lev + 1, q)])
        else:
            P.op("act", lambda e: e.activation(out=Rn[:, q * 4:q * 4 + nt_, :], in_=self.ps[bR][:, 0:nt_ * 128].rearrange("p (a b) -> p a b", b=128), func=AF.Copy),
                 reads=[("ps", bR)], writes=[(u, "R", lev + 1, q)])

    def dn_uw(self, u, q, TT, vb, kbg, usb, nwT):
        P = self.P
        nt_ = 4 if q < 4 else 2
        tiles = [q * 4 + j for j in range(nt_)]
        bU = self.nb()
        for j, t in enumerate(tiles):
            P.op("pe", lambda e, j=j, t=t: e.matmul(self.ps[bU][:, j * 128:(j + 1) * 128], lhsT=TT[:, t, :], rhs=vb[:, t, :], start=True, stop=True),
                 reads=[(u, "R", 5, q), (u, "vb", t)], writes=[("ps", bU)])
        P.op("dve", lambda e: e.tensor_copy(out=usb[:, q * 4:q * 4 + nt_, :], in_=self.ps[bU][:, 0:nt_ * 128].rearrange("p (a b) -> p a b", b=128)),
             reads=[("ps", bU)], writes=[(u, "usb", q)])
        bW = self.nb()
        for j, t in enumerate(tiles):
            P.op("pe", lambda e, j=j, t=t: e.matmul(self.ps[bW][:, j * 128:(j + 1) * 128], lhsT=kbg[:, t, :], rhs=TT[:, t, :], start=True, stop=True),
                 reads=[(u, "R", 5, q), (u, "kbg", t)], writes=[("ps", bW)])
        P.op("act", lambda e: e.activation(out=nwT[:, q * 512:q * 512 + nt_ * 128], in_=self.ps[bW][:, 0:nt_ * 128], func=AF.Copy, scale=-1.0),
             reads=[("ps", bW)], writes=[(u, "nwT", q)])

    def dn_chain_step(self, u, d, pos, c, col, nwT, usb, vn, kdz, Sall, S32):
        P = self.P
        t, hf = c // 2, c % 2
        b1 = self.nb()
        P.op("pe", lambda e: e.matmul(self.ps[b1][:, 0:128], lhsT=nwT[:, t * 128:(t + 1) * 128], rhs=Sall[:, pos, :], start=True, stop=True),
             reads=[(u, "nwT", t // 4), (u, "Sall", pos)], writes=[("ps", b1)])
        P.op("dve", lambda e: e.tensor_tensor(out=vn[:, c, :], in0=self.ps[b1][:, 0:128], in1=usb[:, t, :], op=ALU.add),
             reads=[("ps", b1), (u, "usb", t // 4)], writes=[(u, "vn", c)])
        b2 = self.nb()
        P.op("pe", lambda e: e.matmul(self.ps[b2][:, 0:128], lhsT=kdz[:, t, hf, :], rhs=vn[:, c, :], start=True, stop=True),
             reads=[(u, "kdz", t, hf), (u, "kdz0"), (u, "vn", c)], writes=[("ps", b2)])
        if pos + 1 < NT // 64:
            P.op("dve", lambda e: e.scalar_tensor_tensor(out=Sall[:, pos + 1, :], in0=S32[:], scalar=self.dn_dl[:, t, hf, col:col + 1], in1=self.ps[b2][:, 0:128],
                                                         op0=ALU.mult, op1=ALU.add),
                 reads=[("ps", b2), (u, "S32"), ("dn_dl", t)], writes=[(u, "Sall", pos + 1)])
            P.op("dve", lambda e: e.scalar_tensor_tensor(out=S32[:], in0=S32[:], scalar=self.dn_dl[:, t, hf, col:col + 1], in1=self.ps[b2][:, 0:128],
                                                         op0=ALU.mult, op1=ALU.add),
                 reads=[("ps", b2), (u, "S32"), ("dn_dl", t), (u, "Sall", pos + 1)], writes=[(u, "S32")])

    def dn_out(self, u, u0, d, q, Sall, qgT, vn, qkT, oT):
        P = self.P
        nt_ = 4 if q < 4 else 2
        bank = self.nb()
        for j in range(nt_):
            t = q * 4 + j
            for hf in range(2):
                c = 2 * t + hf
                pos = self.chain_pos(d, c)
                cs = slice(j * 128 + hf * 64, j * 128 + hf * 64 + 64)
                P.op("pe", lambda e, c=c, pos=pos, cs=cs: e.matmul(self.ps[bank][:, cs], lhsT=Sall[:, pos, :], rhs=qgT[:, c * 64:(c + 1) * 64], start=True, stop=False),
                     reads=[(u, "Sall", pos), (u, "qgT", t)], writes=[("ps", bank)])
                P.op("pe", lambda e, c=c, t=t, hf=hf, cs=cs: e.matmul(self.ps[bank][:, cs], lhsT=vn[:, c, :], rhs=qkT[:, t, hf * 64:hf * 64 + 64], start=False, stop=True),
                     reads=[(u, "vn", c), (u, "qkT", t)], writes=[("ps", bank)])
        c0_, c1_ = q * 512, q * 512 + nt_ * 128
        if d == 0:
            P.op("act", lambda e: e.activation(out=oT[:, c0_:c1_], in_=self.ps[bank][:, 0:nt_ * 128], func=AF.Copy),
                 reads=[("ps", bank)], writes=[(u0, "oT", q)])
        else:
            P.op("dve", lambda e: e.tensor_tensor(out=oT[:, c0_:c1_], in0=self.ps[bank][:, 0:nt_ * 128], in1=oT[:, c0_:c1_], op=ALU.add),
                 reads=[("ps", bank), (u0, "oT", q)], writes=[(u0, "oT", q)])

    def merge(self, l):
        A = self.A
        A.push()
        wa = [A.tile([128, 8, 128], BF16, "wa") for _ in range(2)]
        wb = [A.tile([128, 8, 128], BF16, "wb") for _ in range(2)]
        wga = [A.tile([128, NKC, 128], BF16, "wga") for _ in range(2)]
        wgb = [A.tile([128, NKC, 128], BF16, "wgb") for _ in range(2)]
        wo = [A.tile([128, NKC, 128], BF16, "wo") for _ in range(2)]
        for (t0, G, v) in self.groups():
            self.merge_group(l, t0, G, v, wa, wb, wga, wgb, wo)
        A.pop()

    def merge_group(self, l, t0, G, v, wa, wb, wga, wgb, wo):
        P, A = self.P, self.A
        u = self.new("mg")
        nh = (G + 511) // 512
        hw = G // nh
        A.push()
        yaT = A.tile([128, 8, G], BF16, "yaT")
        ybT = A.tile([128, 8, G], BF16, "ybT")
        hT = A.tile([128, NKC, G], BF16, "hTg")
        yT = A.tile([128, NKC, G], BF16, "yT")
        sga = [A.tile([128, hw], F32, "sga") for _ in range(2)]
        sgb = [A.tile([128, hw], F32, "sgb") for _ in range(2)]
        xc = [A.tile([128, G], F32, "xm") for _ in range(3)]
        ya1 = A.tile([128, 8, G], BF16, "ya1")
        yb1 = A.tile([128, 8, G], BF16, "yb1")
        for (ab, dst0, dst1, nm) in ((0, yaT, ya1, "yaT"), (1, ybT, yb1, "ybT")):
            for r in range(2):
                for hp in range(2):
                    k = ab * 2 + hp
                    h0 = r * 4 + hp * 2
                    P.op("sp", lambda e, dst0=dst0, r=r, k=k, h0=h0: e.dma_start(
                        out=dst0[:, h0:h0 + 2, :], in_=self.y_all_d[k, r][:, :, t0:t0 + G].rearrange("h p n -> p h n")),
                        reads=[("yall", k)], writes=[(u, nm, 0, r, hp)], lane=f"mg{ab}a")
                    P.op("sp", lambda e, dst1=dst1, r=r, k=k, h0=h0: e.dma_start(
                        out=dst1[:, h0:h0 + 2, :], in_=self.y_all_d[k, r][:, :, NTO + t0:NTO + t0 + G].rearrange("h p n -> p h n")),
                        reads=[("yall", k)], writes=[(u, nm, 1, r, hp)], lane=f"mg{ab}b")
            P.op("dve", lambda e, dst0=dst0: e.tensor_scalar(out=dst0[:], in0=dst0[:], scalar1=self.selT[:, 0:1], scalar2=None, op0=ALU.mult),
                 reads=[(u, nm, 0, r, hp) for r in range(2) for hp in range(2)] + ["selT"], writes=[(u, nm)])
            P.op("dve", lambda e, dst0=dst0, dst1=dst1: e.scalar_tensor_tensor(out=dst0[:], in0=dst1[:], scalar=self.selT[:, 1:2], in1=dst0[:],
                                                                               op0=ALU.mult, op1=ALU.add),
                 reads=[(u, nm)] + [(u, nm, 1, r, hp) for r in range(2) for hp in range(2)] + ["selT"], writes=[(u, nm)])
        P.op("sp", lambda e: e.dma_start(out=hT[:], in_=self.hT_d[:, :, t0:t0 + G].rearrange("c p n -> p c n")),
             reads=[("hTd", t) for t in range(t0 // 128, (t0 + G) // 128)], writes=[(u, "hT")], lane="mgh")
        wav = self.inp["w_branch_a"][l].rearrange("(hd p) n -> p hd n", p=128)
        wbv = self.inp["w_branch_b"][l].rearrange("(hd p) n -> p hd n", p=128)
        wiv = self.inp["w_in"][l].rearrange("(kc p) n -> p kc n", p=128)
        wov = self.inp["w_out"][l].rearrange("(kc p) n -> p kc n", p=128)

        def load1(n):
            s = n % 2
            cs = slice(n * 128, (n + 1) * 128)
            P.op("pool", lambda e: e.dma_start(out=wa[s][:], in_=wav[:, :, cs]), writes=[("wa", s)], lane=f"wa{s}")
            P.op("pool", lambda e: e.dma_start(out=wb[s][:], in_=wbv[:, :, cs]), writes=[("wb", s)], lane=f"wb{s}")
            P.op("pool", lambda e: e.dma_start(out=wga[s][:], in_=wiv[:, :, 9248 + n * 128:9248 + (n + 1) * 128]), writes=[("wga", s)], lane=f"wga{s}")
            P.op("pool", lambda e: e.dma_start(out=wgb[s][:], in_=wiv[:, :, 11296 + n * 128:11296 + (n + 1) * 128]), writes=[("wgb", s)], lane=f"wgb{s}")

        def load2(n):
            s = n % 2
            P.op("pool", lambda e: e.dma_start(out=wo[s][:], in_=wov[:, :, n * 128:(n + 1) * 128]), writes=[("wo", s)], lane=f"wo{s}")

        def acc(wt, wtok, nk, rhs_t, rtok, h):
            bank = self.nb()
            for k in range(nk):
                P.op("pe", lambda e, k=k: e.matmul(self.ps[bank][:, 0:hw], lhsT=wt[:, k, :], rhs=rhs_t[:, k, h * hw:(h + 1) * hw],
                                                   start=(k == 0), stop=(k == nk - 1)),
                     reads=[wtok] + rtok, writes=[("ps", bank)])
            return bank

        load1(0)
        for n in range(NKC):
            if n + 1 < NKC:
                load1(n + 1)
            else:
                load2(0)
            s = n % 2
            for h in range(nh):
                i = h % 2
                bga = acc(wga[s], ("wga", s), NKC, hT, [(u, "hT")], h)
                P.op("act", lambda e, i=i, bga=bga: e.activation(out=sga[i][:], in_=self.ps[bga][:, 0:hw], func=AF.Sigmoid),
                     reads=[("ps", bga)], writes=[(u, "sga", i)])
                bgb = acc(wgb[s], ("wgb", s), NKC, hT, [(u, "hT")], h)
                P.op("act", lambda e, i=i, bgb=bgb: e.activation(out=sgb[i][:], in_=self.ps[bgb][:, 0:hw], func=AF.Sigmoid),
                     reads=[("ps", bgb)], writes=[(u, "sgb", i)])
                ba = acc(wa[s], ("wa", s), 8, yaT, [(u, "yaT")], h)
                P.op("dve", lambda e, i=i, ba=ba: e.tensor_tensor(out=sga[i][:], in0=sga[i][:], in1=self.ps[ba][:, 0:hw], op=ALU.mult),
                     reads=[("ps", ba), (u, "sga", i)], writes=[(u, "sga", i)])
                bb = acc(wb[s], ("wb", s), 8, ybT, [(u, "ybT")], h)
                P.op("dve", lambda e, i=i, bb=bb: e.tensor_tensor(out=sgb[i][:], in0=sgb[i][:], in1=self.ps[bb][:, 0:hw], op=ALU.mult),
                     reads=[("ps", bb), (u, "sgb", i)], writes=[(u, "sgb", i)])
                P.op("pool", lambda e, i=i, n=n, h=h: e.tensor_tensor(out=yT[:, n, h * hw:(h + 1) * hw], in0=sga[i][:], in1=sgb[i][:], op=ALU.add),
                     reads=[(u, "sga", i), (u, "sgb", i)], writes=[(u, "yT", n, h)])
        for n in range(NKC):
            if n + 1 < NKC:
                load2(n + 1)
            s = n % 2
            xs = n % 3
            P.op("sp", lambda e, xs=xs, n=n: e.dma_start(out=xc[xs][:], in_=self.xT_d[n, :, t0:t0 + G]),
                 reads=[("xT", n, t) for t in range(t0 // 128, (t0 + G) // 128)], writes=[(u, "xm", xs)], lane=f"xe{xs}")
            for h in range(nh):
                bo = acc(wo[s], ("wo", s), NKC, yT, [(u, "yT", k, h) for k in range(NKC)], h)
                P.op("dve", lambda e, xs=xs, h=h, bo=bo, n=n: e.scalar_tensor_tensor(
                    out=xc[xs][:, h * hw:(h + 1) * hw], in0=self.ps[bo][:, 0:hw], scalar=self.gate[:, l, 1, v, n:n + 1],
                    in1=xc[xs][:, h * hw:(h + 1) * hw], op0=ALU.mult, op1=ALU.add),
                    reads=[("ps", bo), (u, "xm", xs)], writes=[(u, "xm", xs)])
            P.op("sp", lambda e, xs=xs, n=n: e.dma_start(out=self.xT_d[n, :, t0:t0 + G], in_=xc[xs][:]),
                 reads=[(u, "xm", xs)], writes=[("xT", n, t) for t in range(t0 // 128, (t0 + G) // 128)], lane=f"xs{xs}")
        A.pop()
        P.barrier()

    def final(self):
        P, A = self.P, self.A
        A.push()
        hT = A.tile([128, NKC, 1024], F32, "hTf")
        to = [A.tile([128, D], F32, "to") for _ in range(2)]
        for gi, (t0, G, v) in enumerate(self.groups()):
            u = self.new("fin")
            self.prologue(t0, G, self.gT[:, 96:112], None, hT, (u, "hT"))
            for t in range(G // 128):
                s = t % 2
                for q in range(4):
                    bank = (t % 2) * 4 + q
                    for j in range(4):
                        c = q * 4 + j
                        P.op("pe", lambda e, c=c, bank=bank, j=j, t=t: e.transpose(
                            self.ps[bank][:, j * 128:(j + 1) * 128], hT[:, c, t * 128:(t + 1) * 128], self.ident[:]),
                            reads=[((u, "hT"), c), "ident"], writes=[("ps", bank)])
                    if q % 2 == 0:
                        P.op("act", lambda e, s=s, q=q, bank=bank: e.activation(out=to[s][:, q * 512:(q + 1) * 512], in_=self.ps[bank][:], func=AF.Copy),
                             reads=[("ps", bank)], writes=[("to", s, q)])
                    else:
                        P.op("dve", lambda e, s=s, q=q, bank=bank: e.tensor_copy(out=to[s][:, q * 512:(q + 1) * 512], in_=self.ps[bank][:]),
                             reads=[("ps", bank)], writes=[("to", s, q)])
                row = t0 + t * 128
                P.op("sp", lambda e, s=s, row=row: e.dma_start(out=self.out[row:row + 128, :], in_=to[s][:]),
                     reads=[("to", s, q) for q in range(4)], lane=f"to{s}")
        A.pop()
        P.barrier()

    def dump_xT(self, name):
        d = self.nc.dram_tensor(name, [NKC, 128, NTO], F32, kind="ExternalOutput").ap()
        self.dbg_out[name] = d
        for c in range(NKC):
            self.P.op("sp", lambda e, c=c: e.dma_start(out=d[c], in_=self.xT_d[c]),
                      reads=[("xT", c, t) for t in range(NTO // 128)], lane="dump")

    def dump_tile(self, name, ap, shape, reads):
        d = self.nc.dram_tensor(name, list(shape), ap.dtype, kind="ExternalOutput").ap()
        self.dbg_out[name] = d
        self.P.op("sp", lambda e: e.dma_start(out=d, in_=ap), reads=reads, lane="dump")

    def build(self):
        st = self.stages
        self.consts()
        self.P.barrier()
        self.phase_in()
        if "dump_in" in st:
            self.dump_xT("dbg_xin")
        self.phase_adaln()
        if "dump_mod" in st:
            for l in range(2):
                self.dump_tile(f"dbg_modT{l}", self.modT[l][:], [128, 144, 2], [("modT", l)])
            self.dump_tile("dbg_gT", self.gT[:], [128, 112], ["gT0", "gT1"])
            self.dump_tile("dbg_gmod", self.gmod[:], [128, 2, 3, 2, NKC], [])
            self.dump_tile("dbg_gate", self.gate[:], [128, 2, 3, 2, NKC], [])
        self.phase_small()
        for l in range(2):
            if ("ffn", l, 0) in st:
                self.ffn(l, 0)
            if ("dump", l, 0) in st:
                self.dump_xT(f"dbg_x_ffn1_{l}")
            if ("mix", l) in st:
                self.mixer_prep(l)
                for h in self.heads:
                    if ("hgrn", l) in st:
                        self.hgrn_head(l, h)
                if ("dn", l) in st:
                    self.A.push()
                    self.dn_scalars(l)
                    for h in self.heads:
                        self.dn_head(l, h)
                    self.A.pop()
                    self.P.barrier()
                if ("merge", l) in st:
                    for k in range(4):
                        ab, hp = k // 2, k % 2
                        self.P.op("pool", lambda e, k=k, ab=ab, hp=hp: e.collective_compute(
                            "AllGather", ALU.bypass, replica_groups=[[0, 1], [2, 3], [4, 5], [6, 7]],
                            ins=[self.y_own_d[ab, 2 * hp:2 * hp + 2].rearrange("h p n -> (h p) n")],
                            outs=[self.y_all_d[k].rearrange("r h p n -> (r h p) n")]),
                            reads=[(("ya", h), gi) for h in range(4) for gi in range(NG)] + [(("yb", h), gi) for h in range(4) for gi in range(NG)],
                            writes=[("yall", k)], lane="cc_y", step=1)
                    self.P.barrier()
                    self.merge(l)
                if ("dumpmix", l) in st:
                    self.dump_xT(f"dbg_x_mix_{l}")
            if ("ffn", l, 1) in st:
                self.ffn(l, 1)
        self.final()
        with ExitStack() as es:
            sems = {k: es.enter_context(self.nc.semaphore("s_" + k)) for k in list(Prog.ENG) + self.P.lanes()}
            self.P.emit(sems)
        return self.nc


FULL = {("ffn", 0, 0), ("ffn", 0, 1), ("ffn", 1, 0), ("ffn", 1, 1), ("mix", 0), ("mix", 1), ("hgrn", 0), ("hgrn", 1),
        ("dn", 0), ("dn", 1), ("merge", 0), ("merge", 1)}


def _swap_heads(a, axis, seg0, nseg, seglen):
    a = np.array(a, copy=True)
    idx = [slice(None)] * a.ndim
    for i in range(nseg):
        lo = seg0 + i * seglen
        h = seglen // 2
        i0 = list(idx); i0[axis] = slice(lo, lo + h)
        i1 = list(idx); i1[axis] = slice(lo + h, lo + seglen)
        tmp = a[tuple(i0)].copy()
        a[tuple(i0)] = a[tuple(i1)]
        a[tuple(i1)] = tmp
    return a


def make_in_maps(inputs, ncores=8):
    inp = {k: np.asarray(v) for k, v in inputs.items()}
    shared = {k: np.ascontiguousarray(inp[k]) for k in ("w_ada", "b_ada", "norm_g", "ffn_w_gate", "ffn_w_up", "ffn_w_down",
                                                         "hgrn_norm_g", "dn_norm_g", "w_branch_a", "w_branch_b", "w_out")}
    shared["final_norm_g"] = np.ascontiguousarray(inp["final_norm_g"].reshape(1, -1))
    per_rank = []
    for r in range(2):
        if r == 0:
            pr = dict(w_in=np.ascontiguousarray(inp["w_in"]), hgrn_lower_bounds=np.ascontiguousarray(inp["hgrn_lower_bounds"]),
                      dn_conv_w=np.ascontiguousarray(inp["dn_conv_w"]), dn_a_log=np.ascontiguousarray(inp["dn_a_log"]),
                      dn_dt_bias=np.ascontiguousarray(inp["dn_dt_bias"]))
        else:
            w = _swap_heads(inp["w_in"], 2, 0, 9, 1024)
            w = _swap_heads(w, 2, 9216, 4, 8)
            pr = dict(w_in=np.ascontiguousarray(w),
                      hgrn_lower_bounds=np.ascontiguousarray(_swap_heads(inp["hgrn_lower_bounds"], 2, 0, 1, 1024)),
                      dn_conv_w=np.ascontiguousarray(_swap_heads(inp["dn_conv_w"], 2, 0, 3, 1024)),
                      dn_a_log=np.ascontiguousarray(_swap_heads(inp["dn_a_log"], 2, 0, 1, 8)),
                      dn_dt_bias=np.ascontiguousarray(_swap_heads(inp["dn_dt_bias"], 2, 0, 1, 8)))
        sel = np.zeros((128, 2), np.float32)
        sel[:, r] = 1.0
        pr["sel"] = sel
        per_rank.append(pr)
    maps = []
    for k in range(ncores):
        b, r = k // 2, k % 2
        m = dict(shared)
        m.update(per_rank[r])
        if r == 0:
            m["x"] = np.ascontiguousarray(np.concatenate([inp["ctx"][b], inp["x"][b][:NTO - NCTX]], axis=0))
            m["c_ctx"] = np.ascontiguousarray(inp["c_ctx"].reshape(1, -1))
        else:
            m["x"] = np.ascontiguousarray(inp["x"][b][NTO - NCTX:])
            m["c_ctx"] = np.ascontiguousarray(inp["c"][b:b + 1])
        m["c"] = np.ascontiguousarray(inp["c"][b:b + 1])
        maps.append(m)
    return maps


def kernel(**inputs):
    bld = Builder(FULL)
    nc = bld.build()
    maps = make_in_maps(inputs, 8)
    res = run_bass_kernel_spmd(nc, maps, core_ids=list(range(8)))
    out = np.zeros((4, 2048, D), np.float32)
    for k in range(8):
        b, r = k // 2, k % 2
        o = np.asarray(res.results[k]["out"])
        if r == 0:
            out[b, :NTO - NCTX] = o[NCTX:]
        else:
            out[b, NTO - NCTX:] = o
    return out
```

```python
import numpy as np
from contextlib import ExitStack
import concourse.bass as bass
import concourse.mybir as mybir
from concourse.bass_utils import run_bass_kernel_spmd

F32 = mybir.dt.float32
F32R = mybir.dt.float32r
BF16 = mybir.dt.bfloat16
AF = mybir.ActivationFunctionType
ALU = mybir.AluOpType

D = 2048
NT = 2304
NCTX = 256
NTO = 1152
NG = 6
DFF = 5632
NKC = 16
NFC = 44
PIN = 13344
EPS = 1e-6
SB0 = 16512
SB1 = 229376


class Prog:
    ENG = ("pe", "act", "dve", "pool", "sp")

    def __init__(self, nc):
        self.nc = nc
        self.ops = []
        self.last_w = {}
        self.readers = {}

    def op(self, eng, fn, reads=(), writes=(), lane=None, step=16):
        i = len(self.ops)
        deps = set()
        for r in reads:
            w = self.last_w.get(r)
            if w is not None:
                deps.add(w)
        for w_ in writes:
            w = self.last_w.get(w_)
            if w is not None:
                deps.add(w)
            for rd in self.readers.get(w_, ()):
                deps.add(rd)
        deps.discard(i)
        for r in reads:
            self.readers.setdefault(r, []).append(i)
        for w_ in writes:
            self.last_w[w_] = i
            self.readers[w_] = []
        self.ops.append(dict(eng=eng, fn=fn, deps=deps, lane=lane, sig=lane is not None, lstep=step))
        return i

    def barrier(self):
        toks = [("__bar", e, len(self.ops)) for e in self.ENG]
        allres = list(self.last_w.keys())
        for e, t in zip(self.ENG, toks):
            self.op(e, None, reads=allres, writes=[t])
        for e in self.ENG:
            self.op(e, None, reads=toks, writes=[("__bar2", e)])
        self.last_w = {k: v for k, v in self.last_w.items() if k[0] == "__bar2"} if False else self.last_w

    def lanes(self):
        return sorted({o["lane"] for o in self.ops if o["lane"] is not None})

    def emit(self, sems):
        ops = self.ops
        for o in ops:
            for d in o["deps"]:
                dop = ops[d]
                if dop["lane"] is None:
                    if dop["eng"] == "pe" and o["eng"] == "pe" and o["lane"] is None:
                        continue
                    dop["sig"] = True
        cnt = {}
        for o in ops:
            if o["sig"]:
                key = o["lane"] if o["lane"] is not None else o["eng"]
                step = o["lstep"] if o["lane"] is not None else 1
                cnt[key] = cnt.get(key, 0) + step
                o["key"] = key
                o["val"] = cnt[key]
                o["step"] = step
        final = dict(cnt)
        per_eng = {e: [] for e in self.ENG}
        for o in ops:
            per_eng[o["eng"]].append(o)

        def run(ename, e):
            seen = {}
            for o in per_eng[ename]:
                waits = {}
                for d in o["deps"]:
                    dop = ops[d]
                    if not dop["sig"]:
                        continue
                    if dop["lane"] is None and dop["eng"] == "pe" and ename == "pe" and o["lane"] is None:
                        continue
                    k, v = dop["key"], dop["val"]
                    if v > waits.get(k, 0):
                        waits[k] = v
                for k, v in waits.items():
                    if v > seen.get(k, 0):
                        e.wait_ge(sems[k], v)
                        seen[k] = v
                if o["fn"] is None:
                    ins = e.nop() if o["sig"] else None
                else:
                    ins = o["fn"](e)
                if o["sig"]:
                    ins.then_inc(sems[o["key"]], o["step"])
            if ename == "sp":
                for k, v in final.items():
                    if v > seen.get(k, 0):
                        e.wait_ge(sems[k], v)

        with self.nc.Block() as block:
            @block.tensor
            def _(e):
                run("pe", e)

            @block.scalar
            def _(e):
                run("act", e)

            @block.vector
            def _(e):
                run("dve", e)

            @block.gpsimd
            def _(e):
                run("pool", e)

            @block.sync
            def _(e):
                run("sp", e)


class Arena:
    def __init__(self, nc):
        self.nc = nc
        self.off = SB0
        self.stack = []
        self.n = 0

    def push(self):
        self.stack.append(self.off)

    def pop(self):
        self.off = self.stack.pop()

    def tile(self, shape, dtype, name="t"):
        esz = 2 if dtype == BF16 else 4
        per = esz
        for s in shape[1:]:
            per *= s
        per = (per + 63) // 64 * 64
        assert self.off + per <= SB1, f"SBUF overflow allocating {name} {shape}: {self.off + per - SB1} bytes over"
        self.n += 1
        h = self.nc.alloc_sbuf_tensor_at(f"{name}_{self.n}", list(shape), dtype, offset=self.off)
        self.off += per
        return h


class Builder:
    def __init__(self, stages, dbg=()):
        self.stages = stages
        self.dbg = dbg
        self.heads = list(range(4))
        nc = self.nc = bass.Bass("TRN2", target_bir_lowering=False)
        self.P = Prog(nc)
        self.A = Arena(nc)
        self.uid = 0
        dt = nc.dram_tensor
        self.inp = {}
        specs = dict(
            x=[NTO, D], c=[1, D], c_ctx=[1, D], sel=[128, 2],
            w_ada=[2, D, 18432], b_ada=[2, 18432], norm_g=[2, 3, D], final_norm_g=[1, D],
            ffn_w_gate=[2, 2, D, DFF], ffn_w_up=[2, 2, D, DFF], ffn_w_down=[2, 2, DFF, D],
            w_in=[2, D, PIN], hgrn_lower_bounds=[2, 2, 1024], hgrn_norm_g=[2, 128],
            dn_conv_w=[2, 5, 3072], dn_a_log=[2, 2, 8], dn_dt_bias=[2, 2, 8], dn_norm_g=[2, 128],
            w_branch_a=[2, 1024, D], w_branch_b=[2, 1024, D], w_out=[2, D, D])
        for k, shp in specs.items():
            self.inp[k] = dt(k, shp, F32, kind="ExternalInput").ap()
        self.out = dt("out", [NTO, D], F32, kind="ExternalOutput").ap()
        self.xT_d = dt("xT_d", [NKC, 128, NTO], F32, kind="Internal").ap()
        self.hT_d = dt("hT_own_d", [NKC, 128, NTO], BF16, kind="Internal").ap()
        self.hT_all_d = dt("hT_all_d", [4, 2, 4, 128, NTO], BF16, kind="Internal").ap()
        self.y_own_d = dt("y_own_d", [2, 4, 128, NT], BF16, kind="Internal").ap()
        self.y_all_d = dt("y_all_d", [4, 2, 2, 128, NT], BF16, kind="Internal").ap()
        self.yaT_d = self.y_own_d[0]
        self.ybT_d = self.y_own_d[1]
        self.dbg_out = {}
        self.ps = [nc.alloc_psum_tensor(f"ps{i}", [128, 512], F32) for i in range(8)]

    def tok(self, *a):
        return a

    def new(self, prefix):
        self.uid += 1
        return f"{prefix}{self.uid}"

    def consts(self):
        P, A, nc = self.P, self.A, self.nc
        self.ident = A.tile([128, 128], F32, "ident")
        self.ones_f = A.tile([128, 128], F32, "ones_f")
        self.ones_r = A.tile([128, 128], F32R, "ones_r")
        self.ident_b = A.tile([128, 128], BF16, "ident_b")
        self.selT = A.tile([128, 2], F32, "selT")
        P.op("sp", lambda e: e.dma_start(out=self.selT[:], in_=self.inp["sel"]), writes=["selT"], lane="selT")
        self.epsc = A.tile([128, 1], F32, "epsc")
        P.op("pool", lambda e: e.memset(self.epsc[:], EPS), writes=["epsc"])
        P.op("pool", lambda e: e.memset(self.ones_f[:], 1.0), writes=["ones_f"])
        P.op("act", lambda e: e.activation(out=self.ones_r[:], in_=self.ones_f[:], func=AF.Copy), reads=["ones_f"], writes=["ones_r"])
        P.op("pool", lambda e: e.affine_select(out=self.ident[:], in_=self.ones_f[:], pattern=[[-1, 128]],
                                               compare_op=ALU.is_equal, fill=0.0, base=0, channel_multiplier=1),
             reads=["ones_f"], writes=["ident"])
        P.op("dve", lambda e: e.tensor_copy(out=self.ident_b[:], in_=self.ident[:]), reads=["ident"], writes=["ident_b"])

    def load_T(self, src2d, rows, dst_ap, dst_tok, scratch, ps_idx=7):
        P = self.P
        lane = "ldT"
        P.op("sp", lambda e: e.dma_start(out=scratch[0:rows, :], in_=src2d), writes=["ldT_s"], lane=lane)
        ps = self.ps[ps_idx]
        P.op("pe", lambda e: e.transpose(ps[:, 0:rows], scratch[0:rows, :], self.ident[0:rows, 0:rows]),
             reads=["ldT_s", "ident"], writes=[("ps", ps_idx)])
        P.op("dve", lambda e: e.tensor_copy(out=dst_ap, in_=ps[:, 0:rows]), reads=[("ps", ps_idx)], writes=[dst_tok])

    def phase_in(self):
        P, A = self.P, self.A
        A.push()
        tin = [A.tile([128, D], F32, "tin") for _ in range(2)]
        tout = [A.tile([128, NKC, 128], F32, "tout") for _ in range(2)]
        for t in range(NTO // 128):
            s = t % 2
            src = self.inp["x"][t * 128:(t + 1) * 128, :]
            P.op("sp", lambda e, s=s, src=src: e.dma_start(out=tin[s][:], in_=src), writes=[("tin", s)], lane=f"tin{s}")
            for q in range(4):
                bank = (t % 2) * 4 + q
                for j in range(4):
                    c = q * 4 + j
                    P.op("pe", lambda e, s=s, c=c, bank=bank, j=j: e.transpose(
                        self.ps[bank][:, j * 128:(j + 1) * 128], tin[s][:, c * 128:(c + 1) * 128], self.ident[:]),
                        reads=[("tin", s), "ident"], writes=[("ps", bank)])
                eng = "act" if q % 2 == 0 else "dve"
                if eng == "act":
                    P.op("act", lambda e, s=s, q=q, bank=bank: e.activation(
                        out=tout[s][:, q * 4:(q + 1) * 4, :], in_=self.ps[bank][:].rearrange("p (a b) -> p a b", a=4), func=AF.Copy),
                        reads=[("ps", bank)], writes=[("tout", s, q)])
                else:
                    P.op("dve", lambda e, s=s, q=q, bank=bank: e.tensor_copy(
                        out=tout[s][:, q * 4:(q + 1) * 4, :], in_=self.ps[bank][:].rearrange("p (a b) -> p a b", a=4)),
                        reads=[("ps", bank)], writes=[("tout", s, q)])
            P.op("sp", lambda e, s=s, t=t: e.dma_start(
                out=self.xT_d[:, :, t * 128:(t + 1) * 128].rearrange("c p n -> p c n"), in_=tout[s][:]),
                reads=[("tout", s, q) for q in range(4)], writes=[("xT", c, t) for c in range(NKC)], lane=f"tout{s}")
        A.pop()
        P.barrier()

    def phase_adaln(self):
        P, A = self.P, self.A
        self.modT = [A.tile([128, 144, 2], F32, f"modT{l}") for l in range(2)]
        self.gT = A.tile([128, 2 * 3 * NKC + NKC], F32, "gT")
        A.push()
        scr = A.tile([128, 128], F32, "scr")
        sc = A.tile([128, NKC, 2], F32, "sc")
        craw = A.tile([128, 2 * NKC], F32, "craw")
        one2 = A.tile([1, 2], F32, "one2")
        P.op("pool", lambda e: e.memset(one2[:], 1.0), writes=["one2"])
        self.load_T(self.inp["c"].rearrange("o (c p) -> (o c) p", p=128), NKC, craw[:, 0:NKC], "craw0", scr)
        self.load_T(self.inp["c_ctx"].rearrange("o (c p) -> (o c) p", p=128), NKC, craw[:, NKC:2 * NKC], "craw1", scr)
        for v in range(2):
            P.op("act", lambda e, v=v: e.activation(out=sc[:, :, v], in_=craw[:, v * NKC:(v + 1) * NKC], func=AF.Silu),
                 reads=[f"craw{v}"], writes=[("sc", v)])
        self.load_T(self.inp["norm_g"].rearrange("l s (c p) -> (l s c) p", p=128), 96, self.gT[:, 0:96], "gT0", scr)
        self.load_T(self.inp["final_norm_g"].rearrange("o (c p) -> (o c) p", p=128), NKC, self.gT[:, 96:112], "gT1", scr)
        wsl = [A.tile([128, NKC, 512], F32, "wada") for _ in range(2)]
        brow = [A.tile([1, 512], F32, "brow") for _ in range(2)]
        for l in range(2):
            wv = self.inp["w_ada"][l].rearrange("(kc p) n -> p kc n", p=128)
            bank = 6
            for s in range(36):
                sl = s % 2
                P.op("sp", lambda e, sl=sl, s=s, wv=wv: e.dma_start(out=wsl[sl][:], in_=wv[:, :, s * 512:(s + 1) * 512]),
                     writes=[("wada", sl)], lane=f"wada{sl}")
                P.op("sp", lambda e, sl=sl, s=s, l=l: e.dma_start(out=brow[sl][:], in_=self.inp["b_ada"][l:l + 1, s * 512:(s + 1) * 512]),
                     writes=[("brow", sl)], lane=f"brow{sl}")
                for j in range(4):
                    ch = s * 4 + j
                    for k in range(NKC):
                        P.op("pe", lambda e, sl=sl, j=j, k=k, ch=ch: e.matmul(
                            self.ps[bank][:, ch * 2:ch * 2 + 2], lhsT=wsl[sl][:, k, j * 128:(j + 1) * 128], rhs=sc[:, k, :],
                            start=(k == 0), stop=False),
                            reads=[("wada", sl), ("sc", 0), ("sc", 1)], writes=[("ps", bank)])
                    P.op("pe", lambda e, sl=sl, j=j, ch=ch: e.matmul(
                        self.ps[bank][:, ch * 2:ch * 2 + 2], lhsT=brow[sl][0:1, j * 128:(j + 1) * 128], rhs=one2[0:1, :],
                        start=False, stop=True),
                        reads=[("brow", sl), "one2"], writes=[("ps", bank)])
            P.op("dve", lambda e, l=l: e.tensor_copy(out=self.modT[l][:].rearrange("p a b -> p (a b)"), in_=self.ps[bank][:, 0:288]),
                 reads=[("ps", bank)], writes=[("modT", l)])
        A.pop()
        self.gmod = A.tile([128, 2, 3, 2, NKC], F32, "gmod")
        self.shift = A.tile([128, 2, 3, 2, NKC], F32, "shift")
        self.gate = A.tile([128, 2, 3, 2, NKC], F32, "gate")
        for l in range(2):
            for sub in range(3):
                for v in range(2):
                    base = sub * 3 * NKC
                    m = self.modT[l]
                    g = self.gT[:, (l * 3 + sub) * NKC:(l * 3 + sub + 1) * NKC]
                    P.op("dve", lambda e, l=l, sub=sub, v=v, m=m, g=g, base=base: e.scalar_tensor_tensor(
                        out=self.gmod[:, l, sub, v, :], in0=m[:, base + NKC:base + 2 * NKC, v], scalar=1.0, in1=g,
                        op0=ALU.add, op1=ALU.mult), reads=[("modT", l), "gT0"], writes=[("gmod", l, sub, v)])
                    P.op("dve", lambda e, l=l, sub=sub, v=v, m=m, base=base: e.tensor_copy(
                        out=self.shift[:, l, sub, v, :], in_=m[:, base:base + NKC, v]),
                        reads=[("modT", l)], writes=[("shift", l, sub, v)])
                    fac = 1.0 if sub == 1 else 0.5
                    P.op("dve", lambda e, l=l, sub=sub, v=v, m=m, base=base, fac=fac: e.tensor_scalar(
                        out=self.gate[:, l, sub, v, :], in0=m[:, base + 2 * NKC:base + 3 * NKC, v], scalar1=fac, scalar2=None,
                        op0=ALU.mult), reads=[("modT", l)], writes=[("gate", l, sub, v)])
        P.barrier()

    def groups(self):
        return [(0, 256, 1), (256, 896, 0)]

    def prologue(self, t0, G, gm_ap, sh_ap, hT, hT_tok, to_bf16=True, o0=0):
        P, A = self.P, self.A
        A.push()
        xc = [A.tile([128, G], F32, "xc") for _ in range(4)]
        sq = [A.tile([128, G], F32R, "sq") for _ in range(2)]
        tmp = [A.tile([128, G], F32, "tmp") for _ in range(2)]
        rstd = A.tile([128, G], F32, "rstd")
        nh = (G + 511) // 512
        hw = G // nh
        u = self.new("pg")
        for c in range(NKC):
            s = c % 4
            P.op("sp", lambda e, s=s, c=c: e.dma_start(out=xc[s][:], in_=self.xT_d[c, :, t0:t0 + G]),
                 reads=[("xT", c, t) for t in range(t0 // 128, (t0 + G) // 128)], writes=[(u, "xc", s)], lane=f"xc{s}")
            q = c % 2
            P.op("act", lambda e, s=s, q=q: e.activation(out=sq[q][:], in_=xc[s][:], func=AF.Square),
                 reads=[(u, "xc", s)], writes=[(u, "sq", q)])
            for h in range(nh):
                P.op("pe", lambda e, q=q, h=h, c=c: e.matmul(self.ps[h][:, 0:hw], lhsT=self.ones_r[:], rhs=sq[q][:, h * hw:(h + 1) * hw],
                                                            start=(c == 0), stop=(c == NKC - 1)),
                     reads=[(u, "sq", q), "ones_r"], writes=[("ps", h)])
        for h in range(nh):
            P.op("act", lambda e, h=h: e.activation(out=rstd[:, h * hw:(h + 1) * hw], in_=self.ps[h][:, 0:hw], func=AF.Ln,
                                                    bias=self.epsc[:, 0:1], scale=1.0 / D),
                 reads=[("ps", h), "epsc"], writes=[(u, "rstd", h)])
            P.op("act", lambda e, h=h: e.activation(out=rstd[:, h * hw:(h + 1) * hw], in_=rstd[:, h * hw:(h + 1) * hw], func=AF.Exp, scale=-0.5),
                 reads=[(u, "rstd", h)], writes=[(u, "rstd", h)])
        for c in range(NKC):
            s = c % 4
            q = c % 2
            P.op("sp", lambda e, s=s, c=c: e.dma_start(out=xc[s][:], in_=self.xT_d[c, :, t0:t0 + G]),
                 reads=[("xT", c, t) for t in range(t0 // 128, (t0 + G) // 128)], writes=[(u, "xc", s)], lane=f"xc{s}")
            P.op("dve", lambda e, s=s, q=q: e.tensor_tensor(out=tmp[q][:], in0=xc[s][:], in1=rstd[:], op=ALU.mult),
                 reads=[(u, "xc", s)] + [(u, "rstd", h) for h in range(nh)], writes=[(u, "tmp", q)])
            if sh_ap is not None:
                P.op("act", lambda e, q=q, c=c: e.activation(out=hT[:, c, o0:o0 + G], in_=tmp[q][:], func=AF.Identity,
                                                             bias=sh_ap[:, c:c + 1], scale=gm_ap[:, c:c + 1]),
                     reads=[(u, "tmp", q)], writes=[(hT_tok, c)])
            else:
                P.op("act", lambda e, q=q, c=c: e.activation(out=hT[:, c, o0:o0 + G], in_=tmp[q][:], func=AF.Identity,
                                                             scale=gm_ap[:, c:c + 1]),
                     reads=[(u, "tmp", q)], writes=[(hT_tok, c)])
        A.pop()
        P.barrier()

    def ffn(self, l, f):
        P, A = self.P, self.A
        sub = 0 if f == 0 else 2
        wgv = self.inp["ffn_w_gate"][l, f].rearrange("(kc p) n -> p kc n", p=128)
        wuv = self.inp["ffn_w_up"][l, f].rearrange("(kc p) n -> p kc n", p=128)
        wdv = self.inp["ffn_w_down"][l, f].rearrange("(kc p) n -> p kc n", p=128)
        A.push()
        Gmax = 1024
        hT = A.tile([128, NKC, Gmax], BF16, "hT")
        wg = [A.tile([128, NKC, 128], BF16, "wg") for _ in range(3)]
        wu = [A.tile([128, NKC, 128], BF16, "wu") for _ in range(3)]
        wd = [A.tile([128, NFC, 128], BF16, "wd") for _ in range(2)]
        for (t0, G, v) in self.groups():
            self.ffn_group(l, f, sub, t0, G, v, hT, wg, wu, wd, wgv, wuv, wdv)
        A.pop()

    def ffn1p(self, l, f):
        P, A = self.P, self.A
        sub = 0 if f == 0 else 2
        wgv = self.inp["ffn_w_gate"][l, f].rearrange("(kc p) n -> p kc n", p=128)
        wuv = self.inp["ffn_w_up"][l, f].rearrange("(kc p) n -> p kc n", p=128)
        wdv = self.inp["ffn_w_down"][l, f].rearrange("(kc p) n -> p kc n", p=128)
        HF = NFC // 2
        NB_, BW = 3, NTO // 3
        u = self.new("f1")
        A.push()
        hT = A.tile([128, NKC, NTO], BF16, "hT1")
        for (t0, G, v) in self.groups():
            self.prologue(t0, G, self.gmod[:, l, sub, v, :], self.shift[:, l, sub, v, :], hT, (u, "hT", t0), o0=t0)
        hreads = [((u, "hT", t0), c) for (t0, G, v) in self.groups() for c in range(NKC)]
        wg = [A.tile([128, NKC, 128], BF16, "wg") for _ in range(3)]
        wu = [A.tile([128, NKC, 128], BF16, "wu") for _ in range(3)]
        wd = [A.tile([128, HF, 128], BF16, "wd") for _ in range(2)]
        actT = A.tile([128, HF, NTO], BF16, "actT")
        sg = [A.tile([128, BW], F32, "sg") for _ in range(4)]
        xc = [A.tile([128, NTO], F32, "xe") for _ in range(3)]
        segs = [(t0, t0 + G, v) for (t0, G, v) in self.groups()]
        cnt = {"sg": 0, "x": 0, "d": 0}

        def load_gu(jg):
            s = jg % 3
            P.op("pool", lambda e: e.dma_start(out=wg[s][:], in_=wgv[:, :, jg * 128:(jg + 1) * 128]), writes=[("wg", s)], lane=f"wg{s}")
            P.op("pool", lambda e: e.dma_start(out=wu[s][:], in_=wuv[:, :, jg * 128:(jg + 1) * 128]), writes=[("wu", s)], lane=f"wu{s}")

        def load_d(p, n):
            s = cnt["d"] % 2
            cnt["d"] += 1
            P.op("pool", lambda e: e.dma_start(out=wd[s][:], in_=wdv[:, p * HF:(p + 1) * HF, n * 128:(n + 1) * 128]), writes=[("wd", s)], lane=f"wd{s}")
            return s

        def gu_chunk(p, j):
            jg = p * HF + j
            s = jg % 3
            for b in range(NB_):
                cs = slice(b * BW, (b + 1) * BW)
                bg = self.nb()
                for k in range(NKC):
                    P.op("pe", lambda e, k=k, bg=bg, cs=cs: e.matmul(self.ps[bg][:, 0:BW], lhsT=wg[s][:, k, :], rhs=hT[:, k, cs], start=(k == 0), stop=(k == NKC - 1)),
                         reads=[("wg", s)] + (hreads if k == 0 else []), writes=[("ps", bg)])
                bu = self.nb()
                for k in range(NKC):
                    P.op("pe", lambda e, k=k, bu=bu, cs=cs: e.matmul(self.ps[bu][:, 0:BW], lhsT=wu[s][:, k, :], rhs=hT[:, k, cs], start=(k == 0), stop=(k == NKC - 1)),
                         reads=[("wu", s)] + (hreads if k == 0 else []), writes=[("ps", bu)])
                si = cnt["sg"] % 4
                cnt["sg"] += 1
                P.op("act", lambda e, si=si, bg=bg: e.activation(out=sg[si][:], in_=self.ps[bg][:, 0:BW], func=AF.Silu), reads=[("ps", bg)], writes=[(u, "sg", si)])
                P.op("dve", lambda e, si=si, bu=bu, cs=cs: e.tensor_tensor(out=actT[:, j, cs], in0=sg[si][:], in1=self.ps[bu][:, 0:BW], op=ALU.mult),
                     reads=[(u, "sg", si), ("ps", bu)], writes=[(u, "actT", j, b)])

        def down_chunk(p, n, s):
            xs = cnt["x"] % 3
            cnt["x"] += 1
            P.op("sp", lambda e: e.dma_start(out=xc[xs][:], in_=self.xT_d[n, :, :]),
                 reads=[("xT", n, t) for t in range(NTO // 128)], writes=[(u, "xe", xs)], lane=f"xe{xs}")
            for b in range(NB_):
                by = self.nb()
                for j in range(HF):
                    P.op("pe", lambda e, j=j, by=by, b=b: e.matmul(self.ps[by][:, 0:BW], lhsT=wd[s][:, j, :], rhs=actT[:, j, b * BW:(b + 1) * BW],
                                                       start=(j == 0), stop=(j == HF - 1)),
                         reads=[("wd", s), (u, "actT", j, b)], writes=[("ps", by)])
                for (a0, a1, v) in segs:
                    lo, hi = max(a0, b * BW), min(a1, (b + 1) * BW)
                    if lo >= hi:
                        continue
                    P.op("dve", lambda e, lo=lo, hi=hi, v=v, by=by, b=b: e.scalar_tensor_tensor(
                        out=xc[xs][:, lo:hi], in0=self.ps[by][:, lo - b * BW:hi - b * BW], scalar=self.gate[:, l, sub, v, n:n + 1],
                        in1=xc[xs][:, lo:hi], op0=ALU.mult, op1=ALU.add),
                        reads=[("ps", by), (u, "xe", xs)], writes=[(u, "xe", xs)])
            P.op("sp", lambda e: e.dma_start(out=self.xT_d[n, :, :], in_=xc[xs][:]),
                 reads=[(u, "xe", xs)], writes=[("xT", n, t) for t in range(NTO // 128)], lane=f"xs{xs}")

        load_gu(0)
        load_gu(1)
        for p in range(2):
            dslots = {}
            for j in range(HF):
                jg = p * HF + j
                if jg + 2 < NFC:
                    load_gu(jg + 2)
                if j == HF - 2:
                    dslots[0] = load_d(p, 0)
                if j == HF - 1:
                    dslots[1] = load_d(p, 1)
                gu_chunk(p, j)
            for n in range(NKC):
                down_chunk(p, n, dslots[n])
                if n + 2 < NKC:
                    dslots[n + 2] = load_d(p, n + 2)
        A.pop()
        P.barrier()

    def ffn_group(self, l, f, sub, t0, G, v, hT, wg, wu, wd, wgv, wuv, wdv):
        P, A = self.P, self.A
        u = self.new("ffn")
        self.prologue(t0, G, self.gmod[:, l, sub, v, :], self.shift[:, l, sub, v, :], hT, (u, "hT"))
        if "dump_ffn" in self.stages and t0 == 0 and l == 0 and f == 0:
            self.dump_tile("dbg_hT", hT[:, :, 0:G], [128, NKC, G], [((u, "hT"), c) for c in range(NKC)])
        A.push()
        actT = A.tile([128, NFC, G], BF16, "actT")
        nh = (G + 511) // 512
        hw = G // nh
        sg = [A.tile([128, hw], F32, "sg") for _ in range(2 * nh)]
        xc = [A.tile([128, G], F32, "xe") for _ in range(3)]
        hreads = [((u, "hT"), c) for c in range(NKC)]

        def load_gu(j):
            s = j % 3
            P.op("pool", lambda e, s=s, j=j: e.dma_start(out=wg[s][:], in_=wgv[:, :, j * 128:(j + 1) * 128]),
                 writes=[("wg", s)], lane=f"wg{s}")
            P.op("pool", lambda e, s=s, j=j: e.dma_start(out=wu[s][:], in_=wuv[:, :, j * 128:(j + 1) * 128]),
                 writes=[("wu", s)], lane=f"wu{s}")

        def load_d(n):
            s = n % 2
            P.op("pool", lambda e, s=s, n=n: e.dma_start(out=wd[s][:], in_=wdv[:, :, n * 128:(n + 1) * 128]),
                 writes=[("wd", s)], lane=f"wd{s}")

        load_gu(0)
        load_gu(1)
        for j in range(NFC):
            if j + 2 < NFC:
                load_gu(j + 2)
            elif j + 2 == NFC:
                load_d(0)
            else:
                load_d(1)
            s = j % 3
            par = j % 2
            for h in range(nh):
                bg = par * 4 + h
                bu = par * 4 + 2 + h
                for k in range(NKC):
                    P.op("pe", lambda e, s=s, k=k, h=h, bg=bg: e.matmul(
                        self.ps[bg][:, 0:hw], lhsT=wg[s][:, k, :], rhs=hT[:, k, h * hw:(h + 1) * hw],
                        start=(k == 0), stop=(k == NKC - 1)),
                        reads=[("wg", s)] + (hreads if k == 0 else []), writes=[("ps", bg)])
                for k in range(NKC):
                    P.op("pe", lambda e, s=s, k=k, h=h, bu=bu: e.matmul(
                        self.ps[bu][:, 0:hw], lhsT=wu[s][:, k, :], rhs=hT[:, k, h * hw:(h + 1) * hw],
                        start=(k == 0), stop=(k == NKC - 1)),
                        reads=[("wu", s)] + (hreads if k == 0 else []), writes=[("ps", bu)])
                si = par * nh + h
                P.op("act", lambda e, si=si, bg=bg: e.activation(out=sg[si][:], in_=self.ps[bg][:, 0:hw], func=AF.Silu),
                     reads=[("ps", bg)], writes=[(u, "sg", si)])
                P.op("dve", lambda e, si=si, bu=bu, j=j, h=h: e.tensor_tensor(
                    out=actT[:, j, h * hw:(h + 1) * hw], in0=sg[si][:], in1=self.ps[bu][:, 0:hw], op=ALU.mult),
                    reads=[(u, "sg", si), ("ps", bu)], writes=[(u, "actT", j, h)])
        if "dump_ffn" in self.stages and t0 == 0 and l == 0 and f == 0:
            self.dump_tile("dbg_actT", actT[:], [128, NFC, G], [(u, "actT", j, h) for j in range(NFC) for h in range(nh)])
            self.dump_tile("dbg_wg", wg[(NFC - 1) % 3][:], [128, NKC, 128], [("wg", (NFC - 1) % 3)])
        for n in range(NKC):
            s = n % 2
            xs = n % 3
            P.op("sp", lambda e, xs=xs, n=n: e.dma_start(out=xc[xs][:], in_=self.xT_d[n, :, t0:t0 + G]),
                 reads=[("xT", n, t) for t in range(t0 // 128, (t0 + G) // 128)], writes=[(u, "xe", xs)], lane=f"xe{xs}")
            for h in range(nh):
                by = (n % 2) * 2 + h
                for j in range(NFC):
                    P.op("pe", lambda e, s=s, j=j, h=h, by=by: e.matmul(
                        self.ps[by][:, 0:hw], lhsT=wd[s][:, j, :], rhs=actT[:, j, h * hw:(h + 1) * hw],
                        start=(j == 0), stop=(j == NFC - 1)),
                        reads=[("wd", s), (u, "actT", j, h)], writes=[("ps", by)])
                P.op("dve", lambda e, xs=xs, h=h, by=by, n=n: e.scalar_tensor_tensor(
                    out=xc[xs][:, h * hw:(h + 1) * hw], in0=self.ps[by][:, 0:hw], scalar=self.gate[:, l, sub, v, n:n + 1],
                    in1=xc[xs][:, h * hw:(h + 1) * hw], op0=ALU.mult, op1=ALU.add),
                    reads=[("ps", by), (u, "xe", xs)], writes=[(u, "xe", xs)])
            P.op("sp", lambda e, xs=xs, n=n: e.dma_start(out=self.xT_d[n, :, t0:t0 + G], in_=xc[xs][:]),
                 reads=[(u, "xe", xs)], writes=[("xT", n, t) for t in range(t0 // 128, (t0 + G) // 128)], lane=f"xs{xs}")
            if n + 2 < NKC:
                load_d(n + 2)
        A.pop()
        P.barrier()

    def nb(self):
        self.bank_rr = (getattr(self, "bank_rr", -1) + 1) % 8
        return self.bank_rr

    def phase_small(self):
        P, A = self.P, self.A
        self.lb = A.tile([128, 2, 16], F32, "lb")
        self.oml = A.tile([128, 2, 16], F32, "oml")
        self.hng = A.tile([128, 2], F32, "hng")
        self.dng = A.tile([128, 2], F32, "dng")
        self.convT = A.tile([128, 2, 120], F32, "convT")
        self.alog = A.tile([128, 2, 16], F32, "alog")
        self.dtb = A.tile([128, 2, 16], F32, "dtb")
        self.maskF = A.tile([128, 128], F32, "maskF")
        self.maskB = A.tile([128, 128], F32, "maskB")
        self.mLT = A.tile([128, 128], F32, "mLT")
        self.mGT = A.tile([128, 128], F32, "mGT")
        self.bigP = [A.tile([128, 128], F32, "bigP") for _ in range(4)]
        self.bigN = [A.tile([128, 128], F32, "bigN") for _ in range(4)]
        self.maskF4 = A.tile([128, 4, 128], F32, "maskF4")
        self.maskB4 = A.tile([128, 4, 128], F32, "maskB4")
        A.push()
        scr = A.tile([128, 128], F32, "scr")
        raw = A.tile([128, 32], F32, "lbraw")
        self.load_T(self.inp["hgrn_lower_bounds"].rearrange("l d (c p) -> (l d c) p", p=128), 32, raw[:, :], "lbraw", scr)
        P.op("pool", lambda e: e.memset(self.lb[:, 0, :], 0.0), writes=["lb0"])
        P.op("dve", lambda e: e.tensor_tensor(out=self.lb[:, 1, :], in0=raw[:, 16:32], in1=raw[:, 0:16], op=ALU.subtract),
             reads=["lbraw"], writes=["lb1"])
        P.op("act", lambda e: e.activation(out=self.lb[:, 1, :], in_=self.lb[:, 1, :], func=AF.Sigmoid), reads=["lb1"], writes=["lb1"])
        P.op("dve", lambda e: e.tensor_scalar(out=self.oml[:].rearrange("p a b -> p (a b)"), in0=self.lb[:].rearrange("p a b -> p (a b)"),
                                              scalar1=-1.0, scalar2=1.0, op0=ALU.mult, op1=ALU.add),
             reads=["lb0", "lb1"], writes=["oml"])
        self.load_T(self.inp["hgrn_norm_g"], 2, self.hng[:, :], "hng", scr)
        self.load_T(self.inp["dn_norm_g"], 2, self.dng[:, :], "dng", scr)
        for l in range(2):
            self.load_T(self.inp["dn_conv_w"][l].rearrange("j (c p) -> (j c) p", p=128), 120, self.convT[:, l, :], ("convT", l), scr)
        P.op("sp", lambda e: e.dma_start(out=self.alog[:].rearrange("p a b -> p (a b)"),
                                         in_=self.inp["dn_a_log"].rearrange("l d h -> (l d h)").partition_broadcast(128)),
             writes=["alog"], lane="smallc")
        P.op("sp", lambda e: e.dma_start(out=self.dtb[:].rearrange("p a b -> p (a b)"),
                                         in_=self.inp["dn_dt_bias"].rearrange("l d h -> (l d h)").partition_broadcast(128)),
             writes=["dtb"], lane="smallc")
        P.op("act", lambda e: e.activation(out=self.alog[:].rearrange("p a b -> p (a b)"), in_=self.alog[:].rearrange("p a b -> p (a b)"), func=AF.Exp),
             reads=["alog"], writes=["alog"])
        P.op("dve", lambda e: e.tensor_scalar(out=self.alog[:].rearrange("p a b -> p (a b)"), in0=self.alog[:].rearrange("p a b -> p (a b)"),
                                              scalar1=-1.0, scalar2=None, op0=ALU.mult), reads=["alog"], writes=["alog"])
        P.op("pool", lambda e: e.affine_select(out=self.maskF[:], in_=self.ones_f[:], pattern=[[1, 128]],
                                               compare_op=ALU.is_ge, fill=0.0, base=0, channel_multiplier=-1),
             reads=["ones_f"], writes=["maskF"])
        P.op("pool", lambda e: e.memset(self.maskF[0:64, 64:128], 0.0), reads=["maskF"], writes=["maskF"])
        P.op("pool", lambda e: e.affine_select(out=self.maskB[:], in_=self.ones_f[:], pattern=[[-1, 128]],
                                               compare_op=ALU.is_ge, fill=0.0, base=0, channel_multiplier=1),
             reads=["ones_f"], writes=["maskB"])
        P.op("pool", lambda e: e.memset(self.maskB[64:128, 0:64], 0.0), reads=["maskB"], writes=["maskB"])
        P.op("pool", lambda e: e.tensor_tensor(out=self.mLT[:], in0=self.maskB[:], in1=self.ident[:], op=ALU.subtract), reads=["maskB", "ident"], writes=["mLT"])
        P.op("pool", lambda e: e.tensor_tensor(out=self.mGT[:], in0=self.maskF[:], in1=self.ident[:], op=ALU.subtract), reads=["maskF", "ident"], writes=["mGT"])
        for i, (m, mt) in enumerate(((self.mLT, "mLT"), (self.mGT, "mGT"), (self.maskF, "maskF"), (self.maskB, "maskB"))):
            P.op("dve", lambda e, i=i, m=m: e.tensor_scalar(out=self.bigP[i][:], in0=m[:], scalar1=-30000.0, scalar2=30000.0, op0=ALU.mult, op1=ALU.add),
                 reads=[mt], writes=["bigm"])
            P.op("dve", lambda e, i=i, m=m: e.tensor_scalar(out=self.bigN[i][:], in0=m[:], scalar1=30000.0, scalar2=-30000.0, op0=ALU.mult, op1=ALU.add),
                 reads=[mt], writes=["bigm"])
        for j in range(4):
            P.op("pool", lambda e, j=j: e.tensor_copy(out=self.maskF4[:, j, :], in_=self.maskF[:]), reads=["maskF"], writes=["mask4"])
            P.op("pool", lambda e, j=j: e.tensor_copy(out=self.maskB4[:, j, :], in_=self.maskB[:]), reads=["maskB"], writes=["mask4"])
        A.pop()
        P.barrier()

    def mixer_prep(self, l):
        A = self.A
        A.push()
        hT = A.tile([128, NKC, 1024], BF16, "hTm")
        for (t0, G, v) in self.groups():
            self.mixer_prep_group(l, t0, G, v, hT)
        A.pop()
        for j in range(4):
            self.P.op("pool", lambda e, j=j: e.collective_compute(
                "AllGather", ALU.bypass, replica_groups=[[0, 1], [2, 3], [4, 5], [6, 7]],
                ins=[self.hT_d[4 * j:4 * j + 4].rearrange("c p n -> (c p) n")], outs=[self.hT_all_d[j].rearrange("r c p n -> (r c p) n")]),
                reads=[("hTd", t) for t in range(NTO // 128)], writes=[("hTall", j)], lane="cc_h", step=1)
        self.P.barrier()

    def mixer_prep_group(self, l, t0, G, v, hT):
        P = self.P
        u = self.new("mp")
        self.prologue(t0, G, self.gmod[:, l, 1, v, :], self.shift[:, l, 1, v, :], hT, (u, "hT"))
        P.op("sp", lambda e: e.dma_start(out=self.hT_d[:, :, t0:t0 + G].rearrange("c p n -> p c n"), in_=hT[:, :, 0:G]),
             reads=[((u, "hT"), c) for c in range(NKC)], writes=[("hTd", t) for t in range(t0 // 128, (t0 + G) // 128)], lane="hTd")
        P.barrier()

    TG = [(0, 384), (384, 384), (768, 384), (1152, 384), (1536, 384), (1920, 384)]

    def proj_fm(self, w, hg, G, evac):
        P = self.P
        bank = self.nb()
        wt, wtok = w
        hgt, hgtok = hg
        for k in range(NKC):
            P.op("pe", lambda e, k=k: e.matmul(self.ps[bank][:, 0:G], lhsT=wt[:, k, :], rhs=hgt[:, k, 0:G],
                                               start=(k == 0), stop=(k == NKC - 1)),
                 reads=[wtok] + [hgtok + (j,) for j in range(4)], writes=[("ps", bank)])
        evac(bank)

    def proj_tm(self, w, hg, G, ncols, evac):
        P = self.P
        wt, wtok = w
        hgt, hgtok = hg
        for ti in range(G // 128):
            bank = self.nb()
            for k in range(NKC):
                P.op("pe", lambda e, k=k, ti=ti, bank=bank: e.matmul(self.ps[bank][:, 0:ncols], lhsT=hgt[:, k, ti * 128:(ti + 1) * 128],
                                                                   rhs=wt[:, k, 0:ncols], start=(k == 0), stop=(k == NKC - 1)),
                     reads=[wtok] + [hgtok + (j,) for j in range(4)], writes=[("ps", bank)])
            evac(bank, ti)

    def load_w_in(self, l, col, ncols, tile_, tok, lane):
        wv = self.inp["w_in"][l].rearrange("(kc p) n -> p kc n", p=128)
        self.P.op("pool", lambda e: e.dma_start(out=tile_[:, :, 0:ncols], in_=wv[:, :, col:col + ncols]), writes=[tok], lane=lane)

    def load_hg(self, hg, slot, t0, G, u):
        r, off = t0 // NTO, t0 % NTO
        for j in range(4):
            self.P.op("sp", lambda e, j=j: e.dma_start(out=hg[slot][:, 4 * j:4 * j + 4, 0:G],
                                                       in_=self.hT_all_d[j, r][:, :, off:off + G].rearrange("c p n -> p c n")),
                      reads=[("hTall", j)], writes=[(u, "hg", slot, j)], lane=f"hg{slot}")

    def hgrn_head(self, l, h):
        P, A = self.P, self.A
        u = self.new("hg")
        A.push()
        qa = A.tile([128, NT], F32, "qa")
        lf = [A.tile([128, NT], F32, "lf") for _ in range(2)]
        kk = [A.tile([128, NT], F32, "kk") for _ in range(2)]
        ga = A.tile([128, NT], F32, "ga")
        Vt = A.tile([128, NT // 128, 128], BF16, "Vt")
        oT = A.tile([128, NT], F32, "oT")
        self.hgrn_proj(l, h, u, qa, lf, kk, ga, Vt)
        if "hg_stop1" in self.stages:
            self.dump_tile("dbg_qa", qa[:], [128, NT], [(u, "qa", gi) for gi in range(NG)])
            self.dump_tile("dbg_lf0", lf[0][:], [128, NT], [(u, "lf", 0, gi) for gi in range(NG)])
            self.dump_tile("dbg_kk1", kk[1][:], [128, NT], [(u, "kk", 1, gi) for gi in range(NG)])
            self.dump_tile("dbg_Vt", Vt[:], [128, NT // 128, 128], [(u, "Vt", t) for t in range(18)])
            A.pop()
            P.barrier()
            return
        for d in range(2):
            self.hgrn_dir(l, h, u, d, qa, lf[d], kk[d], Vt, oT)
            if "hg_stop2" in self.stages:
                self.dump_tile("dbg_oT0", oT[:], [128, NT], [(u, "oT", q) for q in range(5)])
                A.pop()
                P.barrier()
                return
        self.head_out(u, oT, ga, "ga", self.hng[:, l:l + 1], self.yaT_d[h], ("ya", h))
        if "dump_oa" in self.stages and l == 0:
            self.dump_tile(f"dbg_oa{h}", oT[:], [128, NT], [(u, "oT", q) for q in range(5)])
        A.pop()
        P.barrier()

    def hgrn_proj(self, l, h, u, qa, lf, kk, ga, Vt):
        P, A = self.P, self.A
        A.push()
        cols = [0 + 128 * h, 1024 + 128 * h, 2048 + 128 * h, 3072 + 128 * h, 4096 + 128 * h]
        w = [A.tile([128, NKC, 128], BF16, "wm") for _ in range(5)]
        hg = [A.tile([128, NKC, 512], BF16, "hgm") for _ in range(2)]
        sgm = [A.tile([128, 512], F32, "sgm") for _ in range(2)]
        for i in range(5):
            self.load_w_in(l, cols[i], 128, w[i], (u, "w", i), f"wm{i}")
        for gi, (t0, G) in enumerate(self.TG):
            self.hgrn_proj_group(l, h, u, gi, t0, G, w, hg, sgm, qa, lf, kk, ga, Vt)
        A.pop()
        P.barrier()

    def hgrn_proj_group(self, l, h, u, gi, t0, G, w, hg, sgm, qa, lf, kk, ga, Vt):
        P = self.P
        slot = gi % 2
        self.load_hg(hg, slot, t0, G, u)
        hgs = (hg[slot], (u, "hg", slot))
        tl = (u, "g", gi)
        self.proj_fm((w[0], (u, "w", 0)), hgs, G, lambda bank: P.op(
            "act", lambda e: e.activation(out=qa[:, t0:t0 + G], in_=self.ps[bank][:, 0:G], func=AF.Silu),
            reads=[("ps", bank)], writes=[(u, "qa", gi)]))
        for d in range(2):
            idx = d * 8 + h

            def ev(bank, d=d, idx=idx):
                P.op("act", lambda e: e.activation(out=sgm[d][:, 0:G], in_=self.ps[bank][:, 0:G], func=AF.Sigmoid),
                     reads=[("ps", bank)], writes=[(u, "sgm", d)])
                P.op("dve", lambda e: e.tensor_scalar(out=sgm[d][:, 0:G], in0=sgm[d][:, 0:G], scalar1=self.oml[:, l, idx:idx + 1],
                                                      scalar2=self.lb[:, l, idx:idx + 1], op0=ALU.mult, op1=ALU.add),
                     reads=[(u, "sgm", d), "oml", "lb0", "lb1"], writes=[(u, "sgm", d)])
                P.op("dve", lambda e: e.tensor_scalar(out=kk[d][:, t0:t0 + G], in0=sgm[d][:, 0:G], scalar1=-1.0, scalar2=1.0,
                                                      op0=ALU.mult, op1=ALU.add),
                     reads=[(u, "sgm", d)], writes=[(u, "kk", d, gi)])
                P.op("act", lambda e: e.activation(out=lf[d][:, t0:t0 + G], in_=sgm[d][:, 0:G], func=AF.Ln),
                     reads=[(u, "sgm", d)], writes=[(u, "lf", d, gi)])
            self.proj_fm((w[1 + d], (u, "w", 1 + d)), hgs, G, ev)
        self.proj_fm((w[4], (u, "w", 4)), hgs, G, lambda bank: P.op(
            "act", lambda e: e.activation(out=ga[:, t0:t0 + G], in_=self.ps[bank][:, 0:G], func=AF.Silu),
            reads=[("ps", bank)], writes=[(u, "ga", gi)]))
        self.proj_tm((w[3], (u, "w", 3)), hgs, G, 128, lambda bank, ti: P.op(
            "dve", lambda e: e.tensor_copy(out=Vt[:, t0 // 128 + ti, :], in_=self.ps[bank][:, 0:128]),
            reads=[("ps", bank)], writes=[(u, "Vt", t0 // 128 + ti)]))

    def chain_pos(self, d, c):
        if d == 0:
            return c
        return 3 - c if c < 4 else 4 + (35 - c)

    def hgrn_dir(self, l, h, u0, d, qa, lf, kk, Vt, oT):
        P, A = self.P, self.A
        u = self.new("hd")
        NC = NT // 64
        A.push()
        X = A.tile([128, NT], F32, "X")
        dd = A.tile([128, NT], F32, "dd")
        ex = A.tile([128, NT], F32, "ex")
        Qt = A.tile([128, NT], BF16, "Qt")
        Kt = A.tile([128, NT], BF16, "Kt")
        Qi = A.tile([128, NT], BF16, "Qi")
        KuT = A.tile([128, NT], BF16, "KuT")
        Kutok = A.tile([128, NT // 128, 128], BF16, "Kutok")
        ATm = A.tile([128, NT // 128, 128], BF16, "ATm")
        U = A.tile([128, 128, NC], F32, "U")
        decb = A.tile([128, 128, NC], F32, "decb")
        R = A.tile([128, 128, NC], F32, "R")
        S = A.tile([128, NC, 128], BF16, "S")
        refs = A.tile([128, 4, NC], F32, "refs")
        allg = lambda name, dd_=None: [(u0, name, d, gi) for gi in range(5)] if dd_ is None else None
        lf_r = [(u0, "lf", d, gi) for gi in range(NG)]
        kk_r = [(u0, "kk", d, gi) for gi in range(NG)]
        qa_r = [(u0, "qa", gi) for gi in range(NG)]
        decf = decb[:].rearrange("p a b -> p (a b)")
        P.op("pool", lambda e: e.memset(decf[:, 0:NT], 1.0), writes=[(u, "decb")])
        P.op("dve", lambda e: e.tensor_tensor_scan(out=X[:], data0=decf[:, 0:NT], data1=lf[:], initial=0.0,
                                                   op0=ALU.mult, op1=ALU.add), reads=lf_r + [(u, "decb")], writes=[(u, "X")])
        X3 = X[:].rearrange("p (c t) -> p c t", t=64)
        lf3 = lf[:].rearrange("p (c t) -> p c t", t=64)
        if d == 0:
            P.op("dve", lambda e: e.tensor_copy(out=refs[:, 0, :], in_=X3[:, :, 31]), reads=[(u, "X")], writes=[(u, "rm")])
            P.op("dve", lambda e: e.tensor_tensor(out=refs[:, 1, :], in0=X3[:, :, 0], in1=lf3[:, :, 0], op=ALU.subtract),
                 reads=[(u, "X")] + lf_r, writes=[(u, "r0")])
            P.op("dve", lambda e: e.tensor_copy(out=refs[:, 2, :], in_=X3[:, :, 63]), reads=[(u, "X")], writes=[(u, "r1")])
        else:
            P.op("dve", lambda e: e.tensor_copy(out=refs[:, 2, :], in_=X3[:, :, 63]), reads=[(u, "X")], writes=[(u, "r1")])
            P.op("dve", lambda e: e.tensor_tensor(out=X[:], in0=X[:], in1=lf[:], op=ALU.subtract),
                 reads=[(u, "X"), (u, "r1")] + lf_r, writes=[(u, "X")])
            P.op("dve", lambda e: e.tensor_copy(out=refs[:, 0, :], in_=X3[:, :, 32]), reads=[(u, "X")], writes=[(u, "rm")])
            P.op("dve", lambda e: e.tensor_copy(out=refs[:, 1, :], in_=X3[:, :, 0]), reads=[(u, "X")], writes=[(u, "r0")])
        sig = 1.0 if d == 0 else -1.0
        rA = 1 if d == 0 else 2
        rB = 2 if d == 0 else 1
        dd3 = dd[:].rearrange("p (c t) -> p c t", t=64)

        def derive(ri, rtok, outs):
            P.op("dve", lambda e: e.tensor_tensor(out=dd3, in0=X3, in1=refs[:, ri, :].unsqueeze(2).to_broadcast([128, NC, 64]),
                                                  op=ALU.subtract), reads=[(u, "X"), rtok], writes=[(u, "dd")])
            for (sg_, src, sreads, dst, dtok) in outs:
                P.op("act", lambda e, sg_=sg_: e.activation(out=ex[:], in_=dd[:], func=AF.Exp, scale=sg_),
                     reads=[(u, "dd")], writes=[(u, "ex")])
                P.op("dve", lambda e, src=src, dst=dst: e.tensor_tensor(out=dst[:], in0=src[:], in1=ex[:], op=ALU.mult),
                     reads=[(u, "ex")] + sreads, writes=[dtok])
        derive(0, (u, "rm"), [(sig, qa, qa_r, Qt, (u, "Qt")), (-sig, kk, kk_r, Kt, (u, "Kt"))])
        derive(rA, (u, "r0") if rA == 1 else (u, "r1"), [(sig, qa, qa_r, Qi, (u, "Qi"))])
        derive(rB, (u, "r0") if rB == 1 else (u, "r1"), [(-sig, kk, kk_r, KuT, (u, "KuT"))])
        P.op("dve", lambda e: e.tensor_tensor(out=refs[:, 3, :], in0=refs[:, 2, :], in1=refs[:, 1, :], op=ALU.subtract),
             reads=[(u, "r0"), (u, "r1")], writes=[(u, "dec")])
        P.op("act", lambda e: e.activation(out=refs[:, 3, :], in_=refs[:, 3, :], func=AF.Exp), reads=[(u, "dec")], writes=[(u, "dec")])
        if d == 0:
            P.op("dve", lambda e: e.tensor_copy(out=decb[:], in_=refs[:, 3, :].unsqueeze(1).to_broadcast([128, 128, NC])),
                 reads=[(u, "dec")], writes=[(u, "decb")])
        else:
            for c in range(NC):
                pos = self.chain_pos(1, c)
                P.op("act", lambda e, c=c, pos=pos: e.activation(out=refs[:, 0, pos:pos + 1], in_=refs[:, 3, c:c + 1], func=AF.Copy),
                     reads=[(u, "dec"), (u, "Qt"), (u, "Kt")], writes=[(u, "decc")])
            P.op("dve", lambda e: e.tensor_copy(out=decb[:], in_=refs[:, 0, :].unsqueeze(1).to_broadcast([128, 128, NC])),
                 reads=[(u, "decc")], writes=[(u, "decb")])
        P.op("dve", lambda e: e.memset(decb[:, :, 0], 0.0), reads=[(u, "decb")], writes=[(u, "decb")])
        for q in range(5):
            bank = self.nb()
            nt_ = 4 if q < 4 else 2
            psb = self.ps[bank][:].bitcast(BF16)
            for j in range(nt_):
                t = q * 4 + j
                P.op("pe", lambda e, j=j, t=t, psb=psb: e.transpose(psb[:, j * 128:(j + 1) * 128], KuT[:, t * 128:(t + 1) * 128], self.ident_b[:]),
                     reads=[(u, "KuT"), "ident_b"], writes=[("ps", bank)])
            P.op("act", lambda e, q=q, nt_=nt_, psb=psb: e.activation(
                out=Kutok[:, q * 4:q * 4 + nt_, :], in_=psb[:, 0:nt_ * 128].rearrange("p (a b) -> p a b", b=128), func=AF.Copy),
                reads=[("ps", bank)], writes=[(u, "Kutok", q)])
        mask4 = self.maskF4 if d == 0 else self.maskB4
        P.op("pool", lambda e: e.memset(ATm[:], 0.0), writes=[(u, "ATz")])
        for q in range(5):
            bank = self.nb()
            nt_ = 4 if q < 4 else 2
            for j in range(nt_):
                t = q * 4 + j
                P.op("pe", lambda e, j=j, t=t, bank=bank: e.matmul(self.ps[bank][:, j * 128:(j + 1) * 128], lhsT=Kt[:, t * 128:(t + 1) * 128],
                                                                  rhs=Qt[:, t * 128:(t + 1) * 128], start=True, stop=True),
                     reads=[(u, "Kt"), (u, "Qt")], writes=[("ps", bank)])
            P.op("dve", lambda e, q=q, nt_=nt_, bank=bank: e.copy_predicated(
                out=ATm[:, q * 4:q * 4 + nt_, :], mask=mask4[:, 0:nt_, :].bitcast(mybir.dt.uint32),
                data=self.ps[bank][:, 0:nt_ * 128].rearrange("p (a b) -> p a b", b=128)),
                reads=[("ps", bank), "mask4", (u, "ATz")], writes=[(u, "ATm", q)])
        for c in range(NC):
            t, half = c // 2, c % 2
            if c % 8 == 0:
                bankpair = (self.nb(), self.nb())
            bank = bankpair[half]
            j = (c // 2) % 4
            r0_, r1_ = half * 64, half * 64 + 64
            P.op("pe", lambda e, t=t, j=j, r0_=r0_, r1_=r1_, bank=bank: e.matmul(
                self.ps[bank][:, j * 128:(j + 1) * 128], lhsT=Kutok[r0_:r1_, t, :], rhs=Vt[r0_:r1_, t, :], start=True, stop=True),
                reads=[(u, "Kutok", t // 4), (u0, "Vt", t)], writes=[("ps", bank)])
            pos = self.chain_pos(d, c)
            eng = "act" if c % 2 == 0 else "dve"
            if eng == "act":
                P.op("act", lambda e, j=j, pos=pos, bank=bank: e.activation(out=U[:, :, pos], in_=self.ps[bank][:, j * 128:(j + 1) * 128], func=AF.Copy),
                     reads=[("ps", bank)], writes=[(u, "U", c)])
            else:
                P.op("dve", lambda e, j=j, pos=pos, bank=bank: e.tensor_copy(out=U[:, :, pos], in_=self.ps[bank][:, j * 128:(j + 1) * 128]),
                     reads=[("ps", bank)], writes=[(u, "U", c)])
        P.op("dve", lambda e: e.tensor_tensor_scan(out=R[:].rearrange("p a b -> p (a b)"), data0=decb[:].rearrange("p a b -> p (a b)"),
                                                   data1=U[:].rearrange("p a b -> p (a b)"), initial=0.0, op0=ALU.mult, op1=ALU.add),
             reads=[(u, "decb")] + [(u, "U", c) for c in range(NC)], writes=[(u, "R")])
        P.op("pool", lambda e: e.memset(S[:, 0, :], 0.0), writes=[(u, "S0")])
        P.op("act", lambda e: e.activation(out=S[:, 1:NC, :], in_=R[:, :, 0:NC - 1].rearrange("p d c -> p c d"), func=AF.Copy),
             reads=[(u, "R")], writes=[(u, "S")])
        for q in range(5):
            bank = self.nb()
            nt_ = 4 if q < 4 else 2
            for j in range(nt_):
                t = q * 4 + j
                P.op("pe", lambda e, j=j, t=t, bank=bank: e.matmul(self.ps[bank][:, j * 128:(j + 1) * 128], lhsT=Vt[:, t, :], rhs=ATm[:, t, :],
                                                                  start=True, stop=False),
                     reads=[(u0, "Vt", t), (u, "ATm", q)], writes=[("ps", bank)])
                for half in range(2):
                    c = 2 * t + half
                    pos = self.chain_pos(d, c)
                    P.op("pe", lambda e, j=j, c=c, pos=pos, half=half, bank=bank: e.matmul(
                        self.ps[bank][:, j * 128 + half * 64:j * 128 + half * 64 + 64], lhsT=S[:, pos, :], rhs=Qi[:, c * 64:(c + 1) * 64],
                        start=False, stop=(half == 1)),
                        reads=[(u, "S"), (u, "S0"), (u, "Qi")], writes=[("ps", bank)])
            c0_, c1_ = q * 512, q * 512 + nt_ * 128
            if d == 0:
                P.op("act", lambda e, c0_=c0_, c1_=c1_, nt_=nt_, bank=bank: e.activation(out=oT[:, c0_:c1_], in_=self.ps[bank][:, 0:nt_ * 128], func=AF.Copy),
                     reads=[("ps", bank)], writes=[(u0, "oT", q)])
            else:
                P.op("dve", lambda e, c0_=c0_, c1_=c1_, nt_=nt_, bank=bank: e.tensor_tensor(out=oT[:, c0_:c1_], in0=self.ps[bank][:, 0:nt_ * 128],
                                                                                      in1=oT[:, c0_:c1_], op=ALU.add),
                     reads=[("ps", bank), (u0, "oT", q)], writes=[(u0, "oT", q)])
        A.pop()
        P.barrier()

    def head_out(self, u, oT, gate_t, gate_tok, g_ap, dst_d, dtok):
        P, A = self.P, self.A
        A.push()
        sq = [A.tile([128, 512], F32R, "hsq") for _ in range(2)]
        rs = [A.tile([128, 512], F32, "hrs") for _ in range(2)]
        yo = [A.tile([128, 512], BF16, "hyo") for _ in range(2)]
        for gi, (t0, G) in enumerate(self.TG):
            s = gi % 2
            bank = self.nb()
            P.op("act", lambda e, s=s, t0=t0, G=G: e.activation(out=sq[s][:, 0:G], in_=oT[:, t0:t0 + G], func=AF.Square),
                 reads=[(u, "oT", q) for q in range(5)], writes=[(u, "hsq", s)])
            P.op("pe", lambda e, s=s, G=G, bank=bank: e.matmul(self.ps[bank][:, 0:G], lhsT=self.ones_r[:], rhs=sq[s][:, 0:G], start=True, stop=True),
                 reads=[(u, "hsq", s), "ones_r"], writes=[("ps", bank)])
            P.op("act", lambda e, s=s, G=G, bank=bank: e.activation(out=rs[s][:, 0:G], in_=self.ps[bank][:, 0:G], func=AF.Ln, bias=self.epsc[:, 0:1], scale=1.0 / 128),
                 reads=[("ps", bank), "epsc"], writes=[(u, "hrs", s)])
            P.op("act", lambda e, s=s, G=G: e.activation(out=rs[s][:, 0:G], in_=rs[s][:, 0:G], func=AF.Exp, scale=-0.5), reads=[(u, "hrs", s)], writes=[(u, "hrs", s)])
            P.op("dve", lambda e, s=s, t0=t0, G=G: e.scalar_tensor_tensor(out=rs[s][:, 0:G], in0=rs[s][:, 0:G], scalar=g_ap, in1=gate_t[:, t0:t0 + G],
                                                                        op0=ALU.mult, op1=ALU.mult),
                 reads=[(u, "hrs", s), (u, gate_tok, gi), "hng", "dng"], writes=[(u, "hrs", s)])
            P.op("dve", lambda e, s=s, t0=t0, G=G: e.tensor_tensor(out=yo[s][:, 0:G], in0=oT[:, t0:t0 + G], in1=rs[s][:, 0:G], op=ALU.mult),
                 reads=[(u, "hrs", s)] + [(u, "oT", q) for q in range(5)], writes=[(u, "hyo", s)])
            P.op("sp", lambda e, s=s, t0=t0, G=G: e.dma_start(out=dst_d[:, t0:t0 + G], in_=yo[s][:, 0:G]),
                 reads=[(u, "hyo", s)], writes=[(dtok, gi)], lane=f"hyo{s}")
        A.pop()

    def dn_scalars(self, l):
        P, A = self.P, self.A
        NTL = NT // 128
        self.dn_beta = A.tile([128, NTL, 16], F32, "dn_beta")
        self.dn_gc = A.tile([128, NTL, 16], F32, "dn_gc")
        self.dn_egc = A.tile([128, NTL, 16], F32, "dn_egc")
        self.dn_egl = A.tile([128, NTL, 16], F32, "dn_egl")
        self.dn_bg = A.tile([128, NTL, 16], F32, "dn_bg")
        self.dn_dl = A.tile([128, NTL, 2, 16], F32, "dn_dl")
        u = self.new("dns")
        A.push()
        wab = A.tile([128, NKC, 32], BF16, "wab")
        hg = [A.tile([128, NKC, 512], BF16, "hgs") for _ in range(2)]
        ab = A.tile([128, NTL, 32], F32, "ab")
        g = A.tile([128, NTL, 16], F32, "gdn")
        tmp = A.tile([128, NTL, 16], F32, "tdn")
        maskC = A.tile([128, 128], F32, "maskC")
        maskLo = A.tile([128, 128], F32, "maskLo")
        maskHi = A.tile([128, 128], F32, "maskHi")
        P.op("pool", lambda e: e.memset(maskC[:], 0.0), writes=[(u, "mC")])
        P.op("pool", lambda e: e.memset(maskC[0:64, 0:64], 1.0), reads=[(u, "mC")], writes=[(u, "mC")])
        P.op("pool", lambda e: e.memset(maskC[64:128, 64:128], 1.0), reads=[(u, "mC")], writes=[(u, "mC")])
        P.op("pool", lambda e: e.memset(maskLo[:], 0.0), writes=[(u, "mLo")])
        P.op("pool", lambda e: e.memset(maskLo[0:64, :], 1.0), reads=[(u, "mLo")], writes=[(u, "mLo")])
        P.op("pool", lambda e: e.memset(maskHi[:], 0.0), writes=[(u, "mHi")])
        P.op("pool", lambda e: e.memset(maskHi[64:128, :], 1.0), reads=[(u, "mHi")], writes=[(u, "mHi")])
        self.load_w_in(l, 9216, 32, wab, (u, "wab"), "wab")
        for gi, (t0, G) in enumerate(self.TG):
            self.dn_scalars_group(u, gi, t0, G, wab, hg, ab)
        abr = [(u, "ab", t) for t in range(NTL)]
        P.op("dve", lambda e: e.tensor_tensor(out=tmp[:], in0=ab[:, :, 0:16], in1=self.dtb[:, l, :].unsqueeze(1).to_broadcast([128, NTL, 16]), op=ALU.add),
             reads=abr + ["dtb"], writes=[(u, "tmp")])
        P.op("act", lambda e: e.activation(out=tmp[:], in_=tmp[:], func=AF.Exp), reads=[(u, "tmp")], writes=[(u, "tmp")])
        P.op("act", lambda e: e.activation(out=tmp[:], in_=tmp[:], func=AF.Ln, bias=1.0), reads=[(u, "tmp")], writes=[(u, "tmp")])
        P.op("dve", lambda e: e.tensor_tensor(out=g[:], in0=tmp[:], in1=self.alog[:, l, :].unsqueeze(1).to_broadcast([128, NTL, 16]), op=ALU.mult),
             reads=[(u, "tmp"), "alog"], writes=[(u, "g")])
        P.op("act", lambda e: e.activation(out=self.dn_beta[:], in_=ab[:, :, 16:32], func=AF.Sigmoid), reads=abr, writes=["dn_beta"])
        for t in range(NTL):
            bank = self.nb()
            ps = self.ps[bank]
            for (m, mt, c0, c1, o0) in ((self.maskF, "maskF", 0, 8, 0), (self.maskB, "maskB", 8, 16, 8), (maskC, (u, "mC"), 0, 16, 16),
                                        (maskLo, (u, "mLo"), 0, 16, 32), (maskHi, (u, "mHi"), 0, 16, 48)):
                P.op("pe", lambda e, m=m, c0=c0, c1=c1, o0=o0, t=t, ps=ps: e.matmul(ps[:, o0:o0 + (c1 - c0)], lhsT=m[:], rhs=g[:, t, c0:c1], start=True, stop=True),
                     reads=[mt, (u, "g")], writes=[("ps", bank)])
            P.op("dve", lambda e, t=t, ps=ps: e.tensor_copy(out=self.dn_gc[:, t, :], in_=ps[:, 0:16]), reads=[("ps", bank)], writes=[("dn_gc", t)])
            P.op("act", lambda e, t=t, ps=ps: e.activation(out=self.dn_egc[:, t, :], in_=ps[:, 0:16], func=AF.Exp), reads=[("ps", bank)], writes=[("dn_egc", t)])
            P.op("dve", lambda e, t=t, ps=ps: e.tensor_tensor(out=self.dn_egl[:, t, :], in0=ps[:, 16:32], in1=self.dn_gc[:, t, :], op=ALU.subtract),
                 reads=[("ps", bank), ("dn_gc", t)], writes=[("dn_egl", t)])
            P.op("act", lambda e, t=t: e.activation(out=self.dn_egl[:, t, :], in_=self.dn_egl[:, t, :], func=AF.Exp), reads=[("dn_egl", t)], writes=[("dn_egl", t)])
            P.op("act", lambda e, t=t, ps=ps: e.activation(out=self.dn_dl[:, t, :, :], in_=ps[:, 32:64].rearrange("p (a b) -> p a b", a=2), func=AF.Exp),
                 reads=[("ps", bank)], writes=[("dn_dl", t)])
            P.op("dve", lambda e, t=t: e.tensor_tensor(out=self.dn_bg[:, t, :], in0=self.dn_beta[:, t, :], in1=self.dn_egc[:, t, :], op=ALU.mult),
                 reads=["dn_beta", ("dn_egc", t)], writes=[("dn_bg", t)])
        A.pop()
        P.barrier()

    def dn_scalars_group(self, u, gi, t0, G, wab, hg, ab):
        P = self.P
        slot = gi % 2
        self.load_hg(hg, slot, t0, G, u)
        self.proj_tm((wab, (u, "wab")), (hg[slot], (u, "hg", slot)), G, 32, lambda bank, ti: P.op(
            "dve", lambda e: e.tensor_copy(out=ab[:, t0 // 128 + ti, :], in_=self.ps[bank][:, 0:32]),
            reads=[("ps", bank)], writes=[(u, "ab", t0 // 128 + ti)]))

    def dn_head(self, l, h):
        P, A = self.P, self.A
        u = self.new("dn")
        A.push()
        QnT = A.tile([128, NT], BF16, "QnT")
        KnT = A.tile([128, NT], BF16, "KnT")
        Ktok = A.tile([128, NT // 128, 128], BF16, "Ktok")
        Vtok = A.tile([128, NT // 128, 128], BF16, "Vtok")
        gb = A.tile([128, NT], F32, "gb")
        oT = A.tile([128, NT], F32, "oTd")
        self.dn_proj(l, h, u, QnT, KnT, Ktok, Vtok, gb)
        if "dn_stop1" in self.stages:
            self.dump_tile("dbg_QnT", QnT[:], [128, NT], [(u, "QnT", gi) for gi in range(NG)])
            self.dump_tile("dbg_KnT", KnT[:], [128, NT], [(u, "KnT", gi) for gi in range(NG)])
            self.dump_tile("dbg_Vtok", Vtok[:], [128, NT // 128, 128], [(u, "Vtok", q) for q in range(5)])
            A.pop()
            P.barrier()
            return
        for d in range(2):
            self.dn_dir(l, h, u, d, QnT, KnT, Ktok, Vtok, oT)
        if "dump_ob" in self.stages and l == 0:
            self.dump_tile(f"dbg_ob{h}", oT[:], [128, NT], [(u, "oT", q) for q in range(5)])
        self.head_out(u, oT, gb, "gb", self.dng[:, l:l + 1], self.ybT_d[h], ("yb", h))
        A.pop()
        P.barrier()

    def dn_proj(self, l, h, u, QnT, KnT, Ktok, Vtok, gb):
        P, A = self.P, self.A
        A.push()
        cols = [5120 + 128 * h, 6144 + 128 * h, 7168 + 128 * h, 8192 + 128 * h]
        raw = [A.tile([128, NT], F32, "raw") for _ in range(3)]
        cv = [A.tile([128, NT], F32, "cv") for _ in range(3)]
        VnT = A.tile([128, NT], BF16, "VnT")
        ctmp = A.tile([128, NT], F32, "ctmp")
        A.push()
        w = [A.tile([128, NKC, 128], BF16, "wd_") for _ in range(4)]
        hg = [A.tile([128, NKC, 512], BF16, "hgd") for _ in range(2)]
        for i in range(4):
            self.load_w_in(l, cols[i], 128, w[i], (u, "w", i), f"wm{i}")
        for gi, (t0, G) in enumerate(self.TG):
            self.dn_proj_group(u, gi, t0, G, w, hg, raw, gb)
        A.pop()
        P.barrier()
        sq = [A.tile([128, 512], F32R, "dsq") for _ in range(2)]
        rn = [A.tile([128, 512], F32, "drn") for _ in range(2)]
        for i in range(3):
            self.dn_conv(l, u, i, i * 8 + h, raw[i], cv[i], ctmp)
        for i in range(2):
            dst = QnT if i == 0 else KnT
            nm = "QnT" if i == 0 else "KnT"
            scl = 128.0 ** -0.5 if i == 0 else 1.0
            for gi, (t0, G) in enumerate(self.TG):
                self.dn_l2(u, i, gi, t0, G, cv[i], sq, rn, dst, nm, scl)
        P.op("act", lambda e: e.activation(out=VnT[:], in_=cv[2][:], func=AF.Copy), reads=[(u, "cv", 2)], writes=[(u, "VnT")])
        for (src, srd, dst, nm) in ((KnT, [(u, "KnT", gi) for gi in range(NG)], Ktok, "Ktok"), (VnT, [(u, "VnT")], Vtok, "Vtok")):
            for q in range(5):
                self.tr_group(u, src, srd, dst, nm, q)
        A.pop()
        P.barrier()

    def tr_group(self, u, src, srd, dst, nm, q):
        P = self.P
        bank = self.nb()
        nt_ = 4 if q < 4 else 2
        psb = self.ps[bank][:].bitcast(BF16)
        for j in range(nt_):
            t = q * 4 + j
            P.op("pe", lambda e, j=j, t=t: e.transpose(psb[:, j * 128:(j + 1) * 128], src[:, t * 128:(t + 1) * 128], self.ident_b[:]),
                 reads=srd + ["ident_b"], writes=[("ps", bank)])
        P.op("act", lambda e: e.activation(out=dst[:, q * 4:q * 4 + nt_, :], in_=psb[:, 0:nt_ * 128].rearrange("p (a b) -> p a b", b=128), func=AF.Copy),
             reads=[("ps", bank)], writes=[(u, nm, q)])

    def dn_proj_group(self, u, gi, t0, G, w, hg, raw, gb):
        P = self.P
        slot = gi % 2
        self.load_hg(hg, slot, t0, G, u)
        hgs = (hg[slot], (u, "hg", slot))
        for i in range(3):
            if i % 2 == 0:
                self.proj_fm((w[i], (u, "w", i)), hgs, G, lambda bank, i=i: P.op(
                    "act", lambda e: e.activation(out=raw[i][:, t0:t0 + G], in_=self.ps[bank][:, 0:G], func=AF.Copy),
                    reads=[("ps", bank)], writes=[(u, "raw", i, gi)]))
            else:
                self.proj_fm((w[i], (u, "w", i)), hgs, G, lambda bank, i=i: P.op(
                    "dve", lambda e: e.tensor_copy(out=raw[i][:, t0:t0 + G], in_=self.ps[bank][:, 0:G]),
                    reads=[("ps", bank)], writes=[(u, "raw", i, gi)]))
        self.proj_fm((w[3], (u, "w", 3)), hgs, G, lambda bank: P.op(
            "act", lambda e: e.activation(out=gb[:, t0:t0 + G], in_=self.ps[bank][:, 0:G], func=AF.Silu),
            reads=[("ps", bank)], writes=[(u, "gb", gi)]))

    def dn_conv(self, l, u, i, ci, raw, cv, ctmp):
        P = self.P
        rr = [(u, "raw", i, gi) for gi in range(NG)]
        wcol = lambda j: self.convT[:, l, j * 24 + ci:j * 24 + ci + 1]
        P.op("act", lambda e: e.activation(out=cv[:], in_=raw[:], func=AF.Identity, scale=wcol(2)),
             reads=rr + [("convT", l)], writes=[(u, "cv", i)])
        segs = [(raw[:, 0:NCTX].rearrange("p (r w) -> p r w", w=NCTX), cv[:, 0:NCTX].rearrange("p (r w) -> p r w", w=NCTX), NCTX),
                (raw[:, NCTX:NT].rearrange("p (r w) -> p r w", w=64), cv[:, NCTX:NT].rearrange("p (r w) -> p r w", w=64), 64)]
        for j in (0, 1, 3, 4):
            o = j - 2
            for (r3, c3, W) in segs:
                d0, d1 = max(0, -o), W - max(0, o)
                P.op("dve", lambda e, r3=r3, c3=c3, d0=d0, d1=d1, o=o, j=j: e.scalar_tensor_tensor(
                    out=c3[:, :, d0:d1], in0=r3[:, :, d0 + o:d1 + o], scalar=wcol(j), in1=c3[:, :, d0:d1], op0=ALU.mult, op1=ALU.add),
                    reads=[(u, "cv", i)], writes=[(u, "cv", i)])
        P.op("act", lambda e: e.activation(out=cv[:], in_=cv[:], func=AF.Silu), reads=[(u, "cv", i)], writes=[(u, "cv", i)])

    def dn_l2(self, u, i, gi, t0, G, cv, sq, rn, dst, nm, scl):
        P = self.P
        s = gi % 2
        bank = self.nb()
        P.op("act", lambda e: e.activation(out=sq[s][:, 0:G], in_=cv[:, t0:t0 + G], func=AF.Square), reads=[(u, "cv", i)], writes=[(u, "dsq", s)])
        P.op("pe", lambda e: e.matmul(self.ps[bank][:, 0:G], lhsT=self.ones_r[:], rhs=sq[s][:, 0:G], start=True, stop=True),
             reads=[(u, "dsq", s), "ones_r"], writes=[("ps", bank)])
        P.op("act", lambda e: e.activation(out=rn[s][:, 0:G], in_=self.ps[bank][:, 0:G], func=AF.Ln, bias=self.epsc[:, 0:1], scale=1.0),
             reads=[("ps", bank), "epsc"], writes=[(u, "drn", s)])
        P.op("act", lambda e: e.activation(out=rn[s][:, 0:G], in_=rn[s][:, 0:G], func=AF.Exp, scale=-0.5), reads=[(u, "drn", s)], writes=[(u, "drn", s)])
        P.op("dve", lambda e: e.scalar_tensor_tensor(out=dst[:, t0:t0 + G], in0=cv[:, t0:t0 + G], scalar=scl, in1=rn[s][:, 0:G], op0=ALU.mult, op1=ALU.mult),
             reads=[(u, "drn", s), (u, "cv", i)], writes=[(u, nm, gi)])

    def dn_dir(self, l, h, u0, d, QnT, KnT, Ktok, Vtok, oT):
        P, A = self.P, self.A
        u = self.new("dd")
        NTL = NT // 128
        NC = NT // 64
        col = d * 8 + h
        A.push()
        kbg = A.tile([128, NTL, 128], BF16, "kbg")
        vb = A.tile([128, NTL, 128], BF16, "vb")
        kdz = A.tile([128, NTL, 2, 128], BF16, "kdz")
        Lc = [A.tile([128, NTL, 128], BF16, "Lc") for _ in range(2)]
        Nc = [A.tile([128, NTL, 128], BF16, "Nc") for _ in range(2)]
        Rc = [A.tile([128, NTL, 128], BF16, "Rc") for _ in range(2)]
        qkT = A.tile([128, NTL, 128], BF16, "qkT")
        qgT = A.tile([128, NT], BF16, "qgT")
        nwT = A.tile([128, NT], BF16, "nwT")
        usb = A.tile([128, NTL, 128], F32, "usb")
        vn = A.tile([128, NC, 128], BF16, "vn")
        Sall = A.tile([128, NC, 128], BF16, "Sall")
        S32 = A.tile([128, 128], F32, "S32")
        dg3 = [A.tile([128, 384], F32, "dg3") for _ in range(2)]
        tA = [A.tile([128, 128], F32, "tA") for _ in range(2)]
        tB = [A.tile([128, 128], F32, "tB") for _ in range(2)]
        WA = [A.tile([128, 128], F32, "WA") for _ in range(2)]
        WBs = [A.tile([128, 128], F32, "WBs") for _ in range(2)]
        WBi = [A.tile([128, 128], F32, "WBi") for _ in range(2)]
        t2 = [A.tile([128, 128], F32, "t2") for _ in range(2)]
        mAs = self.bigP[0] if d == 0 else self.bigP[1]
        mBs = self.bigN[1] if d == 0 else self.bigN[0]
        mBi = self.bigN[2] if d == 0 else self.bigN[3]
        Kr = [(u0, "Ktok", q) for q in range(5)]
        Vr = [(u0, "Vtok", q) for q in range(5)]
        KnR = [(u0, "KnT", gi) for gi in range(NG)]
        QnR = [(u0, "QnT", gi) for gi in range(NG)]
        P.op("pool", lambda e: e.memset(kdz[:].rearrange("p a b c -> p (a b c)"), 0.0), writes=[(u, "kdz0")])
        P.op("pool", lambda e: e.memset(Sall[:, 0, :], 0.0), writes=[(u, "Sall", 0)])
        P.op("pool", lambda e: e.memset(S32[:], 0.0), writes=[(u, "S32")])
        for t in range(NTL):
            self.dn_prep_tile(u, u0, d, t, col, kbg, vb, kdz, Lc[0], Nc[0], Rc[0], qkT, qgT, dg3[t % 2], tA[t % 2], tB[t % 2], WA[t % 2], WBs[t % 2],
                              WBi[t % 2], t2[t % 2], mAs, mBs, mBi, Ktok, Vtok, KnT, QnT, Kr, Vr, KnR, QnR)
        cur = 0
        for lev in range(5):
            nxt = 1 - cur
            for q in range(5):
                self.dn_neumann(u, lev, q, Lc[cur], Nc[cur], Rc[cur], Lc[nxt], Nc[nxt], Rc[nxt])
            cur = nxt
        TT = Rc[cur]
        for q in range(5):
            self.dn_uw(u, q, TT, vb, kbg, usb, nwT)
        if "dn_stop2" in self.stages:
            self.dump_tile("dbg_TT", TT[:], [128, NTL, 128], [(u, "R", 5, q) for q in range(5)])
            self.dump_tile("dbg_usb", usb[:], [128, NTL, 128], [(u, "usb", q) for q in range(5)])
            self.dump_tile("dbg_nwT", nwT[:], [128, NT], [(u, "nwT", q) for q in range(5)])
            self.dump_tile("dbg_qkT", qkT[:], [128, NTL, 128], [(u, "qkT", t) for t in range(NTL)])
        order = sorted(range(NC), key=lambda c: self.chain_pos(d, c))
        for pos, c in enumerate(order):
            self.dn_chain_step(u, d, pos, c, col, nwT, usb, vn, kdz, Sall, S32)
        for q in range(5):
            self.dn_out(u, u0, d, q, Sall, qgT, vn, qkT, oT)
        A.pop()
        P.barrier()

    def dn_prep_tile(self, u, u0, d, t, col, kbg, vb, kdz, L0, N0, R0, qkT, qgT, dg3, tA, tB, WA, WBs, WBi, t2, mAs, mBs, mBi,
                     Ktok, Vtok, KnT, QnT, Kr, Vr, KnR, QnR):
        P = self.P
        s = t % 2
        beta = self.dn_beta[:, t, col:col + 1]
        gc = self.dn_gc[:, t, col:col + 1]
        egc = self.dn_egc[:, t, col:col + 1]
        bg = self.dn_bg[:, t, col:col + 1]
        sc_r = ["dn_beta", ("dn_gc", t), ("dn_egc", t), ("dn_egl", t), ("dn_bg", t)]
        P.op("act", lambda e: e.activation(out=kbg[:, t, :], in_=Ktok[:, t, :], func=AF.Identity, scale=bg),
             reads=Kr + sc_r, writes=[(u, "kbg", t)])
        P.op("act", lambda e: e.activation(out=vb[:, t, :], in_=Vtok[:, t, :], func=AF.Identity, scale=beta),
             reads=Vr + sc_r, writes=[(u, "vb", t)])
        for hf in range(2):
            r0, r1 = hf * 64, hf * 64 + 64
            P.op("act", lambda e, hf=hf, r0=r0, r1=r1: e.activation(out=kdz[r0:r1, t, hf, :], in_=Ktok[r0:r1, t, :], func=AF.Identity,
                                                                 scale=self.dn_egl[r0:r1, t, col:col + 1]),
                 reads=Kr + sc_r + [(u, "kdz0")], writes=[(u, "kdz", t, hf)])
        for i, sc in enumerate((gc, beta, egc)):
            if i == 1:
                P.op("act", lambda e, i=i, sc=sc: e.activation(out=dg3[:, i * 128:(i + 1) * 128], in_=self.ident[:], func=AF.Identity, scale=sc),
                     reads=["ident"] + sc_r, writes=[(u, "dg3", s, i)])
            else:
                P.op("dve", lambda e, i=i, sc=sc: e.tensor_scalar(out=dg3[:, i * 128:(i + 1) * 128], in0=self.ident[:], scalar1=sc, scalar2=None, op0=ALU.mult),
                     reads=["ident"] + sc_r, writes=[(u, "dg3", s, i)])
        bR = self.nb()
        P.op("pe", lambda e: e.matmul(self.ps[bR][:, 0:384], lhsT=self.ones_f[:], rhs=dg3[:, :], start=True, stop=True),
             reads=[(u, "dg3", s, i) for i in range(3)] + ["ones_f"], writes=[("ps", bR)])
        RB = self.ps[bR][:, 0:128]
        RBb = self.ps[bR][:, 128:256]
        RBe = self.ps[bR][:, 256:384]
        bG = self.nb()
        ts = slice(t * 128, (t + 1) * 128)
        P.op("pe", lambda e: e.matmul(self.ps[bG][:, 0:128], lhsT=KnT[:, ts], rhs=KnT[:, ts], start=True, stop=True),
             reads=KnR, writes=[("ps", bG)])
        P.op("pe", lambda e: e.matmul(self.ps[bG][:, 128:256], lhsT=KnT[:, ts], rhs=QnT[:, ts], start=True, stop=True),
             reads=KnR + QnR, writes=[("ps", bG)])
        Gm = self.ps[bG][:, 0:128]
        QK = self.ps[bG][:, 128:256]
        P.op("dve", lambda e: e.scalar_tensor_tensor(out=WA[:], in0=RB, scalar=gc, in1=mAs[:], op0=ALU.subtract, op1=ALU.max),
             reads=[("ps", bR), "bigm"] + sc_r, writes=[(u, "WA", s)])
        P.op("act", lambda e: e.activation(out=WA[:], in_=WA[:], func=AF.Exp, scale=-1.0), reads=[(u, "WA", s)], writes=[(u, "WA", s)])
        P.op("dve", lambda e: e.scalar_tensor_tensor(out=WBs[:], in0=RB, scalar=gc, in1=mBs[:], op0=ALU.subtract, op1=ALU.min),
             reads=[("ps", bR), "bigm"] + sc_r, writes=[(u, "WBs", s)])
        P.op("act", lambda e: e.activation(out=WBs[:], in_=WBs[:], func=AF.Exp), reads=[(u, "WBs", s)], writes=[(u, "WBs", s)])
        P.op("dve", lambda e: e.scalar_tensor_tensor(out=WBi[:], in0=RB, scalar=gc, in1=mBi[:], op0=ALU.subtract, op1=ALU.min),
             reads=[("ps", bR), "bigm"] + sc_r, writes=[(u, "WBi", s)])
        P.op("act", lambda e: e.activation(out=WBi[:], in_=WBi[:], func=AF.Exp), reads=[(u, "WBi", s)], writes=[(u, "WBi", s)])
        P.op("dve", lambda e: e.scalar_tensor_tensor(out=L0[:, t, :], in0=Gm, scalar=beta, in1=WA[:], op0=ALU.mult, op1=ALU.mult),
             reads=[("ps", bG), (u, "WA", s)] + sc_r, writes=[(u, "L", 0, t)])
        P.op("dve", lambda e: e.tensor_tensor(out=t2[:], in0=RBb, in1=WBs[:], op=ALU.mult), reads=[("ps", bR), (u, "WBs", s)], writes=[(u, "t2", s)])
        P.op("dve", lambda e: e.tensor_tensor(out=N0[:, t, :], in0=Gm, in1=t2[:], op=ALU.mult), reads=[("ps", bG), (u, "t2", s)], writes=[(u, "N", 0, t)])
        P.op("dve", lambda e: e.scalar_tensor_tensor(out=R0[:, t, :], in0=N0[:, t, :], scalar=-1.0, in1=self.ident[:], op0=ALU.mult, op1=ALU.add),
             reads=[(u, "N", 0, t), "ident"], writes=[(u, "R", 0, t)])
        P.op("dve", lambda e: e.tensor_tensor(out=qkT[:, t, :], in0=QK, in1=WBi[:], op=ALU.mult), reads=[("ps", bG), (u, "WBi", s)], writes=[(u, "qkT", t)])
        P.op("dve", lambda e: e.tensor_tensor(out=qgT[:, ts], in0=RBe, in1=QnT[:, ts], op=ALU.mult), reads=[("ps", bR)] + QnR, writes=[(u, "qgT", t)])

    def dn_neumann(self, u, lev, q, Lc, Nc, Rc, Ln, Nn, Rn):
        P = self.P
        nt_ = 4 if q < 4 else 2
        tiles = [q * 4 + j for j in range(nt_)]
        last = lev == 4

        def rd(nm, t):
            return [(u, nm, lev, t)] if lev == 0 else [(u, nm, lev, t // 4)]
        bL = self.nb()
        for j, t in enumerate(tiles):
            P.op("pe", lambda e, j=j, t=t: e.matmul(self.ps[bL][:, j * 128:(j + 1) * 128], lhsT=Nc[:, t, :], rhs=Lc[:, t, :], start=True, stop=True),
                 reads=rd("N", t) + rd("L", t), writes=[("ps", bL)])
        P.op("act", lambda e: e.activation(out=Ln[:, q * 4:q * 4 + nt_, :], in_=self.ps[bL][:, 0:nt_ * 128].rearrange("p (a b) -> p a b", b=128), func=AF.Copy),
             reads=[("ps", bL)], writes=[(u, "L", lev + 1, q)])
        if not last:
            bN = self.nb()
            for j, t in enumerate(tiles):
                P.op("pe", lambda e, j=j, t=t: e.matmul(self.ps[bN][:, j * 128:(j + 1) * 128], lhsT=Lc[:, t, :], rhs=Nc[:, t, :], start=True, stop=True),
                     reads=rd("N", t) + rd("L", t), writes=[("ps", bN)])
            P.op("dve", lambda e: e.tensor_copy(out=Nn[:, q * 4:q * 4 + nt_, :], in_=self.ps[bN][:, 0:nt_ * 128].rearrange("p (a b) -> p a b", b=128)),
                 reads=[("ps", bN)], writes=[(u, "N", lev + 1, q)])
        bR = self.nb()
        for j, t in enumerate(tiles):
            P.op("pe", lambda e, j=j, t=t: e.matmul(self.ps[bR][:, j * 128:(j + 1) * 128], lhsT=self.ident_b[:], rhs=Rc[:, t, :], start=True, stop=False),
                 reads=rd("R", t) + ["ident_b"], writes=[("ps", bR)])
            P.op("pe", lambda e, j=j, t=t: e.matmul(self.ps[bR][:, j * 128:(j + 1) * 128], lhsT=Ln[:, t, :], rhs=Rc[:, t, :], start=False, stop=True),
                 reads=rd("R", t) + [(u, "L", lev + 1, q)], writes=[("ps", bR)])
        eng = "dve" if q % 2 == 0 else "act"
        if eng == "dve":
            P.op("dve", lambda e: e.tensor_copy(out=Rn[:, q * 4:q * 4 + nt_, :], in_=self.ps[bR][:, 0:nt_ * 128].rearrange("p (a b) -> p a b", b=128)),
                 reads=[("ps", bR)], writes=[(u, "R", lev + 1, q)])
        else:
            P.op("act", lambda e: e.activation(out=Rn[:, q * 4:q * 4 + nt_, :], in_=self.ps[bR][:, 0:nt_ * 128].rearrange("p (a b) -> p a b", b=128), func=AF.Copy),
                 reads=[("ps", bR)], writes=[(u, "R", lev + 1, q)])

    def dn_uw(self, u, q, TT, vb, kbg, usb, nwT):
        P = self.P
        nt_ = 4 if q < 4 else 2
        tiles = [q * 4 + j for j in range(nt_)]
        bU = self.nb()
        for j, t in enumerate(tiles):
            P.op("pe", lambda e, j=j, t=t: e.matmul(self.ps[bU][:, j * 128:(j + 1) * 128], lhsT=TT[:, t, :], rhs=vb[:, t, :], start=True, stop=True),
                 reads=[(u, "R", 5, q), (u, "vb", t)], writes=[("ps", bU)])
        P.op("dve", lambda e: e.tensor_copy(out=usb[:, q * 4:q * 4 + nt_, :], in_=self.ps[bU][:, 0:nt_ * 128].rearrange("p (a b) -> p a b", b=128)),
             reads=[("ps", bU)], writes=[(u, "usb", q)])
        bW = self.nb()
        for j, t in enumerate(tiles):
            P.op("pe", lambda e, j=j, t=t: e.matmul(self.ps[bW][:, j * 128:(j + 1) * 128], lhsT=kbg[:, t, :], rhs=TT[:, t, :], start=True, stop=True),
                 reads=[(u, "R", 5, q), (u, "kbg", t)], writes=[("ps", bW)])
        P.op("act", lambda e: e.activation(out=nwT[:, q * 512:q * 512 + nt_ * 128], in_=self.ps[bW][:, 0:nt_ * 128], func=AF.Copy, scale=-1.0),
             reads=[("ps", bW)], writes=[(u, "nwT", q)])

    def dn_chain_step(self, u, d, pos, c, col, nwT, usb, vn, kdz, Sall, S32):
        P = self.P
        t, hf = c // 2, c % 2
        b1 = self.nb()
        P.op("pe", lambda e: e.matmul(self.ps[b1][:, 0:128], lhsT=nwT[:, t * 128:(t + 1) * 128], rhs=Sall[:, pos, :], start=True, stop=True),
             reads=[(u, "nwT", t // 4), (u, "Sall", pos)], writes=[("ps", b1)])
        P.op("dve", lambda e: e.tensor_tensor(out=vn[:, c, :], in0=self.ps[b1][:, 0:128], in1=usb[:, t, :], op=ALU.add),
             reads=[("ps", b1), (u, "usb", t // 4)], writes=[(u, "vn", c)])
        b2 = self.nb()
        P.op("pe", lambda e: e.matmul(self.ps[b2][:, 0:128], lhsT=kdz[:, t, hf, :], rhs=vn[:, c, :], start=True, stop=True),
             reads=[(u, "kdz", t, hf), (u, "kdz0"), (u, "vn", c)], writes=[("ps", b2)])
        if pos + 1 < NT // 64:
            P.op("dve", lambda e: e.scalar_tensor_tensor(out=Sall[:, pos + 1, :], in0=S32[:], scalar=self.dn_dl[:, t, hf, col:col + 1], in1=self.ps[b2][:, 0:128],
                                                         op0=ALU.mult, op1=ALU.add),
                 reads=[("ps", b2), (u, "S32"), ("dn_dl", t)], writes=[(u, "Sall", pos + 1)])
            P.op("dve", lambda e: e.scalar_tensor_tensor(out=S32[:], in0=S32[:], scalar=self.dn_dl[:, t, hf, col:col + 1], in1=self.ps[b2][:, 0:128],
                                                         op0=ALU.mult, op1=ALU.add),
                 reads=[("ps", b2), (u, "S32"), ("dn_dl", t), (u, "Sall", pos + 1)], writes=[(u, "S32")])

    def dn_out(self, u, u0, d, q, Sall, qgT, vn, qkT, oT):
        P = self.P
        nt_ = 4 if q < 4 else 2
        bank = self.nb()
        for j in range(nt_):
            t = q * 4 + j
            for hf in range(2):
                c = 2 * t + hf
                pos = self.chain_pos(d, c)
                cs = slice(j * 128 + hf * 64, j * 128 + hf * 64 + 64)
                P.op("pe", lambda e, c=c, pos=pos, cs=cs: e.matmul(self.ps[bank][:, cs], lhsT=Sall[:, pos, :], rhs=qgT[:, c * 64:(c + 1) * 64], start=True, stop=False),
                     reads=[(u, "Sall", pos), (u, "qgT", t)], writes=[("ps", bank)])
                P.op("pe", lambda e, c=c, t=t, hf=hf, cs=cs: e.matmul(self.ps[bank][:, cs], lhsT=vn[:, c, :], rhs=qkT[:, t, hf * 64:hf * 64 + 64], start=False, stop=True),
                     reads=[(u, "vn", c), (u, "qkT", t)], writes=[("ps", bank)])
        c0_, c1_ = q * 512, q * 512 + nt_ * 128
        if d == 0:
            P.op("act", lambda e: e.activation(out=oT[:, c0_:c1_], in_=self.ps[bank][:, 0:nt_ * 128], func=AF.Copy),
                 reads=[("ps", bank)], writes=[(u0, "oT", q)])
        else:
            P.op("dve", lambda e: e.tensor_tensor(out=oT[:, c0_:c1_], in0=self.ps[bank][:, 0:nt_ * 128], in1=oT[:, c0_:c1_], op=ALU.add),
                 reads=[("ps", bank), (u0, "oT", q)], writes=[(u0, "oT", q)])

    def merge(self, l):
        A = self.A
        A.push()
        wa = [A.tile([128, 8, 128], BF16, "wa") for _ in range(2)]
        wb = [A.tile([128, 8, 128], BF16, "wb") for _ in range(2)]
        wga = [A.tile([128, NKC, 128], BF16, "wga") for _ in range(2)]
        wgb = [A.tile([128, NKC, 128], BF16, "wgb") for _ in range(2)]
        wo = [A.tile([128, NKC, 128], BF16, "wo") for _ in range(2)]
        for (t0, G, v) in self.groups():
            self.merge_group(l, t0, G, v, wa, wb, wga, wgb, wo)
        A.pop()

    def merge_group(self, l, t0, G, v, wa, wb, wga, wgb, wo):
        P, A = self.P, self.A
        u = self.new("mg")
        nh = (G + 511) // 512
        hw = G // nh
        A.push()
        yaT = A.tile([128, 8, G], BF16, "yaT")
        ybT = A.tile([128, 8, G], BF16, "ybT")
        hT = A.tile([128, NKC, G], BF16, "hTg")
        yT = A.tile([128, NKC, G], BF16, "yT")
        sga = [A.tile([128, hw], F32, "sga") for _ in range(2)]
        sgb = [A.tile([128, hw], F32, "sgb") for _ in range(2)]
        xc = [A.tile([128, G], F32, "xm") for _ in range(3)]
        ya1 = A.tile([128, 8, G], BF16, "ya1")
        yb1 = A.tile([128, 8, G], BF16, "yb1")
        for (ab, dst0, dst1, nm) in ((0, yaT, ya1, "yaT"), (1, ybT, yb1, "ybT")):
            for r in range(2):
                for hp in range(2):
                    k = ab * 2 + hp
                    h0 = r * 4 + hp * 2
                    P.op("sp", lambda e, dst0=dst0, r=r, k=k, h0=h0: e.dma_start(
                        out=dst0[:, h0:h0 + 2, :], in_=self.y_all_d[k, r][:, :, t0:t0 + G].rearrange("h p n -> p h n")),
                        reads=[("yall", k)], writes=[(u, nm, 0, r, hp)], lane=f"mg{ab}a")
                    P.op("sp", lambda e, dst1=dst1, r=r, k=k, h0=h0: e.dma_start(
                        out=dst1[:, h0:h0 + 2, :], in_=self.y_all_d[k, r][:, :, NTO + t0:NTO + t0 + G].rearrange("h p n -> p h n")),
                        reads=[("yall", k)], writes=[(u, nm, 1, r, hp)], lane=f"mg{ab}b")
            P.op("dve", lambda e, dst0=dst0: e.tensor_scalar(out=dst0[:], in0=dst0[:], scalar1=self.selT[:, 0:1], scalar2=None, op0=ALU.mult),
                 reads=[(u, nm, 0, r, hp) for r in range(2) for hp in range(2)] + ["selT"], writes=[(u, nm)])
            P.op("dve", lambda e, dst0=dst0, dst1=dst1: e.scalar_tensor_tensor(out=dst0[:], in0=dst1[:], scalar=self.selT[:, 1:2], in1=dst0[:],
                                                                               op0=ALU.mult, op1=ALU.add),
                 reads=[(u, nm)] + [(u, nm, 1, r, hp) for r in range(2) for hp in range(2)] + ["selT"], writes=[(u, nm)])
        P.op("sp", lambda e: e.dma_start(out=hT[:], in_=self.hT_d[:, :, t0:t0 + G].rearrange("c p n -> p c n")),
             reads=[("hTd", t) for t in range(t0 // 128, (t0 + G) // 128)], writes=[(u, "hT")], lane="mgh")
        wav = self.inp["w_branch_a"][l].rearrange("(hd p) n -> p hd n", p=128)
        wbv = self.inp["w_branch_b"][l].rearrange("(hd p) n -> p hd n", p=128)
        wiv = self.inp["w_in"][l].rearrange("(kc p) n -> p kc n", p=128)
        wov = self.inp["w_out"][l].rearrange("(kc p) n -> p kc n", p=128)

        def load1(n):
            s = n % 2
            cs = slice(n * 128, (n + 1) * 128)
            P.op("pool", lambda e: e.dma_start(out=wa[s][:], in_=wav[:, :, cs]), writes=[("wa", s)], lane=f"wa{s}")
            P.op("pool", lambda e: e.dma_start(out=wb[s][:], in_=wbv[:, :, cs]), writes=[("wb", s)], lane=f"wb{s}")
            P.op("pool", lambda e: e.dma_start(out=wga[s][:], in_=wiv[:, :, 9248 + n * 128:9248 + (n + 1) * 128]), writes=[("wga", s)], lane=f"wga{s}")
            P.op("pool", lambda e: e.dma_start(out=wgb[s][:], in_=wiv[:, :, 11296 + n * 128:11296 + (n + 1) * 128]), writes=[("wgb", s)], lane=f"wgb{s}")

        def load2(n):
            s = n % 2
            P.op("pool", lambda e: e.dma_start(out=wo[s][:], in_=wov[:, :, n * 128:(n + 1) * 128]), writes=[("wo", s)], lane=f"wo{s}")

        def acc(wt, wtok, nk, rhs_t, rtok, h):
            bank = self.nb()
            for k in range(nk):
                P.op("pe", lambda e, k=k: e.matmul(self.ps[bank][:, 0:hw], lhsT=wt[:, k, :], rhs=rhs_t[:, k, h * hw:(h + 1) * hw],
                                                   start=(k == 0), stop=(k == nk - 1)),
                     reads=[wtok] + rtok, writes=[("ps", bank)])
            return bank

        load1(0)
        for n in range(NKC):
            if n + 1 < NKC:
                load1(n + 1)
            else:
                load2(0)
            s = n % 2
            for h in range(nh):
                i = h % 2
                bga = acc(wga[s], ("wga", s), NKC, hT, [(u, "hT")], h)
                P.op("act", lambda e, i=i, bga=bga: e.activation(out=sga[i][:], in_=self.ps[bga][:, 0:hw], func=AF.Sigmoid),
                     reads=[("ps", bga)], writes=[(u, "sga", i)])
                bgb = acc(wgb[s], ("wgb", s), NKC, hT, [(u, "hT")], h)
                P.op("act", lambda e, i=i, bgb=bgb: e.activation(out=sgb[i][:], in_=self.ps[bgb][:, 0:hw], func=AF.Sigmoid),
                     reads=[("ps", bgb)], writes=[(u, "sgb", i)])
                ba = acc(wa[s], ("wa", s), 8, yaT, [(u, "yaT")], h)
                P.op("dve", lambda e, i=i, ba=ba: e.tensor_tensor(out=sga[i][:], in0=sga[i][:], in1=self.ps[ba][:, 0:hw], op=ALU.mult),
                     reads=[("ps", ba), (u, "sga", i)], writes=[(u, "sga", i)])
                bb = acc(wb[s], ("wb", s), 8, ybT, [(u, "ybT")], h)
                P.op("dve", lambda e, i=i, bb=bb: e.tensor_tensor(out=sgb[i][:], in0=sgb[i][:], in1=self.ps[bb][:, 0:hw], op=ALU.mult),
                     reads=[("ps", bb), (u, "sgb", i)], writes=[(u, "sgb", i)])
                P.op("pool", lambda e, i=i, n=n, h=h: e.tensor_tensor(out=yT[:, n, h * hw:(h + 1) * hw], in0=sga[i][:], in1=sgb[i][:], op=ALU.add),
                     reads=[(u, "sga", i), (u, "sgb", i)], writes=[(u, "yT", n, h)])
        for n in range(NKC):
            if n + 1 < NKC:
                load2(n + 1)
            s = n % 2
            xs = n % 3
            P.op("sp", lambda e, xs=xs, n=n: e.dma_start(out=xc[xs][:], in_=self.xT_d[n, :, t0:t0 + G]),
                 reads=[("xT", n, t) for t in range(t0 // 128, (t0 + G) // 128)], writes=[(u, "xm", xs)], lane=f"xe{xs}")
            for h in range(nh):
                bo = acc(wo[s], ("wo", s), NKC, yT, [(u, "yT", k, h) for k in range(NKC)], h)
                P.op("dve", lambda e, xs=xs, h=h, bo=bo, n=n: e.scalar_tensor_tensor(
                    out=xc[xs][:, h * hw:(h + 1) * hw], in0=self.ps[bo][:, 0:hw], scalar=self.gate[:, l, 1, v, n:n + 1],
                    in1=xc[xs][:, h * hw:(h + 1) * hw], op0=ALU.mult, op1=ALU.add),
                    reads=[("ps", bo), (u, "xm", xs)], writes=[(u, "xm", xs)])
            P.op("sp", lambda e, xs=xs, n=n: e.dma_start(out=self.xT_d[n, :, t0:t0 + G], in_=xc[xs][:]),
                 reads=[(u, "xm", xs)], writes=[("xT", n, t) for t in range(t0 // 128, (t0 + G) // 128)], lane=f"xs{xs}")
        A.pop()
        P.barrier()

    def final(self):
        P, A = self.P, self.A
        A.push()
        hT = A.tile([128, NKC, 1024], F32, "hTf")
        to = [A.tile([128, D], F32, "to") for _ in range(2)]
        for gi, (t0, G, v) in enumerate(self.groups()):
            u = self.new("fin")
            self.prologue(t0, G, self.gT[:, 96:112], None, hT, (u, "hT"))
            for t in range(G // 128):
                s = t % 2
                for q in range(4):
                    bank = (t % 2) * 4 + q
                    for j in range(4):
                        c = q * 4 + j
                        P.op("pe", lambda e, c=c, bank=bank, j=j, t=t: e.transpose(
                            self.ps[bank][:, j * 128:(j + 1) * 128], hT[:, c, t * 128:(t + 1) * 128], self.ident[:]),
                            reads=[((u, "hT"), c), "ident"], writes=[("ps", bank)])
                    if q % 2 == 0:
                        P.op("act", lambda e, s=s, q=q, bank=bank: e.activation(out=to[s][:, q * 512:(q + 1) * 512], in_=self.ps[bank][:], func=AF.Copy),
                             reads=[("ps", bank)], writes=[("to", s, q)])
                    else:
                        P.op("dve", lambda e, s=s, q=q, bank=bank: e.tensor_copy(out=to[s][:, q * 512:(q + 1) * 512], in_=self.ps[bank][:]),
                             reads=[("ps", bank)], writes=[("to", s, q)])
                row = t0 + t * 128
                P.op("sp", lambda e, s=s, row=row: e.dma_start(out=self.out[row:row + 128, :], in_=to[s][:]),
                     reads=[("to", s, q) for q in range(4)], lane=f"to{s}")
        A.pop()
        P.barrier()

    def dump_xT(self, name):
        d = self.nc.dram_tensor(name, [NKC, 128, NTO], F32, kind="ExternalOutput").ap()
        self.dbg_out[name] = d
        for c in range(NKC):
            self.P.op("sp", lambda e, c=c: e.dma_start(out=d[c], in_=self.xT_d[c]),
                      reads=[("xT", c, t) for t in range(NTO // 128)], lane="dump")

    def dump_tile(self, name, ap, shape, reads):
        d = self.nc.dram_tensor(name, list(shape), ap.dtype, kind="ExternalOutput").ap()
        self.dbg_out[name] = d
        self.P.op("sp", lambda e: e.dma_start(out=d, in_=ap), reads=reads, lane="dump")

    def build(self):
        st = self.stages
        self.consts()
        self.P.barrier()
        self.phase_in()
        if "dump_in" in st:
            self.dump_xT("dbg_xin")
        self.phase_adaln()
        if "dump_mod" in st:
            for l in range(2):
                self.dump_tile(f"dbg_modT{l}", self.modT[l][:], [128, 144, 2], [("modT", l)])
            self.dump_tile("dbg_gT", self.gT[:], [128, 112], ["gT0", "gT1"])
            self.dump_tile("dbg_gmod", self.gmod[:], [128, 2, 3, 2, NKC], [])
            self.dump_tile("dbg_gate", self.gate[:], [128, 2, 3, 2, NKC], [])
        self.phase_small()
        for l in range(2):
            if ("ffn", l, 0) in st:
                self.ffn1p(l, 0)
            if ("dump", l, 0) in st:
                self.dump_xT(f"dbg_x_ffn1_{l}")
            if ("mix", l) in st:
                self.mixer_prep(l)
                for h in self.heads:
                    if ("hgrn", l) in st:
                        self.hgrn_head(l, h)
                if ("dn", l) in st:
                    self.A.push()
                    self.dn_scalars(l)
                    for h in self.heads:
                        self.dn_head(l, h)
                    self.A.pop()
                    self.P.barrier()
                if ("merge", l) in st:
                    for k in range(4):
                        ab, hp = k // 2, k % 2
                        self.P.op("pool", lambda e, k=k, ab=ab, hp=hp: e.collective_compute(
                            "AllGather", ALU.bypass, replica_groups=[[0, 1], [2, 3], [4, 5], [6, 7]],
                            ins=[self.y_own_d[ab, 2 * hp:2 * hp + 2].rearrange("h p n -> (h p) n")],
                            outs=[self.y_all_d[k].rearrange("r h p n -> (r h p) n")]),
                            reads=[(("ya", h), gi) for h in range(4) for gi in range(NG)] + [(("yb", h), gi) for h in range(4) for gi in range(NG)],
                            writes=[("yall", k)], lane="cc_y", step=1)
                    self.P.barrier()
                    self.merge(l)
                if ("dumpmix", l) in st:
                    self.dump_xT(f"dbg_x_mix_{l}")
            if ("ffn", l, 1) in st:
                self.ffn1p(l, 1)
        self.final()
        with ExitStack() as es:
            sems = {k: es.enter_context(self.nc.semaphore("s_" + k)) for k in list(Prog.ENG) + self.P.lanes()}
            self.P.emit(sems)
        return self.nc


FULL = {("ffn", 0, 0), ("ffn", 0, 1), ("ffn", 1, 0), ("ffn", 1, 1), ("mix", 0), ("mix", 1), ("hgrn", 0), ("hgrn", 1),
        ("dn", 0), ("dn", 1), ("merge", 0), ("merge", 1)}


def _swap_heads(a, axis, seg0, nseg, seglen):
    a = np.array(a, copy=True)
    idx = [slice(None)] * a.ndim
    for i in range(nseg):
        lo = seg0 + i * seglen
        h = seglen // 2
        i0 = list(idx); i0[axis] = slice(lo, lo + h)
        i1 = list(idx); i1[axis] = slice(lo + h, lo + seglen)
        tmp = a[tuple(i0)].copy()
        a[tuple(i0)] = a[tuple(i1)]
        a[tuple(i1)] = tmp
    return a


def make_in_maps(inputs, ncores=8):
    inp = {k: np.asarray(v) for k, v in inputs.items()}
    shared = {k: np.ascontiguousarray(inp[k]) for k in ("w_ada", "b_ada", "norm_g", "ffn_w_gate", "ffn_w_up", "ffn_w_down",
                                                         "hgrn_norm_g", "dn_norm_g", "w_branch_a", "w_branch_b", "w_out")}
    shared["final_norm_g"] = np.ascontiguousarray(inp["final_norm_g"].reshape(1, -1))
    per_rank = []
    for r in range(2):
        if r == 0:
            pr = dict(w_in=np.ascontiguousarray(inp["w_in"]), hgrn_lower_bounds=np.ascontiguousarray(inp["hgrn_lower_bounds"]),
                      dn_conv_w=np.ascontiguousarray(inp["dn_conv_w"]), dn_a_log=np.ascontiguousarray(inp["dn_a_log"]),
                      dn_dt_bias=np.ascontiguousarray(inp["dn_dt_bias"]))
        else:
            w = _swap_heads(inp["w_in"], 2, 0, 9, 1024)
            w = _swap_heads(w, 2, 9216, 4, 8)
            pr = dict(w_in=np.ascontiguousarray(w),
                      hgrn_lower_bounds=np.ascontiguousarray(_swap_heads(inp["hgrn_lower_bounds"], 2, 0, 1, 1024)),
                      dn_conv_w=np.ascontiguousarray(_swap_heads(inp["dn_conv_w"], 2, 0, 3, 1024)),
                      dn_a_log=np.ascontiguousarray(_swap_heads(inp["dn_a_log"], 2, 0, 1, 8)),
                      dn_dt_bias=np.ascontiguousarray(_swap_heads(inp["dn_dt_bias"], 2, 0, 1, 8)))
        sel = np.zeros((128, 2), np.float32)
        sel[:, r] = 1.0
        pr["sel"] = sel
        per_rank.append(pr)
    maps = []
    for k in range(ncores):
        b, r = k // 2, k % 2
        m = dict(shared)
        m.update(per_rank[r])
        if r == 0:
            m["x"] = np.ascontiguousarray(np.concatenate([inp["ctx"][b], inp["x"][b][:NTO - NCTX]], axis=0))
            m["c_ctx"] = np.ascontiguousarray(inp["c_ctx"].reshape(1, -1))
        else:
            m["x"] = np.ascontiguousarray(inp["x"][b][NTO - NCTX:])
            m["c_ctx"] = np.ascontiguousarray(inp["c"][b:b + 1])
        m["c"] = np.ascontiguousarray(inp["c"][b:b + 1])
        maps.append(m)
    return maps


def kernel(**inputs):
    bld = Builder(FULL)
    nc = bld.build()
    maps = make_in_maps(inputs, 8)
    res = run_bass_kernel_spmd(nc, maps, core_ids=list(range(8)))
    out = np.zeros((4, 2048, D), np.float32)
    for k in range(8):
        b, r = k // 2, k % 2
        o = np.asarray(res.results[k]["out"])
        if r == 0:
            out[b, :NTO - NCTX] = o[NCTX:]
        else:
            out[b, NTO - NCTX:] = o
    return out
```

```python
import numpy as np
from contextlib import ExitStack
import concourse.bass as bass
import concourse.mybir as mybir
from concourse.bass_utils import run_bass_kernel_spmd

F32 = mybir.dt.float32
F32R = mybir.dt.float32r
BF16 = mybir.dt.bfloat16
AF = mybir.ActivationFunctionType
ALU = mybir.AluOpType

D = 2048
NT = 2304
NCTX = 256
NTO = 1152
NG = 6
DFF = 5632
NKC = 16
NFC = 44
PIN = 13344
EPS = 1e-6
SB0 = 16512
SB1 = 229376


class Prog:
    ENG = ("pe", "act", "dve", "pool", "sp")

    def __init__(self, nc):
        self.nc = nc
        self.ops = []
        self.last_w = {}
        self.readers = {}

    def op(self, eng, fn, reads=(), writes=(), lane=None, step=16):
        i = len(self.ops)
        deps = set()
        for r in reads:
            w = self.last_w.get(r)
            if w is not None:
                deps.add(w)
        for w_ in writes:
            w = self.last_w.get(w_)
            if w is not None:
                deps.add(w)
            for rd in self.readers.get(w_, ()):
                deps.add(rd)
        deps.discard(i)
        for r in reads:
            self.readers.setdefault(r, []).append(i)
        for w_ in writes:
            self.last_w[w_] = i
            self.readers[w_] = []
        self.ops.append(dict(eng=eng, fn=fn, deps=deps, lane=lane, sig=lane is not None, lstep=step))
        return i

    def barrier(self):
        toks = [("__bar", e, len(self.ops)) for e in self.ENG]
        allres = list(self.last_w.keys())
        for e, t in zip(self.ENG, toks):
            self.op(e, None, reads=allres, writes=[t])
        for e in self.ENG:
            self.op(e, None, reads=toks, writes=[("__bar2", e)])
        self.last_w = {k: v for k, v in self.last_w.items() if k[0] == "__bar2"} if False else self.last_w

    def lanes(self):
        return sorted({o["lane"] for o in self.ops if o["lane"] is not None})

    def emit(self, sems):
        ops = self.ops
        for o in ops:
            for d in o["deps"]:
                dop = ops[d]
                if dop["lane"] is None:
                    if dop["eng"] == "pe" and o["eng"] == "pe" and o["lane"] is None:
                        continue
                    dop["sig"] = True
        cnt = {}
        for o in ops:
            if o["sig"]:
                key = o["lane"] if o["lane"] is not None else o["eng"]
                step = o["lstep"] if o["lane"] is not None else 1
                cnt[key] = cnt.get(key, 0) + step
                o["key"] = key
                o["val"] = cnt[key]
                o["step"] = step
        final = dict(cnt)
        per_eng = {e: [] for e in self.ENG}
        for o in ops:
            per_eng[o["eng"]].append(o)

        def run(ename, e):
            seen = {}
            for o in per_eng[ename]:
                waits = {}
                for d in o["deps"]:
                    dop = ops[d]
                    if not dop["sig"]:
                        continue
                    if dop["lane"] is None and dop["eng"] == "pe" and ename == "pe" and o["lane"] is None:
                        continue
                    k, v = dop["key"], dop["val"]
                    if v > waits.get(k, 0):
                        waits[k] = v
                for k, v in waits.items():
                    if v > seen.get(k, 0):
                        e.wait_ge(sems[k], v)
                        seen[k] = v
                if o["fn"] is None:
                    ins = e.nop() if o["sig"] else None
                else:
                    ins = o["fn"](e)
                if o["sig"]:
                    ins.then_inc(sems[o["key"]], o["step"])
            if ename == "sp":
                for k, v in final.items():
                    if v > seen.get(k, 0):
                        e.wait_ge(sems[k], v)

        with self.nc.Block() as block:
            @block.tensor
            def _(e):
                run("pe", e)

            @block.scalar
            def _(e):
                run("act", e)

            @block.vector
            def _(e):
                run("dve", e)

            @block.gpsimd
            def _(e):
                run("pool", e)

            @block.sync
            def _(e):
                run("sp", e)


class Arena:
    def __init__(self, nc):
        self.nc = nc
        self.off = SB0
        self.stack = []
        self.n = 0

    def push(self):
        self.stack.append(self.off)

    def pop(self):
        self.off = self.stack.pop()

    def tile(self, shape, dtype, name="t"):
        esz = 2 if dtype == BF16 else 4
        per = esz
        for s in shape[1:]:
            per *= s
        per = (per + 63) // 64 * 64
        assert self.off + per <= SB1, f"SBUF overflow allocating {name} {shape}: {self.off + per - SB1} bytes over"
        self.n += 1
        h = self.nc.alloc_sbuf_tensor_at(f"{name}_{self.n}", list(shape), dtype, offset=self.off)
        self.off += per
        return h


class Builder:
    def __init__(self, stages, dbg=()):
        self.stages = stages
        self.dbg = dbg
        self.heads = list(range(4))
        nc = self.nc = bass.Bass("TRN2", target_bir_lowering=False)
        self.P = Prog(nc)
        self.A = Arena(nc)
        self.uid = 0
        dt = nc.dram_tensor
        self.inp = {}
        specs = dict(
            x=[NTO, D], c=[1, D], c_ctx=[1, D], sel=[128, 2],
            w_ada=[2, D, 9216], b_ada=[2, 9216], norm_g=[2, 3, D], final_norm_g=[1, D],
            ffn_w_gate=[2, 2, D, DFF], ffn_w_up=[2, 2, D, DFF], ffn_w_down=[2, 2, DFF, D],
            w_in=[2, D, PIN], hgrn_lower_bounds=[2, 2, 1024], hgrn_norm_g=[2, 128],
            dn_conv_w=[2, 5, 3072], dn_a_log=[2, 2, 8], dn_dt_bias=[2, 2, 8], dn_norm_g=[2, 128],
            w_branch_a=[2, 1024, D], w_branch_b=[2, 1024, D], w_out=[2, D, D])
        for k, shp in specs.items():
            self.inp[k] = dt(k, shp, F32, kind="ExternalInput").ap()
        self.out = dt("out", [NTO, D], F32, kind="ExternalOutput").ap()
        self.xT_d = dt("xT_d", [NKC, 128, NTO], F32, kind="Internal").ap()
        self.hT_d = dt("hT_own_d", [NKC, 128, NTO], BF16, kind="Internal").ap()
        self.hT_all_d = dt("hT_all_d", [4, 2, 4, 128, NTO], BF16, kind="Internal").ap()
        self.y_own_d = dt("y_own_d", [2, 4, 128, NT], BF16, kind="Internal").ap()
        self.y_all_d = dt("y_all_d", [4, 2, 2, 128, NT], BF16, kind="Internal").ap()
        self.yaT_d = self.y_own_d[0]
        self.ybT_d = self.y_own_d[1]
        self.mod_own_d = dt("mod_own_d", [2, 128, 144], F32, kind="Internal").ap()
        self.mod_all_d = dt("mod_all_d", [2, 2, 128, 144], F32, kind="Internal").ap()
        self.dbg_out = {}
        self.ps = [nc.alloc_psum_tensor(f"ps{i}", [128, 512], F32) for i in range(8)]

    def tok(self, *a):
        return a

    def new(self, prefix):
        self.uid += 1
        return f"{prefix}{self.uid}"

    def consts(self):
        P, A, nc = self.P, self.A, self.nc
        self.ident = A.tile([128, 128], F32, "ident")
        self.ones_f = A.tile([128, 128], F32, "ones_f")
        self.ones_r = A.tile([128, 128], F32R, "ones_r")
        self.ident_b = A.tile([128, 128], BF16, "ident_b")
        self.selT = A.tile([128, 2], F32, "selT")
        P.op("sp", lambda e: e.dma_start(out=self.selT[:], in_=self.inp["sel"]), writes=["selT"], lane="selT")
        self.epsc = A.tile([128, 1], F32, "epsc")
        P.op("pool", lambda e: e.memset(self.epsc[:], EPS), writes=["epsc"])
        P.op("pool", lambda e: e.memset(self.ones_f[:], 1.0), writes=["ones_f"])
        P.op("act", lambda e: e.activation(out=self.ones_r[:], in_=self.ones_f[:], func=AF.Copy), reads=["ones_f"], writes=["ones_r"])
        P.op("pool", lambda e: e.affine_select(out=self.ident[:], in_=self.ones_f[:], pattern=[[-1, 128]],
                                               compare_op=ALU.is_equal, fill=0.0, base=0, channel_multiplier=1),
             reads=["ones_f"], writes=["ident"])
        P.op("dve", lambda e: e.tensor_copy(out=self.ident_b[:], in_=self.ident[:]), reads=["ident"], writes=["ident_b"])

    def load_T(self, src2d, rows, dst_ap, dst_tok, scratch, ps_idx=7):
        P = self.P
        lane = "ldT"
        P.op("sp", lambda e: e.dma_start(out=scratch[0:rows, :], in_=src2d), writes=["ldT_s"], lane=lane)
        ps = self.ps[ps_idx]
        P.op("pe", lambda e: e.transpose(ps[:, 0:rows], scratch[0:rows, :], self.ident[0:rows, 0:rows]),
             reads=["ldT_s", "ident"], writes=[("ps", ps_idx)])
        P.op("dve", lambda e: e.tensor_copy(out=dst_ap, in_=ps[:, 0:rows]), reads=[("ps", ps_idx)], writes=[dst_tok])

    def phase_in(self):
        P, A = self.P, self.A
        A.push()
        tin = [A.tile([128, D], F32, "tin") for _ in range(2)]
        tout = [A.tile([128, NKC, 128], F32, "tout") for _ in range(2)]
        for t in range(NTO // 128):
            s = t % 2
            src = self.inp["x"][t * 128:(t + 1) * 128, :]
            P.op("sp", lambda e, s=s, src=src: e.dma_start(out=tin[s][:], in_=src), writes=[("tin", s)], lane=f"tin{s}")
            for q in range(4):
                bank = (t % 2) * 4 + q
                for j in range(4):
                    c = q * 4 + j
                    P.op("pe", lambda e, s=s, c=c, bank=bank, j=j: e.transpose(
                        self.ps[bank][:, j * 128:(j + 1) * 128], tin[s][:, c * 128:(c + 1) * 128], self.ident[:]),
                        reads=[("tin", s), "ident"], writes=[("ps", bank)])
                eng = "act" if q % 2 == 0 else "dve"
                if eng == "act":
                    P.op("act", lambda e, s=s, q=q, bank=bank: e.activation(
                        out=tout[s][:, q * 4:(q + 1) * 4, :], in_=self.ps[bank][:].rearrange("p (a b) -> p a b", a=4), func=AF.Copy),
                        reads=[("ps", bank)], writes=[("tout", s, q)])
                else:
                    P.op("dve", lambda e, s=s, q=q, bank=bank: e.tensor_copy(
                        out=tout[s][:, q * 4:(q + 1) * 4, :], in_=self.ps[bank][:].rearrange("p (a b) -> p a b", a=4)),
                        reads=[("ps", bank)], writes=[("tout", s, q)])
            P.op("sp", lambda e, s=s, t=t: e.dma_start(
                out=self.xT_d[:, :, t * 128:(t + 1) * 128].rearrange("c p n -> p c n"), in_=tout[s][:]),
                reads=[("tout", s, q) for q in range(4)], writes=[("xT", c, t) for c in range(NKC)], lane=f"tout{s}")
        A.pop()
        P.barrier()

    def phase_adaln(self):
        P, A = self.P, self.A
        self.modT = [A.tile([128, 144, 2], F32, f"modT{l}") for l in range(2)]
        self.gT = A.tile([128, 2 * 3 * NKC + NKC], F32, "gT")
        A.push()
        scr = A.tile([128, 128], F32, "scr")
        sc = A.tile([128, NKC, 2], BF16, "sc")
        craw = A.tile([128, 2 * NKC], F32, "craw")
        one2 = A.tile([1, 2], F32, "one2")
        modh = A.tile([128, 2, 144], F32, "modh")
        P.op("pool", lambda e: e.memset(one2[:], 1.0), writes=["one2"])
        self.load_T(self.inp["c"].rearrange("o (c p) -> (o c) p", p=128), NKC, craw[:, 0:NKC], "craw0", scr)
        self.load_T(self.inp["c_ctx"].rearrange("o (c p) -> (o c) p", p=128), NKC, craw[:, NKC:2 * NKC], "craw1", scr)
        for v in range(2):
            P.op("act", lambda e, v=v: e.activation(out=sc[:, :, v], in_=craw[:, v * NKC:(v + 1) * NKC], func=AF.Silu),
                 reads=[f"craw{v}"], writes=[("sc", v)])
        self.load_T(self.inp["norm_g"].rearrange("l s (c p) -> (l s c) p", p=128), 96, self.gT[:, 0:96], "gT0", scr)
        self.load_T(self.inp["final_norm_g"].rearrange("o (c p) -> (o c) p", p=128), NKC, self.gT[:, 96:112], "gT1", scr)
        wsl = [A.tile([128, NKC, 512], BF16, "wada") for _ in range(2)]
        brow = [A.tile([1, 512], F32, "brow") for _ in range(2)]
        for l in range(2):
            wv = self.inp["w_ada"][l].rearrange("(kc p) n -> p kc n", p=128)
            bank = 6
            for s in range(18):
                sl = s % 2
                P.op("pool", lambda e, sl=sl, s=s, wv=wv: e.dma_start(out=wsl[sl][:], in_=wv[:, :, s * 512:(s + 1) * 512]),
                     writes=[("wada", sl)], lane=f"wada{sl}")
                P.op("sp", lambda e, sl=sl, s=s, l=l: e.dma_start(out=brow[sl][:], in_=self.inp["b_ada"][l:l + 1, s * 512:(s + 1) * 512]),
                     writes=[("brow", sl)], lane=f"brow{sl}")
                for j in range(4):
                    ch = s * 4 + j
                    for k in range(NKC):
                        P.op("pe", lambda e, sl=sl, j=j, k=k, ch=ch: e.matmul(
                            self.ps[bank][:, ch * 2:ch * 2 + 2], lhsT=wsl[sl][:, k, j * 128:(j + 1) * 128], rhs=sc[:, k, :],
                            start=(k == 0), stop=False),
                            reads=[("wada", sl), ("sc", 0), ("sc", 1)], writes=[("ps", bank)])
                    P.op("pe", lambda e, sl=sl, j=j, ch=ch: e.matmul(
                        self.ps[bank][:, ch * 2:ch * 2 + 2], lhsT=brow[sl][0:1, j * 128:(j + 1) * 128], rhs=one2[0:1, :],
                        start=False, stop=True),
                        reads=[("brow", sl), "one2"], writes=[("ps", bank)])
            P.op("dve", lambda e, l=l: e.tensor_copy(out=modh[:, l, :], in_=self.ps[bank][:, 0:144]),
                 reads=[("ps", bank)], writes=[("modh", l)])
        P.op("sp", lambda e: e.dma_start(out=self.mod_own_d.rearrange("l p n -> p l n"), in_=modh[:]),
             reads=[("modh", 0), ("modh", 1)], writes=["mod_own"], lane="modo")
        P.op("pool", lambda e: e.collective_compute(
            "AllGather", ALU.bypass, replica_groups=[[0, 1], [2, 3], [4, 5], [6, 7]],
            ins=[self.mod_own_d.rearrange("l p n -> (l p) n")], outs=[self.mod_all_d.rearrange("r l p n -> (r l p) n")]),
            reads=["mod_own"], writes=["mod_all"], lane="cc_m", step=1)
        for l in range(2):
            for r in range(2):
                P.op("sp", lambda e, l=l, r=r: e.dma_start(out=self.modT[l][:, r * 72:(r + 1) * 72, :].rearrange("p a b -> p (a b)"),
                                                           in_=self.mod_all_d[r, l]),
                     reads=["mod_all"], writes=[("modT", l)], lane="modl")
        A.pop()
        self.gmod = A.tile([128, 2, 3, 2, NKC], F32, "gmod")
        self.shift = A.tile([128, 2, 3, 2, NKC], F32, "shift")
        self.gate = A.tile([128, 2, 3, 2, NKC], F32, "gate")
        for l in range(2):
            for sub in range(3):
                for v in range(2):
                    base = sub * 3 * NKC
                    m = self.modT[l]
                    g = self.gT[:, (l * 3 + sub) * NKC:(l * 3 + sub + 1) * NKC]
                    P.op("dve", lambda e, l=l, sub=sub, v=v, m=m, g=g, base=base: e.scalar_tensor_tensor(
                        out=self.gmod[:, l, sub, v, :], in0=m[:, base + NKC:base + 2 * NKC, v], scalar=1.0, in1=g,
                        op0=ALU.add, op1=ALU.mult), reads=[("modT", l), "gT0"], writes=[("gmod", l, sub, v)])
                    P.op("dve", lambda e, l=l, sub=sub, v=v, m=m, base=base: e.tensor_copy(
                        out=self.shift[:, l, sub, v, :], in_=m[:, base:base + NKC, v]),
                        reads=[("modT", l)], writes=[("shift", l, sub, v)])
                    fac = 1.0 if sub == 1 else 0.5
                    P.op("dve", lambda e, l=l, sub=sub, v=v, m=m, base=base, fac=fac: e.tensor_scalar(
                        out=self.gate[:, l, sub, v, :], in0=m[:, base + 2 * NKC:base + 3 * NKC, v], scalar1=fac, scalar2=None,
                        op0=ALU.mult), reads=[("modT", l)], writes=[("gate", l, sub, v)])
        for nm, X in (("gmod", self.gmod), ("shift", self.shift), ("gate", self.gate)):
            allr = [(nm, l, sub, v) for l in range(2) for sub in range(3) for v in range(2)]
            P.op("dve", lambda e, X=X: e.tensor_scalar(out=X[:, :, :, 1, :], in0=X[:, :, :, 1, :], scalar1=self.selT[:, 0:1], scalar2=None, op0=ALU.mult),
                 reads=allr + ["selT"], writes=allr)
            P.op("dve", lambda e, X=X: e.scalar_tensor_tensor(out=X[:, :, :, 1, :], in0=X[:, :, :, 0, :], scalar=self.selT[:, 1:2], in1=X[:, :, :, 1, :],
                                                              op0=ALU.mult, op1=ALU.add),
                 reads=allr + ["selT"], writes=allr)
        P.barrier()

    def groups(self):
        return [(0, 256, 1), (256, 896, 0)]

    def prologue(self, t0, G, gm_ap, sh_ap, hT, hT_tok, to_bf16=True, o0=0):
        P, A = self.P, self.A
        A.push()
        xc = [A.tile([128, G], F32, "xc") for _ in range(4)]
        sq = [A.tile([128, G], F32R, "sq") for _ in range(2)]
        tmp = [A.tile([128, G], F32, "tmp") for _ in range(2)]
        rstd = A.tile([128, G], F32, "rstd")
        nh = (G + 511) // 512
        hw = G // nh
        u = self.new("pg")
        for c in range(NKC):
            s = c % 4
            P.op("sp", lambda e, s=s, c=c: e.dma_start(out=xc[s][:], in_=self.xT_d[c, :, t0:t0 + G]),
                 reads=[("xT", c, t) for t in range(t0 // 128, (t0 + G) // 128)], writes=[(u, "xc", s)], lane=f"xc{s}")
            q = c % 2
            P.op("act", lambda e, s=s, q=q: e.activation(out=sq[q][:], in_=xc[s][:], func=AF.Square),
                 reads=[(u, "xc", s)], writes=[(u, "sq", q)])
            for h in range(nh):
                P.op("pe", lambda e, q=q, h=h, c=c: e.matmul(self.ps[h][:, 0:hw], lhsT=self.ones_r[:], rhs=sq[q][:, h * hw:(h + 1) * hw],
                                                            start=(c == 0), stop=(c == NKC - 1)),
                     reads=[(u, "sq", q), "ones_r"], writes=[("ps", h)])
        for h in range(nh):
            P.op("act", lambda e, h=h: e.activation(out=rstd[:, h * hw:(h + 1) * hw], in_=self.ps[h][:, 0:hw], func=AF.Ln,
                                                    bias=self.epsc[:, 0:1], scale=1.0 / D),
                 reads=[("ps", h), "epsc"], writes=[(u, "rstd", h)])
            P.op("act", lambda e, h=h: e.activation(out=rstd[:, h * hw:(h + 1) * hw], in_=rstd[:, h * hw:(h + 1) * hw], func=AF.Exp, scale=-0.5),
                 reads=[(u, "rstd", h)], writes=[(u, "rstd", h)])
        for c in range(NKC):
            s = c % 4
            q = c % 2
            P.op("sp", lambda e, s=s, c=c: e.dma_start(out=xc[s][:], in_=self.xT_d[c, :, t0:t0 + G]),
                 reads=[("xT", c, t) for t in range(t0 // 128, (t0 + G) // 128)], writes=[(u, "xc", s)], lane=f"xc{s}")
            P.op("dve", lambda e, s=s, q=q: e.tensor_tensor(out=tmp[q][:], in0=xc[s][:], in1=rstd[:], op=ALU.mult),
                 reads=[(u, "xc", s)] + [(u, "rstd", h) for h in range(nh)], writes=[(u, "tmp", q)])
            if sh_ap is not None:
                P.op("act", lambda e, q=q, c=c: e.activation(out=hT[:, c, o0:o0 + G], in_=tmp[q][:], func=AF.Identity,
                                                             bias=sh_ap[:, c:c + 1], scale=gm_ap[:, c:c + 1]),
                     reads=[(u, "tmp", q)], writes=[(hT_tok, c)])
            else:
                P.op("act", lambda e, q=q, c=c: e.activation(out=hT[:, c, o0:o0 + G], in_=tmp[q][:], func=AF.Identity,
                                                             scale=gm_ap[:, c:c + 1]),
                     reads=[(u, "tmp", q)], writes=[(hT_tok, c)])
        A.pop()
        P.barrier()

    def ffn(self, l, f):
        P, A = self.P, self.A
        sub = 0 if f == 0 else 2
        wgv = self.inp["ffn_w_gate"][l, f].rearrange("(kc p) n -> p kc n", p=128)
        wuv = self.inp["ffn_w_up"][l, f].rearrange("(kc p) n -> p kc n", p=128)
        wdv = self.inp["ffn_w_down"][l, f].rearrange("(kc p) n -> p kc n", p=128)
        A.push()
        Gmax = 1024
        hT = A.tile([128, NKC, Gmax], BF16, "hT")
        wg = [A.tile([128, NKC, 128], BF16, "wg") for _ in range(3)]
        wu = [A.tile([128, NKC, 128], BF16, "wu") for _ in range(3)]
        wd = [A.tile([128, NFC, 128], BF16, "wd") for _ in range(2)]
        for (t0, G, v) in self.groups():
            self.ffn_group(l, f, sub, t0, G, v, hT, wg, wu, wd, wgv, wuv, wdv)
        A.pop()

    def ffn1p(self, l, f):
        P, A = self.P, self.A
        sub = 0 if f == 0 else 2
        wgv = self.inp["ffn_w_gate"][l, f].rearrange("(kc p) n -> p kc n", p=128)
        wuv = self.inp["ffn_w_up"][l, f].rearrange("(kc p) n -> p kc n", p=128)
        wdv = self.inp["ffn_w_down"][l, f].rearrange("(kc p) n -> p kc n", p=128)
        HF = NFC // 2
        NB_, BW = 3, NTO // 3
        u = self.new("f1")
        A.push()
        hT = A.tile([128, NKC, NTO], BF16, "hT1")
        for (t0, G, v) in self.groups():
            self.prologue(t0, G, self.gmod[:, l, sub, v, :], self.shift[:, l, sub, v, :], hT, (u, "hT", t0), o0=t0)
        hreads = [((u, "hT", t0), c) for (t0, G, v) in self.groups() for c in range(NKC)]
        wg = [A.tile([128, NKC, 128], BF16, "wg") for _ in range(3)]
        wu = [A.tile([128, NKC, 128], BF16, "wu") for _ in range(3)]
        wd = [A.tile([128, HF, 128], BF16, "wd") for _ in range(2)]
        actT = A.tile([128, HF, NTO], BF16, "actT")
        sg = [A.tile([128, BW], F32, "sg") for _ in range(4)]
        xc = [A.tile([128, NTO], F32, "xe") for _ in range(3)]
        segs = [(t0, t0 + G, v) for (t0, G, v) in self.groups()]
        cnt = {"sg": 0, "x": 0, "d": 0}

        def load_gu(jg):
            s = jg % 3
            P.op("pool", lambda e: e.dma_start(out=wg[s][:], in_=wgv[:, :, jg * 128:(jg + 1) * 128]), writes=[("wg", s)], lane=f"wg{s}")
            P.op("pool", lambda e: e.dma_start(out=wu[s][:], in_=wuv[:, :, jg * 128:(jg + 1) * 128]), writes=[("wu", s)], lane=f"wu{s}")

        def load_d(p, n):
            s = cnt["d"] % 2
            cnt["d"] += 1
            P.op("pool", lambda e: e.dma_start(out=wd[s][:], in_=wdv[:, p * HF:(p + 1) * HF, n * 128:(n + 1) * 128]), writes=[("wd", s)], lane=f"wd{s}")
            return s

        def gu_chunk(p, j):
            jg = p * HF + j
            s = jg % 3
            for b in range(NB_):
                cs = slice(b * BW, (b + 1) * BW)
                bg = self.nb()
                for k in range(NKC):
                    P.op("pe", lambda e, k=k, bg=bg, cs=cs: e.matmul(self.ps[bg][:, 0:BW], lhsT=wg[s][:, k, :], rhs=hT[:, k, cs], start=(k == 0), stop=(k == NKC - 1)),
                         reads=[("wg", s)] + (hreads if k == 0 else []), writes=[("ps", bg)])
                bu = self.nb()
                for k in range(NKC):
                    P.op("pe", lambda e, k=k, bu=bu, cs=cs: e.matmul(self.ps[bu][:, 0:BW], lhsT=wu[s][:, k, :], rhs=hT[:, k, cs], start=(k == 0), stop=(k == NKC - 1)),
                         reads=[("wu", s)] + (hreads if k == 0 else []), writes=[("ps", bu)])
                si = cnt["sg"] % 4
                cnt["sg"] += 1
                P.op("act", lambda e, si=si, bg=bg: e.activation(out=sg[si][:], in_=self.ps[bg][:, 0:BW], func=AF.Silu), reads=[("ps", bg)], writes=[(u, "sg", si)])
                P.op("dve", lambda e, si=si, bu=bu, cs=cs: e.tensor_tensor(out=actT[:, j, cs], in0=sg[si][:], in1=self.ps[bu][:, 0:BW], op=ALU.mult),
                     reads=[(u, "sg", si), ("ps", bu)], writes=[(u, "actT", j, b)])

        def down_chunk(p, n, s):
            xs = cnt["x"] % 3
            cnt["x"] += 1
            P.op("sp", lambda e: e.dma_start(out=xc[xs][:], in_=self.xT_d[n, :, :]),
                 reads=[("xT", n, t) for t in range(NTO // 128)], writes=[(u, "xe", xs)], lane=f"xe{xs}")
            for b in range(NB_):
                by = self.nb()
                for j in range(HF):
                    P.op("pe", lambda e, j=j, by=by, b=b: e.matmul(self.ps[by][:, 0:BW], lhsT=wd[s][:, j, :], rhs=actT[:, j, b * BW:(b + 1) * BW],
                                                       start=(j == 0), stop=(j == HF - 1)),
                         reads=[("wd", s), (u, "actT", j, b)], writes=[("ps", by)])
                for (a0, a1, v) in segs:
                    lo, hi = max(a0, b * BW), min(a1, (b + 1) * BW)
                    if lo >= hi:
                        continue
                    P.op("dve", lambda e, lo=lo, hi=hi, v=v, by=by, b=b: e.scalar_tensor_tensor(
                        out=xc[xs][:, lo:hi], in0=self.ps[by][:, lo - b * BW:hi - b * BW], scalar=self.gate[:, l, sub, v, n:n + 1],
                        in1=xc[xs][:, lo:hi], op0=ALU.mult, op1=ALU.add),
                        reads=[("ps", by), (u, "xe", xs)], writes=[(u, "xe", xs)])
            P.op("sp", lambda e: e.dma_start(out=self.xT_d[n, :, :], in_=xc[xs][:]),
                 reads=[(u, "xe", xs)], writes=[("xT", n, t) for t in range(NTO // 128)], lane=f"xs{xs}")

        load_gu(0)
        load_gu(1)
        for p in range(2):
            dslots = {}
            for j in range(HF):
                jg = p * HF + j
                if jg + 2 < NFC:
                    load_gu(jg + 2)
                if j == HF - 2:
                    dslots[0] = load_d(p, 0)
                if j == HF - 1:
                    dslots[1] = load_d(p, 1)
                gu_chunk(p, j)
            for n in range(NKC):
                down_chunk(p, n, dslots[n])
                if n + 2 < NKC:
                    dslots[n + 2] = load_d(p, n + 2)
        A.pop()
        P.barrier()

    def ffn_group(self, l, f, sub, t0, G, v, hT, wg, wu, wd, wgv, wuv, wdv):
        P, A = self.P, self.A
        u = self.new("ffn")
        self.prologue(t0, G, self.gmod[:, l, sub, v, :], self.shift[:, l, sub, v, :], hT, (u, "hT"))
        if "dump_ffn" in self.stages and t0 == 0 and l == 0 and f == 0:
            self.dump_tile("dbg_hT", hT[:, :, 0:G], [128, NKC, G], [((u, "hT"), c) for c in range(NKC)])
        A.push()
        actT = A.tile([128, NFC, G], BF16, "actT")
        nh = (G + 511) // 512
        hw = G // nh
        sg = [A.tile([128, hw], F32, "sg") for _ in range(2 * nh)]
        xc = [A.tile([128, G], F32, "xe") for _ in range(3)]
        hreads = [((u, "hT"), c) for c in range(NKC)]

        def load_gu(j):
            s = j % 3
            P.op("pool", lambda e, s=s, j=j: e.dma_start(out=wg[s][:], in_=wgv[:, :, j * 128:(j + 1) * 128]),
                 writes=[("wg", s)], lane=f"wg{s}")
            P.op("pool", lambda e, s=s, j=j: e.dma_start(out=wu[s][:], in_=wuv[:, :, j * 128:(j + 1) * 128]),
                 writes=[("wu", s)], lane=f"wu{s}")

        def load_d(n):
            s = n % 2
            P.op("pool", lambda e, s=s, n=n: e.dma_start(out=wd[s][:], in_=wdv[:, :, n * 128:(n + 1) * 128]),
                 writes=[("wd", s)], lane=f"wd{s}")

        load_gu(0)
        load_gu(1)
        for j in range(NFC):
            if j + 2 < NFC:
                load_gu(j + 2)
            elif j + 2 == NFC:
                load_d(0)
            else:
                load_d(1)
            s = j % 3
            par = j % 2
            for h in range(nh):
                bg = par * 4 + h
                bu = par * 4 + 2 + h
                for k in range(NKC):
                    P.op("pe", lambda e, s=s, k=k, h=h, bg=bg: e.matmul(
                        self.ps[bg][:, 0:hw], lhsT=wg[s][:, k, :], rhs=hT[:, k, h * hw:(h + 1) * hw],
                        start=(k == 0), stop=(k == NKC - 1)),
                        reads=[("wg", s)] + (hreads if k == 0 else []), writes=[("ps", bg)])
                for k in range(NKC):
                    P.op("pe", lambda e, s=s, k=k, h=h, bu=bu: e.matmul(
                        self.ps[bu][:, 0:hw], lhsT=wu[s][:, k, :], rhs=hT[:, k, h * hw:(h + 1) * hw],
                        start=(k == 0), stop=(k == NKC - 1)),
                        reads=[("wu", s)] + (hreads if k == 0 else []), writes=[("ps", bu)])
                si = par * nh + h
                P.op("act", lambda e, si=si, bg=bg: e.activation(out=sg[si][:], in_=self.ps[bg][:, 0:hw], func=AF.Silu),
                     reads=[("ps", bg)], writes=[(u, "sg", si)])
                P.op("dve", lambda e, si=si, bu=bu, j=j, h=h: e.tensor_tensor(
                    out=actT[:, j, h * hw:(h + 1) * hw], in0=sg[si][:], in1=self.ps[bu][:, 0:hw], op=ALU.mult),
                    reads=[(u, "sg", si), ("ps", bu)], writes=[(u, "actT", j, h)])
        if "dump_ffn" in self.stages and t0 == 0 and l == 0 and f == 0:
            self.dump_tile("dbg_actT", actT[:], [128, NFC, G], [(u, "actT", j, h) for j in range(NFC) for h in range(nh)])
            self.dump_tile("dbg_wg", wg[(NFC - 1) % 3][:], [128, NKC, 128], [("wg", (NFC - 1) % 3)])
        for n in range(NKC):
            s = n % 2
            xs = n % 3
            P.op("sp", lambda e, xs=xs, n=n: e.dma_start(out=xc[xs][:], in_=self.xT_d[n, :, t0:t0 + G]),
                 reads=[("xT", n, t) for t in range(t0 // 128, (t0 + G) // 128)], writes=[(u, "xe", xs)], lane=f"xe{xs}")
            for h in range(nh):
                by = (n % 2) * 2 + h
                for j in range(NFC):
                    P.op("pe", lambda e, s=s, j=j, h=h, by=by: e.matmul(
                        self.ps[by][:, 0:hw], lhsT=wd[s][:, j, :], rhs=actT[:, j, h * hw:(h + 1) * hw],
                        start=(j == 0), stop=(j == NFC - 1)),
                        reads=[("wd", s), (u, "actT", j, h)], writes=[("ps", by)])
                P.op("dve", lambda e, xs=xs, h=h, by=by, n=n: e.scalar_tensor_tensor(
                    out=xc[xs][:, h * hw:(h + 1) * hw], in0=self.ps[by][:, 0:hw], scalar=self.gate[:, l, sub, v, n:n + 1],
                    in1=xc[xs][:, h * hw:(h + 1) * hw], op0=ALU.mult, op1=ALU.add),
                    reads=[("ps", by), (u, "xe", xs)], writes=[(u, "xe", xs)])
            P.op("sp", lambda e, xs=xs, n=n: e.dma_start(out=self.xT_d[n, :, t0:t0 + G], in_=xc[xs][:]),
                 reads=[(u, "xe", xs)], writes=[("xT", n, t) for t in range(t0 // 128, (t0 + G) // 128)], lane=f"xs{xs}")
            if n + 2 < NKC:
                load_d(n + 2)
        A.pop()
        P.barrier()

    def nb(self):
        self.bank_rr = (getattr(self, "bank_rr", -1) + 1) % 8
        return self.bank_rr

    def phase_small(self):
        P, A = self.P, self.A
        self.lb = A.tile([128, 2, 16], F32, "lb")
        self.oml = A.tile([128, 2, 16], F32, "oml")
        self.hng = A.tile([128, 2], F32, "hng")
        self.dng = A.tile([128, 2], F32, "dng")
        self.convT = A.tile([128, 2, 120], F32, "convT")
        self.alog = A.tile([128, 2, 16], F32, "alog")
        self.dtb = A.tile([128, 2, 16], F32, "dtb")
        self.maskF = A.tile([128, 128], F32, "maskF")
        self.maskB = A.tile([128, 128], F32, "maskB")
        self.mLT = A.tile([128, 128], F32, "mLT")
        self.mGT = A.tile([128, 128], F32, "mGT")
        self.bigP = [A.tile([128, 128], F32, "bigP") for _ in range(4)]
        self.bigN = [A.tile([128, 128], F32, "bigN") for _ in range(4)]
        self.maskF4 = A.tile([128, 4, 128], F32, "maskF4")
        self.maskB4 = A.tile([128, 4, 128], F32, "maskB4")
        A.push()
        scr = A.tile([128, 128], F32, "scr")
        raw = A.tile([128, 32], F32, "lbraw")
        self.load_T(self.inp["hgrn_lower_bounds"].rearrange("l d (c p) -> (l d c) p", p=128), 32, raw[:, :], "lbraw", scr)
        P.op("pool", lambda e: e.memset(self.lb[:, 0, :], 0.0), writes=["lb0"])
        P.op("dve", lambda e: e.tensor_tensor(out=self.lb[:, 1, :], in0=raw[:, 16:32], in1=raw[:, 0:16], op=ALU.subtract),
             reads=["lbraw"], writes=["lb1"])
        P.op("act", lambda e: e.activation(out=self.lb[:, 1, :], in_=self.lb[:, 1, :], func=AF.Sigmoid), reads=["lb1"], writes=["lb1"])
        P.op("dve", lambda e: e.tensor_scalar(out=self.oml[:].rearrange("p a b -> p (a b)"), in0=self.lb[:].rearrange("p a b -> p (a b)"),
                                              scalar1=-1.0, scalar2=1.0, op0=ALU.mult, op1=ALU.add),
             reads=["lb0", "lb1"], writes=["oml"])
        self.load_T(self.inp["hgrn_norm_g"], 2, self.hng[:, :], "hng", scr)
        self.load_T(self.inp["dn_norm_g"], 2, self.dng[:, :], "dng", scr)
        for l in range(2):
            self.load_T(self.inp["dn_conv_w"][l].rearrange("j (c p) -> (j c) p", p=128), 120, self.convT[:, l, :], ("convT", l), scr)
        P.op("sp", lambda e: e.dma_start(out=self.alog[:].rearrange("p a b -> p (a b)"),
                                         in_=self.inp["dn_a_log"].rearrange("l d h -> (l d h)").partition_broadcast(128)),
             writes=["alog"], lane="smallc")
        P.op("sp", lambda e: e.dma_start(out=self.dtb[:].rearrange("p a b -> p (a b)"),
                                         in_=self.inp["dn_dt_bias"].rearrange("l d h -> (l d h)").partition_broadcast(128)),
             writes=["dtb"], lane="smallc")
        P.op("act", lambda e: e.activation(out=self.alog[:].rearrange("p a b -> p (a b)"), in_=self.alog[:].rearrange("p a b -> p (a b)"), func=AF.Exp),
             reads=["alog"], writes=["alog"])
        P.op("dve", lambda e: e.tensor_scalar(out=self.alog[:].rearrange("p a b -> p (a b)"), in0=self.alog[:].rearrange("p a b -> p (a b)"),
                                              scalar1=-1.0, scalar2=None, op0=ALU.mult), reads=["alog"], writes=["alog"])
        P.op("pool", lambda e: e.affine_select(out=self.maskF[:], in_=self.ones_f[:], pattern=[[1, 128]],
                                               compare_op=ALU.is_ge, fill=0.0, base=0, channel_multiplier=-1),
             reads=["ones_f"], writes=["maskF"])
        P.op("pool", lambda e: e.memset(self.maskF[0:64, 64:128], 0.0), reads=["maskF"], writes=["maskF"])
        P.op("pool", lambda e: e.affine_select(out=self.maskB[:], in_=self.ones_f[:], pattern=[[-1, 128]],
                                               compare_op=ALU.is_ge, fill=0.0, base=0, channel_multiplier=1),
             reads=["ones_f"], writes=["maskB"])
        P.op("pool", lambda e: e.memset(self.maskB[64:128, 0:64], 0.0), reads=["maskB"], writes=["maskB"])
        P.op("pool", lambda e: e.tensor_tensor(out=self.mLT[:], in0=self.maskB[:], in1=self.ident[:], op=ALU.subtract), reads=["maskB", "ident"], writes=["mLT"])
        P.op("pool", lambda e: e.tensor_tensor(out=self.mGT[:], in0=self.maskF[:], in1=self.ident[:], op=ALU.subtract), reads=["maskF", "ident"], writes=["mGT"])
        for i, (m, mt) in enumerate(((self.mLT, "mLT"), (self.mGT, "mGT"), (self.maskF, "maskF"), (self.maskB, "maskB"))):
            P.op("dve", lambda e, i=i, m=m: e.tensor_scalar(out=self.bigP[i][:], in0=m[:], scalar1=-30000.0, scalar2=30000.0, op0=ALU.mult, op1=ALU.add),
                 reads=[mt], writes=["bigm"])
            P.op("dve", lambda e, i=i, m=m: e.tensor_scalar(out=self.bigN[i][:], in0=m[:], scalar1=30000.0, scalar2=-30000.0, op0=ALU.mult, op1=ALU.add),
                 reads=[mt], writes=["bigm"])
        for j in range(4):
            P.op("pool", lambda e, j=j: e.tensor_copy(out=self.maskF4[:, j, :], in_=self.maskF[:]), reads=["maskF"], writes=["mask4"])
            P.op("pool", lambda e, j=j: e.tensor_copy(out=self.maskB4[:, j, :], in_=self.maskB[:]), reads=["maskB"], writes=["mask4"])
        A.pop()
        P.barrier()

    def mixer_prep(self, l):
        A = self.A
        A.push()
        hT = A.tile([128, NKC, 1024], BF16, "hTm")
        for (t0, G, v) in self.groups():
            self.mixer_prep_group(l, t0, G, v, hT)
        A.pop()
        for j in range(4):
            self.P.op("pool", lambda e, j=j: e.collective_compute(
                "AllGather", ALU.bypass, replica_groups=[[0, 1], [2, 3], [4, 5], [6, 7]],
                ins=[self.hT_d[4 * j:4 * j + 4].rearrange("c p n -> (c p) n")], outs=[self.hT_all_d[j].rearrange("r c p n -> (r c p) n")]),
                reads=[("hTd", t) for t in range(NTO // 128)], writes=[("hTall", j)], lane="cc_h", step=1)
        self.P.barrier()

    def mixer_prep_group(self, l, t0, G, v, hT):
        P = self.P
        u = self.new("mp")
        self.prologue(t0, G, self.gmod[:, l, 1, v, :], self.shift[:, l, 1, v, :], hT, (u, "hT"))
        P.op("sp", lambda e: e.dma_start(out=self.hT_d[:, :, t0:t0 + G].rearrange("c p n -> p c n"), in_=hT[:, :, 0:G]),
             reads=[((u, "hT"), c) for c in range(NKC)], writes=[("hTd", t) for t in range(t0 // 128, (t0 + G) // 128)], lane="hTd")
        P.barrier()

    TG = [(0, 384), (384, 384), (768, 384), (1152, 384), (1536, 384), (1920, 384)]

    def proj_fm(self, w, hg, G, evac):
        P = self.P
        bank = self.nb()
        wt, wtok = w
        hgt, hgtok = hg
        for k in range(NKC):
            P.op("pe", lambda e, k=k: e.matmul(self.ps[bank][:, 0:G], lhsT=wt[:, k, :], rhs=hgt[:, k, 0:G],
                                               start=(k == 0), stop=(k == NKC - 1)),
                 reads=[wtok] + [hgtok + (j,) for j in range(4)], writes=[("ps", bank)])
        evac(bank)

    def proj_tm(self, w, hg, G, ncols, evac):
        P = self.P
        wt, wtok = w
        hgt, hgtok = hg
        for ti in range(G // 128):
            bank = self.nb()
            for k in range(NKC):
                P.op("pe", lambda e, k=k, ti=ti, bank=bank: e.matmul(self.ps[bank][:, 0:ncols], lhsT=hgt[:, k, ti * 128:(ti + 1) * 128],
                                                                   rhs=wt[:, k, 0:ncols], start=(k == 0), stop=(k == NKC - 1)),
                     reads=[wtok] + [hgtok + (j,) for j in range(4)], writes=[("ps", bank)])
            evac(bank, ti)

    def load_w_in(self, l, col, ncols, tile_, tok, lane):
        wv = self.inp["w_in"][l].rearrange("(kc p) n -> p kc n", p=128)
        self.P.op("pool", lambda e: e.dma_start(out=tile_[:, :, 0:ncols], in_=wv[:, :, col:col + ncols]), writes=[tok], lane=lane)

    def load_hg(self, hg, slot, t0, G, u):
        r, off = t0 // NTO, t0 % NTO
        for j in range(4):
            self.P.op("sp", lambda e, j=j: e.dma_start(out=hg[slot][:, 4 * j:4 * j + 4, 0:G],
                                                       in_=self.hT_all_d[j, r][:, :, off:off + G].rearrange("c p n -> p c n")),
                      reads=[("hTall", j)], writes=[(u, "hg", slot, j)], lane=f"hg{slot}")

    def hgrn_head(self, l, h):
        P, A = self.P, self.A
        u = self.new("hg")
        A.push()
        qa = A.tile([128, NT], F32, "qa")
        lf = [A.tile([128, NT], F32, "lf") for _ in range(2)]
        kk = [A.tile([128, NT], F32, "kk") for _ in range(2)]
        ga = A.tile([128, NT], F32, "ga")
        Vt = A.tile([128, NT // 128, 128], BF16, "Vt")
        oT = A.tile([128, NT], F32, "oT")
        self.hgrn_proj(l, h, u, qa, lf, kk, ga, Vt)
        if "hg_stop1" in self.stages:
            self.dump_tile("dbg_qa", qa[:], [128, NT], [(u, "qa", gi) for gi in range(NG)])
            self.dump_tile("dbg_lf0", lf[0][:], [128, NT], [(u, "lf", 0, gi) for gi in range(NG)])
            self.dump_tile("dbg_kk1", kk[1][:], [128, NT], [(u, "kk", 1, gi) for gi in range(NG)])
            self.dump_tile("dbg_Vt", Vt[:], [128, NT // 128, 128], [(u, "Vt", t) for t in range(18)])
            A.pop()
            P.barrier()
            return
        for d in range(2):
            self.hgrn_dir(l, h, u, d, qa, lf[d], kk[d], Vt, oT)
            if "hg_stop2" in self.stages:
                self.dump_tile("dbg_oT0", oT[:], [128, NT], [(u, "oT", q) for q in range(5)])
                A.pop()
                P.barrier()
                return
        self.head_out(u, oT, ga, "ga", self.hng[:, l:l + 1], self.yaT_d[h], ("ya", h))
        if "dump_oa" in self.stages and l == 0:
            self.dump_tile(f"dbg_oa{h}", oT[:], [128, NT], [(u, "oT", q) for q in range(5)])
        A.pop()
        P.barrier()

    def hgrn_proj(self, l, h, u, qa, lf, kk, ga, Vt):
        P, A = self.P, self.A
        A.push()
        cols = [0 + 128 * h, 1024 + 128 * h, 2048 + 128 * h, 3072 + 128 * h, 4096 + 128 * h]
        w = [A.tile([128, NKC, 128], BF16, "wm") for _ in range(5)]
        hg = [A.tile([128, NKC, 512], BF16, "hgm") for _ in range(2)]
        sgm = [A.tile([128, 512], F32, "sgm") for _ in range(2)]
        for i in range(5):
            self.load_w_in(l, cols[i], 128, w[i], (u, "w", i), f"wm{i}")
        for gi, (t0, G) in enumerate(self.TG):
            self.hgrn_proj_group(l, h, u, gi, t0, G, w, hg, sgm, qa, lf, kk, ga, Vt)
        A.pop()
        P.barrier()

    def hgrn_proj_group(self, l, h, u, gi, t0, G, w, hg, sgm, qa, lf, kk, ga, Vt):
        P = self.P
        slot = gi % 2
        self.load_hg(hg, slot, t0, G, u)
        hgs = (hg[slot], (u, "hg", slot))
        tl = (u, "g", gi)
        self.proj_fm((w[0], (u, "w", 0)), hgs, G, lambda bank: P.op(
            "act", lambda e: e.activation(out=qa[:, t0:t0 + G], in_=self.ps[bank][:, 0:G], func=AF.Silu),
            reads=[("ps", bank)], writes=[(u, "qa", gi)]))
        for d in range(2):
            idx = d * 8 + h

            def ev(bank, d=d, idx=idx):
                P.op("act", lambda e: e.activation(out=sgm[d][:, 0:G], in_=self.ps[bank][:, 0:G], func=AF.Sigmoid),
                     reads=[("ps", bank)], writes=[(u, "sgm", d)])
                P.op("dve", lambda e: e.tensor_scalar(out=sgm[d][:, 0:G], in0=sgm[d][:, 0:G], scalar1=self.oml[:, l, idx:idx + 1],
                                                      scalar2=self.lb[:, l, idx:idx + 1], op0=ALU.mult, op1=ALU.add),
                     reads=[(u, "sgm", d), "oml", "lb0", "lb1"], writes=[(u, "sgm", d)])
                P.op("dve", lambda e: e.tensor_scalar(out=kk[d][:, t0:t0 + G], in0=sgm[d][:, 0:G], scalar1=-1.0, scalar2=1.0,
                                                      op0=ALU.mult, op1=ALU.add),
                     reads=[(u, "sgm", d)], writes=[(u, "kk", d, gi)])
                P.op("act", lambda e: e.activation(out=lf[d][:, t0:t0 + G], in_=sgm[d][:, 0:G], func=AF.Ln),
                     reads=[(u, "sgm", d)], writes=[(u, "lf", d, gi)])
            self.proj_fm((w[1 + d], (u, "w", 1 + d)), hgs, G, ev)
        self.proj_fm((w[4], (u, "w", 4)), hgs, G, lambda bank: P.op(
            "act", lambda e: e.activation(out=ga[:, t0:t0 + G], in_=self.ps[bank][:, 0:G], func=AF.Silu),
            reads=[("ps", bank)], writes=[(u, "ga", gi)]))
        self.proj_tm((w[3], (u, "w", 3)), hgs, G, 128, lambda bank, ti: P.op(
            "dve", lambda e: e.tensor_copy(out=Vt[:, t0 // 128 + ti, :], in_=self.ps[bank][:, 0:128]),
            reads=[("ps", bank)], writes=[(u, "Vt", t0 // 128 + ti)]))

    def chain_pos(self, d, c):
        if d == 0:
            return c
        return 3 - c if c < 4 else 4 + (35 - c)

    def hgrn_dir(self, l, h, u0, d, qa, lf, kk, Vt, oT):
        P, A = self.P, self.A
        u = self.new("hd")
        NC = NT // 64
        A.push()
        X = A.tile([128, NT], F32, "X")
        dd = A.tile([128, NT], F32, "dd")
        ex = A.tile([128, NT], F32, "ex")
        Qt = A.tile([128, NT], BF16, "Qt")
        Kt = A.tile([128, NT], BF16, "Kt")
        Qi = A.tile([128, NT], BF16, "Qi")
        KuT = A.tile([128, NT], BF16, "KuT")
        Kutok = A.tile([128, NT // 128, 128], BF16, "Kutok")
        ATm = A.tile([128, NT // 128, 128], BF16, "ATm")
        U = A.tile([128, 128, NC], F32, "U")
        decb = A.tile([128, 128, NC], F32, "decb")
        R = A.tile([128, 128, NC], F32, "R")
        S = A.tile([128, NC, 128], BF16, "S")
        refs = A.tile([128, 4, NC], F32, "refs")
        allg = lambda name, dd_=None: [(u0, name, d, gi) for gi in range(5)] if dd_ is None else None
        lf_r = [(u0, "lf", d, gi) for gi in range(NG)]
        kk_r = [(u0, "kk", d, gi) for gi in range(NG)]
        qa_r = [(u0, "qa", gi) for gi in range(NG)]
        decf = decb[:].rearrange("p a b -> p (a b)")
        P.op("pool", lambda e: e.memset(decf[:, 0:NT], 1.0), writes=[(u, "decb")])
        P.op("dve", lambda e: e.tensor_tensor_scan(out=X[:], data0=decf[:, 0:NT], data1=lf[:], initial=0.0,
                                                   op0=ALU.mult, op1=ALU.add), reads=lf_r + [(u, "decb")], writes=[(u, "X")])
        X3 = X[:].rearrange("p (c t) -> p c t", t=64)
        lf3 = lf[:].rearrange("p (c t) -> p c t", t=64)
        if d == 0:
            P.op("dve", lambda e: e.tensor_copy(out=refs[:, 0, :], in_=X3[:, :, 31]), reads=[(u, "X")], writes=[(u, "rm")])
            P.op("dve", lambda e: e.tensor_tensor(out=refs[:, 1, :], in0=X3[:, :, 0], in1=lf3[:, :, 0], op=ALU.subtract),
                 reads=[(u, "X")] + lf_r, writes=[(u, "r0")])
            P.op("dve", lambda e: e.tensor_copy(out=refs[:, 2, :], in_=X3[:, :, 63]), reads=[(u, "X")], writes=[(u, "r1")])
        else:
            P.op("dve", lambda e: e.tensor_copy(out=refs[:, 2, :], in_=X3[:, :, 63]), reads=[(u, "X")], writes=[(u, "r1")])
            P.op("dve", lambda e: e.tensor_tensor(out=X[:], in0=X[:], in1=lf[:], op=ALU.subtract),
                 reads=[(u, "X"), (u, "r1")] + lf_r, writes=[(u, "X")])
            P.op("dve", lambda e: e.tensor_copy(out=refs[:, 0, :], in_=X3[:, :, 32]), reads=[(u, "X")], writes=[(u, "rm")])
            P.op("dve", lambda e: e.tensor_copy(out=refs[:, 1, :], in_=X3[:, :, 0]), reads=[(u, "X")], writes=[(u, "r0")])
        sig = 1.0 if d == 0 else -1.0
        rA = 1 if d == 0 else 2
        rB = 2 if d == 0 else 1
        dd3 = dd[:].rearrange("p (c t) -> p c t", t=64)

        def derive(ri, rtok, outs):
            P.op("dve", lambda e: e.tensor_tensor(out=dd3, in0=X3, in1=refs[:, ri, :].unsqueeze(2).to_broadcast([128, NC, 64]),
                                                  op=ALU.subtract), reads=[(u, "X"), rtok], writes=[(u, "dd")])
            for (sg_, src, sreads, dst, dtok) in outs:
                P.op("act", lambda e, sg_=sg_: e.activation(out=ex[:], in_=dd[:], func=AF.Exp, scale=sg_),
                     reads=[(u, "dd")], writes=[(u, "ex")])
                P.op("dve", lambda e, src=src, dst=dst: e.tensor_tensor(out=dst[:], in0=src[:], in1=ex[:], op=ALU.mult),
                     reads=[(u, "ex")] + sreads, writes=[dtok])
        derive(0, (u, "rm"), [(sig, qa, qa_r, Qt, (u, "Qt")), (-sig, kk, kk_r, Kt, (u, "Kt"))])
        derive(rA, (u, "r0") if rA == 1 else (u, "r1"), [(sig, qa, qa_r, Qi, (u, "Qi"))])
        derive(rB, (u, "r0") if rB == 1 else (u, "r1"), [(-sig, kk, kk_r, KuT, (u, "KuT"))])
        P.op("dve", lambda e: e.tensor_tensor(out=refs[:, 3, :], in0=refs[:, 2, :], in1=refs[:, 1, :], op=ALU.subtract),
             reads=[(u, "r0"), (u, "r1")], writes=[(u, "dec")])
        P.op("act", lambda e: e.activation(out=refs[:, 3, :], in_=refs[:, 3, :], func=AF.Exp), reads=[(u, "dec")], writes=[(u, "dec")])
        if d == 0:
            P.op("dve", lambda e: e.tensor_copy(out=decb[:], in_=refs[:, 3, :].unsqueeze(1).to_broadcast([128, 128, NC])),
                 reads=[(u, "dec")], writes=[(u, "decb")])
        else:
            for c in range(NC):
                pos = self.chain_pos(1, c)
                P.op("act", lambda e, c=c, pos=pos: e.activation(out=refs[:, 0, pos:pos + 1], in_=refs[:, 3, c:c + 1], func=AF.Copy),
                     reads=[(u, "dec"), (u, "Qt"), (u, "Kt")], writes=[(u, "decc")])
            P.op("dve", lambda e: e.tensor_copy(out=decb[:], in_=refs[:, 0, :].unsqueeze(1).to_broadcast([128, 128, NC])),
                 reads=[(u, "decc")], writes=[(u, "decb")])
        P.op("dve", lambda e: e.memset(decb[:, :, 0], 0.0), reads=[(u, "decb")], writes=[(u, "decb")])
        for q in range(5):
            bank = self.nb()
            nt_ = 4 if q < 4 else 2
            psb = self.ps[bank][:].bitcast(BF16)
            for j in range(nt_):
                t = q * 4 + j
                P.op("pe", lambda e, j=j, t=t, psb=psb: e.transpose(psb[:, j * 128:(j + 1) * 128], KuT[:, t * 128:(t + 1) * 128], self.ident_b[:]),
                     reads=[(u, "KuT"), "ident_b"], writes=[("ps", bank)])
            P.op("act", lambda e, q=q, nt_=nt_, psb=psb: e.activation(
                out=Kutok[:, q * 4:q * 4 + nt_, :], in_=psb[:, 0:nt_ * 128].rearrange("p (a b) -> p a b", b=128), func=AF.Copy),
                reads=[("ps", bank)], writes=[(u, "Kutok", q)])
        mask4 = self.maskF4 if d == 0 else self.maskB4
        P.op("pool", lambda e: e.memset(ATm[:], 0.0), writes=[(u, "ATz")])
        for q in range(5):
            bank = self.nb()
            nt_ = 4 if q < 4 else 2
            for j in range(nt_):
                t = q * 4 + j
                P.op("pe", lambda e, j=j, t=t, bank=bank: e.matmul(self.ps[bank][:, j * 128:(j + 1) * 128], lhsT=Kt[:, t * 128:(t + 1) * 128],
                                                                  rhs=Qt[:, t * 128:(t + 1) * 128], start=True, stop=True),
                     reads=[(u, "Kt"), (u, "Qt")], writes=[("ps", bank)])
            P.op("dve", lambda e, q=q, nt_=nt_, bank=bank: e.copy_predicated(
                out=ATm[:, q * 4:q * 4 + nt_, :], mask=mask4[:, 0:nt_, :].bitcast(mybir.dt.uint32),
                data=self.ps[bank][:, 0:nt_ * 128].rearrange("p (a b) -> p a b", b=128)),
                reads=[("ps", bank), "mask4", (u, "ATz")], writes=[(u, "ATm", q)])
        for c in range(NC):
            t, half = c // 2, c % 2
            if c % 8 == 0:
                bankpair = (self.nb(), self.nb())
            bank = bankpair[half]
            j = (c // 2) % 4
            r0_, r1_ = half * 64, half * 64 + 64
            P.op("pe", lambda e, t=t, j=j, r0_=r0_, r1_=r1_, bank=bank: e.matmul(
                self.ps[bank][:, j * 128:(j + 1) * 128], lhsT=Kutok[r0_:r1_, t, :], rhs=Vt[r0_:r1_, t, :], start=True, stop=True),
                reads=[(u, "Kutok", t // 4), (u0, "Vt", t)], writes=[("ps", bank)])
            pos = self.chain_pos(d, c)
            eng = "act" if c % 2 == 0 else "dve"
            if eng == "act":
                P.op("act", lambda e, j=j, pos=pos, bank=bank: e.activation(out=U[:, :, pos], in_=self.ps[bank][:, j * 128:(j + 1) * 128], func=AF.Copy),
                     reads=[("ps", bank)], writes=[(u, "U", c)])
            else:
                P.op("dve", lambda e, j=j, pos=pos, bank=bank: e.tensor_copy(out=U[:, :, pos], in_=self.ps[bank][:, j * 128:(j + 1) * 128]),
                     reads=[("ps", bank)], writes=[(u, "U", c)])
        P.op("dve", lambda e: e.tensor_tensor_scan(out=R[:].rearrange("p a b -> p (a b)"), data0=decb[:].rearrange("p a b -> p (a b)"),
                                                   data1=U[:].rearrange("p a b -> p (a b)"), initial=0.0, op0=ALU.mult, op1=ALU.add),
             reads=[(u, "decb")] + [(u, "U", c) for c in range(NC)], writes=[(u, "R")])
        P.op("pool", lambda e: e.memset(S[:, 0, :], 0.0), writes=[(u, "S0")])
        P.op("act", lambda e: e.activation(out=S[:, 1:NC, :], in_=R[:, :, 0:NC - 1].rearrange("p d c -> p c d"), func=AF.Copy),
             reads=[(u, "R")], writes=[(u, "S")])
        for q in range(5):
            bank = self.nb()
            nt_ = 4 if q < 4 else 2
            for j in range(nt_):
                t = q * 4 + j
                P.op("pe", lambda e, j=j, t=t, bank=bank: e.matmul(self.ps[bank][:, j * 128:(j + 1) * 128], lhsT=Vt[:, t, :], rhs=ATm[:, t, :],
                                                                  start=True, stop=False),
                     reads=[(u0, "Vt", t), (u, "ATm", q)], writes=[("ps", bank)])
                for half in range(2):
                    c = 2 * t + half
                    pos = self.chain_pos(d, c)
                    P.op("pe", lambda e, j=j, c=c, pos=pos, half=half, bank=bank: e.matmul(
                        self.ps[bank][:, j * 128 + half * 64:j * 128 + half * 64 + 64], lhsT=S[:, pos, :], rhs=Qi[:, c * 64:(c + 1) * 64],
                        start=False, stop=(half == 1)),
                        reads=[(u, "S"), (u, "S0"), (u, "Qi")], writes=[("ps", bank)])
            c0_, c1_ = q * 512, q * 512 + nt_ * 128
            if d == 0:
                P.op("act", lambda e, c0_=c0_, c1_=c1_, nt_=nt_, bank=bank: e.activation(out=oT[:, c0_:c1_], in_=self.ps[bank][:, 0:nt_ * 128], func=AF.Copy),
                     reads=[("ps", bank)], writes=[(u0, "oT", q)])
            else:
                P.op("dve", lambda e, c0_=c0_, c1_=c1_, nt_=nt_, bank=bank: e.tensor_tensor(out=oT[:, c0_:c1_], in0=self.ps[bank][:, 0:nt_ * 128],
                                                                                      in1=oT[:, c0_:c1_], op=ALU.add),
                     reads=[("ps", bank), (u0, "oT", q)], writes=[(u0, "oT", q)])
        A.pop()
        P.barrier()

    def head_out(self, u, oT, gate_t, gate_tok, g_ap, dst_d, dtok):
        P, A = self.P, self.A
        A.push()
        sq = [A.tile([128, 512], F32R, "hsq") for _ in range(2)]
        rs = [A.tile([128, 512], F32, "hrs") for _ in range(2)]
        yo = [A.tile([128, 512], BF16, "hyo") for _ in range(2)]
        for gi, (t0, G) in enumerate(self.TG):
            s = gi % 2
            bank = self.nb()
            P.op("act", lambda e, s=s, t0=t0, G=G: e.activation(out=sq[s][:, 0:G], in_=oT[:, t0:t0 + G], func=AF.Square),
                 reads=[(u, "oT", q) for q in range(5)], writes=[(u, "hsq", s)])
            P.op("pe", lambda e, s=s, G=G, bank=bank: e.matmul(self.ps[bank][:, 0:G], lhsT=self.ones_r[:], rhs=sq[s][:, 0:G], start=True, stop=True),
                 reads=[(u, "hsq", s), "ones_r"], writes=[("ps", bank)])
            P.op("act", lambda e, s=s, G=G, bank=bank: e.activation(out=rs[s][:, 0:G], in_=self.ps[bank][:, 0:G], func=AF.Ln, bias=self.epsc[:, 0:1], scale=1.0 / 128),
                 reads=[("ps", bank), "epsc"], writes=[(u, "hrs", s)])
            P.op("act", lambda e, s=s, G=G: e.activation(out=rs[s][:, 0:G], in_=rs[s][:, 0:G], func=AF.Exp, scale=-0.5), reads=[(u, "hrs", s)], writes=[(u, "hrs", s)])
            P.op("dve", lambda e, s=s, t0=t0, G=G: e.scalar_tensor_tensor(out=rs[s][:, 0:G], in0=rs[s][:, 0:G], scalar=g_ap, in1=gate_t[:, t0:t0 + G],
                                                                        op0=ALU.mult, op1=ALU.mult),
                 reads=[(u, "hrs", s), (u, gate_tok, gi), "hng", "dng"], writes=[(u, "hrs", s)])
            P.op("dve", lambda e, s=s, t0=t0, G=G: e.tensor_tensor(out=yo[s][:, 0:G], in0=oT[:, t0:t0 + G], in1=rs[s][:, 0:G], op=ALU.mult),
                 reads=[(u, "hrs", s)] + [(u, "oT", q) for q in range(5)], writes=[(u, "hyo", s)])
            P.op("sp", lambda e, s=s, t0=t0, G=G: e.dma_start(out=dst_d[:, t0:t0 + G], in_=yo[s][:, 0:G]),
                 reads=[(u, "hyo", s)], writes=[(dtok, gi)], lane=f"hyo{s}")
        A.pop()

    def dn_scalars(self, l):
        P, A = self.P, self.A
        NTL = NT // 128
        self.dn_beta = A.tile([128, NTL, 16], F32, "dn_beta")
        self.dn_gc = A.tile([128, NTL, 16], F32, "dn_gc")
        self.dn_egc = A.tile([128, NTL, 16], F32, "dn_egc")
        self.dn_egl = A.tile([128, NTL, 16], F32, "dn_egl")
        self.dn_bg = A.tile([128, NTL, 16], F32, "dn_bg")
        self.dn_dl = A.tile([128, NTL, 2, 16], F32, "dn_dl")
        u = self.new("dns")
        A.push()
        wab = A.tile([128, NKC, 32], BF16, "wab")
        hg = [A.tile([128, NKC, 512], BF16, "hgs") for _ in range(2)]
        ab = A.tile([128, NTL, 32], F32, "ab")
        g = A.tile([128, NTL, 16], F32, "gdn")
        tmp = A.tile([128, NTL, 16], F32, "tdn")
        maskC = A.tile([128, 128], F32, "maskC")
        maskLo = A.tile([128, 128], F32, "maskLo")
        maskHi = A.tile([128, 128], F32, "maskHi")
        P.op("pool", lambda e: e.memset(maskC[:], 0.0), writes=[(u, "mC")])
        P.op("pool", lambda e: e.memset(maskC[0:64, 0:64], 1.0), reads=[(u, "mC")], writes=[(u, "mC")])
        P.op("pool", lambda e: e.memset(maskC[64:128, 64:128], 1.0), reads=[(u, "mC")], writes=[(u, "mC")])
        P.op("pool", lambda e: e.memset(maskLo[:], 0.0), writes=[(u, "mLo")])
        P.op("pool", lambda e: e.memset(maskLo[0:64, :], 1.0), reads=[(u, "mLo")], writes=[(u, "mLo")])
        P.op("pool", lambda e: e.memset(maskHi[:], 0.0), writes=[(u, "mHi")])
        P.op("pool", lambda e: e.memset(maskHi[64:128, :], 1.0), reads=[(u, "mHi")], writes=[(u, "mHi")])
        self.load_w_in(l, 9216, 32, wab, (u, "wab"), "wab")
        for gi, (t0, G) in enumerate(self.TG):
            self.dn_scalars_group(u, gi, t0, G, wab, hg, ab)
        abr = [(u, "ab", t) for t in range(NTL)]
        P.op("dve", lambda e: e.tensor_tensor(out=tmp[:], in0=ab[:, :, 0:16], in1=self.dtb[:, l, :].unsqueeze(1).to_broadcast([128, NTL, 16]), op=ALU.add),
             reads=abr + ["dtb"], writes=[(u, "tmp")])
        P.op("act", lambda e: e.activation(out=tmp[:], in_=tmp[:], func=AF.Exp), reads=[(u, "tmp")], writes=[(u, "tmp")])
        P.op("act", lambda e: e.activation(out=tmp[:], in_=tmp[:], func=AF.Ln, bias=1.0), reads=[(u, "tmp")], writes=[(u, "tmp")])
        P.op("dve", lambda e: e.tensor_tensor(out=g[:], in0=tmp[:], in1=self.alog[:, l, :].unsqueeze(1).to_broadcast([128, NTL, 16]), op=ALU.mult),
             reads=[(u, "tmp"), "alog"], writes=[(u, "g")])
        P.op("act", lambda e: e.activation(out=self.dn_beta[:], in_=ab[:, :, 16:32], func=AF.Sigmoid), reads=abr, writes=["dn_beta"])
        for t in range(NTL):
            bank = self.nb()
            ps = self.ps[bank]
            for (m, mt, c0, c1, o0) in ((self.maskF, "maskF", 0, 8, 0), (self.maskB, "maskB", 8, 16, 8), (maskC, (u, "mC"), 0, 16, 16),
                                        (maskLo, (u, "mLo"), 0, 16, 32), (maskHi, (u, "mHi"), 0, 16, 48)):
                P.op("pe", lambda e, m=m, c0=c0, c1=c1, o0=o0, t=t, ps=ps: e.matmul(ps[:, o0:o0 + (c1 - c0)], lhsT=m[:], rhs=g[:, t, c0:c1], start=True, stop=True),
                     reads=[mt, (u, "g")], writes=[("ps", bank)])
            P.op("dve", lambda e, t=t, ps=ps: e.tensor_copy(out=self.dn_gc[:, t, :], in_=ps[:, 0:16]), reads=[("ps", bank)], writes=[("dn_gc", t)])
            P.op("act", lambda e, t=t, ps=ps: e.activation(out=self.dn_egc[:, t, :], in_=ps[:, 0:16], func=AF.Exp), reads=[("ps", bank)], writes=[("dn_egc", t)])
            P.op("dve", lambda e, t=t, ps=ps: e.tensor_tensor(out=self.dn_egl[:, t, :], in0=ps[:, 16:32], in1=self.dn_gc[:, t, :], op=ALU.subtract),
                 reads=[("ps", bank), ("dn_gc", t)], writes=[("dn_egl", t)])
            P.op("act", lambda e, t=t: e.activation(out=self.dn_egl[:, t, :], in_=self.dn_egl[:, t, :], func=AF.Exp), reads=[("dn_egl", t)], writes=[("dn_egl", t)])
            P.op("act", lambda e, t=t, ps=ps: e.activation(out=self.dn_dl[:, t, :, :], in_=ps[:, 32:64].rearrange("p (a b) -> p a b", a=2), func=AF.Exp),
                 reads=[("ps", bank)], writes=[("dn_dl", t)])
            P.op("dve", lambda e, t=t: e.tensor_tensor(out=self.dn_bg[:, t, :], in0=self.dn_beta[:, t, :], in1=self.dn_egc[:, t, :], op=ALU.mult),
                 reads=["dn_beta", ("dn_egc", t)], writes=[("dn_bg", t)])
        A.pop()
        P.barrier()

    def dn_scalars_group(self, u, gi, t0, G, wab, hg, ab):
        P = self.P
        slot = gi % 2
        self.load_hg(hg, slot, t0, G, u)
        self.proj_tm((wab, (u, "wab")), (hg[slot], (u, "hg", slot)), G, 32, lambda bank, ti: P.op(
            "dve", lambda e: e.tensor_copy(out=ab[:, t0 // 128 + ti, :], in_=self.ps[bank][:, 0:32]),
            reads=[("ps", bank)], writes=[(u, "ab", t0 // 128 + ti)]))

    def dn_head(self, l, h):
        P, A = self.P, self.A
        u = self.new("dn")
        A.push()
        QnT = A.tile([128, NT], BF16, "QnT")
        KnT = A.tile([128, NT], BF16, "KnT")
        Ktok = A.tile([128, NT // 128, 128], BF16, "Ktok")
        Vtok = A.tile([128, NT // 128, 128], BF16, "Vtok")
        gb = A.tile([128, NT], F32, "gb")
        oT = A.tile([128, NT], F32, "oTd")
        self.dn_proj(l, h, u, QnT, KnT, Ktok, Vtok, gb)
        if "dn_stop1" in self.stages:
            self.dump_tile("dbg_QnT", QnT[:], [128, NT], [(u, "QnT", gi) for gi in range(NG)])
            self.dump_tile("dbg_KnT", KnT[:], [128, NT], [(u, "KnT", gi) for gi in range(NG)])
            self.dump_tile("dbg_Vtok", Vtok[:], [128, NT // 128, 128], [(u, "Vtok", q) for q in range(5)])
            A.pop()
            P.barrier()
            return
        for d in range(2):
            self.dn_dir(l, h, u, d, QnT, KnT, Ktok, Vtok, oT)
        if "dump_ob" in self.stages and l == 0:
            self.dump_tile(f"dbg_ob{h}", oT[:], [128, NT], [(u, "oT", q) for q in range(5)])
        self.head_out(u, oT, gb, "gb", self.dng[:, l:l + 1], self.ybT_d[h], ("yb", h))
        A.pop()
        P.barrier()

    def dn_proj(self, l, h, u, QnT, KnT, Ktok, Vtok, gb):
        P, A = self.P, self.A
        A.push()
        cols = [5120 + 128 * h, 6144 + 128 * h, 7168 + 128 * h, 8192 + 128 * h]
        raw = [A.tile([128, NT], F32, "raw") for _ in range(3)]
        cv = [A.tile([128, NT], F32, "cv") for _ in range(3)]
        VnT = A.tile([128, NT], BF16, "VnT")
        ctmp = A.tile([128, NT], F32, "ctmp")
        A.push()
        w = [A.tile([128, NKC, 128], BF16, "wd_") for _ in range(4)]
        hg = [A.tile([128, NKC, 512], BF16, "hgd") for _ in range(2)]
        for i in range(4):
            self.load_w_in(l, cols[i], 128, w[i], (u, "w", i), f"wm{i}")
        for gi, (t0, G) in enumerate(self.TG):
            self.dn_proj_group(u, gi, t0, G, w, hg, raw, gb)
        A.pop()
        P.barrier()
        sq = [A.tile([128, 512], F32R, "dsq") for _ in range(2)]
        rn = [A.tile([128, 512], F32, "drn") for _ in range(2)]
        for i in range(3):
            self.dn_conv(l, u, i, i * 8 + h, raw[i], cv[i], ctmp)
        for i in range(2):
            dst = QnT if i == 0 else KnT
            nm = "QnT" if i == 0 else "KnT"
            scl = 128.0 ** -0.5 if i == 0 else 1.0
            for gi, (t0, G) in enumerate(self.TG):
                self.dn_l2(u, i, gi, t0, G, cv[i], sq, rn, dst, nm, scl)
        P.op("act", lambda e: e.activation(out=VnT[:], in_=cv[2][:], func=AF.Copy), reads=[(u, "cv", 2)], writes=[(u, "VnT")])
        for (src, srd, dst, nm) in ((KnT, [(u, "KnT", gi) for gi in range(NG)], Ktok, "Ktok"), (VnT, [(u, "VnT")], Vtok, "Vtok")):
            for q in range(5):
                self.tr_group(u, src, srd, dst, nm, q)
        A.pop()
        P.barrier()

    def tr_group(self, u, src, srd, dst, nm, q):
        P = self.P
        bank = self.nb()
        nt_ = 4 if q < 4 else 2
        psb = self.ps[bank][:].bitcast(BF16)
        for j in range(nt_):
            t = q * 4 + j
            P.op("pe", lambda e, j=j, t=t: e.transpose(psb[:, j * 128:(j + 1) * 128], src[:, t * 128:(t + 1) * 128], self.ident_b[:]),
                 reads=srd + ["ident_b"], writes=[("ps", bank)])
        P.op("act", lambda e: e.activation(out=dst[:, q * 4:q * 4 + nt_, :], in_=psb[:, 0:nt_ * 128].rearrange("p (a b) -> p a b", b=128), func=AF.Copy),
             reads=[("ps", bank)], writes=[(u, nm, q)])

    def dn_proj_group(self, u, gi, t0, G, w, hg, raw, gb):
        P = self.P
        slot = gi % 2
        self.load_hg(hg, slot, t0, G, u)
        hgs = (hg[slot], (u, "hg", slot))
        for i in range(3):
            if i % 2 == 0:
                self.proj_fm((w[i], (u, "w", i)), hgs, G, lambda bank, i=i: P.op(
                    "act", lambda e: e.activation(out=raw[i][:, t0:t0 + G], in_=self.ps[bank][:, 0:G], func=AF.Copy),
                    reads=[("ps", bank)], writes=[(u, "raw", i, gi)]))
            else:
                self.proj_fm((w[i], (u, "w", i)), hgs, G, lambda bank, i=i: P.op(
                    "dve", lambda e: e.tensor_copy(out=raw[i][:, t0:t0 + G], in_=self.ps[bank][:, 0:G]),
                    reads=[("ps", bank)], writes=[(u, "raw", i, gi)]))
        self.proj_fm((w[3], (u, "w", 3)), hgs, G, lambda bank: P.op(
            "act", lambda e: e.activation(out=gb[:, t0:t0 + G], in_=self.ps[bank][:, 0:G], func=AF.Silu),
            reads=[("ps", bank)], writes=[(u, "gb", gi)]))

    def dn_conv(self, l, u, i, ci, raw, cv, ctmp):
        P = self.P
        rr = [(u, "raw", i, gi) for gi in range(NG)]
        wcol = lambda j: self.convT[:, l, j * 24 + ci:j * 24 + ci + 1]
        P.op("act", lambda e: e.activation(out=cv[:], in_=raw[:], func=AF.Identity, scale=wcol(2)),
             reads=rr + [("convT", l)], writes=[(u, "cv", i)])
        segs = [(raw[:, 0:NCTX].rearrange("p (r w) -> p r w", w=NCTX), cv[:, 0:NCTX].rearrange("p (r w) -> p r w", w=NCTX), NCTX),
                (raw[:, NCTX:NT].rearrange("p (r w) -> p r w", w=64), cv[:, NCTX:NT].rearrange("p (r w) -> p r w", w=64), 64)]
        for j in (0, 1, 3, 4):
            o = j - 2
            for (r3, c3, W) in segs:
                d0, d1 = max(0, -o), W - max(0, o)
                P.op("dve", lambda e, r3=r3, c3=c3, d0=d0, d1=d1, o=o, j=j: e.scalar_tensor_tensor(
                    out=c3[:, :, d0:d1], in0=r3[:, :, d0 + o:d1 + o], scalar=wcol(j), in1=c3[:, :, d0:d1], op0=ALU.mult, op1=ALU.add),
                    reads=[(u, "cv", i)], writes=[(u, "cv", i)])
        P.op("act", lambda e: e.activation(out=cv[:], in_=cv[:], func=AF.Silu), reads=[(u, "cv", i)], writes=[(u, "cv", i)])

    def dn_l2(self, u, i, gi, t0, G, cv, sq, rn, dst, nm, scl):
        P = self.P
        s = gi % 2
        bank = self.nb()
        P.op("act", lambda e: e.activation(out=sq[s][:, 0:G], in_=cv[:, t0:t0 + G], func=AF.Square), reads=[(u, "cv", i)], writes=[(u, "dsq", s)])
        P.op("pe", lambda e: e.matmul(self.ps[bank][:, 0:G], lhsT=self.ones_r[:], rhs=sq[s][:, 0:G], start=True, stop=True),
             reads=[(u, "dsq", s), "ones_r"], writes=[("ps", bank)])
        P.op("act", lambda e: e.activation(out=rn[s][:, 0:G], in_=self.ps[bank][:, 0:G], func=AF.Ln, bias=self.epsc[:, 0:1], scale=1.0),
             reads=[("ps", bank), "epsc"], writes=[(u, "drn", s)])
        P.op("act", lambda e: e.activation(out=rn[s][:, 0:G], in_=rn[s][:, 0:G], func=AF.Exp, scale=-0.5), reads=[(u, "drn", s)], writes=[(u, "drn", s)])
        P.op("dve", lambda e: e.scalar_tensor_tensor(out=dst[:, t0:t0 + G], in0=cv[:, t0:t0 + G], scalar=scl, in1=rn[s][:, 0:G], op0=ALU.mult, op1=ALU.mult),
             reads=[(u, "drn", s), (u, "cv", i)], writes=[(u, nm, gi)])

    def dn_dir(self, l, h, u0, d, QnT, KnT, Ktok, Vtok, oT):
        P, A = self.P, self.A
        u = self.new("dd")
        NTL = NT // 128
        NC = NT // 64
        col = d * 8 + h
        A.push()
        kbg = A.tile([128, NTL, 128], BF16, "kbg")
        vb = A.tile([128, NTL, 128], BF16, "vb")
        kdz = A.tile([128, NTL, 2, 128], BF16, "kdz")
        Lc = [A.tile([128, NTL, 128], BF16, "Lc") for _ in range(2)]
        Nc = [A.tile([128, NTL, 128], BF16, "Nc") for _ in range(2)]
        Rc = [A.tile([128, NTL, 128], BF16, "Rc") for _ in range(2)]
        qkT = A.tile([128, NTL, 128], BF16, "qkT")
        qgT = A.tile([128, NT], BF16, "qgT")
        nwT = A.tile([128, NT], BF16, "nwT")
        usb = A.tile([128, NTL, 128], F32, "usb")
        vn = A.tile([128, NC, 128], BF16, "vn")
        Sall = A.tile([128, NC, 128], BF16, "Sall")
        S32 = A.tile([128, 128], F32, "S32")
        dg3 = [A.tile([128, 384], F32, "dg3") for _ in range(2)]
        tA = [A.tile([128, 128], F32, "tA") for _ in range(2)]
        tB = [A.tile([128, 128], F32, "tB") for _ in range(2)]
        WA = [A.tile([128, 128], F32, "WA") for _ in range(2)]
        WBs = [A.tile([128, 128], F32, "WBs") for _ in range(2)]
        WBi = [A.tile([128, 128], F32, "WBi") for _ in range(2)]
        t2 = [A.tile([128, 128], F32, "t2") for _ in range(2)]
        mAs = self.bigP[0] if d == 0 else self.bigP[1]
        mBs = self.bigN[1] if d == 0 else self.bigN[0]
        mBi = self.bigN[2] if d == 0 else self.bigN[3]
        Kr = [(u0, "Ktok", q) for q in range(5)]
        Vr = [(u0, "Vtok", q) for q in range(5)]
        KnR = [(u0, "KnT", gi) for gi in range(NG)]
        QnR = [(u0, "QnT", gi) for gi in range(NG)]
        P.op("pool", lambda e: e.memset(kdz[:].rearrange("p a b c -> p (a b c)"), 0.0), writes=[(u, "kdz0")])
        P.op("pool", lambda e: e.memset(Sall[:, 0, :], 0.0), writes=[(u, "Sall", 0)])
        P.op("pool", lambda e: e.memset(S32[:], 0.0), writes=[(u, "S32")])
        for t in range(NTL):
            self.dn_prep_tile(u, u0, d, t, col, kbg, vb, kdz, Lc[0], Nc[0], Rc[0], qkT, qgT, dg3[t % 2], tA[t % 2], tB[t % 2], WA[t % 2], WBs[t % 2],
                              WBi[t % 2], t2[t % 2], mAs, mBs, mBi, Ktok, Vtok, KnT, QnT, Kr, Vr, KnR, QnR)
        cur = 0
        for lev in range(5):
            nxt = 1 - cur
            for q in range(5):
                self.dn_neumann(u, lev, q, Lc[cur], Nc[cur], Rc[cur], Lc[nxt], Nc[nxt], Rc[nxt])
            cur = nxt
        TT = Rc[cur]
        for q in range(5):
            self.dn_uw(u, q, TT, vb, kbg, usb, nwT)
        if "dn_stop2" in self.stages:
            self.dump_tile("dbg_TT", TT[:], [128, NTL, 128], [(u, "R", 5, q) for q in range(5)])
            self.dump_tile("dbg_usb", usb[:], [128, NTL, 128], [(u, "usb", q) for q in range(5)])
            self.dump_tile("dbg_nwT", nwT[:], [128, NT], [(u, "nwT", q) for q in range(5)])
            self.dump_tile("dbg_qkT", qkT[:], [128, NTL, 128], [(u, "qkT", t) for t in range(NTL)])
        order = sorted(range(NC), key=lambda c: self.chain_pos(d, c))
        for pos, c in enumerate(order):
            self.dn_chain_step(u, d, pos, c, col, nwT, usb, vn, kdz, Sall, S32)
        for q in range(5):
            self.dn_out(u, u0, d, q, Sall, qgT, vn, qkT, oT)
        A.pop()
        P.barrier()

    def dn_prep_tile(self, u, u0, d, t, col, kbg, vb, kdz, L0, N0, R0, qkT, qgT, dg3, tA, tB, WA, WBs, WBi, t2, mAs, mBs, mBi,
                     Ktok, Vtok, KnT, QnT, Kr, Vr, KnR, QnR):
        P = self.P
        s = t % 2
        beta = self.dn_beta[:, t, col:col + 1]
        gc = self.dn_gc[:, t, col:col + 1]
        egc = self.dn_egc[:, t, col:col + 1]
        bg = self.dn_bg[:, t, col:col + 1]
        sc_r = ["dn_beta", ("dn_gc", t), ("dn_egc", t), ("dn_egl", t), ("dn_bg", t)]
        P.op("act", lambda e: e.activation(out=kbg[:, t, :], in_=Ktok[:, t, :], func=AF.Identity, scale=bg),
             reads=Kr + sc_r, writes=[(u, "kbg", t)])
        P.op("act", lambda e: e.activation(out=vb[:, t, :], in_=Vtok[:, t, :], func=AF.Identity, scale=beta),
             reads=Vr + sc_r, writes=[(u, "vb", t)])
        for hf in range(2):
            r0, r1 = hf * 64, hf * 64 + 64
            P.op("act", lambda e, hf=hf, r0=r0, r1=r1: e.activation(out=kdz[r0:r1, t, hf, :], in_=Ktok[r0:r1, t, :], func=AF.Identity,
                                                                 scale=self.dn_egl[r0:r1, t, col:col + 1]),
                 reads=Kr + sc_r + [(u, "kdz0")], writes=[(u, "kdz", t, hf)])
        for i, sc in enumerate((gc, beta, egc)):
            if i == 1:
                P.op("act", lambda e, i=i, sc=sc: e.activation(out=dg3[:, i * 128:(i + 1) * 128], in_=self.ident[:], func=AF.Identity, scale=sc),
                     reads=["ident"] + sc_r, writes=[(u, "dg3", s, i)])
            else:
                P.op("dve", lambda e, i=i, sc=sc: e.tensor_scalar(out=dg3[:, i * 128:(i + 1) * 128], in0=self.ident[:], scalar1=sc, scalar2=None, op0=ALU.mult),
                     reads=["ident"] + sc_r, writes=[(u, "dg3", s, i)])
        bR = self.nb()
        P.op("pe", lambda e: e.matmul(self.ps[bR][:, 0:384], lhsT=self.ones_f[:], rhs=dg3[:, :], start=True, stop=True),
             reads=[(u, "dg3", s, i) for i in range(3)] + ["ones_f"], writes=[("ps", bR)])
        RB = self.ps[bR][:, 0:128]
        RBb = self.ps[bR][:, 128:256]
        RBe = self.ps[bR][:, 256:384]
        bG = self.nb()
        ts = slice(t * 128, (t + 1) * 128)
        P.op("pe", lambda e: e.matmul(self.ps[bG][:, 0:128], lhsT=KnT[:, ts], rhs=KnT[:, ts], start=True, stop=True),
             reads=KnR, writes=[("ps", bG)])
        P.op("pe", lambda e: e.matmul(self.ps[bG][:, 128:256], lhsT=KnT[:, ts], rhs=QnT[:, ts], start=True, stop=True),
             reads=KnR + QnR, writes=[("ps", bG)])
        Gm = self.ps[bG][:, 0:128]
        QK = self.ps[bG][:, 128:256]
        P.op("dve", lambda e: e.scalar_tensor_tensor(out=WA[:], in0=RB, scalar=gc, in1=mAs[:], op0=ALU.subtract, op1=ALU.max),
             reads=[("ps", bR), "bigm"] + sc_r, writes=[(u, "WA", s)])
        P.op("act", lambda e: e.activation(out=WA[:], in_=WA[:], func=AF.Exp, scale=-1.0), reads=[(u, "WA", s)], writes=[(u, "WA", s)])
        P.op("dve", lambda e: e.scalar_tensor_tensor(out=WBs[:], in0=RB, scalar=gc, in1=mBs[:], op0=ALU.subtract, op1=ALU.min),
             reads=[("ps", bR), "bigm"] + sc_r, writes=[(u, "WBs", s)])
        P.op("act", lambda e: e.activation(out=WBs[:], in_=WBs[:], func=AF.Exp), reads=[(u, "WBs", s)], writes=[(u, "WBs", s)])
        P.op("dve", lambda e: e.scalar_tensor_tensor(out=WBi[:], in0=RB, scalar=gc, in1=mBi[:], op0=ALU.subtract, op1=ALU.min),
             reads=[("ps", bR), "bigm"] + sc_r, writes=[(u, "WBi", s)])
        P.op("act", lambda e: e.activation(out=WBi[:], in_=WBi[:], func=AF.Exp), reads=[(u, "WBi", s)], writes=[(u, "WBi", s)])
        P.op("dve", lambda e: e.scalar_tensor_tensor(out=L0[:, t, :], in0=Gm, scalar=beta, in1=WA[:], op0=ALU.mult, op1=ALU.mult),
             reads=[("ps", bG), (u, "WA", s)] + sc_r, writes=[(u, "L", 0, t)])
        P.op("dve", lambda e: e.tensor_tensor(out=t2[:], in0=RBb, in1=WBs[:], op=ALU.mult), reads=[("ps", bR), (u, "WBs", s)], writes=[(u, "t2", s)])
        P.op("dve", lambda e: e.tensor_tensor(out=N0[:, t, :], in0=Gm, in1=t2[:], op=ALU.mult), reads=[("ps", bG), (u, "t2", s)], writes=[(u, "N", 0, t)])
        P.op("dve", lambda e: e.scalar_tensor_tensor(out=R0[:, t, :], in0=N0[:, t, :], scalar=-1.0, in1=self.ident[:], op0=ALU.mult, op1=ALU.add),
             reads=[(u, "N", 0, t), "ident"], writes=[(u, "R", 0, t)])
        P.op("dve", lambda e: e.tensor_tensor(out=qkT[:, t, :], in0=QK, in1=WBi[:], op=ALU.mult), reads=[("ps", bG), (u, "WBi", s)], writes=[(u, "qkT", t)])
        P.op("dve", lambda e: e.tensor_tensor(out=qgT[:, ts], in0=RBe, in1=QnT[:, ts], op=ALU.mult), reads=[("ps", bR)] + QnR, writes=[(u, "qgT", t)])

    def dn_neumann(self, u, lev, q, Lc, Nc, Rc, Ln, Nn, Rn):
        P = self.P
        nt_ = 4 if q < 4 else 2
        tiles = [q * 4 + j for j in range(nt_)]
        last = lev == 4

        def rd(nm, t):
            return [(u, nm, lev, t)] if lev == 0 else [(u, nm, lev, t // 4)]
        bL = self.nb()
        for j, t in enumerate(tiles):
            P.op("pe", lambda e, j=j, t=t: e.matmul(self.ps[bL][:, j * 128:(j + 1) * 128], lhsT=Nc[:, t, :], rhs=Lc[:, t, :], start=True, stop=True),
                 reads=rd("N", t) + rd("L", t), writes=[("ps", bL)])
        P.op("act", lambda e: e.activation(out=Ln[:, q * 4:q * 4 + nt_, :], in_=self.ps[bL][:, 0:nt_ * 128].rearrange("p (a b) -> p a b", b=128), func=AF.Copy),
             reads=[("ps", bL)], writes=[(u, "L", lev + 1, q)])
        if not last:
            bN = self.nb()
            for j, t in enumerate(tiles):
                P.op("pe", lambda e, j=j, t=t: e.matmul(self.ps[bN][:, j * 128:(j + 1) * 128], lhsT=Lc[:, t, :], rhs=Nc[:, t, :], start=True, stop=True),
                     reads=rd("N", t) + rd("L", t), writes=[("ps", bN)])
            P.op("dve", lambda e: e.tensor_copy(out=Nn[:, q * 4:q * 4 + nt_, :], in_=self.ps[bN][:, 0:nt_ * 128].rearrange("p (a b) -> p a b", b=128)),
                 reads=[("ps", bN)], writes=[(u, "N", lev + 1, q)])
        bR = self.nb()
        for j, t in enumerate(tiles):
            P.op("pe", lambda e, j=j, t=t: e.matmul(self.ps[bR][:, j * 128:(j + 1) * 128], lhsT=self.ident_b[:], rhs=Rc[:, t, :], start=True, stop=False),
                 reads=rd("R", t) + ["ident_b"], writes=[("ps", bR)])
            P.op("pe", lambda e, j=j, t=t: e.matmul(self.ps[bR][:, j * 128:(j + 1) * 128], lhsT=Ln[:, t, :], rhs=Rc[:, t, :], start=False, stop=True),
                 reads=rd("R", t) + [(u, "L", lev + 1, q)], writes=[("ps", bR)])
        eng = "dve" if q % 2 == 0 else "act"
        if eng == "dve":
            P.op("dve", lambda e: e.tensor_copy(out=Rn[:, q * 4:q * 4 + nt_, :], in_=self.ps[bR][:, 0:nt_ * 128].rearrange("p (a b) -> p a b", b=128)),
                 reads=[("ps", bR)], writes=[(u, "R", lev + 1, q)])
        else:
            P.op("act", lambda e: e.activation(out=Rn[:, q * 4:q * 4 + nt_, :], in_=self.ps[bR][:, 0:nt_ * 128].rearrange("p (a b) -> p a b", b=128), func=AF.Copy),
                 reads=[("ps", bR)], writes=[(u, "R", lev + 1, q)])

    def dn_uw(self, u, q, TT, vb, kbg, usb, nwT):
        P = self.P
        nt_ = 4 if q < 4 else 2
        tiles = [q * 4 + j for j in range(nt_)]
        bU = self.nb()
        for j, t in enumerate(tiles):
            P.op("pe", lambda e, j=j, t=t: e.matmul(self.ps[bU][:, j * 128:(j + 1) * 128], lhsT=TT[:, t, :], rhs=vb[:, t, :], start=True, stop=True),
                 reads=[(u, "R", 5, q), (u, "vb", t)], writes=[("ps", bU)])
        P.op("dve", lambda e: e.tensor_copy(out=usb[:, q * 4:q * 4 + nt_, :], in_=self.ps[bU][:, 0:nt_ * 128].rearrange("p (a b) -> p a b", b=128)),
             reads=[("ps", bU)], writes=[(u, "usb", q)])
        bW = self.nb()
        for j, t in enumerate(tiles):
            P.op("pe", lambda e, j=j, t=t: e.matmul(self.ps[bW][:, j * 128:(j + 1) * 128], lhsT=kbg[:, t, :], rhs=TT[:, t, :], start=True, stop=True),
                 reads=[(u, "R", 5, q), (u, "kbg", t)], writes=[("ps", bW)])
        P.op("act", lambda e: e.activation(out=nwT[:, q * 512:q * 512 + nt_ * 128], in_=self.ps[bW][:, 0:nt_ * 128], func=AF.Copy, scale=-1.0),
             reads=[("ps", bW)], writes=[(u, "nwT", q)])

    def dn_chain_step(self, u, d, pos, c, col, nwT, usb, vn, kdz, Sall, S32):
        P = self.P
        t, hf = c // 2, c % 2
        b1 = self.nb()
        P.op("pe", lambda e: e.matmul(self.ps[b1][:, 0:128], lhsT=nwT[:, t * 128:(t + 1) * 128], rhs=Sall[:, pos, :], start=True, stop=True),
             reads=[(u, "nwT", t // 4), (u, "Sall", pos)], writes=[("ps", b1)])
        P.op("dve", lambda e: e.tensor_tensor(out=vn[:, c, :], in0=self.ps[b1][:, 0:128], in1=usb[:, t, :], op=ALU.add),
             reads=[("ps", b1), (u, "usb", t // 4)], writes=[(u, "vn", c)])
        b2 = self.nb()
        P.op("pe", lambda e: e.matmul(self.ps[b2][:, 0:128], lhsT=kdz[:, t, hf, :], rhs=vn[:, c, :], start=True, stop=True),
             reads=[(u, "kdz", t, hf), (u, "kdz0"), (u, "vn", c)], writes=[("ps", b2)])
        if pos + 1 < NT // 64:
            P.op("dve", lambda e: e.scalar_tensor_tensor(out=Sall[:, pos + 1, :], in0=S32[:], scalar=self.dn_dl[:, t, hf, col:col + 1], in1=self.ps[b2][:, 0:128],
                                                         op0=ALU.mult, op1=ALU.add),
                 reads=[("ps", b2), (u, "S32"), ("dn_dl", t)], writes=[(u, "Sall", pos + 1)])
            P.op("dve", lambda e: e.scalar_tensor_tensor(out=S32[:], in0=S32[:], scalar=self.dn_dl[:, t, hf, col:col + 1], in1=self.ps[b2][:, 0:128],
                                                         op0=ALU.mult, op1=ALU.add),
                 reads=[("ps", b2), (u, "S32"), ("dn_dl", t), (u, "Sall", pos + 1)], writes=[(u, "S32")])

    def dn_out(self, u, u0, d, q, Sall, qgT, vn, qkT, oT):
        P = self.P
        nt_ = 4 if q < 4 else 2
        bank = self.nb()
        for j in range(nt_):
            t = q * 4 + j
            for hf in range(2):
                c = 2 * t + hf
                pos = self.chain_pos(d, c)
                cs = slice(j * 128 + hf * 64, j * 128 + hf * 64 + 64)
                P.op("pe", lambda e, c=c, pos=pos, cs=cs: e.matmul(self.ps[bank][:, cs], lhsT=Sall[:, pos, :], rhs=qgT[:, c * 64:(c + 1) * 64], start=True, stop=False),
                     reads=[(u, "Sall", pos), (u, "qgT", t)], writes=[("ps", bank)])
                P.op("pe", lambda e, c=c, t=t, hf=hf, cs=cs: e.matmul(self.ps[bank][:, cs], lhsT=vn[:, c, :], rhs=qkT[:, t, hf * 64:hf * 64 + 64], start=False, stop=True),
                     reads=[(u, "vn", c), (u, "qkT", t)], writes=[("ps", bank)])
        c0_, c1_ = q * 512, q * 512 + nt_ * 128
        if d == 0:
            P.op("act", lambda e: e.activation(out=oT[:, c0_:c1_], in_=self.ps[bank][:, 0:nt_ * 128], func=AF.Copy),
                 reads=[("ps", bank)], writes=[(u0, "oT", q)])
        else:
            P.op("dve", lambda e: e.tensor_tensor(out=oT[:, c0_:c1_], in0=self.ps[bank][:, 0:nt_ * 128], in1=oT[:, c0_:c1_], op=ALU.add),
                 reads=[("ps", bank), (u0, "oT", q)], writes=[(u0, "oT", q)])

    def merge(self, l):
        A = self.A
        A.push()
        wa = [A.tile([128, 8, 128], BF16, "wa") for _ in range(2)]
        wb = [A.tile([128, 8, 128], BF16, "wb") for _ in range(2)]
        wga = [A.tile([128, NKC, 128], BF16, "wga") for _ in range(2)]
        wgb = [A.tile([128, NKC, 128], BF16, "wgb") for _ in range(2)]
        wo = [A.tile([128, NKC, 128], BF16, "wo") for _ in range(2)]
        for (t0, G, v) in self.groups():
            self.merge_group(l, t0, G, v, wa, wb, wga, wgb, wo)
        A.pop()

    def merge_group(self, l, t0, G, v, wa, wb, wga, wgb, wo):
        P, A = self.P, self.A
        u = self.new("mg")
        nh = (G + 511) // 512
        hw = G // nh
        A.push()
        yaT = A.tile([128, 8, G], BF16, "yaT")
        ybT = A.tile([128, 8, G], BF16, "ybT")
        hT = A.tile([128, NKC, G], BF16, "hTg")
        yT = A.tile([128, NKC, G], BF16, "yT")
        sga = [A.tile([128, hw], F32, "sga") for _ in range(2)]
        sgb = [A.tile([128, hw], F32, "sgb") for _ in range(2)]
        xc = [A.tile([128, G], F32, "xm") for _ in range(3)]
        ya1 = A.tile([128, 8, G], BF16, "ya1")
        yb1 = A.tile([128, 8, G], BF16, "yb1")
        for (ab, dst0, dst1, nm) in ((0, yaT, ya1, "yaT"), (1, ybT, yb1, "ybT")):
            for r in range(2):
                for hp in range(2):
                    k = ab * 2 + hp
                    h0 = r * 4 + hp * 2
                    P.op("sp", lambda e, dst0=dst0, r=r, k=k, h0=h0: e.dma_start(
                        out=dst0[:, h0:h0 + 2, :], in_=self.y_all_d[k, r][:, :, t0:t0 + G].rearrange("h p n -> p h n")),
                        reads=[("yall", k)], writes=[(u, nm, 0, r, hp)], lane=f"mg{ab}a")
                    P.op("sp", lambda e, dst1=dst1, r=r, k=k, h0=h0: e.dma_start(
                        out=dst1[:, h0:h0 + 2, :], in_=self.y_all_d[k, r][:, :, NTO + t0:NTO + t0 + G].rearrange("h p n -> p h n")),
                        reads=[("yall", k)], writes=[(u, nm, 1, r, hp)], lane=f"mg{ab}b")
            P.op("dve", lambda e, dst0=dst0: e.tensor_scalar(out=dst0[:], in0=dst0[:], scalar1=self.selT[:, 0:1], scalar2=None, op0=ALU.mult),
                 reads=[(u, nm, 0, r, hp) for r in range(2) for hp in range(2)] + ["selT"], writes=[(u, nm)])
            P.op("dve", lambda e, dst0=dst0, dst1=dst1: e.scalar_tensor_tensor(out=dst0[:], in0=dst1[:], scalar=self.selT[:, 1:2], in1=dst0[:],
                                                                               op0=ALU.mult, op1=ALU.add),
                 reads=[(u, nm)] + [(u, nm, 1, r, hp) for r in range(2) for hp in range(2)] + ["selT"], writes=[(u, nm)])
        P.op("sp", lambda e: e.dma_start(out=hT[:], in_=self.hT_d[:, :, t0:t0 + G].rearrange("c p n -> p c n")),
             reads=[("hTd", t) for t in range(t0 // 128, (t0 + G) // 128)], writes=[(u, "hT")], lane="mgh")
        wav = self.inp["w_branch_a"][l].rearrange("(hd p) n -> p hd n", p=128)
        wbv = self.inp["w_branch_b"][l].rearrange("(hd p) n -> p hd n", p=128)
        wiv = self.inp["w_in"][l].rearrange("(kc p) n -> p kc n", p=128)
        wov = self.inp["w_out"][l].rearrange("(kc p) n -> p kc n", p=128)

        def load1(n):
            s = n % 2
            cs = slice(n * 128, (n + 1) * 128)
            P.op("pool", lambda e: e.dma_start(out=wa[s][:], in_=wav[:, :, cs]), writes=[("wa", s)], lane=f"wa{s}")
            P.op("pool", lambda e: e.dma_start(out=wb[s][:], in_=wbv[:, :, cs]), writes=[("wb", s)], lane=f"wb{s}")
            P.op("pool", lambda e: e.dma_start(out=wga[s][:], in_=wiv[:, :, 9248 + n * 128:9248 + (n + 1) * 128]), writes=[("wga", s)], lane=f"wga{s}")
            P.op("pool", lambda e: e.dma_start(out=wgb[s][:], in_=wiv[:, :, 11296 + n * 128:11296 + (n + 1) * 128]), writes=[("wgb", s)], lane=f"wgb{s}")

        def load2(n):
            s = n % 2
            P.op("pool", lambda e: e.dma_start(out=wo[s][:], in_=wov[:, :, n * 128:(n + 1) * 128]), writes=[("wo", s)], lane=f"wo{s}")

        def acc(wt, wtok, nk, rhs_t, rtok, h):
            bank = self.nb()
            for k in range(nk):
                P.op("pe", lambda e, k=k: e.matmul(self.ps[bank][:, 0:hw], lhsT=wt[:, k, :], rhs=rhs_t[:, k, h * hw:(h + 1) * hw],
                                                   start=(k == 0), stop=(k == nk - 1)),
                     reads=[wtok] + rtok, writes=[("ps", bank)])
            return bank

        load1(0)
        for n in range(NKC):
            if n + 1 < NKC:
                load1(n + 1)
            else:
                load2(0)
            s = n % 2
            for h in range(nh):
                i = h % 2
                bga = acc(wga[s], ("wga", s), NKC, hT, [(u, "hT")], h)
                P.op("act", lambda e, i=i, bga=bga: e.activation(out=sga[i][:], in_=self.ps[bga][:, 0:hw], func=AF.Sigmoid),
                     reads=[("ps", bga)], writes=[(u, "sga", i)])
                bgb = acc(wgb[s], ("wgb", s), NKC, hT, [(u, "hT")], h)
                P.op("act", lambda e, i=i, bgb=bgb: e.activation(out=sgb[i][:], in_=self.ps[bgb][:, 0:hw], func=AF.Sigmoid),
                     reads=[("ps", bgb)], writes=[(u, "sgb", i)])
                ba = acc(wa[s], ("wa", s), 8, yaT, [(u, "yaT")], h)
                P.op("dve", lambda e, i=i, ba=ba: e.tensor_tensor(out=sga[i][:], in0=sga[i][:], in1=self.ps[ba][:, 0:hw], op=ALU.mult),
                     reads=[("ps", ba), (u, "sga", i)], writes=[(u, "sga", i)])
                bb = acc(wb[s], ("wb", s), 8, ybT, [(u, "ybT")], h)
                P.op("dve", lambda e, i=i, bb=bb: e.tensor_tensor(out=sgb[i][:], in0=sgb[i][:], in1=self.ps[bb][:, 0:hw], op=ALU.mult),
                     reads=[("ps", bb), (u, "sgb", i)], writes=[(u, "sgb", i)])
                P.op("pool", lambda e, i=i, n=n, h=h: e.tensor_tensor(out=yT[:, n, h * hw:(h + 1) * hw], in0=sga[i][:], in1=sgb[i][:], op=ALU.add),
                     reads=[(u, "sga", i), (u, "sgb", i)], writes=[(u, "yT", n, h)])
        for n in range(NKC):
            if n + 1 < NKC:
                load2(n + 1)
            s = n % 2
            xs = n % 3
            P.op("sp", lambda e, xs=xs, n=n: e.dma_start(out=xc[xs][:], in_=self.xT_d[n, :, t0:t0 + G]),
                 reads=[("xT", n, t) for t in range(t0 // 128, (t0 + G) // 128)], writes=[(u, "xm", xs)], lane=f"xe{xs}")
            for h in range(nh):
                bo = acc(wo[s], ("wo", s), NKC, yT, [(u, "yT", k, h) for k in range(NKC)], h)
                P.op("dve", lambda e, xs=xs, h=h, bo=bo, n=n: e.scalar_tensor_tensor(
                    out=xc[xs][:, h * hw:(h + 1) * hw], in0=self.ps[bo][:, 0:hw], scalar=self.gate[:, l, 1, v, n:n + 1],
                    in1=xc[xs][:, h * hw:(h + 1) * hw], op0=ALU.mult, op1=ALU.add),
                    reads=[("ps", bo), (u, "xm", xs)], writes=[(u, "xm", xs)])
            P.op("sp", lambda e, xs=xs, n=n: e.dma_start(out=self.xT_d[n, :, t0:t0 + G], in_=xc[xs][:]),
                 reads=[(u, "xm", xs)], writes=[("xT", n, t) for t in range(t0 // 128, (t0 + G) // 128)], lane=f"xs{xs}")
        A.pop()
        P.barrier()

    def final(self):
        P, A = self.P, self.A
        A.push()
        hT = A.tile([128, NKC, 1024], F32, "hTf")
        to = [A.tile([128, D], F32, "to") for _ in range(2)]
        for gi, (t0, G, v) in enumerate(self.groups()):
            u = self.new("fin")
            self.prologue(t0, G, self.gT[:, 96:112], None, hT, (u, "hT"))
            for t in range(G // 128):
                s = t % 2
                for q in range(4):
                    bank = (t % 2) * 4 + q
                    for j in range(4):
                        c = q * 4 + j
                        P.op("pe", lambda e, c=c, bank=bank, j=j, t=t: e.transpose(
                            self.ps[bank][:, j * 128:(j + 1) * 128], hT[:, c, t * 128:(t + 1) * 128], self.ident[:]),
                            reads=[((u, "hT"), c), "ident"], writes=[("ps", bank)])
                    if q % 2 == 0:
                        P.op("act", lambda e, s=s, q=q, bank=bank: e.activation(out=to[s][:, q * 512:(q + 1) * 512], in_=self.ps[bank][:], func=AF.Copy),
                             reads=[("ps", bank)], writes=[("to", s, q)])
                    else:
                        P.op("dve", lambda e, s=s, q=q, bank=bank: e.tensor_copy(out=to[s][:, q * 512:(q + 1) * 512], in_=self.ps[bank][:]),
                             reads=[("ps", bank)], writes=[("to", s, q)])
                row = t0 + t * 128
                P.op("sp", lambda e, s=s, row=row: e.dma_start(out=self.out[row:row + 128, :], in_=to[s][:]),
                     reads=[("to", s, q) for q in range(4)], lane=f"to{s}")
        A.pop()
        P.barrier()

    def dump_xT(self, name):
        d = self.nc.dram_tensor(name, [NKC, 128, NTO], F32, kind="ExternalOutput").ap()
        self.dbg_out[name] = d
        for c in range(NKC):
            self.P.op("sp", lambda e, c=c: e.dma_start(out=d[c], in_=self.xT_d[c]),
                      reads=[("xT", c, t) for t in range(NTO // 128)], lane="dump")

    def dump_tile(self, name, ap, shape, reads):
        d = self.nc.dram_tensor(name, list(shape), ap.dtype, kind="ExternalOutput").ap()
        self.dbg_out[name] = d
        self.P.op("sp", lambda e: e.dma_start(out=d, in_=ap), reads=reads, lane="dump")

    def build(self):
        st = self.stages
        self.consts()
        self.P.barrier()
        self.phase_in()
        if "dump_in" in st:
            self.dump_xT("dbg_xin")
        self.phase_adaln()
        if "dump_mod" in st:
            for l in range(2):
                self.dump_tile(f"dbg_modT{l}", self.modT[l][:], [128, 144, 2], [("modT", l)])
            self.dump_tile("dbg_gT", self.gT[:], [128, 112], ["gT0", "gT1"])
            self.dump_tile("dbg_gmod", self.gmod[:], [128, 2, 3, 2, NKC], [])
            self.dump_tile("dbg_gate", self.gate[:], [128, 2, 3, 2, NKC], [])
        self.phase_small()
        for l in range(2):
            if ("ffn", l, 0) in st:
                self.ffn1p(l, 0)
            if ("dump", l, 0) in st:
                self.dump_xT(f"dbg_x_ffn1_{l}")
            if ("mix", l) in st:
                self.mixer_prep(l)
                for h in self.heads:
                    if ("hgrn", l) in st:
                        self.hgrn_head(l, h)
                if ("dn", l) in st:
                    self.A.push()
                    self.dn_scalars(l)
                    for h in self.heads:
                        self.dn_head(l, h)
                    self.A.pop()
                    self.P.barrier()
                if ("merge", l) in st:
                    for k in range(4):
                        ab, hp = k // 2, k % 2
                        self.P.op("pool", lambda e, k=k, ab=ab, hp=hp: e.collective_compute(
                            "AllGather", ALU.bypass, replica_groups=[[0, 1], [2, 3], [4, 5], [6, 7]],
                            ins=[self.y_own_d[ab, 2 * hp:2 * hp + 2].rearrange("h p n -> (h p) n")],
                            outs=[self.y_all_d[k].rearrange("r h p n -> (r h p) n")]),
                            reads=[(("ya", h), gi) for h in range(4) for gi in range(NG)] + [(("yb", h), gi) for h in range(4) for gi in range(NG)],
                            writes=[("yall", k)], lane="cc_y", step=1)
                    self.P.barrier()
                    self.merge(l)
                if ("dumpmix", l) in st:
                    self.dump_xT(f"dbg_x_mix_{l}")
            if ("ffn", l, 1) in st:
                self.ffn1p(l, 1)
        self.final()
        with ExitStack() as es:
            sems = {k: es.enter_context(self.nc.semaphore("s_" + k)) for k in list(Prog.ENG) + self.P.lanes()}
            self.P.emit(sems)
        return self.nc


FULL = {("ffn", 0, 0), ("ffn", 0, 1), ("ffn", 1, 0), ("ffn", 1, 1), ("mix", 0), ("mix", 1), ("hgrn", 0), ("hgrn", 1),
        ("dn", 0), ("dn", 1), ("merge", 0), ("merge", 1)}


def _swap_heads(a, axis, seg0, nseg, seglen):
    a = np.array(a, copy=True)
    idx = [slice(None)] * a.ndim
    for i in range(nseg):
        lo = seg0 + i * seglen
        h = seglen // 2
        i0 = list(idx); i0[axis] = slice(lo, lo + h)
        i1 = list(idx); i1[axis] = slice(lo + h, lo + seglen)
        tmp = a[tuple(i0)].copy()
        a[tuple(i0)] = a[tuple(i1)]
        a[tuple(i1)] = tmp
    return a


def make_in_maps(inputs, ncores=8):
    inp = {k: np.asarray(v) for k, v in inputs.items()}
    shared = {k: np.ascontiguousarray(inp[k]) for k in ("norm_g", "ffn_w_gate", "ffn_w_up", "ffn_w_down",
                                                         "hgrn_norm_g", "dn_norm_g", "w_branch_a", "w_branch_b", "w_out")}
    shared["final_norm_g"] = np.ascontiguousarray(inp["final_norm_g"].reshape(1, -1))
    per_rank = []
    for r in range(2):
        if r == 0:
            pr = dict(w_in=np.ascontiguousarray(inp["w_in"]), hgrn_lower_bounds=np.ascontiguousarray(inp["hgrn_lower_bounds"]),
                      dn_conv_w=np.ascontiguousarray(inp["dn_conv_w"]), dn_a_log=np.ascontiguousarray(inp["dn_a_log"]),
                      dn_dt_bias=np.ascontiguousarray(inp["dn_dt_bias"]))
        else:
            w = _swap_heads(inp["w_in"], 2, 0, 9, 1024)
            w = _swap_heads(w, 2, 9216, 4, 8)
            pr = dict(w_in=np.ascontiguousarray(w),
                      hgrn_lower_bounds=np.ascontiguousarray(_swap_heads(inp["hgrn_lower_bounds"], 2, 0, 1, 1024)),
                      dn_conv_w=np.ascontiguousarray(_swap_heads(inp["dn_conv_w"], 2, 0, 3, 1024)),
                      dn_a_log=np.ascontiguousarray(_swap_heads(inp["dn_a_log"], 2, 0, 1, 8)),
                      dn_dt_bias=np.ascontiguousarray(_swap_heads(inp["dn_dt_bias"], 2, 0, 1, 8)))
        pr["w_ada"] = np.ascontiguousarray(inp["w_ada"][:, :, r * 9216:(r + 1) * 9216])
        pr["b_ada"] = np.ascontiguousarray(inp["b_ada"][:, r * 9216:(r + 1) * 9216])
        sel = np.zeros((128, 2), np.float32)
        sel[:, r] = 1.0
        pr["sel"] = sel
        per_rank.append(pr)
    maps = []
    for k in range(ncores):
        b, r = k // 2, k % 2
        m = dict(shared)
        m.update(per_rank[r])
        if r == 0:
            m["x"] = np.ascontiguousarray(np.concatenate([inp["ctx"][b], inp["x"][b][:NTO - NCTX]], axis=0))
        else:
            m["x"] = np.ascontiguousarray(inp["x"][b][NTO - NCTX:])
        m["c_ctx"] = np.ascontiguousarray(inp["c_ctx"].reshape(1, -1))
        m["c"] = np.ascontiguousarray(inp["c"][b:b + 1])
        maps.append(m)
    return maps


def kernel(**inputs):
    bld = Builder(FULL)
    nc = bld.build()
    maps = make_in_maps(inputs, 8)
    res = run_bass_kernel_spmd(nc, maps, core_ids=list(range(8)))
    out = np.zeros((4, 2048, D), np.float32)
    for k in range(8):
        b, r = k // 2, k % 2
        o = np.asarray(res.results[k]["out"])
        if r == 0:
            out[b, :NTO - NCTX] = o[NCTX:]
        else:
            out[b, NTO - NCTX:] = o
    return out
```
